# Optimizing a Trainium2 kernel written in Bass

```python
import math
import jax, jax.numpy as jnp
from jax import lax
import numpy as np

D_MODEL = 1024
BATCH = 1
SEQ = 16384
DEPTH = 2
DEC_BATCH = 8
DEC_SEQ = 8192
PAST_LEN = 128

HEAD_DIM = 64
GRID_W = 64
PLE_DIM = 256
EPS = 1e-6
BLOCK = 128
A_HEADS = 8
A_KV = 2
A_WINDOW = 128
B_HEADS = 8
NB_ROWS_MAX = 8
NB_COLS = 16
NB_QCOLS = 16
NB_KCOLS = 32
C_HEADS = 8
C_KV = 2
ROPE_THETA = 10000.0
D_HEADS = 4
D_VDIM = 2 * HEAD_DIM

A_W = A_HEADS * HEAD_DIM
B_W = B_HEADS * HEAD_DIM
C_W = C_HEADS * HEAD_DIM
D_W = D_HEADS * D_VDIM
D_QK = D_HEADS * 2 * HEAD_DIM
MIX_W = A_W + B_W
AB_SIZES = [A_W, A_KV * HEAD_DIM, A_KV * HEAD_DIM, A_W, B_W, B_W, B_W, B_W]
CD_SIZES = [C_W, C_KV * HEAD_DIM, C_KV * HEAD_DIM, C_W, D_QK, D_QK, D_W, D_W]
AB_WIDTH = sum(AB_SIZES)
CD_WIDTH = sum(CD_SIZES)
AB_SPLITS = [int(v) for v in np.cumsum(AB_SIZES)[:-1]]
CD_SPLITS = [int(v) for v in np.cumsum(CD_SIZES)[:-1]]
N_EVEN = (DEPTH + 1) // 2
N_ODD = DEPTH // 2

kernel_name = "hybrid_window_neighbourhood_axial_diff_encoder"


def rmsnorm(x, g):
    xf = x.astype(jnp.float32)
    y = xf * lax.rsqrt(jnp.mean(xf * xf, axis=-1, keepdims=True) + EPS)
    return (y * g.astype(jnp.float32)).astype(x.dtype)


def alibi_slopes(n):
    return (2.0 ** (-8.0 * np.arange(1, n + 1) / n)).astype(np.float32)


def lambda_init_for(layer):
    return 0.8 - 0.6 * math.exp(-0.3 * layer)


def window_attention(q, k, v, sink):
    B, S, H, d = q.shape
    KV = k.shape[2]
    G = H // KV
    nb = S // BLOCK
    span = 3 * BLOCK
    kp = jnp.pad(k, ((0, 0), (BLOCK, BLOCK), (0, 0), (0, 0)))
    vp = jnp.pad(v, ((0, 0), (BLOCK, BLOCK), (0, 0), (0, 0)))
    qb = q.reshape(B, nb, BLOCK, KV, G, d).transpose(1, 0, 2, 3, 4, 5)
    rel = np.arange(span)[None, :] - BLOCK - np.arange(BLOCK)[:, None]
    band = jnp.asarray(np.abs(rel) <= A_WINDOW)
    alibi = jnp.asarray(-alibi_slopes(H).reshape(KV, G, 1, 1) * np.abs(rel).astype(np.float32))
    sink32 = sink.astype(jnp.float32).reshape(KV, G)[:, :, None]
    scale = HEAD_DIM ** -0.5

    def one_block(args):
        q_blk, n = args
        k_blk = lax.dynamic_slice_in_dim(kp, n * BLOCK, span, axis=1)
        v_blk = lax.dynamic_slice_in_dim(vp, n * BLOCK, span, axis=1)
        kpos = n * BLOCK - BLOCK + jnp.arange(span)
        ok = band & ((kpos >= 0) & (kpos < S))[None, :]
        s = jnp.einsum('bqhgd,bchd->bhgqc', q_blk, k_blk).astype(jnp.float32) * scale + alibi
        s = jnp.where(ok, s, -jnp.inf)
        m = jnp.maximum(jnp.max(s, axis=-1), sink32)
        e = jnp.exp(s - m[..., None])
        denom = jnp.sum(e, axis=-1) + jnp.exp(sink32 - m)
        w = (e / denom[..., None]).astype(v.dtype)
        return jnp.einsum('bhgqc,bchd->bqhgd', w, v_blk)

    out = lax.map(one_block, (qb, jnp.arange(nb)))
    return out.transpose(1, 0, 2, 3, 4, 5).reshape(B, S, H, d)


def neighbourhood_attention(q, k, v, rpb):
    B, S, H, d = q.shape
    rows = S // GRID_W
    kr = min(NB_ROWS_MAX, rows)
    n_cb = GRID_W // NB_QCOLS
    r = np.arange(rows)
    row_start = np.clip(r - kr // 2, 0, rows - kr)
    qcol = np.arange(GRID_W).reshape(n_cb, NB_QCOLS)
    win_start = np.clip(qcol - NB_COLS // 2, 0, GRID_W - NB_COLS)
    kcol0 = np.clip(np.arange(n_cb) * NB_QCOLS - NB_COLS // 2, 0, GRID_W - NB_KCOLS)
    kcol = kcol0[:, None] + np.arange(NB_KCOLS)
    col_ok = (kcol[:, None, :] >= win_start[..., None]) & (kcol[:, None, :] < win_start[..., None] + NB_COLS)
    key_mask = jnp.asarray(np.broadcast_to(col_ok[:, :, None, :], (n_cb, NB_QCOLS, kr, NB_KCOLS))
                           .reshape(n_cb, NB_QCOLS, kr * NB_KCOLS))[:, None]
    dc = jnp.asarray(np.clip(kcol[:, None, :] - qcol[..., None], -(NB_COLS - 1), NB_COLS - 1)
                     + NB_COLS - 1, dtype=jnp.int32)
    dr = jnp.asarray(row_start[:, None] + np.arange(kr)[None, :] - r[:, None] + NB_ROWS_MAX - 1,
                     dtype=jnp.int32)
    kcol_j = jnp.asarray(kcol, dtype=jnp.int32)
    rpb32 = rpb.astype(jnp.float32)
    kg = k.reshape(B, rows, GRID_W, H, d)
    vg = v.reshape(B, rows, GRID_W, H, d)
    qg = q.reshape(B, rows, n_cb, NB_QCOLS, H, d).transpose(1, 0, 2, 3, 4, 5)
    scale = HEAD_DIM ** -0.5
    n_keys = kr * NB_KCOLS

    def one_row(args):
        q_row, r0, dr_row = args
        krows = lax.dynamic_slice_in_dim(kg, r0, kr, axis=1)
        vrows = lax.dynamic_slice_in_dim(vg, r0, kr, axis=1)
        kb = jnp.take(krows, kcol_j, axis=2).transpose(0, 2, 1, 3, 4, 5).reshape(B, n_cb, n_keys, H, d)
        vb = jnp.take(vrows, kcol_j, axis=2).transpose(0, 2, 1, 3, 4, 5).reshape(B, n_cb, n_keys, H, d)
        s = jnp.einsum('bmqhd,bmkhd->bmhqk', q_row, kb).astype(jnp.float32) * scale
        bias = rpb32[:, dr_row[None, None, :, None], dc[:, :, None, :]]
        bias = bias.reshape(H, n_cb, NB_QCOLS, n_keys).transpose(1, 0, 2, 3)
        s = jnp.where(key_mask, s + bias, -jnp.inf)
        w = jax.nn.softmax(s, axis=-1).astype(v.dtype)
        return jnp.einsum('bmhqk,bmkhd->bmqhd', w, vb)

    out = lax.map(one_row, (qg, jnp.asarray(row_start, dtype=jnp.int32), dr))
    return out.transpose(1, 0, 2, 3, 4, 5).reshape(B, S, H, d)


def axial_rope(x, row, col):
    B, S, H, d = x.shape
    n_pair = d // 4
    inv = jnp.asarray((ROPE_THETA ** (-2.0 * np.arange(n_pair) / (d // 2))).astype(np.float32))
    ang = jnp.concatenate([row[:, None] * inv, col[:, None] * inv], axis=-1)
    cos = jnp.cos(ang)[None, :, None, :]
    sin = jnp.sin(ang)[None, :, None, :]
    xf = x.astype(jnp.float32).reshape(B, S, H, d // 2, 2)
    x0, x1 = xf[..., 0], xf[..., 1]
    out = jnp.stack([x0 * cos - x1 * sin, x0 * sin + x1 * cos], axis=-1).reshape(B, S, H, d)
    return out.astype(x.dtype)


def dense_gqa(q, k, v):
    B, S, H, d = q.shape
    KV = k.shape[2]
    G = H // KV
    nb = S // BLOCK
    qb = q.reshape(B, nb, BLOCK, KV, G, d).transpose(1, 0, 2, 3, 4, 5)
    scale = HEAD_DIM ** -0.5

    def one_block(q_blk):
        s = jnp.einsum('bqhgd,bshd->bhgqs', q_blk, k).astype(jnp.float32) * scale
        w = jax.nn.softmax(s, axis=-1).astype(v.dtype)
        return jnp.einsum('bhgqs,bshd->bqhgd', w, v)

    out = lax.map(one_block, qb)
    return out.transpose(1, 0, 2, 3, 4, 5).reshape(B, S, H, d)


def diff_attention(q, k, v, lam):
    B, S, H, _, d = q.shape
    nb = S // BLOCK
    qb = q.reshape(B, nb, BLOCK, H, 2, d).transpose(1, 0, 2, 3, 4, 5)
    slopes = jnp.asarray(alibi_slopes(H)).reshape(H, 1, 1, 1)
    kpos = jnp.arange(S)
    scale = HEAD_DIM ** -0.5

    def one_block(args):
        q_blk, n = args
        qpos = n * BLOCK + jnp.arange(BLOCK)
        dist = jnp.abs(qpos[:, None] - kpos[None, :]).astype(jnp.float32)
        s = jnp.einsum('bqhmd,bshmd->bhmqs', q_blk, k).astype(jnp.float32) * scale - slopes * dist
        a = jax.nn.softmax(s, axis=-1)
        w = (a[:, :, 0] - lam * a[:, :, 1]).astype(v.dtype)
        return jnp.einsum('bhqs,bshe->bqhe', w, v)

    out = lax.map(one_block, (qb, jnp.arange(nb)))
    return out.transpose(1, 0, 2, 3, 4).reshape(B, S, H, 2 * d)


def layer_ab(h, w_in, w_out, sink, rpb):
    B, S, _ = h.shape
    qa, ka, va, ga, qb, kb, vb, gb = jnp.split(h @ w_in, AB_SPLITS, axis=-1)
    oa = window_attention(qa.reshape(B, S, A_HEADS, HEAD_DIM), ka.reshape(B, S, A_KV, HEAD_DIM),
                          va.reshape(B, S, A_KV, HEAD_DIM), sink)
    ob = neighbourhood_attention(qb.reshape(B, S, B_HEADS, HEAD_DIM), kb.reshape(B, S, B_HEADS, HEAD_DIM),
                                 vb.reshape(B, S, B_HEADS, HEAD_DIM), rpb)
    y = jnp.concatenate([oa.reshape(B, S, A_W) * jax.nn.silu(ga),
                         ob.reshape(B, S, B_W) * jax.nn.silu(gb)], axis=-1)
    return y @ w_out


def layer_cd(h, w_in, w_out, q_norm, k_norm, lq1, lk1, lq2, lk2, subln, lam_init):
    B, S, _ = h.shape
    qc, kc, vc, gc, qd, kd, vd, gd = jnp.split(h @ w_in, CD_SPLITS, axis=-1)
    t = jnp.arange(S)
    row = (t // GRID_W).astype(jnp.float32)
    col = (t % GRID_W).astype(jnp.float32)
    qc = axial_rope(rmsnorm(qc.reshape(B, S, C_HEADS, HEAD_DIM), q_norm), row, col)
    kc = axial_rope(rmsnorm(kc.reshape(B, S, C_KV, HEAD_DIM), k_norm), row, col)
    oc = dense_gqa(qc, kc, vc.reshape(B, S, C_KV, HEAD_DIM))
    f32 = jnp.float32
    lam = (jnp.exp(jnp.sum(lq1.astype(f32) * lk1.astype(f32)))
           - jnp.exp(jnp.sum(lq2.astype(f32) * lk2.astype(f32))) + lam_init)
    od = diff_attention(qd.reshape(B, S, D_HEADS, 2, HEAD_DIM), kd.reshape(B, S, D_HEADS, 2, HEAD_DIM),
                        vd.reshape(B, S, D_HEADS, D_VDIM), lam)
    od = rmsnorm(od, subln) * (1.0 - lam_init)
    y = jnp.concatenate([oc.reshape(B, S, C_W) * jax.nn.silu(gc),
                         od.reshape(B, S, D_W) * jax.nn.silu(gd)], axis=-1)
    return y @ w_out


def trunk(x, p, norm_pre, norm_post, w_ple, w_ple_gate, w_in_ab, w_out_ab, a_sink, b_rpb,
          w_in_cd, w_out_cd, c_q_norm, c_k_norm, d_lambda_q1, d_lambda_k1, d_lambda_q2, d_lambda_k2, d_subln):
    for i in range(DEPTH):
        j = i // 2
        hn = rmsnorm(x, norm_pre[i])
        if i % 2 == 0:
            mix = layer_ab(hn, w_in_ab[j], w_out_ab[j], a_sink[j], b_rpb[j])
        else:
            mix = layer_cd(hn, w_in_cd[j], w_out_cd[j], c_q_norm[j], c_k_norm[j], d_lambda_q1[j],
                           d_lambda_k1[j], d_lambda_q2[j], d_lambda_k2[j], d_subln[j], lambda_init_for(i))
        x = x + rmsnorm(mix, norm_post[i])
        x = x + jax.nn.sigmoid(x @ w_ple_gate[i]) * (p[i] @ w_ple[i])
    return x


def setup_inputs(seed: int = 0) -> dict:
    key = jax.random.key(seed)
    ks = jax.random.split(key, 21)

    def nrm(k, shape, scale):
        return jax.random.normal(k, shape, jnp.float32) * scale

    return {
        "x_prompt": nrm(ks[0], (BATCH, SEQ, D_MODEL), 1.0),
        "x_sample": nrm(ks[1], (DEC_BATCH, DEC_SEQ, D_MODEL), 1.0),
        "p_prompt": nrm(ks[2], (DEPTH, BATCH, SEQ, PLE_DIM), 1.0),
        "p_sample": nrm(ks[3], (DEPTH, DEC_BATCH, DEC_SEQ, PLE_DIM), 1.0),
        "norm_pre": 1.0 + nrm(ks[4], (DEPTH, D_MODEL), 0.05),
        "norm_post": 1.0 + nrm(ks[5], (DEPTH, D_MODEL), 0.05),
        "w_ple": nrm(ks[6], (DEPTH, PLE_DIM, D_MODEL), PLE_DIM ** -0.5),
        "w_ple_gate": nrm(ks[7], (DEPTH, D_MODEL, D_MODEL), D_MODEL ** -0.5),
        "w_in_ab": nrm(ks[8], (N_EVEN, D_MODEL, AB_WIDTH), D_MODEL ** -0.5),
        "w_out_ab": nrm(ks[9], (N_EVEN, MIX_W, D_MODEL), MIX_W ** -0.5),
        "a_sink": nrm(ks[10], (N_EVEN, A_HEADS), 0.5),
        "b_rpb": nrm(ks[11], (N_EVEN, B_HEADS, 2 * NB_ROWS_MAX - 1, 2 * NB_COLS - 1), 0.1),
        "w_in_cd": nrm(ks[12], (N_ODD, D_MODEL, CD_WIDTH), D_MODEL ** -0.5),
        "w_out_cd": nrm(ks[13], (N_ODD, MIX_W, D_MODEL), MIX_W ** -0.5),
        "c_q_norm": 1.0 + nrm(ks[14], (N_ODD, HEAD_DIM), 0.05),
        "c_k_norm": 1.0 + nrm(ks[15], (N_ODD, HEAD_DIM), 0.05),
        "d_lambda_q1": nrm(ks[16], (N_ODD, HEAD_DIM), 0.1),
        "d_lambda_k1": nrm(ks[17], (N_ODD, HEAD_DIM), 0.1),
        "d_lambda_q2": nrm(ks[18], (N_ODD, HEAD_DIM), 0.1),
        "d_lambda_k2": nrm(ks[19], (N_ODD, HEAD_DIM), 0.1),
        "d_subln": 1.0 + nrm(ks[20], (N_ODD, D_VDIM), 0.05),
    }


def reference(x_prompt, x_sample, p_prompt, p_sample, norm_pre, norm_post, w_ple, w_ple_gate,
              w_in_ab, w_out_ab, a_sink, b_rpb, w_in_cd, w_out_cd, c_q_norm, c_k_norm,
              d_lambda_q1, d_lambda_k1, d_lambda_q2, d_lambda_k2, d_subln):
    y_prompt = trunk(x_prompt, p_prompt, norm_pre, norm_post, w_ple, w_ple_gate, w_in_ab, w_out_ab,
                     a_sink, b_rpb, w_in_cd, w_out_cd, c_q_norm, c_k_norm, d_lambda_q1, d_lambda_k1,
                     d_lambda_q2, d_lambda_k2, d_subln)
    y_sample = trunk(x_sample, p_sample, norm_pre, norm_post, w_ple, w_ple_gate, w_in_ab, w_out_ab,
                     a_sink, b_rpb, w_in_cd, w_out_cd, c_q_norm, c_k_norm, d_lambda_q1, d_lambda_k1,
                     d_lambda_q2, d_lambda_k2, d_subln)
    return (y_prompt, y_sample)
```

```python
import math
import numpy as np
import ml_dtypes
import concourse.bass as bass
import concourse.mybir as mybir
from concourse.bass_utils import run_bass_kernel_spmd

F32 = mybir.dt.float32
BF16 = mybir.dt.bfloat16
AF = mybir.ActivationFunctionType
ALU = mybir.AluOpType
NPBF = ml_dtypes.bfloat16

NCORE = 8
D = 1024
PAD = 4
EPS = 1e-6
NEG = -30000.0
NSEM_DMA = 8


class Prog:
    ENG = ("pe", "act", "dve", "pool", "sp")

    def __init__(self, nc, sems):
        self.nc = nc
        self.sems = sems
        self.sigcount = {e: 0 for e in self.ENG}
        self.dmacount = {"sp": 0, "pool": 0}
        self.waited = {}
        self.reset()

    def reset(self):
        self.ops = []
        self.lw = {}
        self.rd = {}

    def op(self, eng, fn, r=(), w=(), dma=False):
        idx = len(self.ops)
        deps = set()
        for b in r:
            x = self.lw.get(b)
            if x is not None:
                deps.add(x)
        for b in w:
            x = self.lw.get(b)
            if x is not None:
                deps.add(x)
            deps.update(self.rd.get(b, ()))
        for b in r:
            self.rd.setdefault(b, []).append(idx)
        for b in w:
            self.lw[b] = idx
            self.rd[b] = []
        self.ops.append(dict(eng=eng, fn=fn, deps=deps, dma=dma, need=False))
        return idx

    def pe(self, fn, r=(), w=()): return self.op("pe", fn, r, w)
    def act(self, fn, r=(), w=()): return self.op("act", fn, r, w)
    def dve(self, fn, r=(), w=()): return self.op("dve", fn, r, w)
    def pool(self, fn, r=(), w=()): return self.op("pool", fn, r, w)
    def dma(self, q, fn, r=(), w=()): return self.op(q, fn, r, w, dma=True)

    def emit_block(self, name=None):
        nc = self.nc
        ops = self.ops
        for o in ops:
            for d in o["deps"]:
                p = ops[d]
                if p["eng"] == o["eng"] and o["eng"] == "pe" and not p["dma"]:
                    continue
                p["need"] = True
        for o in ops:
            e = o["eng"]
            if o["dma"]:
                k = self.dmacount[e]
                self.dmacount[e] = k + 1
                o["sig"] = ((e, k % NSEM_DMA), 16 * (k // NSEM_DMA + 1))
                o["pre"] = ((e, k % NSEM_DMA), 16 * (k // NSEM_DMA)) if k >= NSEM_DMA else None
            elif o["need"]:
                self.sigcount[e] += 1
                o["sig"] = (e, self.sigcount[e])
                o["pre"] = None
            else:
                o["sig"] = None
                o["pre"] = None
        per = {e: [] for e in self.ENG}
        for o in ops:
            e = o["eng"]
            waits = []
            cand = {}
            for d in o["deps"]:
                p = ops[d]
                if p["eng"] == e and e == "pe" and not p["dma"]:
                    continue
                s, v = p["sig"]
                cand[s] = max(cand.get(s, 0), v)
            if o["pre"] is not None:
                s, v = o["pre"]
                cand[s] = max(cand.get(s, 0), v)
            for s, v in cand.items():
                if self.waited.get((e, s), 0) >= v:
                    continue
                self.waited[(e, s)] = v
                waits.append((s, v))
            per[e].append((o, waits))
        tails = {}
        for q in ("sp", "pool"):
            k = self.dmacount[q]
            tl = []
            for i in range(NSEM_DMA):
                n = (k - i + NSEM_DMA - 1) // NSEM_DMA if k > i else 0
                if n > 0 and self.waited.get((q, (q, i)), 0) < 16 * n:
                    self.waited[(q, (q, i))] = 16 * n
                    tl.append(((q, i), 16 * n))
            tails[q] = tl
        sems = self.sems

        def run(eh, e):
            for o, waits in per[e]:
                for s, v in waits:
                    eh.wait_ge(sems[s], v)
                ins = o["fn"](eh)
                if o["sig"] is not None:
                    s, v = o["sig"]
                    ins.then_inc(sems[s], 16 if o["dma"] else 1)
            for s, v in tails.get(e, ()):
                eh.wait_ge(sems[s], v)

        with nc.Block() as block:
            @block.tensor
            def _(eh): run(eh, "pe")

            @block.scalar
            def _(eh): run(eh, "act")

            @block.vector
            def _(eh): run(eh, "dve")

            @block.gpsimd
            def _(eh): run(eh, "pool")

            @block.sync
            def _(eh): run(eh, "sp")
        self.reset()


def alibi_slopes(n):
    return (2.0 ** (-8.0 * np.arange(1, n + 1) / n)).astype(np.float32)


def nb_rowmask_tile(rows, n, kb):
    m = np.full((2, 64, 2, 64), NEG, np.float32)
    nblk = rows // 2
    if n < 0 or n >= nblk or kb < 0 or kb >= nblk:
        return m.reshape(128, 128)
    for qr in range(2):
        r = 2 * n + qr
        rs = min(max(r - 4, 0), rows - 8)
        for kr in range(2):
            kk = 2 * kb + kr
            if rs <= kk < rs + 8:
                m[kr, :, qr, :] = 0.0
    return m.reshape(128, 128)


def nb_static():
    kc = np.arange(64)[:, None]
    qc = np.arange(64)[None, :]
    ws = np.clip(qc - 8, 0, 48)
    colok = (kc >= ws) & (kc < ws + 16)
    colmask = np.where(colok, 0.0, NEG).astype(np.float32)
    colmask = np.broadcast_to(colmask[None, :, None, :], (2, 64, 2, 64)).reshape(128, 128)
    dc = np.clip(kc - qc, -15, 15) + 15
    return colmask, dc


def cfg_make(nbs, nbpf):
    c = dict(NBS=nbs, NBPF=nbpf, NBPO=nbpf // NCORE)
    segs = []
    e0 = 0
    o0 = 0
    for name, nb in (("S", nbs), ("PO", nbpf // NCORE), ("PF", nbpf)):
        segs.append(dict(name=name, nb=nb, e0=e0, o0=o0, ne=nb + 2 * PAD))
        e0 += nb + 2 * PAD
        o0 += nb
    c["segs"] = segs
    c["NE"] = e0
    c["NO"] = o0
    c["NQ"] = nbs + nbpf // NCORE
    return c


def blk_class(seg, i):
    nb = seg["nb"]
    if i == 0: return 0
    if i == 1: return 1
    if i == nb - 2: return 3
    if i == nb - 1: return 4
    return 2


def blk_rels(seg, i):
    cl = blk_class(seg, i)
    if seg["name"] == "PO":
        return {0: list(range(-2, 4)), 1: list(range(-2, 3)), 2: list(range(-2, 3)),
                3: list(range(-2, 3)), 4: list(range(-3, 3))}[cl]
    return {0: [0, 1, 2, 3], 1: [-1, 0, 1, 2], 2: [-2, -1, 0, 1, 2], 3: [-2, -1, 0, 1], 4: [-3, -2, -1, 0]}[cl]


AB_FM = [("qa", 0, 4, "q"), ("ka", 512, 1, "k"), ("ga", 768, 4, "g"),
         ("qb", 1280, 4, "q"), ("kb", 1792, 4, "k"), ("gb", 2816, 4, "g")]
CD_FM = [("gc", 768, 4, "g"), ("qd", 1280, 4, "q"), ("kd", 1792, 4, "k"), ("gd", 2816, 4, "g")]


def build(cfg, dbg=0):
    nc = bass.Bass("TRN2", target_bir_lowering=False)
    NE, NO, NQ = cfg["NE"], cfg["NO"], cfg["NQ"]
    segs = cfg["segs"]
    TE, TO, TQ = NE * 128, NO * 128, NQ * 128

    def din(name, shape, dt=F32):
        return nc.dram_tensor(name, list(shape), dt, kind="ExternalInput").ap()

    def dscr(name, shape, dt):
        kind = "ExternalOutput" if (dbg and name in ("YT0", "FT0", "VT0", "X1", "FT1", "VT1", "YT1")) else "Internal"
        return nc.dram_tensor(name, list(shape), dt, kind=kind).ap()

    xin = din("xin", [TE, D])
    validin = din("validin", [128, NE])
    pin = din("pin", [2, TO, 256])
    gvec = din("gvec", [4, D])
    w_in_ab = din("w_in_ab", [D, 3328]); w_out_ab = din("w_out_ab", [D, D])
    w_in_cd = din("w_in_cd", [D, 3328]); w_out_cd = din("w_out_cd", [D, D])
    w_ple = din("w_ple", [2, 256, D]); w_gate = din("w_gate", [2, D, D])
    a_sink = din("a_sink", [1, 8])
    rpbg = din("rpbg", [128, 7, 8, 128])
    colmask = din("colmask", [128, 128])
    alibia = din("alibia", [128, 3, 8, 128], BF16)
    rowmask = din("rowmask", [128, 2, 5, 7, 128], BF16)
    qkgain = din("qkgain", [1, 640])
    rope = din("rope", [TO, 64])
    lamv = din("lamv", [1, 4, 64])
    subln = din("subln", [128, 1])
    qtab = din("qtab", [4, 2, 4, TQ], BF16)
    ktab = din("ktab", [4, 4, TO], BF16)
    diagb = din("diagb", [128, 4, 128], BF16)
    yout = nc.dram_tensor("yout", [TQ, D], F32, kind="ExternalOutput").ap()

    FT0 = dscr("FT0", [21 * 128, TE], BF16)
    VT0 = dscr("VT0", [TE, 650], BF16)
    YT0 = dscr("YT0", [D, TO], BF16)
    X1 = dscr("X1", [TO, D], F32)
    FT1 = dscr("FT1", [21 * 128, TO], BF16)
    VT1 = dscr("VT1", [TO, 642], BF16)
    YT1 = dscr("YT1", [D, TQ], BF16)

    import contextlib
    es = contextlib.ExitStack()
    with es:
        sems = {}
        for e in Prog.ENG:
            sems[e] = es.enter_context(nc.semaphore("s_" + e))
        for q in ("sp", "pool"):
            for i in range(NSEM_DMA):
                sems[(q, i)] = es.enter_context(nc.semaphore("d_%s%d" % (q, i)))
        P = Prog(nc, sems)

        ucnt = [0]

        def sb(stack, name, shape, dt):
            ucnt[0] += 1
            return stack.enter_context(nc.sbuf_tensor("%s_u%d" % (name, ucnt[0]), list(shape), dt))

        def ps(stack, name, shape, dt=F32):
            ucnt[0] += 1
            return stack.enter_context(nc.psum_tensor("%s_u%d" % (name, ucnt[0]), list(shape), dt))

        ident = sb(es, "ident", [128, 128], BF16)
        identf = sb(es, "identf", [128, 128], F32)
        ones_f = sb(es, "ones_f", [128, 128], F32)
        zeros_b = sb(es, "zeros_b", [128, 512], BF16)
        gbc = sb(es, "gbc", [128, 4, D], F32)
        valid_sb = sb(es, "valid_sb", [128, NE], F32)
        ones10 = sb(es, "ones10", [128, 10], F32)

        eye = din("eye", [128, 128])
        P.dma("sp", lambda e: e.dma_start(out=identf[:], in_=eye[:, :]), w=["identf"])
        P.dve(lambda e: e.tensor_copy(out=ident[:], in_=identf[:]), r=["identf"], w=["ident"])
        P.dve(lambda e: e.memset(ones_f[:], 1.0), w=["ones_f"])
        P.dve(lambda e: e.memset(zeros_b[:], 0.0), w=["zeros_b"])
        P.dve(lambda e: e.memset(ones10[:], 1.0), w=["ones10"])
        P.dma("sp", lambda e: e.dma_start(out=valid_sb[:], in_=validin[:, :]), w=["valid"])
        for i in range(4):
            P.dma("sp", lambda e, i=i: e.dma_start(out=gbc[:, i, :], in_=gvec[i:i + 1, :].partition_broadcast(128)),
                  w=["gbc"])
        P.emit_block()

        def load_weight(stack_bufs, wdst, wsrc, kch, ncols, key, colchunk=1664):
            stg = stack_bufs
            j = 0
            for k in range(kch):
                for c0 in range(0, ncols, colchunk):
                    c1 = min(ncols, c0 + colchunk)
                    s = stg[j % 2]
                    sk = "wstg%d" % (j % 2)
                    P.dma("sp", lambda e, s=s, k=k, c0=c0, c1=c1: e.dma_start(
                        out=s[:, 0:c1 - c0], in_=wsrc[k * 128:(k + 1) * 128, c0:c1]), w=[sk])
                    if j % 2 == 0:
                        P.dve(lambda e, s=s, k=k, c0=c0, c1=c1: e.tensor_copy(out=wdst[:, k, c0:c1], in_=s[:, 0:c1 - c0]),
                              r=[sk], w=[key])
                    else:
                        P.pool(lambda e, s=s, k=k, c0=c0, c1=c1: e.tensor_copy(out=wdst[:, k, c0:c1], in_=s[:, 0:c1 - c0]),
                               r=[sk], w=[key])
                    j += 1

        def norm_block(xt, xk, gi, hn, hnk, ss, rstd, junk, tagk):
            P.act(lambda e: e.activation(out=junk[:], in_=xt, func=AF.Square, scale=1.0 / math.sqrt(D), accum_out=ss[:, 0:1]),
                  r=[xk], w=["junk", "ss" + tagk])
            P.dve(lambda e: e.tensor_scalar(out=rstd[:, 0:1], in0=ss[:, 0:1], scalar1=EPS, scalar2=None,
                                            op0=ALU.add), r=["ss" + tagk], w=["rstd" + tagk])
            P.act(lambda e: e.activation(out=rstd[:, 0:1], in_=rstd[:, 0:1], func=AF.Sqrt), r=["rstd" + tagk], w=["rstd" + tagk])
            P.dve(lambda e: e.reciprocal(out=rstd[:, 0:1], in_=rstd[:, 0:1]), r=["rstd" + tagk], w=["rstd" + tagk])
            P.dve(lambda e: e.scalar_tensor_tensor(out=hn, in0=xt, scalar=rstd[:, 0:1], in1=gbc[:, gi, :],
                                                   op0=ALU.mult, op1=ALU.mult),
                  r=[xk, "rstd" + tagk, "gbc"], w=[hnk])

        def transpose_to(src, srck, nk, tp, tpk, dst, dstk, use_act):
            for k in range(nk):
                P.pe(lambda e, k=k: e.transpose(out=tp[:, k, :], in_=src[:, k * 128:(k + 1) * 128], identity=ident[:]),
                     r=[srck, "ident"], w=[tpk])
            if use_act:
                P.act(lambda e: e.copy(out=dst, in_=tp[:, 0:nk, :]), r=[tpk], w=[dstk])
            else:
                P.dve(lambda e: e.tensor_copy(out=dst, in_=tp[:, 0:nk, :]), r=[tpk], w=[dstk])

        with contextlib.ExitStack() as st:
            w_in = sb(st, "w_in", [128, 8, 3328], BF16)
            wstg = [sb(st, "wstg0", [128, 1664], F32), sb(st, "wstg1", [128, 1664], F32)]
            xt = [sb(st, "xt%d" % i, [128, D], F32) for i in range(2)]
            hn = [sb(st, "hn%d" % i, [128, D], BF16) for i in range(2)]
            hnT = [sb(st, "hnT%d" % i, [128, 8, 512], BF16) for i in range(2)]
            junk = sb(st, "junk", [128, D], BF16)
            ss = sb(st, "ss", [128, 2], F32)
            rstd = sb(st, "rstd", [128, 2], F32)
            fstage = [sb(st, "fstage%d" % i, [128, 21, 512], BF16) for i in range(2)]
            vstage = [sb(st, "vstage%d" % i, [128, 10, 65], BF16) for i in range(2)]
            tp = [ps(st, "tp%d" % i, [128, 8, 128], BF16) for i in range(2)]
            fm = [ps(st, "fm%d" % i, [128, 512]) for i in range(2)]
            tmA = [ps(st, "tmA%d" % i, [128, 512]) for i in range(1)]
            tmB = [ps(st, "tmB%d" % i, [128, 128]) for i in range(1)]

            load_weight(wstg, w_in, w_in_ab, 8, 3328, "w_in")
            nst = NE // 4
            bc = 0
            for sti in range(nst):
                hT = hnT[sti % 2]
                hTk = "hnT%d" % (sti % 2)
                for b in range(4):
                    eb = sti * 4 + b
                    x_ = xt[bc % 2]
                    xk = "xt%d" % (bc % 2)
                    h_ = hn[bc % 2]
                    hk = "hn%d" % (bc % 2)
                    P.dma("sp", lambda e, x_=x_, eb=eb: e.dma_start(out=x_[:], in_=xin[eb * 128:(eb + 1) * 128, :]), w=[xk])
                    sx = ss[:, bc % 2:bc % 2 + 1]
                    rx = rstd[:, bc % 2:bc % 2 + 1]
                    norm_block(x_[:], xk, 0, h_[:], hk, sx, rx, junk, str(bc % 2))
                    t_ = tp[bc % 2]
                    transpose_to(h_, hk, 8, t_, "tp%d" % (bc % 2), hT[:, :, b * 128:(b + 1) * 128], hTk, bc % 2 == 0)
                    bc += 1
                fs = fstage[sti % 2]
                fsk = "fstage%d" % (sti % 2)
                ci = 0
                for (nm, c0, nch, kind) in AB_FM:
                    for j in range(nch):
                        f0 = c0 + j * 128
                        pf = fm[ci % 2]
                        pk = "fm%d" % (ci % 2)
                        for k in range(8):
                            P.pe(lambda e, pf=pf, k=k, f0=f0, hT=hT: e.matmul(pf[:], lhsT=w_in[:, k, f0:f0 + 128], rhs=hT[:, k, :],
                                                                         start=(k == 0), stop=(k == 7)),
                                 r=["w_in", hTk], w=[pk])
                        if kind == "g":
                            P.act(lambda e, pf=pf, ci=ci, fs=fs: e.activation(out=fs[:, ci, :], in_=pf[:], func=AF.Silu),
                                  r=[pk], w=[fsk])
                        elif kind == "q":
                            P.dve(lambda e, pf=pf, ci=ci, fs=fs: e.tensor_scalar(out=fs[:, ci, :], in0=pf[:], scalar1=0.125,
                                                                           scalar2=None, op0=ALU.mult),
                                  r=[pk], w=[fsk])
                        else:
                            P.dve(lambda e, pf=pf, ci=ci, fs=fs: e.tensor_copy(out=fs[:, ci, :], in_=pf[:]), r=[pk], w=[fsk])
                        ci += 1
                P.dma("pool", lambda e, fs=fs, sti=sti: e.dma_start(
                    out=FT0.rearrange("(c p) t -> p c t", p=128)[:, :, sti * 512:(sti + 1) * 512], in_=fs[:]), r=[fsk])
                for b in range(4):
                    eb = sti * 4 + b
                    vs = vstage[b % 2]
                    vk = "vstage%d" % (b % 2)
                    for k in range(8):
                        P.pe(lambda e, k=k, b=b, hT=hT: e.matmul(tmA[0][:], lhsT=hT[:, k, b * 128:(b + 1) * 128],
                                                           rhs=w_in[:, k, 2304:2816], start=(k == 0), stop=(k == 7)),
                             r=["w_in", hTk], w=["tmA"])
                    for k in range(8):
                        P.pe(lambda e, k=k, b=b, hT=hT: e.matmul(tmB[0][:], lhsT=hT[:, k, b * 128:(b + 1) * 128],
                                                           rhs=w_in[:, k, 640:768], start=(k == 0), stop=(k == 7)),
                             r=["w_in", hTk], w=["tmB"])
                    P.dve(lambda e, vs=vs: e.tensor_copy(out=vs[:, 2:10, 0:64], in_=tmA[0][:].rearrange("p (h d) -> p h d", d=64)),
                          r=["tmA"], w=[vk])
                    P.act(lambda e, vs=vs: e.copy(out=vs[:, 0:2, 0:64], in_=tmB[0][:].rearrange("p (h d) -> p h d", d=64)),
                          r=["tmB"], w=[vk])
                    P.dve(lambda e, vs=vs, eb=eb: e.tensor_scalar(out=vs[:, :, 64], in0=ones10[:], scalar1=valid_sb[:, eb:eb + 1],
                                                             scalar2=None, op0=ALU.mult), r=["valid", "ones10"], w=[vk])
                    P.dma("pool", lambda e, vs=vs, eb=eb: e.dma_start(out=VT0[eb * 128:(eb + 1) * 128, :],
                                                                 in_=vs[:].rearrange("p h d -> p (h d)")), r=[vk])
            P.emit_block()

        with contextlib.ExitStack() as st:
            rpbcol = sb(st, "rpbcol", [128, 7, 8, 128], BF16)
            rpbstg = sb(st, "rpbstg", [128, 7, 8, 128], F32)
            cmask = sb(st, "cmask", [128, 128], F32)
            alib = sb(st, "alib", [128, 3, 8, 128], BF16)
            rmask = sb(st, "rmask", [128, 2, 5, 7, 128], BF16)
            sinkrow = sb(st, "sinkrow", [65, 8, 128], F32)
            sinkv = sb(st, "sinkv", [65, 8], F32)
            KAb = [sb(st, "KA%d" % i, [64, 2, 7 * 128], BF16) for i in range(2)]
            KBb = [sb(st, "KB%d" % i, [64, 8, 7 * 128], BF16) for i in range(2)]
            VRb = [sb(st, "VR%d" % i, [128, 7, 650], BF16) for i in range(2)]
            QAb = [sb(st, "QA%d" % i, [64, 8, 128], BF16) for i in range(2)]
            QBb = [sb(st, "QB%d" % i, [64, 8, 128], BF16) for i in range(2)]
            GAb = [sb(st, "GA%d" % i, [64, 8, 128], BF16) for i in range(2)]
            GBb = [sb(st, "GB%d" % i, [64, 8, 128], BF16) for i in range(2)]
            PT = [sb(st, "PT%d" % i, [128, 3, 512], BF16) for i in range(2)]
            den = sb(st, "den", [65, 512], F32)
            rinv = sb(st, "rinv", [65, 512], F32)
            on = [sb(st, "on%d" % i, [64, 512], F32) for i in range(2)]
            ystage = [sb(st, "ystage%d" % i, [64, 16, 128], BF16) for i in range(2)]
            ST = [ps(st, "ST%d" % i, [128, 3, 512]) for i in range(2)]
            OT = ps(st, "OT", [128, 512])
            BC = ps(st, "BC", [128, 512])

            P.dma("sp", lambda e: e.dma_start(out=rpbstg[:], in_=rpbg[:, :, :, :]), w=["rpbstg"])
            P.dma("sp", lambda e: e.dma_start(out=cmask[:], in_=colmask[:, :]), w=["cmask"])
            P.dma("sp", lambda e: e.dma_start(out=alib[:], in_=alibia[:, :, :, :]), w=["alib"])
            P.dma("sp", lambda e: e.dma_start(out=rmask[:], in_=rowmask[:, :, :, :, :]), w=["rmask"])
            P.dma("sp", lambda e: e.dma_start(out=sinkv[64:65, :], in_=a_sink[0:1, :]), w=["sinkv"])
            for r_ in range(7):
                for h in range(8):
                    P.dve(lambda e, r_=r_, h=h: e.tensor_tensor(out=rpbcol[:, r_, h, :], in0=rpbstg[:, r_, h, :], in1=cmask[:],
                                                             op=ALU.add), r=["rpbstg", "cmask"], w=["rpbcol"])
            P.act(lambda e: e.activation(out=sinkv[64:65, :], in_=sinkv[64:65, :], func=AF.Exp), r=["sinkv"], w=["sinkv"])
            P.dve(lambda e: e.tensor_copy(out=sinkrow[64:65, :, :], in_=sinkv[64:65, :].unsqueeze(2).to_broadcast([1, 8, 128])),
                  r=["sinkv"], w=["sinkrow"])

            FT0v = FT0.rearrange("(c h d) t -> d c h t", h=2, d=64)
            bi = 0
            gb = 0
            jb = 0
            for seg in segs:
                tab = 1 if seg["name"] == "PO" else 0
                for i in range(seg["nb"]):
                    eb = seg["e0"] + PAD + i
                    ob = seg["o0"] + i
                    s2 = bi % 2
                    ka, kb_, vr = KAb[s2], KBb[s2], VRb[s2]
                    qa, qb, ga, gb_ = QAb[s2], QBb[s2], GAb[s2], GBb[s2]
                    t0 = (eb - 3) * 128
                    t1 = (eb + 4) * 128
                    kk = "kv%d" % s2
                    qk = "qg%d" % s2
                    P.dma("sp", lambda e, ka=ka, t0=t0, t1=t1: e.dma_start(out=ka[:], in_=FT0v[:, 4, :, t0:t1]), w=[kk])
                    for hh in range(4):
                        P.dma("sp", lambda e, kb_=kb_, t0=t0, t1=t1, hh=hh: e.dma_start(
                            out=kb_[:, 2 * hh:2 * hh + 2, :], in_=FT0v[:, 13 + hh, :, t0:t1]), w=[kk])
                    P.dma("sp", lambda e, vr=vr, t0=t0, t1=t1: e.dma_start(
                        out=vr[:], in_=VT0[t0:t1, :].rearrange("(b p) f -> p b f", p=128)), w=[kk])
                    q0 = eb * 128
                    for hh in range(4):
                        P.dma("pool", lambda e, qa=qa, hh=hh, q0=q0: e.dma_start(out=qa[:, 2 * hh:2 * hh + 2, :], in_=FT0v[:, 0 + hh, :, q0:q0 + 128]), w=[qk])
                        P.dma("pool", lambda e, ga=ga, hh=hh, q0=q0: e.dma_start(out=ga[:, 2 * hh:2 * hh + 2, :], in_=FT0v[:, 5 + hh, :, q0:q0 + 128]), w=[qk])
                        P.dma("pool", lambda e, qb=qb, hh=hh, q0=q0: e.dma_start(out=qb[:, 2 * hh:2 * hh + 2, :], in_=FT0v[:, 9 + hh, :, q0:q0 + 128]), w=[qk])
                        P.dma("pool", lambda e, gb_=gb_, hh=hh, q0=q0: e.dma_start(out=gb_[:, 2 * hh:2 * hh + 2, :], in_=FT0v[:, 17 + hh, :, q0:q0 + 128]), w=[qk])
                    cl = blk_class(seg, i)
                    rels_b = blk_rels(seg, i)
                    ys = ystage[bi % 2]
                    ysk = "ystage%d" % (bi % 2)
                    for job in range(4):
                        isA = job < 2
                        g = job % 2
                        rels = [-1, 0, 1] if isA else rels_b
                        batches = [rels[j:j + 3] for j in range(0, len(rels), 3)]
                        P.pe(lambda e: e.matmul(OT[0:65, :], lhsT=zeros_b[:, 0:65], rhs=zeros_b[:, :], start=True, stop=True),
                             r=["zeros_b"], w=["OT"])
                        for bt in batches:
                            S_ = ST[gb % 2]
                            Sk = "ST%d" % (gb % 2)
                            p_ = PT[gb % 2]
                            pk = "PT%d" % (gb % 2)
                            for j, rel in enumerate(bt):
                                kof = (rel + 3) * 128
                                if isA:
                                    P.pe(lambda e, S_=S_, j=j, rel=rel, g=g: e.matmul(
                                        S_[:, j, :], lhsT=ident[:], rhs=alib[:, rel + 1, 4 * g:4 * g + 4, :], start=True, stop=False),
                                        r=["ident", "alib"], w=[Sk])
                                    P.pe(lambda e, S_=S_, j=j, kof=kof, g=g, ka=ka, qa=qa: e.matmul(
                                        S_[:, j, :], lhsT=ka[:, g, kof:kof + 128], rhs=qa[:, 4 * g:4 * g + 4, :], start=False, stop=True),
                                        r=[kk, qk], w=[Sk])
                                else:
                                    P.pe(lambda e, S_=S_, j=j, rel=rel, g=g: e.matmul(
                                        S_[:, j, :], lhsT=ident[:], rhs=rpbcol[:, rel + 3, 4 * g:4 * g + 4, :], start=True, stop=False),
                                        r=["ident", "rpbcol"], w=[Sk])
                                    for h in range(4):
                                        P.pe(lambda e, S_=S_, j=j, rel=rel, h=h, tab=tab, cl=cl: e.matmul(
                                            S_[:, j, h * 128:(h + 1) * 128], lhsT=ident[:], rhs=rmask[:, tab, cl, rel + 3, :],
                                            start=False, stop=False), r=["ident", "rmask"], w=[Sk])
                                    for h in range(4):
                                        P.pe(lambda e, S_=S_, j=j, kof=kof, g=g, h=h, kb_=kb_, qb=qb: e.matmul(
                                            S_[:, j, h * 128:(h + 1) * 128], lhsT=kb_[:, 4 * g + h, kof:kof + 128],
                                            rhs=qb[:, 4 * g + h, :], start=False, stop=(h == 3)), r=[kk, qk], w=[Sk])
                            nb_ = len(bt)
                            P.act(lambda e, S_=S_, p_=p_, nb_=nb_: e.activation(out=p_[:, 0:nb_, :], in_=S_[:, 0:nb_, :], func=AF.Exp),
                                  r=[Sk], w=[pk])
                            for j, rel in enumerate(bt):
                                ko = rel + 3
                                if isA:
                                    P.pe(lambda e, p_=p_, j=j, ko=ko, g=g, vr=vr: e.matmul(
                                        OT[0:65, :], lhsT=vr[:, ko, g * 65:(g + 1) * 65], rhs=p_[:, j, :], start=False, stop=True),
                                        r=[kk, pk], w=["OT"])
                                else:
                                    for h in range(4):
                                        hv = 2 + 4 * g + h
                                        P.pe(lambda e, p_=p_, j=j, ko=ko, hv=hv, h=h, vr=vr: e.matmul(
                                            OT[0:65, h * 128:(h + 1) * 128], lhsT=vr[:, ko, hv * 65:(hv + 1) * 65],
                                            rhs=p_[:, j, h * 128:(h + 1) * 128], start=False, stop=True), r=[kk, pk], w=["OT"])
                            gb += 1
                        o_ = on[jb % 2]
                        ok_ = "on%d" % (jb % 2)
                        if isA:
                            P.dve(lambda e, g=g: e.tensor_tensor(out=den[64:65, :], in0=OT[64:65, :],
                                                                 in1=sinkrow[64:65, 4 * g:4 * g + 4, :], op=ALU.add),
                                  r=["OT", "sinkrow"], w=["den"])
                        else:
                            P.dve(lambda e: e.tensor_copy(out=den[64:65, :], in_=OT[64:65, :]), r=["OT"], w=["den"])
                        P.dve(lambda e: e.reciprocal(out=rinv[64:65, :], in_=den[64:65, :]), r=["den"], w=["rinv"])
                        gsrc = ga if isA else gb_
                        P.dve(lambda e, o_=o_, gsrc=gsrc, g=g: e.tensor_tensor(out=o_[:], in0=OT[0:64, :], in1=gsrc[:, 4 * g:4 * g + 4, :],
                                                                          op=ALU.mult), r=["OT", qk], w=[ok_])
                        P.pe(lambda e: e.matmul(BC[0:64, :], lhsT=ones_f[64:65, 0:64], rhs=rinv[64:65, :], start=True, stop=True),
                             r=["ones_f", "rinv"], w=["BC"])
                        hb = (0 if isA else 8) + 4 * g
                        P.dve(lambda e, o_=o_, ys=ys, hb=hb: e.tensor_tensor(out=ys[:, hb:hb + 4, :], in0=o_[:], in1=BC[0:64, :], op=ALU.mult),
                              r=[ok_, "BC"], w=[ysk])
                        jb += 1
                    P.dma("pool", lambda e, ys=ys, ob=ob: e.dma_start(
                        out=YT0.rearrange("(h d) t -> d h t", d=64)[:, :, ob * 128:(ob + 1) * 128], in_=ys[:]), r=[ysk])
                    bi += 1
            P.emit_block()

        if dbg == 1:
            return nc
        _phase345(nc, P, cfg, sb, ps, dbg=dbg, G=dict(
            ident=ident, ones_f=ones_f, zeros_b=zeros_b, gbc=gbc, xin=xin, pin=pin, w_out_ab=w_out_ab, w_in_cd=w_in_cd,
            w_out_cd=w_out_cd, w_ple=w_ple, w_gate=w_gate, qkgain=qkgain, rope=rope, lamv=lamv, subln=subln, qtab=qtab,
            ktab=ktab, diagb=diagb, yout=yout, YT0=YT0, X1=X1, FT1=FT1, VT1=VT1, YT1=YT1,
            load_weight=load_weight, norm_block=norm_block, transpose_to=transpose_to))
    return nc


LAM_INIT1 = 0.8 - 0.6 * math.exp(-0.3 * 1)


def _phase345(nc, P, cfg, sb, ps, G, dbg=0):
    import contextlib
    segs = cfg["segs"]
    ident, ones_f, zeros_b, gbc = G["ident"], G["ones_f"], G["zeros_b"], G["gbc"]
    pin = G["pin"]
    load_weight, norm_block, transpose_to = G["load_weight"], G["norm_block"], G["transpose_to"]
    FT1, VT1, X1, YT1 = G["FT1"], G["VT1"], G["X1"], G["YT1"]

    def out_phase(layer, YT, xsrc_fn, dst_fn, seglist, w_out_d, tok_fn):
        with contextlib.ExitStack() as st:
            wout = sb(st, "wout", [128, 8, D], BF16)
            wg = sb(st, "wg", [128, 8, D], BF16)
            wp = sb(st, "wp", [128, 2, D], BF16)
            wstg = [sb(st, "wstg0", [128, 1024], F32), sb(st, "wstg1", [128, 1024], F32)]
            yts = [sb(st, "yts%d" % i, [128, 8, 512], BF16) for i in range(2)]
            xt = [sb(st, "xt%d" % i, [128, D], F32) for i in range(2)]
            psb = [sb(st, "psb%d" % i, [128, 256], F32) for i in range(2)]
            pb = sb(st, "pb", [128, 256], BF16)
            tmix = sb(st, "tmix", [128, D], F32)
            xa = sb(st, "xa", [128, D], F32)
            xab = sb(st, "xab", [128, D], BF16)
            xaT = sb(st, "xaT", [128, 8, 128], BF16)
            pT = sb(st, "pT", [128, 2, 128], BF16)
            sg = sb(st, "sg", [128, D], F32)
            xo = [sb(st, "xo%d" % i, [128, D], F32) for i in range(2)]
            junk = sb(st, "junk", [128, D], BF16)
            ss = sb(st, "ss", [128, 2], F32)
            mix = ps(st, "mix", [128, D])
            gate = ps(st, "gate", [128, D])
            pp = ps(st, "pp", [128, D])
            tp = ps(st, "tp", [128, 8, 128], BF16)
            tp2 = ps(st, "tp2", [128, 8, 128], BF16)
            load_weight(wstg, wout, w_out_d, 8, D, "wout", colchunk=1024)
            load_weight(wstg, wg, G["w_gate"][layer], 8, D, "wg", colchunk=1024)
            load_weight(wstg, wp, G["w_ple"][layer], 2, D, "wp", colchunk=1024)
            gi = 1 + 2 * layer
            bc = 0
            sc = 0
            for seg in seglist:
                for sti in range(seg["nb"] // 4):
                    y_ = yts[sc % 2]
                    yk = "yts%d" % (sc % 2)
                    ot0 = tok_fn(seg, sti * 4)
                    P.dma("sp", lambda e, y_=y_, ot0=ot0: e.dma_start(
                        out=y_[:], in_=YT.rearrange("(k p) t -> p k t", p=128)[:, :, ot0:ot0 + 512]), w=[yk])
                    sc += 1
                    for b in range(4):
                        i = sti * 4 + b
                        x_ = xt[bc % 2]; xk = "xt%d" % (bc % 2)
                        p_ = psb[bc % 2]; pk = "psb%d" % (bc % 2)
                        o_ = xo[bc % 2]; ok_ = "xo%d" % (bc % 2)
                        xs_ap = xsrc_fn(seg, i)
                        po = (seg["o0"] + i) * 128
                        P.dma("sp", lambda e, x_=x_, xs_ap=xs_ap: e.dma_start(out=x_[:], in_=xs_ap), w=[xk])
                        P.dma("sp", lambda e, p_=p_, po=po: e.dma_start(out=p_[:], in_=pin[layer, po:po + 128, :]), w=[pk])
                        for half in range(2):
                            for k in range(8):
                                P.pe(lambda e, y_=y_, k=k, b=b, half=half: e.matmul(
                                    mix[:, half * 512:(half + 1) * 512], lhsT=y_[:, k, b * 128:(b + 1) * 128],
                                    rhs=wout[:, k, half * 512:(half + 1) * 512], start=(k == 0), stop=(k == 7)),
                                    r=[yk, "wout"], w=["mix"])
                        sx = ss[:, 0:1]
                        P.act(lambda e: e.activation(out=junk[:], in_=mix[:], func=AF.Square, scale=1.0 / math.sqrt(D),
                                                     accum_out=sx), r=["mix"], w=["junk", "ss"])
                        P.dve(lambda e: e.tensor_scalar(out=sx, in0=sx, scalar1=EPS, scalar2=None, op0=ALU.add), r=["ss"], w=["ss"])
                        P.act(lambda e: e.activation(out=sx, in_=sx, func=AF.Sqrt), r=["ss"], w=["ss"])
                        P.dve(lambda e: e.reciprocal(out=sx, in_=sx), r=["ss"], w=["ss"])
                        P.dve(lambda e: e.scalar_tensor_tensor(out=tmix[:], in0=mix[:], scalar=sx, in1=gbc[:, gi, :],
                                                               op0=ALU.mult, op1=ALU.mult), r=["mix", "ss", "gbc"], w=["tmix"])
                        P.pool(lambda e, x_=x_: e.tensor_tensor(out=xa[:], in0=x_[:], in1=tmix[:], op=ALU.add),
                               r=[xk, "tmix"], w=["xa"])
                        P.act(lambda e: e.copy(out=xab[:], in_=xa[:]), r=["xa"], w=["xab"])
                        transpose_to(xab, "xab", 8, tp, "tp", xaT[:], "xaT", False)
                        P.act(lambda e, p_=p_: e.copy(out=pb[:], in_=p_[:]), r=[pk], w=["pb"])
                        transpose_to(pb, "pb", 2, tp2, "tp2", pT[:], "pT", False)
                        for half in range(2):
                            for k in range(8):
                                P.pe(lambda e, k=k, half=half: e.matmul(
                                    gate[:, half * 512:(half + 1) * 512], lhsT=xaT[:, k, :],
                                    rhs=wg[:, k, half * 512:(half + 1) * 512], start=(k == 0), stop=(k == 7)),
                                    r=["xaT", "wg"], w=["gate"])
                            for k in range(2):
                                P.pe(lambda e, k=k, half=half: e.matmul(
                                    pp[:, half * 512:(half + 1) * 512], lhsT=pT[:, k, :],
                                    rhs=wp[:, k, half * 512:(half + 1) * 512], start=(k == 0), stop=(k == 1)),
                                    r=["pT", "wp"], w=["pp"])
                        P.act(lambda e: e.activation(out=sg[:], in_=gate[:], func=AF.Sigmoid), r=["gate"], w=["sg"])
                        P.dve(lambda e: e.tensor_tensor(out=tmix[:], in0=sg[:], in1=pp[:], op=ALU.mult), r=["sg", "pp"], w=["tmix"])
                        P.pool(lambda e, o_=o_: e.tensor_tensor(out=o_[:], in0=xa[:], in1=tmix[:], op=ALU.add),
                               r=["xa", "tmix"], w=[ok_])
                        d_ap = dst_fn(seg, i)
                        P.dma("pool", lambda e, o_=o_, d_ap=d_ap: e.dma_start(out=d_ap, in_=o_[:]), r=[ok_])
                        bc += 1
            P.emit_block()

    xin = G["xin"]
    out_phase(0, G["YT0"],
              lambda seg, i: xin[(seg["e0"] + PAD + i) * 128:(seg["e0"] + PAD + i + 1) * 128, :],
              lambda seg, i: X1[(seg["o0"] + i) * 128:(seg["o0"] + i + 1) * 128, :],
              segs, G["w_out_ab"], lambda seg, i: (seg["o0"] + i) * 128)
    if dbg == 2:
        return

    with contextlib.ExitStack() as st:
        w_in = sb(st, "w_in", [128, 8, 3328], BF16)
        wstg = [sb(st, "wstg0", [128, 1664], F32), sb(st, "wstg1", [128, 1664], F32)]
        xt = [sb(st, "xt%d" % i, [128, D], F32) for i in range(2)]
        hn = [sb(st, "hn%d" % i, [128, D], BF16) for i in range(2)]
        hnT = [sb(st, "hnT%d" % i, [128, 8, 512], BF16) for i in range(2)]
        junk = sb(st, "junk", [128, D], BF16)
        ss = sb(st, "ss", [128, 2], F32)
        rstd = sb(st, "rstd", [128, 2], F32)
        fstage = sb(st, "fstage", [128, 16, 512], BF16)
        qkts = sb(st, "qkts", [128, 5, 512], BF16)
        vst = [sb(st, "vst%d" % i, [128, 642], BF16) for i in range(2)]
        qkf = sb(st, "qkf", [128, 10, 64], F32)
        sqj = sb(st, "sqj", [128, 10, 64], F32)
        ssq = sb(st, "ssq", [128, 10], F32)
        qn = sb(st, "qn", [128, 10, 64], F32)
        ra = sb(st, "ra", [128, 10, 32], F32)
        rb = sb(st, "rb", [128, 10, 32], F32)
        qr = sb(st, "qr", [128, 640], BF16)
        gain = sb(st, "gain", [128, 10, 64], F32)
        rp = [sb(st, "rp%d" % i, [128, 64], F32) for i in range(2)]
        tp = [ps(st, "tp%d" % i, [128, 8, 128], BF16) for i in range(2)]
        fm = [ps(st, "fm%d" % i, [128, 512]) for i in range(2)]
        tqA = ps(st, "tqA", [128, 512])
        tqB = ps(st, "tqB", [128, 256])
        tvD = ps(st, "tvD", [128, 512])
        load_weight(wstg, w_in, G["w_in_cd"], 8, 3328, "w_in")
        P.dma("sp", lambda e: e.dma_start(out=gain[:].rearrange("p h d -> p (h d)"), in_=G["qkgain"][0:1, :].partition_broadcast(128)), w=["gain"])
        P.dve(lambda e: e.tensor_scalar(out=gain[:, 0:8, :], in0=gain[:, 0:8, :], scalar1=0.125, scalar2=None, op0=ALU.mult),
              r=["gain"], w=["gain"])
        for i in range(2):
            P.dve(lambda e, i=i: e.memset(vst[i][:], 1.0), w=["vst%d" % i])
        bc = 0
        sc = 0
        rope = G["rope"]
        for seg in segs:
            pf = seg["name"] == "PF"
            h0 = 8 if pf else 0
            for sti in range(seg["nb"] // 4):
                hT = hnT[sc % 2]; hTk = "hnT%d" % (sc % 2)
                ot0 = (seg["o0"] + sti * 4) * 128
                for b in range(4):
                    ob = seg["o0"] + sti * 4 + b
                    x_ = xt[bc % 2]; xk = "xt%d" % (bc % 2)
                    h_ = hn[bc % 2]; hk = "hn%d" % (bc % 2)
                    P.dma("sp", lambda e, x_=x_, ob=ob: e.dma_start(out=x_[:], in_=X1[ob * 128:(ob + 1) * 128, :]), w=[xk])
                    norm_block(x_[:], xk, 2, h_[:], hk, ss[:, bc % 2:bc % 2 + 1], rstd[:, bc % 2:bc % 2 + 1], junk, str(bc % 2))
                    transpose_to(h_, hk, 8, tp[bc % 2], "tp%d" % (bc % 2), hT[:, :, b * 128:(b + 1) * 128], hTk, bc % 2 == 0)
                    bc += 1
                lst = [("kd", 1792, 4, "k")] if pf else CD_FM
                ci = 0
                for (nm, c0, nch, kind) in lst:
                    for j in range(nch):
                        f0 = c0 + j * 128
                        pf_ = fm[ci % 2]; pk = "fm%d" % (ci % 2)
                        for k in range(8):
                            P.pe(lambda e, pf_=pf_, k=k, f0=f0, hT=hT: e.matmul(pf_[:], lhsT=w_in[:, k, f0:f0 + 128], rhs=hT[:, k, :],
                                                                           start=(k == 0), stop=(k == 7)), r=["w_in", hTk], w=[pk])
                        if kind == "g":
                            P.act(lambda e, pf_=pf_, ci=ci: e.activation(out=fstage[:, ci, :], in_=pf_[:], func=AF.Silu), r=[pk], w=["fstage"])
                        elif kind == "q":
                            P.dve(lambda e, pf_=pf_, ci=ci: e.tensor_scalar(out=fstage[:, ci, :], in0=pf_[:], scalar1=0.125, scalar2=None,
                                                                       op0=ALU.mult), r=[pk], w=["fstage"])
                        else:
                            P.dve(lambda e, pf_=pf_, ci=ci: e.tensor_copy(out=fstage[:, ci, :], in_=pf_[:]), r=[pk], w=["fstage"])
                        ci += 1
                FT1v = FT1.rearrange("(c p) t -> p c t", p=128)
                if pf:
                    P.dma("pool", lambda e, ot0=ot0: e.dma_start(out=FT1v[:, 13:17, ot0:ot0 + 512], in_=fstage[:, 0:4, :]), r=["fstage"])
                else:
                    P.dma("pool", lambda e, ot0=ot0: e.dma_start(out=FT1v[:, 5:21, ot0:ot0 + 512], in_=fstage[:, 0:16, :]), r=["fstage"])
                for b in range(4):
                    ob = seg["o0"] + sti * 4 + b
                    vs = vst[b % 2]; vk = "vst%d" % (b % 2)
                    r_ = rp[b % 2]; rk = "rp%d" % (b % 2)
                    P.dma("sp", lambda e, r_=r_, ob=ob: e.dma_start(out=r_[:], in_=rope[ob * 128:(ob + 1) * 128, :]), w=[rk])
                    if not pf:
                        for k in range(8):
                            P.pe(lambda e, k=k, b=b, hT=hT: e.matmul(tqA[:], lhsT=hT[:, k, b * 128:(b + 1) * 128], rhs=w_in[:, k, 0:512],
                                                               start=(k == 0), stop=(k == 7)), r=["w_in", hTk], w=["tqA"])
                    for k in range(8):
                        P.pe(lambda e, k=k, b=b, hT=hT: e.matmul(tqB[:], lhsT=hT[:, k, b * 128:(b + 1) * 128], rhs=w_in[:, k, 512:768],
                                                           start=(k == 0), stop=(k == 7)), r=["w_in", hTk], w=["tqB"])
                    for k in range(8):
                        P.pe(lambda e, k=k, b=b, hT=hT: e.matmul(tvD[:], lhsT=hT[:, k, b * 128:(b + 1) * 128], rhs=w_in[:, k, 2304:2816],
                                                           start=(k == 0), stop=(k == 7)), r=["w_in", hTk], w=["tvD"])
                    P.dve(lambda e, vs=vs: e.tensor_copy(out=vs[:, 0:130].rearrange("p (h d) -> p h d", d=65)[:, :, 0:64],
                                                         in_=tqB[:, 128:256].rearrange("p (h d) -> p h d", d=64)), r=["tqB"], w=[vk])
                    P.act(lambda e, vs=vs: e.copy(out=vs[:, 130:642], in_=tvD[:]), r=["tvD"], w=[vk])
                    P.dma("pool", lambda e, vs=vs, ob=ob: e.dma_start(out=VT1[ob * 128:(ob + 1) * 128, :], in_=vs[:]), r=[vk])
                    if not pf:
                        P.act(lambda e: e.copy(out=qkf[:, 0:8, :], in_=tqA[:].rearrange("p (h d) -> p h d", d=64)), r=["tqA"], w=["qkf"])
                    P.act(lambda e: e.copy(out=qkf[:, 8:10, :], in_=tqB[:, 0:128].rearrange("p (h d) -> p h d", d=64)), r=["tqB"], w=["qkf"])
                    hs = slice(h0, 10)
                    nh = 10 - h0
                    P.dve(lambda e, hs=hs: e.tensor_tensor(out=sqj[:, hs, :], in0=qkf[:, hs, :], in1=qkf[:, hs, :], op=ALU.mult), r=["qkf"], w=["sqj"])
                    P.dve(lambda e, hs=hs: e.tensor_reduce(out=ssq[:, hs], in_=sqj[:, hs, :], axis=mybir.AxisListType.X, op=ALU.add),
                          r=["sqj"], w=["ssq"])
                    P.dve(lambda e, hs=hs: e.tensor_scalar(out=ssq[:, hs], in0=ssq[:, hs], scalar1=1.0 / 64, scalar2=EPS, op0=ALU.mult, op1=ALU.add),
                          r=["ssq"], w=["ssq"])
                    P.act(lambda e, hs=hs: e.activation(out=ssq[:, hs], in_=ssq[:, hs], func=AF.Sqrt), r=["ssq"], w=["ssq"])
                    P.dve(lambda e, hs=hs: e.reciprocal(out=ssq[:, hs], in_=ssq[:, hs]), r=["ssq"], w=["ssq"])
                    P.dve(lambda e, hs=hs, nh=nh: e.tensor_tensor(out=qn[:, hs, :], in0=qkf[:, hs, :],
                                                                in1=ssq[:, hs].unsqueeze(2).to_broadcast([128, nh, 64]), op=ALU.mult),
                          r=["qkf", "ssq"], w=["qn"])
                    P.dve(lambda e, hs=hs: e.tensor_tensor(out=qn[:, hs, :], in0=qn[:, hs, :], in1=gain[:, hs, :], op=ALU.mult),
                          r=["qn", "gain"], w=["qn"])
                    qv = qn[:].rearrange("p h (j two) -> p h j two", two=2)
                    qrv = qr[:].rearrange("p (h j two) -> p h j two", two=2, j=32)
                    cs = lambda r_=r_, nh=nh: r_[:, 0:32].unsqueeze(1).to_broadcast([128, nh, 32])
                    sn = lambda r_=r_, nh=nh: r_[:, 32:64].unsqueeze(1).to_broadcast([128, nh, 32])
                    P.dve(lambda e, hs=hs, cs=cs: e.tensor_tensor(out=ra[:, hs, :], in0=qv[:, hs, :, 0], in1=cs(), op=ALU.mult), r=["qn", rk], w=["ra"])
                    P.dve(lambda e, hs=hs, sn=sn: e.tensor_tensor(out=rb[:, hs, :], in0=qv[:, hs, :, 1], in1=sn(), op=ALU.mult), r=["qn", rk], w=["rb"])
                    P.dve(lambda e, hs=hs: e.tensor_tensor(out=qrv[:, hs, :, 0], in0=ra[:, hs, :], in1=rb[:, hs, :], op=ALU.subtract),
                          r=["ra", "rb"], w=["qr"])
                    P.dve(lambda e, hs=hs, sn=sn: e.tensor_tensor(out=ra[:, hs, :], in0=qv[:, hs, :, 0], in1=sn(), op=ALU.mult), r=["qn", rk], w=["ra"])
                    P.dve(lambda e, hs=hs, cs=cs: e.tensor_tensor(out=rb[:, hs, :], in0=qv[:, hs, :, 1], in1=cs(), op=ALU.mult), r=["qn", rk], w=["rb"])
                    P.dve(lambda e, hs=hs: e.tensor_tensor(out=qrv[:, hs, :, 1], in0=ra[:, hs, :], in1=rb[:, hs, :], op=ALU.add),
                          r=["ra", "rb"], w=["qr"])
                    c0_ = 4 if pf else 0
                    t_ = tp[b % 2]; tk = "tp%d" % (b % 2)
                    for k in range(c0_, 5):
                        P.pe(lambda e, k=k, t_=t_: e.transpose(out=t_[:, k, :], in_=qr[:, k * 128:(k + 1) * 128], identity=ident[:]),
                             r=["qr", "ident"], w=[tk])
                    P.dve(lambda e, t_=t_, c0_=c0_, b=b: e.tensor_copy(out=qkts[:, c0_:5, b * 128:(b + 1) * 128], in_=t_[:, c0_:5, :]),
                          r=[tk], w=["qkts"])
                c0_ = 4 if pf else 0
                P.dma("pool", lambda e, ot0=ot0, c0_=c0_: e.dma_start(out=FT1v[:, c0_:5, ot0:ot0 + 512], in_=qkts[:, c0_:5, :]), r=["qkts"])
                sc += 1
        P.emit_block()
    if dbg == 3:
        return
    _phase45(nc, P, cfg, sb, ps, G, out_phase)


def _phase45(nc, P, cfg, sb, ps, G, out_phase):
    import contextlib
    segs = cfg["segs"]
    SEG_S, SEG_PO, SEG_PF = segs
    ones_f = G["ones_f"]
    FT1, VT1, X1, YT1 = G["FT1"], G["VT1"], G["X1"], G["YT1"]
    qtab, ktab = G["qtab"], G["ktab"]
    NBS = cfg["NBS"]
    with contextlib.ExitStack() as st:
        nkbmax = max(SEG_S["nb"], SEG_PF["nb"])
        Kb = [sb(st, "Kb%d" % i, [128, nkbmax * 128], BF16) for i in range(2)]
        Vb = sb(st, "Vb", [128, nkbmax * 128], BF16)
        Qb = [sb(st, "Qb%d" % i, [128, 2, 512], BF16) for i in range(2)]
        Gb = [sb(st, "Gb%d" % i, [128, 512], BF16) for i in range(2)]
        PT = [sb(st, "PT%d" % i, [128, 2, 512], BF16) for i in range(3)]
        Mb = [sb(st, "Mb%d" % i, [128, 512], F32) for i in range(2)]
        rinv = sb(st, "rinv", [128, 512], F32)
        on_ = sb(st, "on_", [128, 512], F32)
        o0 = sb(st, "o0", [128, 512], F32)
        od = sb(st, "od", [128, 512], F32)
        sq = sb(st, "sq", [128, 512], F32)
        ybuf = [sb(st, "ybuf%d" % i, [128, 512], BF16) for i in range(2)]
        ones_b = sb(st, "ones_b", [128, 128], BF16)
        lv = sb(st, "lv", [1, 4, 64], F32)
        pr = sb(st, "pr", [1, 2, 64], F32)
        ls = sb(st, "ls", [1, 4], F32)
        nl = sb(st, "nl", [128, 1], F32)
        gsc = sb(st, "gsc", [128, 1], F32)
        SS = [ps(st, "SS%d" % i, [128, 2, 512]) for i in range(2)]
        OTd = [ps(st, "OTd%d" % i, [128, 512]) for i in range(2)]
        LB = [ps(st, "LB%d" % i, [128, 512]) for i in range(2)]

        P.dve(lambda e: e.memset(ones_b[:], 1.0), w=["ones_b"])
        P.dma("sp", lambda e: e.dma_start(out=lv[:], in_=G["lamv"][:, :, :]), w=["lv"])
        P.dma("sp", lambda e: e.dma_start(out=gsc[:], in_=G["subln"][:, :]), w=["gsc"])
        P.dve(lambda e: e.tensor_scalar(out=gsc[:], in0=gsc[:], scalar1=(1.0 - LAM_INIT1), scalar2=None, op0=ALU.mult), r=["gsc"], w=["gsc"])
        P.dve(lambda e: e.tensor_tensor(out=pr[:, 0, :], in0=lv[:, 0, :], in1=lv[:, 1, :], op=ALU.mult), r=["lv"], w=["pr"])
        P.dve(lambda e: e.tensor_tensor(out=pr[:, 1, :], in0=lv[:, 2, :], in1=lv[:, 3, :], op=ALU.mult), r=["lv"], w=["pr"])
        P.dve(lambda e: e.tensor_reduce(out=ls[:, 0:2], in_=pr[:], axis=mybir.AxisListType.X, op=ALU.add), r=["pr"], w=["ls"])
        P.act(lambda e: e.activation(out=ls[:, 0:2], in_=ls[:, 0:2], func=AF.Exp), r=["ls"], w=["ls"])
        P.dve(lambda e: e.tensor_tensor(out=ls[:, 2:3], in0=ls[:, 1:2], in1=ls[:, 0:1], op=ALU.subtract), r=["ls"], w=["ls"])
        P.dve(lambda e: e.tensor_scalar(out=ls[:, 3:4], in0=ls[:, 2:3], scalar1=-LAM_INIT1, scalar2=None, op0=ALU.add), r=["ls"], w=["ls"])
        P.pe(lambda e: e.matmul(LB[0][:, 0:1], lhsT=ones_f[0:1, 0:128], rhs=ls[0:1, 3:4], start=True, stop=True), r=["ones_f", "ls"], w=["LB0"])
        P.dve(lambda e: e.tensor_copy(out=nl[:], in_=LB[0][:, 0:1]), r=["LB0"], w=["nl"])

        gg = 0
        job = 0
        qj = 0
        gj = 0
        for seg, qbase, kseg in ((SEG_S, 0, SEG_S), (SEG_PO, NBS, SEG_PF)):
            nkb = kseg["nb"]
            ko = kseg["o0"] * 128
            nch = seg["nb"] // 4
            for kv in range(2):
                K_ = Kb[0]
                r0 = 4 * 128 + kv * 64
                for c0 in range(0, nkb * 128, 2048):
                    c1 = min(nkb * 128, c0 + 2048)
                    P.dma("sp", lambda e, c0=c0, c1=c1, r0=r0, K_=K_, ko=ko: e.dma_start(out=K_[0:64, c0:c1], in_=FT1[r0:r0 + 64, ko + c0:ko + c1]), w=["K0"])
                Vv = Vb[:, 0:nkb * 65].rearrange("p (b f) -> p b f", f=65)
                for b0 in range(0, nkb, 16):
                    b1 = min(nkb, b0 + 16)
                    P.dma("sp", lambda e, b0=b0, b1=b1, Vv=Vv, kv=kv, ko=ko: e.dma_start(
                        out=Vv[:, b0:b1, :], in_=VT1[ko + b0 * 128:ko + b1 * 128, kv * 65:(kv + 1) * 65].rearrange("(b p) f -> p b f", p=128)),
                        w=["V"])
                for h in range(4 * kv, 4 * kv + 4):
                    for ci in range(nch):
                        tok = (seg["o0"] + 4 * ci) * 128
                        qcol = (qbase + 4 * ci) * 128
                        Q_ = Qb[qj % 2]; Qk = "Q%d" % (qj % 2); qj += 1
                        G_ = Gb[gj % 2]; Gk = "G%d" % (gj % 2); gj += 1
                        rq = (h // 2) * 128 + (h % 2) * 64
                        rg = (5 + h // 2) * 128 + (h % 2) * 64
                        P.dma("pool", lambda e, Q_=Q_, rq=rq, tok=tok: e.dma_start(out=Q_[0:64, 0, :], in_=FT1[rq:rq + 64, tok:tok + 512]), w=[Qk])
                        P.dma("pool", lambda e, G_=G_, rg=rg, tok=tok: e.dma_start(out=G_[0:64, :], in_=FT1[rg:rg + 64, tok:tok + 512]), w=[Gk])
                        O_ = OTd[job % 2]; Ok = "OTd%d" % (job % 2)
                        L_ = LB[job % 2]; Lk = "LB%d" % (job % 2)
                        ng = nkb // 2

                        def qk(g, gg_):
                            S_ = SS[gg_ % 2]
                            for j in range(2):
                                kb = 2 * g + j
                                P.pe(lambda e, S_=S_, j=j, kb=kb, Q_=Q_, K_=K_: e.matmul(S_[:, j, :], lhsT=K_[0:64, kb * 128:(kb + 1) * 128],
                                                                             rhs=Q_[0:64, 0, :], start=True, stop=True),
                                     r=["K0", Qk], w=["SS%d" % (gg_ % 2)])

                        def ex(g, gg_):
                            S_ = SS[gg_ % 2]; p_ = PT[gg_ % 3]
                            P.act(lambda e, S_=S_, p_=p_: e.activation(out=p_[:], in_=S_[:], func=AF.Exp),
                                  r=["SS%d" % (gg_ % 2)], w=["PT%d" % (gg_ % 3)])

                        def pv(g, gg_):
                            p_ = PT[gg_ % 3]
                            for j in range(2):
                                kb = 2 * g + j
                                P.pe(lambda e, p_=p_, j=j, kb=kb, O_=O_, Vv=Vv, nkb=nkb: e.matmul(O_[0:65, :], lhsT=Vv[:, kb, :], rhs=p_[:, j, :],
                                                                             start=(kb == 0), stop=(kb == nkb - 1)),
                                     r=["V", "PT%d" % (gg_ % 3)], w=[Ok])
                        qk(0, gg)
                        for g in range(1, ng):
                            qk(g, gg + g)
                            ex(g - 1, gg + g - 1)
                            pv(g - 1, gg + g - 1)
                        ex(ng - 1, gg + ng - 1)
                        pv(ng - 1, gg + ng - 1)
                        gg += ng
                        y_ = ybuf[job % 2]; yk = "ybuf%d" % (job % 2)
                        P.dve(lambda e, O_=O_: e.reciprocal(out=rinv[64:65, :], in_=O_[64:65, :]), r=[Ok], w=["rinv"])
                        P.dve(lambda e, O_=O_, G_=G_: e.tensor_tensor(out=on_[0:64, :], in0=O_[0:64, :], in1=G_[0:64, :], op=ALU.mult),
                              r=[Ok, Gk], w=["on_"])
                        P.pe(lambda e, L_=L_: e.matmul(L_[0:64, :], lhsT=ones_f[64:65, 0:64], rhs=rinv[64:65, :], start=True, stop=True),
                             r=["ones_f", "rinv"], w=[Lk])
                        P.dve(lambda e, L_=L_, y_=y_: e.tensor_tensor(out=y_[0:64, :], in0=on_[0:64, :], in1=L_[0:64, :], op=ALU.mult),
                              r=["on_", Lk], w=[yk])
                        P.dma("pool", lambda e, y_=y_, h=h, qcol=qcol: e.dma_start(out=YT1[h * 64:(h + 1) * 64, qcol:qcol + 512], in_=y_[0:64, :]), r=[yk])
                        job += 1
            for h in range(4):
                for m in range(2):
                    K_ = Kb[m]
                    r0 = (13 + h) * 128 + m * 64
                    for c0 in range(0, nkb * 128, 2048):
                        c1 = min(nkb * 128, c0 + 2048)
                        P.dma("sp", lambda e, K_=K_, c0=c0, c1=c1, r0=r0, ko=ko: e.dma_start(out=K_[0:64, c0:c1], in_=FT1[r0:r0 + 64, ko + c0:ko + c1]),
                              w=["K%d" % m])
                    P.dma("sp", lambda e, K_=K_, h=h, nkb=nkb, ko=ko: e.dma_start(out=K_[64:68, 0:nkb * 128], in_=ktab[h, :, ko:ko + nkb * 128]), w=["K%d" % m])
                Vv = Vb[:, 0:nkb * 128].rearrange("p (b f) -> p b f", f=128)
                for b0 in range(0, nkb, 16):
                    b1 = min(nkb, b0 + 16)
                    P.dma("sp", lambda e, b0=b0, b1=b1, Vv=Vv, h=h, ko=ko: e.dma_start(
                        out=Vv[:, b0:b1, :], in_=VT1[ko + b0 * 128:ko + b1 * 128, 130 + h * 128:130 + (h + 1) * 128].rearrange("(b p) f -> p b f", p=128)),
                        w=["V"])
                for ci in range(nch):
                    tok = (seg["o0"] + 4 * ci) * 128
                    qcol = (qbase + 4 * ci) * 128
                    G_ = Gb[gj % 2]; Gk = "G%d" % (gj % 2); gj += 1
                    rg = (17 + h) * 128
                    P.dma("pool", lambda e, G_=G_, rg=rg, tok=tok: e.dma_start(out=G_[:, :], in_=FT1[rg:rg + 128, tok:tok + 512]), w=[Gk])
                    for m in range(2):
                        K_ = Kb[m]; Kk = "K%d" % m
                        Q_ = Qb[qj % 2]; Qk = "Q%d" % (qj % 2); qj += 1
                        rq = (9 + h) * 128 + m * 64
                        for v in range(2):
                            P.dma("pool", lambda e, Q_=Q_, rq=rq, tok=tok, v=v: e.dma_start(out=Q_[0:64, v, :], in_=FT1[rq:rq + 64, tok:tok + 512]), w=[Qk])
                            P.dma("pool", lambda e, Q_=Q_, h=h, v=v, qcol=qcol: e.dma_start(out=Q_[64:68, v, :], in_=qtab[h, v, :, qcol:qcol + 512]), w=[Qk])
                        O_ = OTd[job % 2]; Ok = "OTd%d" % (job % 2)
                        L_ = LB[job % 2]; Lk = "LB%d" % (job % 2)

                        def qk(kb, gg_):
                            S_ = SS[gg_ % 2]
                            for v in range(2):
                                P.pe(lambda e, S_=S_, v=v, kb=kb, Q_=Q_, K_=K_: e.matmul(S_[:, v, :], lhsT=K_[0:68, kb * 128:(kb + 1) * 128],
                                                                                    rhs=Q_[0:68, v, :], start=True, stop=True),
                                     r=[Kk, Qk], w=["SS%d" % (gg_ % 2)])

                        def ex(kb, gg_):
                            S_ = SS[gg_ % 2]; p_ = PT[gg_ % 3]; M_ = Mb[gg_ % 2]
                            P.act(lambda e, S_=S_, M_=M_: e.copy(out=M_[:], in_=S_[:, 0, :]), r=["SS%d" % (gg_ % 2)], w=["Mb%d" % (gg_ % 2)])
                            P.dve(lambda e, S_=S_, M_=M_: e.tensor_tensor(out=M_[:], in0=M_[:], in1=S_[:, 1, :], op=ALU.min),
                                  r=["SS%d" % (gg_ % 2), "Mb%d" % (gg_ % 2)], w=["Mb%d" % (gg_ % 2)])
                            P.act(lambda e, M_=M_, p_=p_: e.activation(out=p_[:, 0, :], in_=M_[:], func=AF.Exp),
                                  r=["Mb%d" % (gg_ % 2)], w=["PT%d" % (gg_ % 3)])

                        def pv(kb, gg_):
                            p_ = PT[gg_ % 3]
                            P.pe(lambda e, p_=p_, kb=kb, O_=O_, Vv=Vv, nkb=nkb: e.matmul(O_[:, :], lhsT=Vv[:, kb, :], rhs=p_[:, 0, :],
                                                                    start=(kb == 0), stop=(kb == nkb - 1)), r=["V", "PT%d" % (gg_ % 3)], w=[Ok])
                            P.pe(lambda e, p_=p_, kb=kb, L_=L_, nkb=nkb: e.matmul(L_[:, :], lhsT=ones_b[:], rhs=p_[:, 0, :],
                                                                    start=(kb == 0), stop=(kb == nkb - 1)), r=["ones_b", "PT%d" % (gg_ % 3)], w=[Lk])
                        qk(0, gg)
                        for g in range(1, nkb):
                            qk(g, gg + g)
                            ex(g - 1, gg + g - 1)
                            pv(g - 1, gg + g - 1)
                        ex(nkb - 1, gg + nkb - 1)
                        pv(nkb - 1, gg + nkb - 1)
                        gg += nkb
                        P.dve(lambda e, L_=L_: e.reciprocal(out=rinv[:], in_=L_[:]), r=[Lk], w=["rinv"])
                        if m == 0:
                            P.dve(lambda e, O_=O_: e.tensor_tensor(out=o0[:], in0=O_[:], in1=rinv[:], op=ALU.mult), r=[Ok, "rinv"], w=["o0"])
                        else:
                            y_ = ybuf[job % 2]; yk = "ybuf%d" % (job % 2)
                            P.dve(lambda e, O_=O_: e.tensor_tensor(out=on_[:], in0=O_[:], in1=rinv[:], op=ALU.mult), r=[Ok, "rinv"], w=["on_"])
                            P.dve(lambda e: e.scalar_tensor_tensor(out=od[:], in0=on_[:], scalar=nl[:, 0:1], in1=o0[:], op0=ALU.mult, op1=ALU.add),
                                  r=["on_", "nl", "o0"], w=["od"])
                            P.dve(lambda e: e.tensor_tensor(out=sq[:], in0=od[:], in1=od[:], op=ALU.mult), r=["od"], w=["sq"])
                            P.pe(lambda e, L_=L_: e.matmul(L_[:, :], lhsT=ones_f[:, :], rhs=sq[:], start=True, stop=True), r=["ones_f", "sq"], w=[Lk])
                            P.dve(lambda e, L_=L_: e.tensor_scalar(out=sq[:], in0=L_[:], scalar1=1.0 / 128, scalar2=EPS, op0=ALU.mult, op1=ALU.add),
                                  r=[Lk], w=["sq"])
                            P.act(lambda e: e.activation(out=sq[:], in_=sq[:], func=AF.Sqrt), r=["sq"], w=["sq"])
                            P.dve(lambda e: e.reciprocal(out=sq[:], in_=sq[:]), r=["sq"], w=["sq"])
                            P.dve(lambda e: e.tensor_tensor(out=od[:], in0=od[:], in1=sq[:], op=ALU.mult), r=["od", "sq"], w=["od"])
                            P.dve(lambda e, y_=y_, G_=G_: e.scalar_tensor_tensor(out=y_[:], in0=od[:], scalar=gsc[:, 0:1], in1=G_[:], op0=ALU.mult, op1=ALU.mult),
                                  r=["od", "gsc", Gk], w=[yk])
                            P.dma("pool", lambda e, y_=y_, h=h, qcol=qcol: e.dma_start(
                                out=YT1[512 + h * 128:512 + (h + 1) * 128, qcol:qcol + 512], in_=y_[:]), r=[yk])
                        job += 1
        P.emit_block()

    yout = G["yout"]

    def qidx(seg, i):
        return (0 if seg["name"] == "S" else NBS) + i
    out_phase(1, YT1,
              lambda seg, i: X1[(seg["o0"] + i) * 128:(seg["o0"] + i + 1) * 128, :],
              lambda seg, i: yout[qidx(seg, i) * 128:(qidx(seg, i) + 1) * 128, :],
              [SEG_S, SEG_PO], G["w_out_cd"], lambda seg, i: qidx(seg, i) * 128)


def host_constants(cfg):
    c = {}
    c["eye"] = np.eye(128, dtype=np.float32)
    colmask, dc = nb_static()
    c["colmask"] = colmask
    k = np.arange(128)[:, None]
    q = np.arange(128)[None, :]
    sl8 = alibi_slopes(8)
    al = np.zeros((128, 3, 8, 128), np.float32)
    for r in range(3):
        dist = np.abs(128 * (r - 1) + k - q).astype(np.float32)
        for h in range(8):
            al[:, r, h, :] = np.where(dist <= 128, -sl8[h] * dist, NEG)
    c["alibia"] = al.astype(NPBF)
    sl4 = alibi_slopes(4)
    dg = np.zeros((128, 4, 128), np.float32)
    for h in range(4):
        dg[:, h, :] = -sl4[h] * np.abs(k - q)
    c["diagb"] = dg.astype(NPBF)
    return c


def seg_positions(cfg, core):
    nbs, nbpo, nbpf = cfg["NBS"], cfg["NBPO"], cfg["NBPF"]
    ps_ = np.arange(nbs * 128)
    ppo = core * nbpo * 128 + np.arange(nbpo * 128)
    ppf = np.arange(nbpf * 128)
    return ps_, ppo, ppf


def prepare_core(cfg, core, inp, consts):
    nbs, nbpo, nbpf = cfg["NBS"], cfg["NBPO"], cfg["NBPF"]
    SS, SP = nbs * 128, nbpf * 128
    pad = PAD * 128
    m = dict(consts)
    xs = inp["x_sample"][core]
    xp = inp["x_prompt"][0]
    z = np.zeros((pad, D), np.float32)
    xpp = np.concatenate([z, xp, z], axis=0)
    lo = core * nbpo * 128
    xin = np.concatenate([z, xs, z, xpp[lo:lo + nbpo * 128 + 2 * pad], xpp], axis=0)
    m["xin"] = np.ascontiguousarray(xin)
    vs = np.concatenate([np.zeros(pad), np.ones(SS), np.zeros(pad)])
    vpf = np.concatenate([np.zeros(pad), np.ones(SP), np.zeros(pad)])
    valid = np.concatenate([vs, vpf[lo:lo + nbpo * 128 + 2 * pad], vpf]).astype(np.float32)
    m["validin"] = np.ascontiguousarray(valid.reshape(-1, 128).T)
    pp = inp["p_prompt"][:, 0]
    m["pin"] = np.ascontiguousarray(np.concatenate(
        [inp["p_sample"][:, core], pp[:, lo:lo + nbpo * 128], pp], axis=1))
    m["gvec"] = np.ascontiguousarray(np.stack([inp["norm_pre"][0], inp["norm_post"][0], inp["norm_pre"][1], inp["norm_post"][1]]))
    m["w_in_ab"] = inp["w_in_ab"][0]; m["w_out_ab"] = inp["w_out_ab"][0]
    m["w_in_cd"] = inp["w_in_cd"][0]; m["w_out_cd"] = inp["w_out_cd"][0]
    m["w_ple"] = inp["w_ple"]; m["w_gate"] = inp["w_ple_gate"]
    m["a_sink"] = inp["a_sink"]
    rpb = inp["b_rpb"][0]
    kr = np.arange(2)[:, None, None, None]; kc = np.arange(64)[None, :, None, None]
    qr = np.arange(2)[None, None, :, None]; qc = np.arange(64)[None, None, None, :]
    dcx = np.broadcast_to(np.clip(kc - qc, -15, 15) + 15, (2, 64, 2, 64)).reshape(128, 128)
    g = np.zeros((128, 7, 8, 128), np.float32)
    for r in range(7):
        drx = np.broadcast_to(np.clip(2 * (r - 3) + kr - qr + 7, 0, 14), (2, 64, 2, 64)).reshape(128, 128)
        for h in range(8):
            g[:, r, h, :] = rpb[h][drx, dcx]
    m["rpbg"] = g
    rm = np.zeros((128, 2, 5, 7, 128), np.float32)
    rows_any = 64
    nblk_any = rows_any // 2
    reps = {0: 0, 1: 1, 2: nblk_any // 2, 3: nblk_any - 2, 4: nblk_any - 1}
    rows_p = nbpf * 2
    for cl in range(5):
        for r in range(7):
            n = reps[cl]
            rm[:, 0, cl, r, :] = nb_rowmask_tile(rows_any, n, n + r - 3)
    seg_po = cfg["segs"][1]
    done = set()
    for i in range(nbpo):
        cl = blk_class(seg_po, i)
        if cl in done:
            continue
        done.add(cl)
        n = core * nbpo + i
        for r in range(7):
            rm[:, 1, cl, r, :] = nb_rowmask_tile(rows_p, n, n + r - 3)
    for cl in range(5):
        if cl not in done:
            rm[:, 1, cl] = NEG
    m["rowmask"] = rm.astype(NPBF)
    m["qkgain"] = np.ascontiguousarray(np.concatenate([np.tile(inp["c_q_norm"][0], 8), np.tile(inp["c_k_norm"][0], 2)])[None, :])
    ps_, ppo, ppf = seg_positions(cfg, core)
    pos = np.concatenate([ps_, ppo, ppf])
    inv = (10000.0 ** (-2.0 * np.arange(16) / 32)).astype(np.float32)
    row = (pos // 64).astype(np.float32); col = (pos % 64).astype(np.float32)
    ang = np.concatenate([row[:, None] * inv, col[:, None] * inv], axis=-1).astype(np.float32)
    m["rope"] = np.ascontiguousarray(np.concatenate([np.cos(ang), np.sin(ang)], axis=-1).astype(np.float32))
    m["lamv"] = np.ascontiguousarray(np.stack([inp["d_lambda_q1"][0], inp["d_lambda_k1"][0], inp["d_lambda_q2"][0], inp["d_lambda_k2"][0]])[None])
    m["subln"] = np.ascontiguousarray(inp["d_subln"][0][:, None])
    sl4 = alibi_slopes(4)
    posq = np.concatenate([ps_, ppo]).astype(np.float32)
    qa_, qb_ = np.floor(posq / 128), np.mod(posq, 128)
    qt = np.zeros((4, 2, 4, posq.size), np.float32)
    kt = np.zeros((4, 4, pos.size), np.float32)
    ka_, kb_ = np.floor(pos / 128).astype(np.float32), np.mod(pos, 128).astype(np.float32)
    for h in range(4):
        s = sl4[h]
        qt[h, 0] = np.stack([-s * 128 * qa_, -s * qb_, np.ones_like(qa_), np.ones_like(qa_)])
        qt[h, 1] = np.stack([s * 128 * qa_, s * qb_, -np.ones_like(qa_), -np.ones_like(qa_)])
        kt[h] = np.stack([np.ones_like(ka_), np.ones_like(ka_), s * 128 * ka_, s * kb_])
    m["qtab"] = qt.astype(NPBF)
    m["ktab"] = kt.astype(NPBF)
    return m


_CACHE = {}


def kernel(**inputs):
    inp = {k: np.asarray(v) for k, v in inputs.items()}
    nbs = inp["x_sample"].shape[1] // 128
    nbpf = inp["x_prompt"].shape[1] // 128
    cfg = cfg_make(nbs, nbpf)
    key = (nbs, nbpf)
    if key not in _CACHE:
        _CACHE[key] = build(cfg)
    nc = _CACHE[key]
    consts = host_constants(cfg)
    maps = [prepare_core(cfg, c, inp, consts) for c in range(NCORE)]
    res = run_bass_kernel_spmd(nc, maps, core_ids=list(range(NCORE)))
    nbpo = cfg["NBPO"]
    ys = np.zeros((NCORE, nbs * 128, D), np.float32)
    yp = np.zeros((1, nbpf * 128, D), np.float32)
    for c in range(NCORE):
        y = np.asarray(res.results[c]["yout"])
        ys[c] = y[:nbs * 128]
        yp[0, c * nbpo * 128:(c + 1) * nbpo * 128] = y[nbs * 128:]
    return (yp, ys)
```

```python
import math
import numpy as np
import ml_dtypes
import concourse.bass as bass
import concourse.mybir as mybir
from concourse.bass_utils import run_bass_kernel_spmd

F32 = mybir.dt.float32
BF16 = mybir.dt.bfloat16
AF = mybir.ActivationFunctionType
ALU = mybir.AluOpType
NPBF = ml_dtypes.bfloat16

NCORE = 8
D = 1024
PAD = 4
EPS = 1e-6
NEG = -30000.0
NSEM_DMA = 8


class Prog:
    ENG = ("pe", "act", "dve", "pool", "sp")

    def __init__(self, nc, sems):
        self.nc = nc
        self.sems = sems
        self.sigcount = {e: 0 for e in self.ENG}
        self.dmacount = {"sp": 0, "pool": 0}
        self.waited = {}
        self.reset()

    def reset(self):
        self.ops = []
        self.lw = {}
        self.rd = {}

    def op(self, eng, fn, r=(), w=(), dma=False):
        idx = len(self.ops)
        deps = set()
        for b in r:
            x = self.lw.get(b)
            if x is not None:
                deps.add(x)
        for b in w:
            x = self.lw.get(b)
            if x is not None:
                deps.add(x)
            deps.update(self.rd.get(b, ()))
        for b in r:
            self.rd.setdefault(b, []).append(idx)
        for b in w:
            self.lw[b] = idx
            self.rd[b] = []
        self.ops.append(dict(eng=eng, fn=fn, deps=deps, dma=dma, need=False))
        return idx

    def pe(self, fn, r=(), w=()): return self.op("pe", fn, r, w)
    def act(self, fn, r=(), w=()): return self.op("act", fn, r, w)
    def dve(self, fn, r=(), w=()): return self.op("dve", fn, r, w)
    def pool(self, fn, r=(), w=()): return self.op("pool", fn, r, w)
    def dma(self, q, fn, r=(), w=()): return self.op(q, fn, r, w, dma=True)

    def emit_block(self, name=None):
        nc = self.nc
        ops = self.ops
        for o in ops:
            for d in o["deps"]:
                p = ops[d]
                if p["eng"] == o["eng"] and o["eng"] == "pe" and not p["dma"]:
                    continue
                p["need"] = True
        for o in ops:
            e = o["eng"]
            if o["dma"]:
                k = self.dmacount[e]
                self.dmacount[e] = k + 1
                o["sig"] = ((e, k % NSEM_DMA), 16 * (k // NSEM_DMA + 1))
                o["pre"] = ((e, k % NSEM_DMA), 16 * (k // NSEM_DMA)) if k >= NSEM_DMA else None
            elif o["need"]:
                self.sigcount[e] += 1
                o["sig"] = (e, self.sigcount[e])
                o["pre"] = None
            else:
                o["sig"] = None
                o["pre"] = None
        per = {e: [] for e in self.ENG}
        for o in ops:
            e = o["eng"]
            waits = []
            cand = {}
            for d in o["deps"]:
                p = ops[d]
                if p["eng"] == e and e == "pe" and not p["dma"]:
                    continue
                s, v = p["sig"]
                cand[s] = max(cand.get(s, 0), v)
            if o["pre"] is not None:
                s, v = o["pre"]
                cand[s] = max(cand.get(s, 0), v)
            for s, v in cand.items():
                if self.waited.get((e, s), 0) >= v:
                    continue
                self.waited[(e, s)] = v
                waits.append((s, v))
            per[e].append((o, waits))
        tails = {}
        for q in ("sp", "pool"):
            k = self.dmacount[q]
            tl = []
            for i in range(NSEM_DMA):
                n = (k - i + NSEM_DMA - 1) // NSEM_DMA if k > i else 0
                if n > 0 and self.waited.get((q, (q, i)), 0) < 16 * n:
                    self.waited[(q, (q, i))] = 16 * n
                    tl.append(((q, i), 16 * n))
            tails[q] = tl
        sems = self.sems

        def run(eh, e):
            for o, waits in per[e]:
                for s, v in waits:
                    eh.wait_ge(sems[s], v)
                ins = o["fn"](eh)
                if o["sig"] is not None:
                    s, v = o["sig"]
                    ins.then_inc(sems[s], 16 if o["dma"] else 1)
            for s, v in tails.get(e, ()):
                eh.wait_ge(sems[s], v)

        with nc.Block() as block:
            @block.tensor
            def _(eh): run(eh, "pe")

            @block.scalar
            def _(eh): run(eh, "act")

            @block.vector
            def _(eh): run(eh, "dve")

            @block.gpsimd
            def _(eh): run(eh, "pool")

            @block.sync
            def _(eh): run(eh, "sp")
        self.reset()


def alibi_slopes(n):
    return (2.0 ** (-8.0 * np.arange(1, n + 1) / n)).astype(np.float32)


def nb_rowmask_tile(rows, n, kb):
    m = np.full((2, 64, 2, 64), NEG, np.float32)
    nblk = rows // 2
    if n < 0 or n >= nblk or kb < 0 or kb >= nblk:
        return m.reshape(128, 128)
    for qr in range(2):
        r = 2 * n + qr
        rs = min(max(r - 4, 0), rows - 8)
        for kr in range(2):
            kk = 2 * kb + kr
            if rs <= kk < rs + 8:
                m[kr, :, qr, :] = 0.0
    return m.reshape(128, 128)


def nb_static():
    kc = np.arange(64)[:, None]
    qc = np.arange(64)[None, :]
    ws = np.clip(qc - 8, 0, 48)
    colok = (kc >= ws) & (kc < ws + 16)
    colmask = np.where(colok, 0.0, NEG).astype(np.float32)
    colmask = np.broadcast_to(colmask[None, :, None, :], (2, 64, 2, 64)).reshape(128, 128)
    dc = np.clip(kc - qc, -15, 15) + 15
    return colmask, dc


def cfg_make(nbs, nbpf):
    c = dict(NBS=nbs, NBPF=nbpf, NBPO=nbpf // NCORE)
    segs = []
    e0 = 0
    o0 = 0
    for name, nb in (("S", nbs), ("PO", nbpf // NCORE), ("PF", nbpf)):
        segs.append(dict(name=name, nb=nb, e0=e0, o0=o0, ne=nb + 2 * PAD))
        e0 += nb + 2 * PAD
        o0 += nb
    c["segs"] = segs
    c["NE"] = e0
    c["NO"] = o0
    c["NQ"] = nbs + nbpf // NCORE
    return c


def blk_class(seg, i):
    nb = seg["nb"]
    if i == 0: return 0
    if i == 1: return 1
    if i == nb - 2: return 3
    if i == nb - 1: return 4
    return 2


def blk_rels(seg, i):
    cl = blk_class(seg, i)
    if seg["name"] == "PO":
        return {0: list(range(-2, 4)), 1: list(range(-2, 3)), 2: list(range(-2, 3)),
                3: list(range(-2, 3)), 4: list(range(-3, 3))}[cl]
    return {0: [0, 1, 2, 3], 1: [-1, 0, 1, 2], 2: [-2, -1, 0, 1, 2], 3: [-2, -1, 0, 1], 4: [-3, -2, -1, 0]}[cl]


AB_FM = [("qa", 0, 4, "q"), ("ka", 512, 1, "k"), ("ga", 768, 4, "g"),
         ("qb", 1280, 4, "q"), ("kb", 1792, 4, "k"), ("gb", 2816, 4, "g")]
CD_FM = [("gc", 768, 4, "g"), ("qd", 1280, 4, "q"), ("kd", 1792, 4, "k"), ("gd", 2816, 4, "g")]


def build(cfg, dbg=0):
    nc = bass.Bass("TRN2", target_bir_lowering=False)
    NE, NO, NQ = cfg["NE"], cfg["NO"], cfg["NQ"]
    segs = cfg["segs"]
    TE, TO, TQ = NE * 128, NO * 128, NQ * 128

    def din(name, shape, dt=F32):
        return nc.dram_tensor(name, list(shape), dt, kind="ExternalInput").ap()

    def dscr(name, shape, dt):
        kind = "ExternalOutput" if (dbg and name in ("YT0", "FT0", "VT0", "X1", "FT1", "VT1", "YT1")) else "Internal"
        return nc.dram_tensor(name, list(shape), dt, kind=kind).ap()

    xin = din("xin", [TE, D])
    validin = din("validin", [128, NE])
    pin = din("pin", [2, TO, 256])
    gvec = din("gvec", [4, D])
    w_in_ab = din("w_in_ab", [D, 3328]); w_out_ab = din("w_out_ab", [D, D])
    w_in_cd = din("w_in_cd", [D, 3328]); w_out_cd = din("w_out_cd", [D, D])
    w_ple = din("w_ple", [2, 256, D]); w_gate = din("w_gate", [2, D, D])
    a_sink = din("a_sink", [1, 8])
    rpbg = din("rpbg", [128, 7, 8, 128])
    colmask = din("colmask", [128, 128])
    alibia = din("alibia", [128, 3, 8, 128], BF16)
    rowmask = din("rowmask", [128, 2, 5, 7, 128], BF16)
    qkgain = din("qkgain", [1, 640])
    rope = din("rope", [TO, 64])
    lamv = din("lamv", [1, 4, 64])
    subln = din("subln", [128, 1])
    qtab = din("qtab", [4, 2, 4, TQ], BF16)
    ktab = din("ktab", [4, 4, TO], BF16)
    diagb = din("diagb", [128, 4, 128], BF16)
    yout = nc.dram_tensor("yout", [TQ, D], F32, kind="ExternalOutput").ap()

    FT0 = dscr("FT0", [21 * 128, TE], BF16)
    VT0 = dscr("VT0", [TE, 650], BF16)
    YT0 = dscr("YT0", [D, TO], BF16)
    X1 = dscr("X1", [TO, D], F32)
    FT1 = dscr("FT1", [21 * 128, TO], BF16)
    VT1 = dscr("VT1", [TO, 642], BF16)
    YT1 = dscr("YT1", [D, TQ], BF16)

    import contextlib
    es = contextlib.ExitStack()
    with es:
        sems = {}
        for e in Prog.ENG:
            sems[e] = es.enter_context(nc.semaphore("s_" + e))
        for q in ("sp", "pool"):
            for i in range(NSEM_DMA):
                sems[(q, i)] = es.enter_context(nc.semaphore("d_%s%d" % (q, i)))
        P = Prog(nc, sems)

        ucnt = [0]

        def sb(stack, name, shape, dt):
            ucnt[0] += 1
            return stack.enter_context(nc.sbuf_tensor("%s_u%d" % (name, ucnt[0]), list(shape), dt))

        def ps(stack, name, shape, dt=F32):
            ucnt[0] += 1
            return stack.enter_context(nc.psum_tensor("%s_u%d" % (name, ucnt[0]), list(shape), dt))

        ident = sb(es, "ident", [128, 128], BF16)
        identf = sb(es, "identf", [128, 128], F32)
        ones_f = sb(es, "ones_f", [128, 128], F32)
        zeros_b = sb(es, "zeros_b", [128, 512], BF16)
        gbc = sb(es, "gbc", [128, 4, D], F32)
        valid_sb = sb(es, "valid_sb", [128, NE], F32)
        ones10 = sb(es, "ones10", [128, 10], F32)

        eye = din("eye", [128, 128])
        P.dma("sp", lambda e: e.dma_start(out=identf[:], in_=eye[:, :]), w=["identf"])
        P.dve(lambda e: e.tensor_copy(out=ident[:], in_=identf[:]), r=["identf"], w=["ident"])
        P.dve(lambda e: e.memset(ones_f[:], 1.0), w=["ones_f"])
        P.dve(lambda e: e.memset(zeros_b[:], 0.0), w=["zeros_b"])
        P.dve(lambda e: e.memset(ones10[:], 1.0), w=["ones10"])
        P.dma("sp", lambda e: e.dma_start(out=valid_sb[:], in_=validin[:, :]), w=["valid"])
        for i in range(4):
            P.dma("sp", lambda e, i=i: e.dma_start(out=gbc[:, i, :], in_=gvec[i:i + 1, :].partition_broadcast(128)),
                  w=["gbc"])
        P.emit_block()

        def load_weight(stack_bufs, wdst, wsrc, kch, ncols, key, colchunk=1664):
            stg = stack_bufs
            j = 0
            for k in range(kch):
                for c0 in range(0, ncols, colchunk):
                    c1 = min(ncols, c0 + colchunk)
                    s = stg[j % 2]
                    sk = "wstg%d" % (j % 2)
                    P.dma("sp", lambda e, s=s, k=k, c0=c0, c1=c1: e.dma_start(
                        out=s[:, 0:c1 - c0], in_=wsrc[k * 128:(k + 1) * 128, c0:c1]), w=[sk])
                    if j % 2 == 0:
                        P.dve(lambda e, s=s, k=k, c0=c0, c1=c1: e.tensor_copy(out=wdst[:, k, c0:c1], in_=s[:, 0:c1 - c0]),
                              r=[sk], w=[key])
                    else:
                        P.pool(lambda e, s=s, k=k, c0=c0, c1=c1: e.tensor_copy(out=wdst[:, k, c0:c1], in_=s[:, 0:c1 - c0]),
                               r=[sk], w=[key])
                    j += 1

        def norm_block(xt, xk, gi, hn, hnk, ss, rstd, junk, tagk):
            P.act(lambda e: e.activation(out=junk[:], in_=xt, func=AF.Square, scale=1.0 / math.sqrt(D), accum_out=ss[:, 0:1]),
                  r=[xk], w=["junk", "ss" + tagk])
            P.dve(lambda e: e.tensor_scalar(out=rstd[:, 0:1], in0=ss[:, 0:1], scalar1=EPS, scalar2=None,
                                            op0=ALU.add), r=["ss" + tagk], w=["rstd" + tagk])
            P.act(lambda e: e.activation(out=rstd[:, 0:1], in_=rstd[:, 0:1], func=AF.Sqrt), r=["rstd" + tagk], w=["rstd" + tagk])
            P.dve(lambda e: e.reciprocal(out=rstd[:, 0:1], in_=rstd[:, 0:1]), r=["rstd" + tagk], w=["rstd" + tagk])
            P.dve(lambda e: e.scalar_tensor_tensor(out=hn, in0=xt, scalar=rstd[:, 0:1], in1=gbc[:, gi, :],
                                                   op0=ALU.mult, op1=ALU.mult),
                  r=[xk, "rstd" + tagk, "gbc"], w=[hnk])

        def transpose_to(src, srck, nk, tp, tpk, dst, dstk, use_act):
            for k in range(nk):
                P.pe(lambda e, k=k: e.transpose(out=tp[:, k, :], in_=src[:, k * 128:(k + 1) * 128], identity=ident[:]),
                     r=[srck, "ident"], w=[tpk])
            if use_act:
                P.act(lambda e: e.copy(out=dst, in_=tp[:, 0:nk, :]), r=[tpk], w=[dstk])
            else:
                P.dve(lambda e: e.tensor_copy(out=dst, in_=tp[:, 0:nk, :]), r=[tpk], w=[dstk])

        with contextlib.ExitStack() as st:
            w_in = sb(st, "w_in", [128, 8, 3328], BF16)
            wstg = [sb(st, "wstg0", [128, 1664], F32), sb(st, "wstg1", [128, 1664], F32)]
            xt = [sb(st, "xt%d" % i, [128, D], F32) for i in range(2)]
            hn = [sb(st, "hn%d" % i, [128, D], BF16) for i in range(2)]
            hnT = [sb(st, "hnT%d" % i, [128, 8, 512], BF16) for i in range(2)]
            junk = sb(st, "junk", [128, D], BF16)
            ss = sb(st, "ss", [128, 2], F32)
            rstd = sb(st, "rstd", [128, 2], F32)
            fstage = [sb(st, "fstage%d" % i, [128, 21, 512], BF16) for i in range(2)]
            vstage = [sb(st, "vstage%d" % i, [128, 10, 65], BF16) for i in range(2)]
            tp = [ps(st, "tp%d" % i, [128, 8, 128], BF16) for i in range(2)]
            fm = [ps(st, "fm%d" % i, [128, 512]) for i in range(2)]
            tmA = [ps(st, "tmA%d" % i, [128, 512]) for i in range(1)]
            tmB = [ps(st, "tmB%d" % i, [128, 128]) for i in range(1)]

            load_weight(wstg, w_in, w_in_ab, 8, 3328, "w_in")
            nst = NE // 4
            bc = 0
            for sti in range(nst):
                hT = hnT[sti % 2]
                hTk = "hnT%d" % (sti % 2)
                for b in range(4):
                    eb = sti * 4 + b
                    x_ = xt[bc % 2]
                    xk = "xt%d" % (bc % 2)
                    h_ = hn[bc % 2]
                    hk = "hn%d" % (bc % 2)
                    P.dma("sp", lambda e, x_=x_, eb=eb: e.dma_start(out=x_[:], in_=xin[eb * 128:(eb + 1) * 128, :]), w=[xk])
                    sx = ss[:, bc % 2:bc % 2 + 1]
                    rx = rstd[:, bc % 2:bc % 2 + 1]
                    norm_block(x_[:], xk, 0, h_[:], hk, sx, rx, junk, str(bc % 2))
                    t_ = tp[bc % 2]
                    transpose_to(h_, hk, 8, t_, "tp%d" % (bc % 2), hT[:, :, b * 128:(b + 1) * 128], hTk, bc % 2 == 0)
                    bc += 1
                fs = fstage[sti % 2]
                fsk = "fstage%d" % (sti % 2)
                ci = 0
                for (nm, c0, nch, kind) in AB_FM:
                    for j in range(nch):
                        f0 = c0 + j * 128
                        pf = fm[ci % 2]
                        pk = "fm%d" % (ci % 2)
                        for k in range(8):
                            P.pe(lambda e, pf=pf, k=k, f0=f0, hT=hT: e.matmul(pf[:], lhsT=w_in[:, k, f0:f0 + 128], rhs=hT[:, k, :],
                                                                         start=(k == 0), stop=(k == 7)),
                                 r=["w_in", hTk], w=[pk])
                        if kind == "g":
                            P.act(lambda e, pf=pf, ci=ci, fs=fs: e.activation(out=fs[:, ci, :], in_=pf[:], func=AF.Silu),
                                  r=[pk], w=[fsk])
                        elif kind == "q":
                            P.dve(lambda e, pf=pf, ci=ci, fs=fs: e.tensor_scalar(out=fs[:, ci, :], in0=pf[:], scalar1=0.125,
                                                                           scalar2=None, op0=ALU.mult),
                                  r=[pk], w=[fsk])
                        else:
                            P.dve(lambda e, pf=pf, ci=ci, fs=fs: e.tensor_copy(out=fs[:, ci, :], in_=pf[:]), r=[pk], w=[fsk])
                        ci += 1
                P.dma("pool", lambda e, fs=fs, sti=sti: e.dma_start(
                    out=FT0.rearrange("(c p) t -> p c t", p=128)[:, :, sti * 512:(sti + 1) * 512], in_=fs[:]), r=[fsk])
                for b in range(4):
                    eb = sti * 4 + b
                    vs = vstage[b % 2]
                    vk = "vstage%d" % (b % 2)
                    for k in range(8):
                        P.pe(lambda e, k=k, b=b, hT=hT: e.matmul(tmA[0][:], lhsT=hT[:, k, b * 128:(b + 1) * 128],
                                                           rhs=w_in[:, k, 2304:2816], start=(k == 0), stop=(k == 7)),
                             r=["w_in", hTk], w=["tmA"])
                    for k in range(8):
                        P.pe(lambda e, k=k, b=b, hT=hT: e.matmul(tmB[0][:], lhsT=hT[:, k, b * 128:(b + 1) * 128],
                                                           rhs=w_in[:, k, 640:768], start=(k == 0), stop=(k == 7)),
                             r=["w_in", hTk], w=["tmB"])
                    P.dve(lambda e, vs=vs: e.tensor_copy(out=vs[:, 2:10, 0:64], in_=tmA[0][:].rearrange("p (h d) -> p h d", d=64)),
                          r=["tmA"], w=[vk])
                    P.act(lambda e, vs=vs: e.copy(out=vs[:, 0:2, 0:64], in_=tmB[0][:].rearrange("p (h d) -> p h d", d=64)),
                          r=["tmB"], w=[vk])
                    P.dve(lambda e, vs=vs, eb=eb: e.tensor_scalar(out=vs[:, :, 64], in0=ones10[:], scalar1=valid_sb[:, eb:eb + 1],
                                                             scalar2=None, op0=ALU.mult), r=["valid", "ones10"], w=[vk])
                    P.dma("pool", lambda e, vs=vs, eb=eb: e.dma_start(out=VT0[eb * 128:(eb + 1) * 128, :],
                                                                 in_=vs[:].rearrange("p h d -> p (h d)")), r=[vk])
            P.emit_block()

        with contextlib.ExitStack() as st:
            rpbcol = sb(st, "rpbcol", [128, 7, 8, 128], BF16)
            rpbstg = sb(st, "rpbstg", [128, 7, 8, 128], F32)
            cmask = sb(st, "cmask", [128, 128], F32)
            alib = sb(st, "alib", [128, 3, 8, 128], BF16)
            rmask = sb(st, "rmask", [128, 2, 5, 7, 128], BF16)
            sinkrow = sb(st, "sinkrow", [65, 8, 128], F32)
            sinkv = sb(st, "sinkv", [65, 8], F32)
            KAb = [sb(st, "KA%d" % i, [64, 2, 7 * 128], BF16) for i in range(2)]
            KBb = [sb(st, "KB%d" % i, [64, 8, 7 * 128], BF16) for i in range(2)]
            VRb = [sb(st, "VR%d" % i, [128, 7, 650], BF16) for i in range(2)]
            QAb = [sb(st, "QA%d" % i, [64, 8, 128], BF16) for i in range(2)]
            QBb = [sb(st, "QB%d" % i, [64, 8, 128], BF16) for i in range(2)]
            GAb = [sb(st, "GA%d" % i, [64, 8, 128], BF16) for i in range(2)]
            GBb = [sb(st, "GB%d" % i, [64, 8, 128], BF16) for i in range(2)]
            PT = [sb(st, "PT%d" % i, [128, 3, 512], BF16) for i in range(2)]
            den = sb(st, "den", [65, 512], F32)
            rinv = sb(st, "rinv", [65, 512], F32)
            on = [sb(st, "on%d" % i, [64, 512], F32) for i in range(2)]
            ystage = [sb(st, "ystage%d" % i, [64, 16, 128], BF16) for i in range(2)]
            ST = [ps(st, "ST%d" % i, [128, 3, 512]) for i in range(2)]
            OT = ps(st, "OT", [128, 512])
            BC = ps(st, "BC", [128, 512])

            P.dma("sp", lambda e: e.dma_start(out=rpbstg[:], in_=rpbg[:, :, :, :]), w=["rpbstg"])
            P.dma("sp", lambda e: e.dma_start(out=cmask[:], in_=colmask[:, :]), w=["cmask"])
            P.dma("sp", lambda e: e.dma_start(out=alib[:], in_=alibia[:, :, :, :]), w=["alib"])
            P.dma("sp", lambda e: e.dma_start(out=rmask[:], in_=rowmask[:, :, :, :, :]), w=["rmask"])
            P.dma("sp", lambda e: e.dma_start(out=sinkv[64:65, :], in_=a_sink[0:1, :]), w=["sinkv"])
            for r_ in range(7):
                for h in range(8):
                    P.dve(lambda e, r_=r_, h=h: e.tensor_tensor(out=rpbcol[:, r_, h, :], in0=rpbstg[:, r_, h, :], in1=cmask[:],
                                                             op=ALU.add), r=["rpbstg", "cmask"], w=["rpbcol"])
            P.act(lambda e: e.activation(out=sinkv[64:65, :], in_=sinkv[64:65, :], func=AF.Exp), r=["sinkv"], w=["sinkv"])
            P.dve(lambda e: e.tensor_copy(out=sinkrow[64:65, :, :], in_=sinkv[64:65, :].unsqueeze(2).to_broadcast([1, 8, 128])),
                  r=["sinkv"], w=["sinkrow"])

            FT0v = FT0.rearrange("(c h d) t -> d c h t", h=2, d=64)
            bi = 0
            gb = 0
            jb = 0
            for seg in segs:
                tab = 1 if seg["name"] == "PO" else 0
                for i in range(seg["nb"]):
                    eb = seg["e0"] + PAD + i
                    ob = seg["o0"] + i
                    s2 = bi % 2
                    ka, kb_, vr = KAb[s2], KBb[s2], VRb[s2]
                    qa, qb, ga, gb_ = QAb[s2], QBb[s2], GAb[s2], GBb[s2]
                    t0 = (eb - 3) * 128
                    t1 = (eb + 4) * 128
                    kk = "kv%d" % s2
                    qk = "qg%d" % s2
                    P.dma("sp", lambda e, ka=ka, t0=t0, t1=t1: e.dma_start(out=ka[:], in_=FT0v[:, 4, :, t0:t1]), w=[kk])
                    for hh in range(4):
                        P.dma("sp", lambda e, kb_=kb_, t0=t0, t1=t1, hh=hh: e.dma_start(
                            out=kb_[:, 2 * hh:2 * hh + 2, :], in_=FT0v[:, 13 + hh, :, t0:t1]), w=[kk])
                    P.dma("sp", lambda e, vr=vr, t0=t0, t1=t1: e.dma_start(
                        out=vr[:], in_=VT0[t0:t1, :].rearrange("(b p) f -> p b f", p=128)), w=[kk])
                    q0 = eb * 128
                    for hh in range(4):
                        P.dma("pool", lambda e, qa=qa, hh=hh, q0=q0: e.dma_start(out=qa[:, 2 * hh:2 * hh + 2, :], in_=FT0v[:, 0 + hh, :, q0:q0 + 128]), w=[qk])
                        P.dma("pool", lambda e, ga=ga, hh=hh, q0=q0: e.dma_start(out=ga[:, 2 * hh:2 * hh + 2, :], in_=FT0v[:, 5 + hh, :, q0:q0 + 128]), w=[qk])
                        P.dma("pool", lambda e, qb=qb, hh=hh, q0=q0: e.dma_start(out=qb[:, 2 * hh:2 * hh + 2, :], in_=FT0v[:, 9 + hh, :, q0:q0 + 128]), w=[qk])
                        P.dma("pool", lambda e, gb_=gb_, hh=hh, q0=q0: e.dma_start(out=gb_[:, 2 * hh:2 * hh + 2, :], in_=FT0v[:, 17 + hh, :, q0:q0 + 128]), w=[qk])
                    cl = blk_class(seg, i)
                    rels_b = blk_rels(seg, i)
                    ys = ystage[bi % 2]
                    ysk = "ystage%d" % (bi % 2)
                    for job in range(4):
                        isA = job < 2
                        g = job % 2
                        rels = [-1, 0, 1] if isA else rels_b
                        batches = [rels[j:j + 3] for j in range(0, len(rels), 3)]
                        P.pe(lambda e: e.matmul(OT[0:65, :], lhsT=zeros_b[:, 0:65], rhs=zeros_b[:, :], start=True, stop=True),
                             r=["zeros_b"], w=["OT"])
                        for bt in batches:
                            S_ = ST[gb % 2]
                            Sk = "ST%d" % (gb % 2)
                            p_ = PT[gb % 2]
                            pk = "PT%d" % (gb % 2)
                            for j, rel in enumerate(bt):
                                kof = (rel + 3) * 128
                                if isA:
                                    P.pe(lambda e, S_=S_, j=j, rel=rel, g=g: e.matmul(
                                        S_[:, j, :], lhsT=ident[:], rhs=alib[:, rel + 1, 4 * g:4 * g + 4, :], start=True, stop=False),
                                        r=["ident", "alib"], w=[Sk])
                                    P.pe(lambda e, S_=S_, j=j, kof=kof, g=g, ka=ka, qa=qa: e.matmul(
                                        S_[:, j, :], lhsT=ka[:, g, kof:kof + 128], rhs=qa[:, 4 * g:4 * g + 4, :], start=False, stop=True),
                                        r=[kk, qk], w=[Sk])
                                else:
                                    P.pe(lambda e, S_=S_, j=j, rel=rel, g=g: e.matmul(
                                        S_[:, j, :], lhsT=ident[:], rhs=rpbcol[:, rel + 3, 4 * g:4 * g + 4, :], start=True, stop=False),
                                        r=["ident", "rpbcol"], w=[Sk])
                                    for h in range(4):
                                        P.pe(lambda e, S_=S_, j=j, rel=rel, h=h, tab=tab, cl=cl: e.matmul(
                                            S_[:, j, h * 128:(h + 1) * 128], lhsT=ident[:], rhs=rmask[:, tab, cl, rel + 3, :],
                                            start=False, stop=False), r=["ident", "rmask"], w=[Sk])
                                    for h in range(4):
                                        P.pe(lambda e, S_=S_, j=j, kof=kof, g=g, h=h, kb_=kb_, qb=qb: e.matmul(
                                            S_[:, j, h * 128:(h + 1) * 128], lhsT=kb_[:, 4 * g + h, kof:kof + 128],
                                            rhs=qb[:, 4 * g + h, :], start=False, stop=(h == 3)), r=[kk, qk], w=[Sk])
                            nb_ = len(bt)
                            P.act(lambda e, S_=S_, p_=p_, nb_=nb_: e.activation(out=p_[:, 0:nb_, :], in_=S_[:, 0:nb_, :], func=AF.Exp),
                                  r=[Sk], w=[pk])
                            for j, rel in enumerate(bt):
                                ko = rel + 3
                                if isA:
                                    P.pe(lambda e, p_=p_, j=j, ko=ko, g=g, vr=vr: e.matmul(
                                        OT[0:65, :], lhsT=vr[:, ko, g * 65:(g + 1) * 65], rhs=p_[:, j, :], start=False, stop=True),
                                        r=[kk, pk], w=["OT"])
                                else:
                                    for h in range(4):
                                        hv = 2 + 4 * g + h
                                        P.pe(lambda e, p_=p_, j=j, ko=ko, hv=hv, h=h, vr=vr: e.matmul(
                                            OT[0:65, h * 128:(h + 1) * 128], lhsT=vr[:, ko, hv * 65:(hv + 1) * 65],
                                            rhs=p_[:, j, h * 128:(h + 1) * 128], start=False, stop=True), r=[kk, pk], w=["OT"])
                            gb += 1
                        o_ = on[jb % 2]
                        ok_ = "on%d" % (jb % 2)
                        if isA:
                            P.dve(lambda e, g=g: e.tensor_tensor(out=den[64:65, :], in0=OT[64:65, :],
                                                                 in1=sinkrow[64:65, 4 * g:4 * g + 4, :], op=ALU.add),
                                  r=["OT", "sinkrow"], w=["den"])
                        else:
                            P.dve(lambda e: e.tensor_copy(out=den[64:65, :], in_=OT[64:65, :]), r=["OT"], w=["den"])
                        P.dve(lambda e: e.reciprocal(out=rinv[64:65, :], in_=den[64:65, :]), r=["den"], w=["rinv"])
                        gsrc = ga if isA else gb_
                        P.dve(lambda e, o_=o_, gsrc=gsrc, g=g: e.tensor_tensor(out=o_[:], in0=OT[0:64, :], in1=gsrc[:, 4 * g:4 * g + 4, :],
                                                                          op=ALU.mult), r=["OT", qk], w=[ok_])
                        P.pe(lambda e: e.matmul(BC[0:64, :], lhsT=ones_f[64:65, 0:64], rhs=rinv[64:65, :], start=True, stop=True),
                             r=["ones_f", "rinv"], w=["BC"])
                        hb = (0 if isA else 8) + 4 * g
                        P.dve(lambda e, o_=o_, ys=ys, hb=hb: e.tensor_tensor(out=ys[:, hb:hb + 4, :], in0=o_[:], in1=BC[0:64, :], op=ALU.mult),
                              r=[ok_, "BC"], w=[ysk])
                        jb += 1
                    P.dma("pool", lambda e, ys=ys, ob=ob: e.dma_start(
                        out=YT0.rearrange("(h d) t -> d h t", d=64)[:, :, ob * 128:(ob + 1) * 128], in_=ys[:]), r=[ysk])
                    bi += 1
            P.emit_block()

        if dbg == 1:
            return nc
        _phase345(nc, P, cfg, sb, ps, dbg=dbg, G=dict(
            ident=ident, ones_f=ones_f, zeros_b=zeros_b, gbc=gbc, xin=xin, pin=pin, w_out_ab=w_out_ab, w_in_cd=w_in_cd,
            w_out_cd=w_out_cd, w_ple=w_ple, w_gate=w_gate, qkgain=qkgain, rope=rope, lamv=lamv, subln=subln, qtab=qtab,
            ktab=ktab, diagb=diagb, yout=yout, YT0=YT0, X1=X1, FT1=FT1, VT1=VT1, YT1=YT1,
            load_weight=load_weight, norm_block=norm_block, transpose_to=transpose_to))
    return nc


LAM_INIT1 = 0.8 - 0.6 * math.exp(-0.3 * 1)


def _phase345(nc, P, cfg, sb, ps, G, dbg=0):
    import contextlib
    segs = cfg["segs"]
    ident, ones_f, zeros_b, gbc = G["ident"], G["ones_f"], G["zeros_b"], G["gbc"]
    pin = G["pin"]
    load_weight, norm_block, transpose_to = G["load_weight"], G["norm_block"], G["transpose_to"]
    FT1, VT1, X1, YT1 = G["FT1"], G["VT1"], G["X1"], G["YT1"]

    def out_phase(layer, YT, xsrc_fn, dst_fn, seglist, w_out_d, tok_fn):
        with contextlib.ExitStack() as st:
            wout = sb(st, "wout", [128, 8, D], BF16)
            wg = sb(st, "wg", [128, 8, D], BF16)
            wp = sb(st, "wp", [128, 2, D], BF16)
            wstg = [sb(st, "wstg0", [128, 1024], F32), sb(st, "wstg1", [128, 1024], F32)]
            yts = [sb(st, "yts%d" % i, [128, 8, 512], BF16) for i in range(2)]
            xt = [sb(st, "xt%d" % i, [128, D], F32) for i in range(2)]
            psb = [sb(st, "psb%d" % i, [128, 256], F32) for i in range(2)]
            pb = sb(st, "pb", [128, 256], BF16)
            tmix = sb(st, "tmix", [128, D], F32)
            xa = sb(st, "xa", [128, D], F32)
            xab = sb(st, "xab", [128, D], BF16)
            xaT = sb(st, "xaT", [128, 8, 128], BF16)
            pT = sb(st, "pT", [128, 2, 128], BF16)
            sg = sb(st, "sg", [128, D], F32)
            xo = [sb(st, "xo%d" % i, [128, D], F32) for i in range(2)]
            junk = sb(st, "junk", [128, D], BF16)
            ss = sb(st, "ss", [128, 2], F32)
            mix = ps(st, "mix", [128, D])
            gate = ps(st, "gate", [128, D])
            pp = ps(st, "pp", [128, D])
            tp = ps(st, "tp", [128, 8, 128], BF16)
            tp2 = ps(st, "tp2", [128, 8, 128], BF16)
            load_weight(wstg, wout, w_out_d, 8, D, "wout", colchunk=1024)
            load_weight(wstg, wg, G["w_gate"][layer], 8, D, "wg", colchunk=1024)
            load_weight(wstg, wp, G["w_ple"][layer], 2, D, "wp", colchunk=1024)
            gi = 1 + 2 * layer
            bc = 0
            sc = 0
            for seg in seglist:
                for sti in range(seg["nb"] // 4):
                    y_ = yts[sc % 2]
                    yk = "yts%d" % (sc % 2)
                    ot0 = tok_fn(seg, sti * 4)
                    P.dma("sp", lambda e, y_=y_, ot0=ot0: e.dma_start(
                        out=y_[:], in_=YT.rearrange("(k p) t -> p k t", p=128)[:, :, ot0:ot0 + 512]), w=[yk])
                    sc += 1
                    for b in range(4):
                        i = sti * 4 + b
                        x_ = xt[bc % 2]; xk = "xt%d" % (bc % 2)
                        p_ = psb[bc % 2]; pk = "psb%d" % (bc % 2)
                        o_ = xo[bc % 2]; ok_ = "xo%d" % (bc % 2)
                        xs_ap = xsrc_fn(seg, i)
                        po = (seg["o0"] + i) * 128
                        P.dma("sp", lambda e, x_=x_, xs_ap=xs_ap: e.dma_start(out=x_[:], in_=xs_ap), w=[xk])
                        P.dma("sp", lambda e, p_=p_, po=po: e.dma_start(out=p_[:], in_=pin[layer, po:po + 128, :]), w=[pk])
                        for half in range(2):
                            for k in range(8):
                                P.pe(lambda e, y_=y_, k=k, b=b, half=half: e.matmul(
                                    mix[:, half * 512:(half + 1) * 512], lhsT=y_[:, k, b * 128:(b + 1) * 128],
                                    rhs=wout[:, k, half * 512:(half + 1) * 512], start=(k == 0), stop=(k == 7)),
                                    r=[yk, "wout"], w=["mix"])
                        sx = ss[:, 0:1]
                        P.act(lambda e: e.activation(out=junk[:], in_=mix[:], func=AF.Square, scale=1.0 / math.sqrt(D),
                                                     accum_out=sx), r=["mix"], w=["junk", "ss"])
                        P.dve(lambda e: e.tensor_scalar(out=sx, in0=sx, scalar1=EPS, scalar2=None, op0=ALU.add), r=["ss"], w=["ss"])
                        P.act(lambda e: e.activation(out=sx, in_=sx, func=AF.Sqrt), r=["ss"], w=["ss"])
                        P.dve(lambda e: e.reciprocal(out=sx, in_=sx), r=["ss"], w=["ss"])
                        P.dve(lambda e: e.scalar_tensor_tensor(out=tmix[:], in0=mix[:], scalar=sx, in1=gbc[:, gi, :],
                                                               op0=ALU.mult, op1=ALU.mult), r=["mix", "ss", "gbc"], w=["tmix"])
                        P.pool(lambda e, x_=x_: e.tensor_tensor(out=xa[:], in0=x_[:], in1=tmix[:], op=ALU.add),
                               r=[xk, "tmix"], w=["xa"])
                        P.act(lambda e: e.copy(out=xab[:], in_=xa[:]), r=["xa"], w=["xab"])
                        transpose_to(xab, "xab", 8, tp, "tp", xaT[:], "xaT", False)
                        P.act(lambda e, p_=p_: e.copy(out=pb[:], in_=p_[:]), r=[pk], w=["pb"])
                        transpose_to(pb, "pb", 2, tp2, "tp2", pT[:], "pT", False)
                        for half in range(2):
                            for k in range(8):
                                P.pe(lambda e, k=k, half=half: e.matmul(
                                    gate[:, half * 512:(half + 1) * 512], lhsT=xaT[:, k, :],
                                    rhs=wg[:, k, half * 512:(half + 1) * 512], start=(k == 0), stop=(k == 7)),
                                    r=["xaT", "wg"], w=["gate"])
                            for k in range(2):
                                P.pe(lambda e, k=k, half=half: e.matmul(
                                    pp[:, half * 512:(half + 1) * 512], lhsT=pT[:, k, :],
                                    rhs=wp[:, k, half * 512:(half + 1) * 512], start=(k == 0), stop=(k == 1)),
                                    r=["pT", "wp"], w=["pp"])
                        P.act(lambda e: e.activation(out=sg[:], in_=gate[:], func=AF.Sigmoid), r=["gate"], w=["sg"])
                        P.dve(lambda e: e.tensor_tensor(out=tmix[:], in0=sg[:], in1=pp[:], op=ALU.mult), r=["sg", "pp"], w=["tmix"])
                        P.pool(lambda e, o_=o_: e.tensor_tensor(out=o_[:], in0=xa[:], in1=tmix[:], op=ALU.add),
                               r=["xa", "tmix"], w=[ok_])
                        d_ap = dst_fn(seg, i)
                        P.dma("pool", lambda e, o_=o_, d_ap=d_ap: e.dma_start(out=d_ap, in_=o_[:]), r=[ok_])
                        bc += 1
            P.emit_block()

    xin = G["xin"]
    out_phase(0, G["YT0"],
              lambda seg, i: xin[(seg["e0"] + PAD + i) * 128:(seg["e0"] + PAD + i + 1) * 128, :],
              lambda seg, i: X1[(seg["o0"] + i) * 128:(seg["o0"] + i + 1) * 128, :],
              segs, G["w_out_ab"], lambda seg, i: (seg["o0"] + i) * 128)
    if dbg == 2:
        return

    with contextlib.ExitStack() as st:
        w_in = sb(st, "w_in", [128, 8, 3328], BF16)
        wstg = [sb(st, "wstg0", [128, 1664], F32), sb(st, "wstg1", [128, 1664], F32)]
        xt = [sb(st, "xt%d" % i, [128, D], F32) for i in range(2)]
        hn = [sb(st, "hn%d" % i, [128, D], BF16) for i in range(2)]
        hnT = [sb(st, "hnT%d" % i, [128, 8, 512], BF16) for i in range(2)]
        junk = sb(st, "junk", [128, D], BF16)
        ss = sb(st, "ss", [128, 2], F32)
        rstd = sb(st, "rstd", [128, 2], F32)
        fstage = sb(st, "fstage", [128, 16, 512], BF16)
        qkts = sb(st, "qkts", [128, 5, 512], BF16)
        vst = [sb(st, "vst%d" % i, [128, 642], BF16) for i in range(2)]
        qkf = sb(st, "qkf", [128, 10, 64], F32)
        sqj = sb(st, "sqj", [128, 10, 64], F32)
        ssq = sb(st, "ssq", [128, 10], F32)
        qn = sb(st, "qn", [128, 10, 64], F32)
        ra = sb(st, "ra", [128, 10, 32], F32)
        rb = sb(st, "rb", [128, 10, 32], F32)
        qr = sb(st, "qr", [128, 640], BF16)
        gain = sb(st, "gain", [128, 10, 64], F32)
        rp = [sb(st, "rp%d" % i, [128, 64], F32) for i in range(2)]
        tp = [ps(st, "tp%d" % i, [128, 8, 128], BF16) for i in range(2)]
        fm = [ps(st, "fm%d" % i, [128, 512]) for i in range(2)]
        tqA = ps(st, "tqA", [128, 512])
        tqB = ps(st, "tqB", [128, 256])
        tvD = ps(st, "tvD", [128, 512])
        load_weight(wstg, w_in, G["w_in_cd"], 8, 3328, "w_in")
        P.dma("sp", lambda e: e.dma_start(out=gain[:].rearrange("p h d -> p (h d)"), in_=G["qkgain"][0:1, :].partition_broadcast(128)), w=["gain"])
        P.dve(lambda e: e.tensor_scalar(out=gain[:, 0:8, :], in0=gain[:, 0:8, :], scalar1=0.125, scalar2=None, op0=ALU.mult),
              r=["gain"], w=["gain"])
        for i in range(2):
            P.dve(lambda e, i=i: e.memset(vst[i][:], 1.0), w=["vst%d" % i])
        bc = 0
        sc = 0
        rope = G["rope"]
        for seg in segs:
            pf = seg["name"] == "PF"
            h0 = 8 if pf else 0
            for sti in range(seg["nb"] // 4):
                hT = hnT[sc % 2]; hTk = "hnT%d" % (sc % 2)
                ot0 = (seg["o0"] + sti * 4) * 128
                for b in range(4):
                    ob = seg["o0"] + sti * 4 + b
                    x_ = xt[bc % 2]; xk = "xt%d" % (bc % 2)
                    h_ = hn[bc % 2]; hk = "hn%d" % (bc % 2)
                    P.dma("sp", lambda e, x_=x_, ob=ob: e.dma_start(out=x_[:], in_=X1[ob * 128:(ob + 1) * 128, :]), w=[xk])
                    norm_block(x_[:], xk, 2, h_[:], hk, ss[:, bc % 2:bc % 2 + 1], rstd[:, bc % 2:bc % 2 + 1], junk, str(bc % 2))
                    transpose_to(h_, hk, 8, tp[bc % 2], "tp%d" % (bc % 2), hT[:, :, b * 128:(b + 1) * 128], hTk, bc % 2 == 0)
                    bc += 1
                lst = [("kd", 1792, 4, "k")] if pf else CD_FM
                ci = 0
                for (nm, c0, nch, kind) in lst:
                    for j in range(nch):
                        f0 = c0 + j * 128
                        pf_ = fm[ci % 2]; pk = "fm%d" % (ci % 2)
                        for k in range(8):
                            P.pe(lambda e, pf_=pf_, k=k, f0=f0, hT=hT: e.matmul(pf_[:], lhsT=w_in[:, k, f0:f0 + 128], rhs=hT[:, k, :],
                                                                           start=(k == 0), stop=(k == 7)), r=["w_in", hTk], w=[pk])
                        if kind == "g":
                            P.act(lambda e, pf_=pf_, ci=ci: e.activation(out=fstage[:, ci, :], in_=pf_[:], func=AF.Silu), r=[pk], w=["fstage"])
                        elif kind == "q":
                            P.dve(lambda e, pf_=pf_, ci=ci: e.tensor_scalar(out=fstage[:, ci, :], in0=pf_[:], scalar1=0.125, scalar2=None,
                                                                       op0=ALU.mult), r=[pk], w=["fstage"])
                        else:
                            P.dve(lambda e, pf_=pf_, ci=ci: e.tensor_copy(out=fstage[:, ci, :], in_=pf_[:]), r=[pk], w=["fstage"])
                        ci += 1
                FT1v = FT1.rearrange("(c p) t -> p c t", p=128)
                if pf:
                    P.dma("pool", lambda e, ot0=ot0: e.dma_start(out=FT1v[:, 13:17, ot0:ot0 + 512], in_=fstage[:, 0:4, :]), r=["fstage"])
                else:
                    P.dma("pool", lambda e, ot0=ot0: e.dma_start(out=FT1v[:, 5:21, ot0:ot0 + 512], in_=fstage[:, 0:16, :]), r=["fstage"])
                for b in range(4):
                    ob = seg["o0"] + sti * 4 + b
                    vs = vst[b % 2]; vk = "vst%d" % (b % 2)
                    r_ = rp[b % 2]; rk = "rp%d" % (b % 2)
                    P.dma("sp", lambda e, r_=r_, ob=ob: e.dma_start(out=r_[:], in_=rope[ob * 128:(ob + 1) * 128, :]), w=[rk])
                    if not pf:
                        for k in range(8):
                            P.pe(lambda e, k=k, b=b, hT=hT: e.matmul(tqA[:], lhsT=hT[:, k, b * 128:(b + 1) * 128], rhs=w_in[:, k, 0:512],
                                                               start=(k == 0), stop=(k == 7)), r=["w_in", hTk], w=["tqA"])
                    for k in range(8):
                        P.pe(lambda e, k=k, b=b, hT=hT: e.matmul(tqB[:], lhsT=hT[:, k, b * 128:(b + 1) * 128], rhs=w_in[:, k, 512:768],
                                                           start=(k == 0), stop=(k == 7)), r=["w_in", hTk], w=["tqB"])
                    for k in range(8):
                        P.pe(lambda e, k=k, b=b, hT=hT: e.matmul(tvD[:], lhsT=hT[:, k, b * 128:(b + 1) * 128], rhs=w_in[:, k, 2304:2816],
                                                           start=(k == 0), stop=(k == 7)), r=["w_in", hTk], w=["tvD"])
                    P.dve(lambda e, vs=vs: e.tensor_copy(out=vs[:, 0:130].rearrange("p (h d) -> p h d", d=65)[:, :, 0:64],
                                                         in_=tqB[:, 128:256].rearrange("p (h d) -> p h d", d=64)), r=["tqB"], w=[vk])
                    P.act(lambda e, vs=vs: e.copy(out=vs[:, 130:642], in_=tvD[:]), r=["tvD"], w=[vk])
                    P.dma("pool", lambda e, vs=vs, ob=ob: e.dma_start(out=VT1[ob * 128:(ob + 1) * 128, :], in_=vs[:]), r=[vk])
                    if not pf:
                        P.act(lambda e: e.copy(out=qkf[:, 0:8, :], in_=tqA[:].rearrange("p (h d) -> p h d", d=64)), r=["tqA"], w=["qkf"])
                    P.act(lambda e: e.copy(out=qkf[:, 8:10, :], in_=tqB[:, 0:128].rearrange("p (h d) -> p h d", d=64)), r=["tqB"], w=["qkf"])
                    hs = slice(h0, 10)
                    nh = 10 - h0
                    P.dve(lambda e, hs=hs: e.tensor_tensor(out=sqj[:, hs, :], in0=qkf[:, hs, :], in1=qkf[:, hs, :], op=ALU.mult), r=["qkf"], w=["sqj"])
                    P.dve(lambda e, hs=hs: e.tensor_reduce(out=ssq[:, hs], in_=sqj[:, hs, :], axis=mybir.AxisListType.X, op=ALU.add),
                          r=["sqj"], w=["ssq"])
                    P.dve(lambda e, hs=hs: e.tensor_scalar(out=ssq[:, hs], in0=ssq[:, hs], scalar1=1.0 / 64, scalar2=EPS, op0=ALU.mult, op1=ALU.add),
                          r=["ssq"], w=["ssq"])
                    P.act(lambda e, hs=hs: e.activation(out=ssq[:, hs], in_=ssq[:, hs], func=AF.Sqrt), r=["ssq"], w=["ssq"])
                    P.dve(lambda e, hs=hs: e.reciprocal(out=ssq[:, hs], in_=ssq[:, hs]), r=["ssq"], w=["ssq"])
                    P.dve(lambda e, hs=hs, nh=nh: e.tensor_tensor(out=qn[:, hs, :], in0=qkf[:, hs, :],
                                                                in1=ssq[:, hs].unsqueeze(2).to_broadcast([128, nh, 64]), op=ALU.mult),
                          r=["qkf", "ssq"], w=["qn"])
                    P.dve(lambda e, hs=hs: e.tensor_tensor(out=qn[:, hs, :], in0=qn[:, hs, :], in1=gain[:, hs, :], op=ALU.mult),
                          r=["qn", "gain"], w=["qn"])
                    qv = qn[:].rearrange("p h (j two) -> p h j two", two=2)
                    qrv = qr[:].rearrange("p (h j two) -> p h j two", two=2, j=32)
                    cs = lambda r_=r_, nh=nh: r_[:, 0:32].unsqueeze(1).to_broadcast([128, nh, 32])
                    sn = lambda r_=r_, nh=nh: r_[:, 32:64].unsqueeze(1).to_broadcast([128, nh, 32])
                    P.dve(lambda e, hs=hs, cs=cs: e.tensor_tensor(out=ra[:, hs, :], in0=qv[:, hs, :, 0], in1=cs(), op=ALU.mult), r=["qn", rk], w=["ra"])
                    P.dve(lambda e, hs=hs, sn=sn: e.tensor_tensor(out=rb[:, hs, :], in0=qv[:, hs, :, 1], in1=sn(), op=ALU.mult), r=["qn", rk], w=["rb"])
                    P.dve(lambda e, hs=hs: e.tensor_tensor(out=qrv[:, hs, :, 0], in0=ra[:, hs, :], in1=rb[:, hs, :], op=ALU.subtract),
                          r=["ra", "rb"], w=["qr"])
                    P.dve(lambda e, hs=hs, sn=sn: e.tensor_tensor(out=ra[:, hs, :], in0=qv[:, hs, :, 0], in1=sn(), op=ALU.mult), r=["qn", rk], w=["ra"])
                    P.dve(lambda e, hs=hs, cs=cs: e.tensor_tensor(out=rb[:, hs, :], in0=qv[:, hs, :, 1], in1=cs(), op=ALU.mult), r=["qn", rk], w=["rb"])
                    P.dve(lambda e, hs=hs: e.tensor_tensor(out=qrv[:, hs, :, 1], in0=ra[:, hs, :], in1=rb[:, hs, :], op=ALU.add),
                          r=["ra", "rb"], w=["qr"])
                    c0_ = 4 if pf else 0
                    t_ = tp[b % 2]; tk = "tp%d" % (b % 2)
                    for k in range(c0_, 5):
                        P.pe(lambda e, k=k, t_=t_: e.transpose(out=t_[:, k, :], in_=qr[:, k * 128:(k + 1) * 128], identity=ident[:]),
                             r=["qr", "ident"], w=[tk])
                    P.dve(lambda e, t_=t_, c0_=c0_, b=b: e.tensor_copy(out=qkts[:, c0_:5, b * 128:(b + 1) * 128], in_=t_[:, c0_:5, :]),
                          r=[tk], w=["qkts"])
                c0_ = 4 if pf else 0
                P.dma("pool", lambda e, ot0=ot0, c0_=c0_: e.dma_start(out=FT1v[:, c0_:5, ot0:ot0 + 512], in_=qkts[:, c0_:5, :]), r=["qkts"])
                sc += 1
        P.emit_block()
    if dbg == 3:
        return
    _phase45(nc, P, cfg, sb, ps, G, out_phase)


def _phase45(nc, P, cfg, sb, ps, G, out_phase):
    import contextlib
    segs = cfg["segs"]
    SEG_S, SEG_PO, SEG_PF = segs
    ones_f = G["ones_f"]
    FT1, VT1, X1, YT1 = G["FT1"], G["VT1"], G["X1"], G["YT1"]
    qtab, ktab = G["qtab"], G["ktab"]
    NBS = cfg["NBS"]
    with contextlib.ExitStack() as st:
        nkbmax = max(SEG_S["nb"], SEG_PF["nb"])
        Kb = [sb(st, "Kb%d" % i, [128, nkbmax * 128], BF16) for i in range(2)]
        Vb = sb(st, "Vb", [128, nkbmax * 128], BF16)
        Qb = [sb(st, "Qb%d" % i, [128, 2, 512], BF16) for i in range(2)]
        Gb = [sb(st, "Gb%d" % i, [128, 512], BF16) for i in range(2)]
        PT = [sb(st, "PT%d" % i, [128, 2, 512], BF16) for i in range(3)]
        Mb = [sb(st, "Mb%d" % i, [128, 512], F32) for i in range(2)]
        rinv = sb(st, "rinv", [128, 512], F32)
        on_ = sb(st, "on_", [128, 512], F32)
        o0 = sb(st, "o0", [128, 512], F32)
        od = sb(st, "od", [128, 512], F32)
        sq = sb(st, "sq", [128, 512], F32)
        ybuf = [sb(st, "ybuf%d" % i, [128, 512], BF16) for i in range(2)]
        ones_b = sb(st, "ones_b", [128, 128], BF16)
        lv = sb(st, "lv", [1, 4, 64], F32)
        pr = sb(st, "pr", [1, 2, 64], F32)
        ls = sb(st, "ls", [1, 4], F32)
        nl = sb(st, "nl", [128, 1], F32)
        gsc = sb(st, "gsc", [128, 1], F32)
        SS = [ps(st, "SS%d" % i, [128, 2, 512]) for i in range(3)]
        OTd = [ps(st, "OTd%d" % i, [128, 512]) for i in range(1)]
        LB = [ps(st, "LB%d" % i, [128, 512]) for i in range(1)]

        P.dve(lambda e: e.memset(ones_b[:], 1.0), w=["ones_b"])
        P.dma("sp", lambda e: e.dma_start(out=lv[:], in_=G["lamv"][:, :, :]), w=["lv"])
        P.dma("sp", lambda e: e.dma_start(out=gsc[:], in_=G["subln"][:, :]), w=["gsc"])
        P.dve(lambda e: e.tensor_scalar(out=gsc[:], in0=gsc[:], scalar1=(1.0 - LAM_INIT1), scalar2=None, op0=ALU.mult), r=["gsc"], w=["gsc"])
        P.dve(lambda e: e.tensor_tensor(out=pr[:, 0, :], in0=lv[:, 0, :], in1=lv[:, 1, :], op=ALU.mult), r=["lv"], w=["pr"])
        P.dve(lambda e: e.tensor_tensor(out=pr[:, 1, :], in0=lv[:, 2, :], in1=lv[:, 3, :], op=ALU.mult), r=["lv"], w=["pr"])
        P.dve(lambda e: e.tensor_reduce(out=ls[:, 0:2], in_=pr[:], axis=mybir.AxisListType.X, op=ALU.add), r=["pr"], w=["ls"])
        P.act(lambda e: e.activation(out=ls[:, 0:2], in_=ls[:, 0:2], func=AF.Exp), r=["ls"], w=["ls"])
        P.dve(lambda e: e.tensor_tensor(out=ls[:, 2:3], in0=ls[:, 1:2], in1=ls[:, 0:1], op=ALU.subtract), r=["ls"], w=["ls"])
        P.dve(lambda e: e.tensor_scalar(out=ls[:, 3:4], in0=ls[:, 2:3], scalar1=-LAM_INIT1, scalar2=None, op0=ALU.add), r=["ls"], w=["ls"])
        P.pe(lambda e: e.matmul(LB[0][:, 0:1], lhsT=ones_f[0:1, 0:128], rhs=ls[0:1, 3:4], start=True, stop=True), r=["ones_f", "ls"], w=["LB0"])
        P.dve(lambda e: e.tensor_copy(out=nl[:], in_=LB[0][:, 0:1]), r=["LB0"], w=["nl"])

        gg = 0
        job = 0
        qj = 0
        gj = 0
        NSS = 3

        def run_units(units, qk, ex, pv):
            nonlocal gg
            n = len(units)
            LA = 2
            for i in range(min(LA, n)):
                qk(units[i], gg + i)
            for i in range(LA, n):
                qk(units[i], gg + i)
                ex(units[i - LA], gg + i - LA)
                pv(units[i - LA], gg + i - LA)
            for i in range(max(0, n - LA), n):
                ex(units[i], gg + i)
                pv(units[i], gg + i)
            gg += n

        for seg, qbase, kseg in ((SEG_S, 0, SEG_S), (SEG_PO, NBS, SEG_PF)):
            nkb = kseg["nb"]
            ko = kseg["o0"] * 128
            nch = seg["nb"] // 4
            static_sign = (seg["name"] == "S")
            for kv in range(2):
                K_ = Kb[0]
                r0 = 4 * 128 + kv * 64
                for c0 in range(0, nkb * 128, 2048):
                    c1 = min(nkb * 128, c0 + 2048)
                    P.dma("sp", lambda e, c0=c0, c1=c1, r0=r0, K_=K_, ko=ko: e.dma_start(out=K_[0:64, c0:c1], in_=FT1[r0:r0 + 64, ko + c0:ko + c1]), w=["K0"])
                Vv = Vb[:, 0:nkb * 65].rearrange("p (b f) -> p b f", f=65)
                for b0 in range(0, nkb, 16):
                    b1 = min(nkb, b0 + 16)
                    P.dma("sp", lambda e, b0=b0, b1=b1, Vv=Vv, kv=kv, ko=ko: e.dma_start(
                        out=Vv[:, b0:b1, :], in_=VT1[ko + b0 * 128:ko + b1 * 128, kv * 65:(kv + 1) * 65].rearrange("(b p) f -> p b f", p=128)),
                        w=["V"])
                for h in range(4 * kv, 4 * kv + 4):
                    for ci in range(nch):
                        tok = (seg["o0"] + 4 * ci) * 128
                        qcol = (qbase + 4 * ci) * 128
                        Q_ = Qb[qj % 2]; Qk = "Q%d" % (qj % 2); qj += 1
                        G_ = Gb[gj % 2]; Gk = "G%d" % (gj % 2); gj += 1
                        rq = (h // 2) * 128 + (h % 2) * 64
                        rg = (5 + h // 2) * 128 + (h % 2) * 64
                        P.dma("pool", lambda e, Q_=Q_, rq=rq, tok=tok: e.dma_start(out=Q_[0:64, 0, :], in_=FT1[rq:rq + 64, tok:tok + 512]), w=[Qk])
                        P.dma("pool", lambda e, G_=G_, rg=rg, tok=tok: e.dma_start(out=G_[0:64, :], in_=FT1[rg:rg + 64, tok:tok + 512]), w=[Gk])
                        O_ = OTd[0]; Ok = "OTd0"
                        L_ = LB[0]; Lk = "LB0"

                        def qk(u, gg_, K_=K_, Q_=Q_, Qk=Qk):
                            S_ = SS[gg_ % NSS]
                            for j in range(2):
                                kb = 2 * u + j
                                P.pe(lambda e, S_=S_, j=j, kb=kb, Q_=Q_, K_=K_: e.matmul(S_[:, j, :], lhsT=K_[0:64, kb * 128:(kb + 1) * 128],
                                                                                    rhs=Q_[0:64, 0, :], start=True, stop=True),
                                     r=["K0", Qk], w=["SS%d" % (gg_ % NSS)])

                        def ex(u, gg_):
                            S_ = SS[gg_ % NSS]; p_ = PT[gg_ % 3]
                            P.act(lambda e, S_=S_, p_=p_: e.activation(out=p_[:], in_=S_[:], func=AF.Exp),
                                  r=["SS%d" % (gg_ % NSS)], w=["PT%d" % (gg_ % 3)])

                        def pv(u, gg_, O_=O_, Ok=Ok, Vv=Vv, nkb=nkb):
                            p_ = PT[gg_ % 3]
                            for j in range(2):
                                kb = 2 * u + j
                                P.pe(lambda e, p_=p_, j=j, kb=kb, O_=O_, Vv=Vv, nkb=nkb: e.matmul(O_[0:65, :], lhsT=Vv[:, kb, :], rhs=p_[:, j, :],
                                                                                             start=(kb == 0), stop=(kb == nkb - 1)),
                                     r=["V", "PT%d" % (gg_ % 3)], w=[Ok])
                        run_units(list(range(nkb // 2)), qk, ex, pv)
                        y_ = ybuf[job % 2]; yk = "ybuf%d" % (job % 2)
                        P.dve(lambda e, O_=O_: e.reciprocal(out=rinv[64:65, :], in_=O_[64:65, :]), r=[Ok], w=["rinv"])
                        P.dve(lambda e, O_=O_, G_=G_: e.tensor_tensor(out=on_[0:64, :], in0=O_[0:64, :], in1=G_[0:64, :], op=ALU.mult),
                              r=[Ok, Gk], w=["on_"])
                        P.pe(lambda e, L_=L_: e.matmul(L_[0:64, :], lhsT=ones_f[64:65, 0:64], rhs=rinv[64:65, :], start=True, stop=True),
                             r=["ones_f", "rinv"], w=[Lk])
                        P.dve(lambda e, L_=L_, y_=y_: e.tensor_tensor(out=y_[0:64, :], in0=on_[0:64, :], in1=L_[0:64, :], op=ALU.mult),
                              r=["on_", Lk], w=[yk])
                        P.dma("pool", lambda e, y_=y_, h=h, qcol=qcol: e.dma_start(out=YT1[h * 64:(h + 1) * 64, qcol:qcol + 512], in_=y_[0:64, :]), r=[yk])
                        job += 1
            for h in range(4):
                for m in range(2):
                    K_ = Kb[m]
                    r0 = (13 + h) * 128 + m * 64
                    for c0 in range(0, nkb * 128, 2048):
                        c1 = min(nkb * 128, c0 + 2048)
                        P.dma("sp", lambda e, K_=K_, c0=c0, c1=c1, r0=r0, ko=ko: e.dma_start(out=K_[0:64, c0:c1], in_=FT1[r0:r0 + 64, ko + c0:ko + c1]),
                              w=["K%d" % m])
                    P.dma("sp", lambda e, K_=K_, h=h, nkb=nkb, ko=ko: e.dma_start(out=K_[64:68, 0:nkb * 128], in_=ktab[h, :, ko:ko + nkb * 128]), w=["K%d" % m])
                Vv = Vb[:, 0:nkb * 128].rearrange("p (b f) -> p b f", f=128)
                for b0 in range(0, nkb, 16):
                    b1 = min(nkb, b0 + 16)
                    P.dma("sp", lambda e, b0=b0, b1=b1, Vv=Vv, h=h, ko=ko: e.dma_start(
                        out=Vv[:, b0:b1, :], in_=VT1[ko + b0 * 128:ko + b1 * 128, 130 + h * 128:130 + (h + 1) * 128].rearrange("(b p) f -> p b f", p=128)),
                        w=["V"])
                for ci in range(nch):
                    tok = (seg["o0"] + 4 * ci) * 128
                    qcol = (qbase + 4 * ci) * 128
                    G_ = Gb[gj % 2]; Gk = "G%d" % (gj % 2); gj += 1
                    rg = (17 + h) * 128
                    P.dma("pool", lambda e, G_=G_, rg=rg, tok=tok: e.dma_start(out=G_[:, :], in_=FT1[rg:rg + 128, tok:tok + 512]), w=[Gk])
                    if static_sign:
                        units = []
                        for g in range(nkb // 2):
                            kb0 = 2 * g
                            if kb0 + 1 < 4 * ci:
                                units.append(("far", 0, kb0))
                            elif kb0 > 4 * ci + 3:
                                units.append(("far", 1, kb0))
                            else:
                                units.append(("near", kb0))
                                units.append(("near", kb0 + 1))
                    else:
                        units = [("near", kb) for kb in range(nkb)]
                    for m in range(2):
                        K_ = Kb[m]; Kk = "K%d" % m
                        Q_ = Qb[qj % 2]; Qk = "Q%d" % (qj % 2); qj += 1
                        rq = (9 + h) * 128 + m * 64
                        for v in range(2):
                            P.dma("pool", lambda e, Q_=Q_, rq=rq, tok=tok, v=v: e.dma_start(out=Q_[0:64, v, :], in_=FT1[rq:rq + 64, tok:tok + 512]), w=[Qk])
                            P.dma("pool", lambda e, Q_=Q_, h=h, v=v, qcol=qcol: e.dma_start(out=Q_[64:68, v, :], in_=qtab[h, v, :, qcol:qcol + 512]), w=[Qk])
                        O_ = OTd[0]; Ok = "OTd0"
                        L_ = LB[0]; Lk = "LB0"

                        def qk(u, gg_, K_=K_, Kk=Kk, Q_=Q_, Qk=Qk):
                            S_ = SS[gg_ % NSS]
                            if u[0] == "far":
                                lst = [(j, u[1], u[2] + j) for j in range(2)]
                            else:
                                lst = [(v, v, u[1]) for v in range(2)]
                            for (slot, v, kb) in lst:
                                P.pe(lambda e, S_=S_, slot=slot, v=v, kb=kb, Q_=Q_, K_=K_: e.matmul(
                                    S_[:, slot, :], lhsT=K_[0:68, kb * 128:(kb + 1) * 128], rhs=Q_[0:68, v, :], start=True, stop=True),
                                    r=[Kk, Qk], w=["SS%d" % (gg_ % NSS)])

                        def ex(u, gg_):
                            S_ = SS[gg_ % NSS]; p_ = PT[gg_ % 3]; M_ = Mb[gg_ % 2]
                            if u[0] == "far":
                                P.act(lambda e, S_=S_, p_=p_: e.activation(out=p_[:], in_=S_[:], func=AF.Exp),
                                      r=["SS%d" % (gg_ % NSS)], w=["PT%d" % (gg_ % 3)])
                            else:
                                if gg_ % 2 == 0:
                                    P.act(lambda e, S_=S_, M_=M_: e.copy(out=M_[:], in_=S_[:, 0, :]), r=["SS%d" % (gg_ % NSS)], w=["Mb%d" % (gg_ % 2)])
                                else:
                                    P.dve(lambda e, S_=S_, M_=M_: e.tensor_copy(out=M_[:], in_=S_[:, 0, :]), r=["SS%d" % (gg_ % NSS)], w=["Mb%d" % (gg_ % 2)])
                                P.dve(lambda e, S_=S_, M_=M_: e.tensor_tensor(out=M_[:], in0=M_[:], in1=S_[:, 1, :], op=ALU.min),
                                      r=["SS%d" % (gg_ % NSS), "Mb%d" % (gg_ % 2)], w=["Mb%d" % (gg_ % 2)])
                                P.act(lambda e, M_=M_, p_=p_: e.activation(out=p_[:, 0, :], in_=M_[:], func=AF.Exp),
                                      r=["Mb%d" % (gg_ % 2)], w=["PT%d" % (gg_ % 3)])

                        def pv(u, gg_, O_=O_, L_=L_, Ok=Ok, Lk=Lk, Vv=Vv, nkb=nkb):
                            p_ = PT[gg_ % 3]
                            lst = [(j, u[2] + j) for j in range(2)] if u[0] == "far" else [(0, u[1])]
                            for (slot, kb) in lst:
                                P.pe(lambda e, p_=p_, kb=kb, slot=slot, O_=O_, Vv=Vv, nkb=nkb: e.matmul(
                                    O_[:, :], lhsT=Vv[:, kb, :], rhs=p_[:, slot, :], start=(kb == 0), stop=(kb == nkb - 1)),
                                    r=["V", "PT%d" % (gg_ % 3)], w=[Ok])
                                P.pe(lambda e, p_=p_, kb=kb, slot=slot, L_=L_, nkb=nkb: e.matmul(
                                    L_[:, :], lhsT=ones_b[:], rhs=p_[:, slot, :], start=(kb == 0), stop=(kb == nkb - 1)),
                                    r=["ones_b", "PT%d" % (gg_ % 3)], w=[Lk])
                        run_units(units, qk, ex, pv)
                        P.dve(lambda e, L_=L_: e.reciprocal(out=rinv[:], in_=L_[:]), r=[Lk], w=["rinv"])
                        if m == 0:
                            P.dve(lambda e, O_=O_: e.tensor_tensor(out=o0[:], in0=O_[:], in1=rinv[:], op=ALU.mult), r=[Ok, "rinv"], w=["o0"])
                        else:
                            y_ = ybuf[job % 2]; yk = "ybuf%d" % (job % 2)
                            P.dve(lambda e, O_=O_: e.tensor_tensor(out=on_[:], in0=O_[:], in1=rinv[:], op=ALU.mult), r=[Ok, "rinv"], w=["on_"])
                            P.dve(lambda e: e.scalar_tensor_tensor(out=od[:], in0=on_[:], scalar=nl[:, 0:1], in1=o0[:], op0=ALU.mult, op1=ALU.add),
                                  r=["on_", "nl", "o0"], w=["od"])
                            P.dve(lambda e: e.tensor_tensor(out=sq[:], in0=od[:], in1=od[:], op=ALU.mult), r=["od"], w=["sq"])
                            P.pe(lambda e, L_=L_: e.matmul(L_[:, :], lhsT=ones_f[:, :], rhs=sq[:], start=True, stop=True), r=["ones_f", "sq"], w=[Lk])
                            P.dve(lambda e, L_=L_: e.tensor_scalar(out=sq[:], in0=L_[:], scalar1=1.0 / 128, scalar2=EPS, op0=ALU.mult, op1=ALU.add),
                                  r=[Lk], w=["sq"])
                            P.act(lambda e: e.activation(out=sq[:], in_=sq[:], func=AF.Sqrt), r=["sq"], w=["sq"])
                            P.dve(lambda e: e.reciprocal(out=sq[:], in_=sq[:]), r=["sq"], w=["sq"])
                            P.dve(lambda e: e.tensor_tensor(out=od[:], in0=od[:], in1=sq[:], op=ALU.mult), r=["od", "sq"], w=["od"])
                            P.dve(lambda e, y_=y_, G_=G_: e.scalar_tensor_tensor(out=y_[:], in0=od[:], scalar=gsc[:, 0:1], in1=G_[:], op0=ALU.mult, op1=ALU.mult),
                                  r=["od", "gsc", Gk], w=[yk])
                            P.dma("pool", lambda e, y_=y_, h=h, qcol=qcol: e.dma_start(
                                out=YT1[512 + h * 128:512 + (h + 1) * 128, qcol:qcol + 512], in_=y_[:]), r=[yk])
                        job += 1
        P.emit_block()

    yout = G["yout"]

    def qidx(seg, i):
        return (0 if seg["name"] == "S" else NBS) + i
    out_phase(1, YT1,
              lambda seg, i: X1[(seg["o0"] + i) * 128:(seg["o0"] + i + 1) * 128, :],
              lambda seg, i: yout[qidx(seg, i) * 128:(qidx(seg, i) + 1) * 128, :],
              [SEG_S, SEG_PO], G["w_out_cd"], lambda seg, i: qidx(seg, i) * 128)


def host_constants(cfg):
    c = {}
    c["eye"] = np.eye(128, dtype=np.float32)
    colmask, dc = nb_static()
    c["colmask"] = colmask
    k = np.arange(128)[:, None]
    q = np.arange(128)[None, :]
    sl8 = alibi_slopes(8)
    al = np.zeros((128, 3, 8, 128), np.float32)
    for r in range(3):
        dist = np.abs(128 * (r - 1) + k - q).astype(np.float32)
        for h in range(8):
            al[:, r, h, :] = np.where(dist <= 128, -sl8[h] * dist, NEG)
    c["alibia"] = al.astype(NPBF)
    sl4 = alibi_slopes(4)
    dg = np.zeros((128, 4, 128), np.float32)
    for h in range(4):
        dg[:, h, :] = -sl4[h] * np.abs(k - q)
    c["diagb"] = dg.astype(NPBF)
    return c


def seg_positions(cfg, core):
    nbs, nbpo, nbpf = cfg["NBS"], cfg["NBPO"], cfg["NBPF"]
    ps_ = np.arange(nbs * 128)
    ppo = core * nbpo * 128 + np.arange(nbpo * 128)
    ppf = np.arange(nbpf * 128)
    return ps_, ppo, ppf


def prepare_core(cfg, core, inp, consts):
    nbs, nbpo, nbpf = cfg["NBS"], cfg["NBPO"], cfg["NBPF"]
    SS, SP = nbs * 128, nbpf * 128
    pad = PAD * 128
    m = dict(consts)
    xs = inp["x_sample"][core]
    xp = inp["x_prompt"][0]
    z = np.zeros((pad, D), np.float32)
    xpp = np.concatenate([z, xp, z], axis=0)
    lo = core * nbpo * 128
    xin = np.concatenate([z, xs, z, xpp[lo:lo + nbpo * 128 + 2 * pad], xpp], axis=0)
    m["xin"] = np.ascontiguousarray(xin)
    vs = np.concatenate([np.zeros(pad), np.ones(SS), np.zeros(pad)])
    vpf = np.concatenate([np.zeros(pad), np.ones(SP), np.zeros(pad)])
    valid = np.concatenate([vs, vpf[lo:lo + nbpo * 128 + 2 * pad], vpf]).astype(np.float32)
    m["validin"] = np.ascontiguousarray(valid.reshape(-1, 128).T)
    pp = inp["p_prompt"][:, 0]
    m["pin"] = np.ascontiguousarray(np.concatenate(
        [inp["p_sample"][:, core], pp[:, lo:lo + nbpo * 128], pp], axis=1))
    m["gvec"] = np.ascontiguousarray(np.stack([inp["norm_pre"][0], inp["norm_post"][0], inp["norm_pre"][1], inp["norm_post"][1]]))
    m["w_in_ab"] = inp["w_in_ab"][0]; m["w_out_ab"] = inp["w_out_ab"][0]
    m["w_in_cd"] = inp["w_in_cd"][0]; m["w_out_cd"] = inp["w_out_cd"][0]
    m["w_ple"] = inp["w_ple"]; m["w_gate"] = inp["w_ple_gate"]
    m["a_sink"] = inp["a_sink"]
    rpb = inp["b_rpb"][0]
    kr = np.arange(2)[:, None, None, None]; kc = np.arange(64)[None, :, None, None]
    qr = np.arange(2)[None, None, :, None]; qc = np.arange(64)[None, None, None, :]
    dcx = np.broadcast_to(np.clip(kc - qc, -15, 15) + 15, (2, 64, 2, 64)).reshape(128, 128)
    g = np.zeros((128, 7, 8, 128), np.float32)
    for r in range(7):
        drx = np.broadcast_to(np.clip(2 * (r - 3) + kr - qr + 7, 0, 14), (2, 64, 2, 64)).reshape(128, 128)
        for h in range(8):
            g[:, r, h, :] = rpb[h][drx, dcx]
    m["rpbg"] = g
    rm = np.zeros((128, 2, 5, 7, 128), np.float32)
    rows_any = 64
    nblk_any = rows_any // 2
    reps = {0: 0, 1: 1, 2: nblk_any // 2, 3: nblk_any - 2, 4: nblk_any - 1}
    rows_p = nbpf * 2
    for cl in range(5):
        for r in range(7):
            n = reps[cl]
            rm[:, 0, cl, r, :] = nb_rowmask_tile(rows_any, n, n + r - 3)
    seg_po = cfg["segs"][1]
    done = set()
    for i in range(nbpo):
        cl = blk_class(seg_po, i)
        if cl in done:
            continue
        done.add(cl)
        n = core * nbpo + i
        for r in range(7):
            rm[:, 1, cl, r, :] = nb_rowmask_tile(rows_p, n, n + r - 3)
    for cl in range(5):
        if cl not in done:
            rm[:, 1, cl] = NEG
    m["rowmask"] = rm.astype(NPBF)
    m["qkgain"] = np.ascontiguousarray(np.concatenate([np.tile(inp["c_q_norm"][0], 8), np.tile(inp["c_k_norm"][0], 2)])[None, :])
    ps_, ppo, ppf = seg_positions(cfg, core)
    pos = np.concatenate([ps_, ppo, ppf])
    inv = (10000.0 ** (-2.0 * np.arange(16) / 32)).astype(np.float32)
    row = (pos // 64).astype(np.float32); col = (pos % 64).astype(np.float32)
    ang = np.concatenate([row[:, None] * inv, col[:, None] * inv], axis=-1).astype(np.float32)
    m["rope"] = np.ascontiguousarray(np.concatenate([np.cos(ang), np.sin(ang)], axis=-1).astype(np.float32))
    m["lamv"] = np.ascontiguousarray(np.stack([inp["d_lambda_q1"][0], inp["d_lambda_k1"][0], inp["d_lambda_q2"][0], inp["d_lambda_k2"][0]])[None])
    m["subln"] = np.ascontiguousarray(inp["d_subln"][0][:, None])
    sl4 = alibi_slopes(4)
    posq = np.concatenate([ps_, ppo]).astype(np.float32)
    qa_, qb_ = np.floor(posq / 128), np.mod(posq, 128)
    qt = np.zeros((4, 2, 4, posq.size), np.float32)
    kt = np.zeros((4, 4, pos.size), np.float32)
    ka_, kb_ = np.floor(pos / 128).astype(np.float32), np.mod(pos, 128).astype(np.float32)
    for h in range(4):
        s = sl4[h]
        qt[h, 0] = np.stack([-s * 128 * qa_, -s * qb_, np.ones_like(qa_), np.ones_like(qa_)])
        qt[h, 1] = np.stack([s * 128 * qa_, s * qb_, -np.ones_like(qa_), -np.ones_like(qa_)])
        kt[h] = np.stack([np.ones_like(ka_), np.ones_like(ka_), s * 128 * ka_, s * kb_])
    m["qtab"] = qt.astype(NPBF)
    m["ktab"] = kt.astype(NPBF)
    return m


_CACHE = {}


def kernel(**inputs):
    inp = {k: np.asarray(v) for k, v in inputs.items()}
    nbs = inp["x_sample"].shape[1] // 128
    nbpf = inp["x_prompt"].shape[1] // 128
    cfg = cfg_make(nbs, nbpf)
    key = (nbs, nbpf)
    if key not in _CACHE:
        _CACHE[key] = build(cfg)
    nc = _CACHE[key]
    consts = host_constants(cfg)
    maps = [prepare_core(cfg, c, inp, consts) for c in range(NCORE)]
    res = run_bass_kernel_spmd(nc, maps, core_ids=list(range(NCORE)))
    nbpo = cfg["NBPO"]
    ys = np.zeros((NCORE, nbs * 128, D), np.float32)
    yp = np.zeros((1, nbpf * 128, D), np.float32)
    for c in range(NCORE):
        y = np.asarray(res.results[c]["yout"])
        ys[c] = y[:nbs * 128]
        yp[0, c * nbpo * 128:(c + 1) * nbpo * 128] = y[nbs * 128:]
    return (yp, ys)
```

```python
import math
import numpy as np
import ml_dtypes
import concourse.bass as bass
import concourse.mybir as mybir
from concourse.bass_utils import run_bass_kernel_spmd

F32 = mybir.dt.float32
BF16 = mybir.dt.bfloat16
AF = mybir.ActivationFunctionType
ALU = mybir.AluOpType
NPBF = ml_dtypes.bfloat16

NCORE = 8
D = 1024
PAD = 4
EPS = 1e-6
NEG = -30000.0
NSEM_DMA = 8


class Prog:
    ENG = ("pe", "act", "dve", "pool", "sp")

    def __init__(self, nc, sems):
        self.nc = nc
        self.sems = sems
        self.sigcount = {e: 0 for e in self.ENG}
        self.dmacount = {"sp": 0, "pool": 0}
        self.waited = {}
        self.reset()

    def reset(self):
        self.ops = []
        self.lw = {}
        self.rd = {}

    def op(self, eng, fn, r=(), w=(), dma=False):
        idx = len(self.ops)
        deps = set()
        for b in r:
            x = self.lw.get(b)
            if x is not None:
                deps.add(x)
        for b in w:
            x = self.lw.get(b)
            if x is not None:
                deps.add(x)
            deps.update(self.rd.get(b, ()))
        for b in r:
            self.rd.setdefault(b, []).append(idx)
        for b in w:
            self.lw[b] = idx
            self.rd[b] = []
        self.ops.append(dict(eng=eng, fn=fn, deps=deps, dma=dma, need=False))
        return idx

    def pe(self, fn, r=(), w=()): return self.op("pe", fn, r, w)
    def act(self, fn, r=(), w=()): return self.op("act", fn, r, w)
    def dve(self, fn, r=(), w=()): return self.op("dve", fn, r, w)
    def pool(self, fn, r=(), w=()): return self.op("pool", fn, r, w)
    def dma(self, q, fn, r=(), w=()): return self.op(q, fn, r, w, dma=True)

    def emit_block(self, name=None):
        nc = self.nc
        ops = self.ops
        for o in ops:
            for d in o["deps"]:
                p = ops[d]
                if p["eng"] == o["eng"] and o["eng"] == "pe" and not p["dma"]:
                    continue
                p["need"] = True
        for o in ops:
            e = o["eng"]
            if o["dma"]:
                k = self.dmacount[e]
                self.dmacount[e] = k + 1
                o["sig"] = ((e, k % NSEM_DMA), 16 * (k // NSEM_DMA + 1))
                o["pre"] = ((e, k % NSEM_DMA), 16 * (k // NSEM_DMA)) if k >= NSEM_DMA else None
            elif o["need"]:
                self.sigcount[e] += 1
                o["sig"] = (e, self.sigcount[e])
                o["pre"] = None
            else:
                o["sig"] = None
                o["pre"] = None
        per = {e: [] for e in self.ENG}
        for o in ops:
            e = o["eng"]
            waits = []
            cand = {}
            for d in o["deps"]:
                p = ops[d]
                if p["eng"] == e and e == "pe" and not p["dma"]:
                    continue
                s, v = p["sig"]
                cand[s] = max(cand.get(s, 0), v)
            if o["pre"] is not None:
                s, v = o["pre"]
                cand[s] = max(cand.get(s, 0), v)
            for s, v in cand.items():
                if self.waited.get((e, s), 0) >= v:
                    continue
                self.waited[(e, s)] = v
                waits.append((s, v))
            per[e].append((o, waits))
        tails = {}
        for q in ("sp", "pool"):
            k = self.dmacount[q]
            tl = []
            for i in range(NSEM_DMA):
                n = (k - i + NSEM_DMA - 1) // NSEM_DMA if k > i else 0
                if n > 0 and self.waited.get((q, (q, i)), 0) < 16 * n:
                    self.waited[(q, (q, i))] = 16 * n
                    tl.append(((q, i), 16 * n))
            tails[q] = tl
        sems = self.sems

        def run(eh, e):
            for o, waits in per[e]:
                for s, v in waits:
                    eh.wait_ge(sems[s], v)
                ins = o["fn"](eh)
                if o["sig"] is not None:
                    s, v = o["sig"]
                    ins.then_inc(sems[s], 16 if o["dma"] else 1)
            for s, v in tails.get(e, ()):
                eh.wait_ge(sems[s], v)

        with nc.Block() as block:
            @block.tensor
            def _(eh): run(eh, "pe")

            @block.scalar
            def _(eh): run(eh, "act")

            @block.vector
            def _(eh): run(eh, "dve")

            @block.gpsimd
            def _(eh): run(eh, "pool")

            @block.sync
            def _(eh): run(eh, "sp")
        self.reset()


def alibi_slopes(n):
    return (2.0 ** (-8.0 * np.arange(1, n + 1) / n)).astype(np.float32)


def nb_rowmask_tile(rows, n, kb):
    m = np.full((2, 64, 2, 64), NEG, np.float32)
    nblk = rows // 2
    if n < 0 or n >= nblk or kb < 0 or kb >= nblk:
        return m.reshape(128, 128)
    for qr in range(2):
        r = 2 * n + qr
        rs = min(max(r - 4, 0), rows - 8)
        for kr in range(2):
            kk = 2 * kb + kr
            if rs <= kk < rs + 8:
                m[kr, :, qr, :] = 0.0
    return m.reshape(128, 128)


def nb_static():
    kc = np.arange(64)[:, None]
    qc = np.arange(64)[None, :]
    ws = np.clip(qc - 8, 0, 48)
    colok = (kc >= ws) & (kc < ws + 16)
    colmask = np.where(colok, 0.0, NEG).astype(np.float32)
    colmask = np.broadcast_to(colmask[None, :, None, :], (2, 64, 2, 64)).reshape(128, 128)
    dc = np.clip(kc - qc, -15, 15) + 15
    return colmask, dc


def cfg_make(nbs, nbpf):
    c = dict(NBS=nbs, NBPF=nbpf, NBPO=nbpf // NCORE)
    segs = []
    e0 = 0
    o0 = 0
    for name, nb in (("S", nbs), ("PO", nbpf // NCORE), ("PF", nbpf)):
        segs.append(dict(name=name, nb=nb, e0=e0, o0=o0, ne=nb + 2 * PAD))
        e0 += nb + 2 * PAD
        o0 += nb
    c["segs"] = segs
    c["NE"] = e0
    c["NO"] = o0
    c["NQ"] = nbs + nbpf // NCORE
    return c


def blk_class(seg, i):
    nb = seg["nb"]
    if i == 0: return 0
    if i == 1: return 1
    if i == nb - 2: return 3
    if i == nb - 1: return 4
    return 2


def blk_rels(seg, i):
    cl = blk_class(seg, i)
    if seg["name"] == "PO":
        return {0: list(range(-2, 4)), 1: list(range(-2, 3)), 2: list(range(-2, 3)),
                3: list(range(-2, 3)), 4: list(range(-3, 3))}[cl]
    return {0: [0, 1, 2, 3], 1: [-1, 0, 1, 2], 2: [-2, -1, 0, 1, 2], 3: [-2, -1, 0, 1], 4: [-3, -2, -1, 0]}[cl]


AB_FM = [("qa", 0, 4, "q"), ("ka", 512, 1, "k"), ("ga", 768, 4, "g"),
         ("qb", 1280, 4, "q"), ("kb", 1792, 4, "k"), ("gb", 2816, 4, "g")]
CD_FM = [("gc", 768, 4, "g"), ("qd", 1280, 4, "q"), ("kd", 1792, 4, "k"), ("gd", 2816, 4, "g")]


def build(cfg, dbg=0):
    nc = bass.Bass("TRN2", target_bir_lowering=False)
    NE, NO, NQ = cfg["NE"], cfg["NO"], cfg["NQ"]
    segs = cfg["segs"]
    TE, TO, TQ = NE * 128, NO * 128, NQ * 128

    def din(name, shape, dt=F32):
        return nc.dram_tensor(name, list(shape), dt, kind="ExternalInput").ap()

    def dscr(name, shape, dt):
        kind = "ExternalOutput" if (dbg and name in ("YT0", "FT0", "VT0", "X1", "FT1", "VT1", "YT1")) else "Internal"
        return nc.dram_tensor(name, list(shape), dt, kind=kind).ap()

    xin = din("xin", [TE, D])
    validin = din("validin", [128, NE])
    pin = din("pin", [2, TO, 256])
    gvec = din("gvec", [4, D])
    w_in_ab = din("w_in_ab", [D, 3328]); w_out_ab = din("w_out_ab", [D, D])
    w_in_cd = din("w_in_cd", [D, 3328]); w_out_cd = din("w_out_cd", [D, D])
    w_ple = din("w_ple", [2, 256, D]); w_gate = din("w_gate", [2, D, D])
    a_sink = din("a_sink", [1, 8])
    rpbg = din("rpbg", [128, 7, 8, 128])
    colmask = din("colmask", [128, 128])
    alibia = din("alibia", [128, 3, 8, 128], BF16)
    rowmask = din("rowmask", [128, 2, 5, 7, 128], BF16)
    qkgain = din("qkgain", [1, 640])
    rope = din("rope", [TO, 64])
    lamv = din("lamv", [1, 4, 64])
    subln = din("subln", [128, 1])
    qtab = din("qtab", [4, 2, 4, TQ], BF16)
    ktab = din("ktab", [4, 4, TO], BF16)
    diagb = din("diagb", [128, 4, 128], BF16)
    yout = nc.dram_tensor("yout", [TQ, D], F32, kind="ExternalOutput").ap()

    FT0 = dscr("FT0", [21 * 128, TE], BF16)
    VT0 = dscr("VT0", [TE, 650], BF16)
    YT0 = dscr("YT0", [D, TO], BF16)
    X1 = dscr("X1", [TO, D], F32)
    FT1 = dscr("FT1", [21 * 128, TO], BF16)
    VT1 = dscr("VT1", [TO, 642], BF16)
    YT1 = dscr("YT1", [D, TQ], BF16)

    import contextlib
    es = contextlib.ExitStack()
    with es:
        sems = {}
        for e in Prog.ENG:
            sems[e] = es.enter_context(nc.semaphore("s_" + e))
        for q in ("sp", "pool"):
            for i in range(NSEM_DMA):
                sems[(q, i)] = es.enter_context(nc.semaphore("d_%s%d" % (q, i)))
        P = Prog(nc, sems)

        ucnt = [0]

        def sb(stack, name, shape, dt):
            ucnt[0] += 1
            return stack.enter_context(nc.sbuf_tensor("%s_u%d" % (name, ucnt[0]), list(shape), dt))

        def ps(stack, name, shape, dt=F32):
            ucnt[0] += 1
            return stack.enter_context(nc.psum_tensor("%s_u%d" % (name, ucnt[0]), list(shape), dt))

        ident = sb(es, "ident", [128, 128], BF16)
        identf = sb(es, "identf", [128, 128], F32)
        ones_f = sb(es, "ones_f", [128, 128], F32)
        zeros_b = sb(es, "zeros_b", [128, 512], BF16)
        gbc = sb(es, "gbc", [128, 4, D], F32)
        valid_sb = sb(es, "valid_sb", [128, NE], F32)
        ones10 = sb(es, "ones10", [128, 10], F32)

        eye = din("eye", [128, 128])
        P.dma("sp", lambda e: e.dma_start(out=identf[:], in_=eye[:, :]), w=["identf"])
        P.dve(lambda e: e.tensor_copy(out=ident[:], in_=identf[:]), r=["identf"], w=["ident"])
        P.dve(lambda e: e.memset(ones_f[:], 1.0), w=["ones_f"])
        P.dve(lambda e: e.memset(zeros_b[:], 0.0), w=["zeros_b"])
        P.dve(lambda e: e.memset(ones10[:], 1.0), w=["ones10"])
        P.dma("sp", lambda e: e.dma_start(out=valid_sb[:], in_=validin[:, :]), w=["valid"])
        for i in range(4):
            P.dma("sp", lambda e, i=i: e.dma_start(out=gbc[:, i, :], in_=gvec[i:i + 1, :].partition_broadcast(128)),
                  w=["gbc"])
        P.emit_block()

        def load_weight(stack_bufs, wdst, wsrc, kch, ncols, key, colchunk=1664):
            stg = stack_bufs
            j = 0
            for k in range(kch):
                for c0 in range(0, ncols, colchunk):
                    c1 = min(ncols, c0 + colchunk)
                    s = stg[j % 2]
                    sk = "wstg%d" % (j % 2)
                    P.dma("sp", lambda e, s=s, k=k, c0=c0, c1=c1: e.dma_start(
                        out=s[:, 0:c1 - c0], in_=wsrc[k * 128:(k + 1) * 128, c0:c1]), w=[sk])
                    if j % 2 == 0:
                        P.dve(lambda e, s=s, k=k, c0=c0, c1=c1: e.tensor_copy(out=wdst[:, k, c0:c1], in_=s[:, 0:c1 - c0]),
                              r=[sk], w=[key])
                    else:
                        P.pool(lambda e, s=s, k=k, c0=c0, c1=c1: e.tensor_copy(out=wdst[:, k, c0:c1], in_=s[:, 0:c1 - c0]),
                               r=[sk], w=[key])
                    j += 1

        def norm_block(xt, xk, gi, hn, hnk, ss, rstd, junk, tagk):
            P.act(lambda e: e.activation(out=junk[:], in_=xt, func=AF.Square, scale=1.0 / math.sqrt(D), accum_out=ss[:, 0:1]),
                  r=[xk], w=["junk", "ss" + tagk])
            P.dve(lambda e: e.tensor_scalar(out=rstd[:, 0:1], in0=ss[:, 0:1], scalar1=EPS, scalar2=None,
                                            op0=ALU.add), r=["ss" + tagk], w=["rstd" + tagk])
            P.act(lambda e: e.activation(out=rstd[:, 0:1], in_=rstd[:, 0:1], func=AF.Sqrt), r=["rstd" + tagk], w=["rstd" + tagk])
            P.dve(lambda e: e.reciprocal(out=rstd[:, 0:1], in_=rstd[:, 0:1]), r=["rstd" + tagk], w=["rstd" + tagk])
            P.dve(lambda e: e.scalar_tensor_tensor(out=hn, in0=xt, scalar=rstd[:, 0:1], in1=gbc[:, gi, :],
                                                   op0=ALU.mult, op1=ALU.mult),
                  r=[xk, "rstd" + tagk, "gbc"], w=[hnk])

        def transpose_to(src, srck, nk, tp, tpk, dst, dstk, use_act):
            for k in range(nk):
                P.pe(lambda e, k=k: e.transpose(out=tp[:, k, :], in_=src[:, k * 128:(k + 1) * 128], identity=ident[:]),
                     r=[srck, "ident"], w=[tpk])
            if use_act:
                P.act(lambda e: e.copy(out=dst, in_=tp[:, 0:nk, :]), r=[tpk], w=[dstk])
            else:
                P.dve(lambda e: e.tensor_copy(out=dst, in_=tp[:, 0:nk, :]), r=[tpk], w=[dstk])

        with contextlib.ExitStack() as st:
            w_in = sb(st, "w_in", [128, 8, 3328], BF16)
            wstg = [sb(st, "wstg0", [128, 1664], F32), sb(st, "wstg1", [128, 1664], F32)]
            xt = [sb(st, "xt%d" % i, [128, D], F32) for i in range(2)]
            hn = [sb(st, "hn%d" % i, [128, D], BF16) for i in range(2)]
            hnT = [sb(st, "hnT%d" % i, [128, 8, 512], BF16) for i in range(2)]
            junk = sb(st, "junk", [128, D], BF16)
            ss = sb(st, "ss", [128, 2], F32)
            rstd = sb(st, "rstd", [128, 2], F32)
            fstage = [sb(st, "fstage%d" % i, [128, 21, 512], BF16) for i in range(2)]
            vstage = [sb(st, "vstage%d" % i, [128, 10, 65], BF16) for i in range(2)]
            tp = [ps(st, "tp%d" % i, [128, 8, 128], BF16) for i in range(2)]
            fm = [ps(st, "fm%d" % i, [128, 512]) for i in range(2)]
            tmA = [ps(st, "tmA%d" % i, [128, 512]) for i in range(1)]
            tmB = [ps(st, "tmB%d" % i, [128, 128]) for i in range(1)]

            load_weight(wstg, w_in, w_in_ab, 8, 3328, "w_in")
            nst = NE // 4
            bc = 0
            for sti in range(nst):
                hT = hnT[sti % 2]
                hTk = "hnT%d" % (sti % 2)
                for b in range(4):
                    eb = sti * 4 + b
                    x_ = xt[bc % 2]
                    xk = "xt%d" % (bc % 2)
                    h_ = hn[bc % 2]
                    hk = "hn%d" % (bc % 2)
                    P.dma("sp", lambda e, x_=x_, eb=eb: e.dma_start(out=x_[:], in_=xin[eb * 128:(eb + 1) * 128, :]), w=[xk])
                    sx = ss[:, bc % 2:bc % 2 + 1]
                    rx = rstd[:, bc % 2:bc % 2 + 1]
                    norm_block(x_[:], xk, 0, h_[:], hk, sx, rx, junk, str(bc % 2))
                    t_ = tp[bc % 2]
                    transpose_to(h_, hk, 8, t_, "tp%d" % (bc % 2), hT[:, :, b * 128:(b + 1) * 128], hTk, bc % 2 == 0)
                    bc += 1
                fs = fstage[sti % 2]
                fsk = "fstage%d" % (sti % 2)
                ci = 0
                for (nm, c0, nch, kind) in AB_FM:
                    for j in range(nch):
                        f0 = c0 + j * 128
                        pf = fm[ci % 2]
                        pk = "fm%d" % (ci % 2)
                        for k in range(8):
                            P.pe(lambda e, pf=pf, k=k, f0=f0, hT=hT: e.matmul(pf[:], lhsT=w_in[:, k, f0:f0 + 128], rhs=hT[:, k, :],
                                                                         start=(k == 0), stop=(k == 7)),
                                 r=["w_in", hTk], w=[pk])
                        if kind == "g":
                            P.act(lambda e, pf=pf, ci=ci, fs=fs: e.activation(out=fs[:, ci, :], in_=pf[:], func=AF.Silu),
                                  r=[pk], w=[fsk])
                        elif kind == "q":
                            P.dve(lambda e, pf=pf, ci=ci, fs=fs: e.tensor_scalar(out=fs[:, ci, :], in0=pf[:], scalar1=0.125,
                                                                           scalar2=None, op0=ALU.mult),
                                  r=[pk], w=[fsk])
                        else:
                            P.dve(lambda e, pf=pf, ci=ci, fs=fs: e.tensor_copy(out=fs[:, ci, :], in_=pf[:]), r=[pk], w=[fsk])
                        ci += 1
                P.dma("pool", lambda e, fs=fs, sti=sti: e.dma_start(
                    out=FT0.rearrange("(c p) t -> p c t", p=128)[:, :, sti * 512:(sti + 1) * 512], in_=fs[:]), r=[fsk])
                for b in range(4):
                    eb = sti * 4 + b
                    vs = vstage[b % 2]
                    vk = "vstage%d" % (b % 2)
                    for k in range(8):
                        P.pe(lambda e, k=k, b=b, hT=hT: e.matmul(tmA[0][:], lhsT=hT[:, k, b * 128:(b + 1) * 128],
                                                           rhs=w_in[:, k, 2304:2816], start=(k == 0), stop=(k == 7)),
                             r=["w_in", hTk], w=["tmA"])
                    for k in range(8):
                        P.pe(lambda e, k=k, b=b, hT=hT: e.matmul(tmB[0][:], lhsT=hT[:, k, b * 128:(b + 1) * 128],
                                                           rhs=w_in[:, k, 640:768], start=(k == 0), stop=(k == 7)),
                             r=["w_in", hTk], w=["tmB"])
                    P.dve(lambda e, vs=vs: e.tensor_copy(out=vs[:, 2:10, 0:64], in_=tmA[0][:].rearrange("p (h d) -> p h d", d=64)),
                          r=["tmA"], w=[vk])
                    P.act(lambda e, vs=vs: e.copy(out=vs[:, 0:2, 0:64], in_=tmB[0][:].rearrange("p (h d) -> p h d", d=64)),
                          r=["tmB"], w=[vk])
                    P.dve(lambda e, vs=vs, eb=eb: e.tensor_scalar(out=vs[:, :, 64], in0=ones10[:], scalar1=valid_sb[:, eb:eb + 1],
                                                             scalar2=None, op0=ALU.mult), r=["valid", "ones10"], w=[vk])
                    P.dma("pool", lambda e, vs=vs, eb=eb: e.dma_start(out=VT0[eb * 128:(eb + 1) * 128, :],
                                                                 in_=vs[:].rearrange("p h d -> p (h d)")), r=[vk])
            P.emit_block()

        with contextlib.ExitStack() as st:
            rpbcol = sb(st, "rpbcol", [128, 7, 8, 128], BF16)
            rpbstg = sb(st, "rpbstg", [128, 7, 8, 128], F32)
            cmask = sb(st, "cmask", [128, 128], F32)
            alib = sb(st, "alib", [128, 3, 8, 128], BF16)
            rmask = sb(st, "rmask", [128, 2, 5, 7, 128], BF16)
            sinkrow = sb(st, "sinkrow", [65, 8, 128], F32)
            sinkv = sb(st, "sinkv", [65, 8], F32)
            KAb = [sb(st, "KA%d" % i, [128, 2, 7 * 128], BF16) for i in range(2)]
            KBb = [sb(st, "KB%d" % i, [128, 8, 7 * 128], BF16) for i in range(2)]
            VRb = [sb(st, "VR%d" % i, [128, 7, 650], BF16) for i in range(2)]
            QAb = [sb(st, "QA%d" % i, [128, 8, 128], BF16) for i in range(2)]
            QBb = [sb(st, "QB%d" % i, [128, 8, 128], BF16) for i in range(2)]
            GAb = [sb(st, "GA%d" % i, [64, 8, 128], BF16) for i in range(2)]
            GBb = [sb(st, "GB%d" % i, [64, 8, 128], BF16) for i in range(2)]
            PT = [sb(st, "PT%d" % i, [128, 3, 512], BF16) for i in range(2)]
            den = sb(st, "den", [65, 512], F32)
            rinv = sb(st, "rinv", [128, 512], F32)
            on = [sb(st, "on%d" % i, [64, 512], F32) for i in range(2)]
            ystage = [sb(st, "ystage%d" % i, [64, 16, 128], BF16) for i in range(2)]
            ST = [ps(st, "ST%d" % i, [128, 3, 512]) for i in range(2)]
            OT = ps(st, "OT", [128, 512])
            BC = ps(st, "BC", [128, 512])

            P.dve(lambda e: e.memset(rinv[:], 0.0), w=["rinv"])
            for i_ in range(2):
                P.dve(lambda e, i_=i_: e.memset(KAb[i_][64:128], 0.0), w=["kv%d" % i_])
                P.pool(lambda e, i_=i_: e.memset(KBb[i_][64:128], 0.0), w=["kv%d" % i_])
                P.dve(lambda e, i_=i_: e.memset(QAb[i_][64:128], 0.0), w=["qg%d" % i_])
                P.dve(lambda e, i_=i_: e.memset(QBb[i_][64:128], 0.0), w=["qg%d" % i_])
            P.dma("sp", lambda e: e.dma_start(out=rpbstg[:], in_=rpbg[:, :, :, :]), w=["rpbstg"])
            P.dma("sp", lambda e: e.dma_start(out=cmask[:], in_=colmask[:, :]), w=["cmask"])
            P.dma("sp", lambda e: e.dma_start(out=alib[:], in_=alibia[:, :, :, :]), w=["alib"])
            P.dma("sp", lambda e: e.dma_start(out=rmask[:], in_=rowmask[:, :, :, :, :]), w=["rmask"])
            P.dma("sp", lambda e: e.dma_start(out=sinkv[64:65, :], in_=a_sink[0:1, :]), w=["sinkv"])
            for r_ in range(7):
                for h in range(8):
                    P.dve(lambda e, r_=r_, h=h: e.tensor_tensor(out=rpbcol[:, r_, h, :], in0=rpbstg[:, r_, h, :], in1=cmask[:],
                                                             op=ALU.add), r=["rpbstg", "cmask"], w=["rpbcol"])
            P.act(lambda e: e.activation(out=sinkv[64:65, :], in_=sinkv[64:65, :], func=AF.Exp), r=["sinkv"], w=["sinkv"])
            P.dve(lambda e: e.tensor_copy(out=sinkrow[64:65, :, :], in_=sinkv[64:65, :].unsqueeze(2).to_broadcast([1, 8, 128])),
                  r=["sinkv"], w=["sinkrow"])

            FT0v = FT0.rearrange("(c h d) t -> d c h t", h=2, d=64)
            bi = 0
            gb = 0
            jb = 0
            for seg in segs:
                tab = 1 if seg["name"] == "PO" else 0
                for i in range(seg["nb"]):
                    eb = seg["e0"] + PAD + i
                    ob = seg["o0"] + i
                    s2 = bi % 2
                    ka, kb_, vr = KAb[s2], KBb[s2], VRb[s2]
                    qa, qb, ga, gb_ = QAb[s2], QBb[s2], GAb[s2], GBb[s2]
                    t0 = (eb - 3) * 128
                    t1 = (eb + 4) * 128
                    kk = "kv%d" % s2
                    qk = "qg%d" % s2
                    P.dma("sp", lambda e, ka=ka, t0=t0, t1=t1: e.dma_start(out=ka[0:64], in_=FT0v[:, 4, :, t0:t1]), w=[kk])
                    for hh in range(4):
                        P.dma("sp", lambda e, kb_=kb_, t0=t0, t1=t1, hh=hh: e.dma_start(
                            out=kb_[0:64, 2 * hh:2 * hh + 2, :], in_=FT0v[:, 13 + hh, :, t0:t1]), w=[kk])
                    P.dma("sp", lambda e, vr=vr, t0=t0, t1=t1: e.dma_start(
                        out=vr[:], in_=VT0[t0:t1, :].rearrange("(b p) f -> p b f", p=128)), w=[kk])
                    q0 = eb * 128
                    for hh in range(4):
                        P.dma("pool", lambda e, qa=qa, hh=hh, q0=q0: e.dma_start(out=qa[0:64, 2 * hh:2 * hh + 2, :], in_=FT0v[:, 0 + hh, :, q0:q0 + 128]), w=[qk])
                        P.dma("pool", lambda e, ga=ga, hh=hh, q0=q0: e.dma_start(out=ga[:, 2 * hh:2 * hh + 2, :], in_=FT0v[:, 5 + hh, :, q0:q0 + 128]), w=[qk])
                        P.dma("pool", lambda e, qb=qb, hh=hh, q0=q0: e.dma_start(out=qb[0:64, 2 * hh:2 * hh + 2, :], in_=FT0v[:, 9 + hh, :, q0:q0 + 128]), w=[qk])
                        P.dma("pool", lambda e, gb_=gb_, hh=hh, q0=q0: e.dma_start(out=gb_[:, 2 * hh:2 * hh + 2, :], in_=FT0v[:, 17 + hh, :, q0:q0 + 128]), w=[qk])
                    cl = blk_class(seg, i)
                    rels_b = blk_rels(seg, i)
                    ys = ystage[bi % 2]
                    ysk = "ystage%d" % (bi % 2)
                    for job in range(4):
                        isA = job < 2
                        g = job % 2
                        rels = [-1, 0, 1] if isA else rels_b
                        batches = [rels[j:j + 3] for j in range(0, len(rels), 3)]
                        P.pe(lambda e: e.matmul(OT[0:65, :], lhsT=zeros_b[:, 0:65], rhs=zeros_b[:, :], start=True, stop=True),
                             r=["zeros_b"], w=["OT"])
                        for bt in batches:
                            S_ = ST[gb % 2]
                            Sk = "ST%d" % (gb % 2)
                            p_ = PT[gb % 2]
                            pk = "PT%d" % (gb % 2)
                            for j, rel in enumerate(bt):
                                kof = (rel + 3) * 128
                                if isA:
                                    P.pe(lambda e, S_=S_, j=j, rel=rel, g=g: e.matmul(
                                        S_[:, j, :], lhsT=ident[:], rhs=alib[:, rel + 1, 4 * g:4 * g + 4, :], start=True, stop=False),
                                        r=["ident", "alib"], w=[Sk])
                                    P.pe(lambda e, S_=S_, j=j, kof=kof, g=g, ka=ka, qa=qa: e.matmul(
                                        S_[:, j, :], lhsT=ka[:, g, kof:kof + 128], rhs=qa[:, 4 * g:4 * g + 4, :], start=False, stop=True),
                                        r=[kk, qk], w=[Sk])
                                else:
                                    P.pe(lambda e, S_=S_, j=j, rel=rel, g=g: e.matmul(
                                        S_[:, j, :], lhsT=ident[:], rhs=rpbcol[:, rel + 3, 4 * g:4 * g + 4, :], start=True, stop=False),
                                        r=["ident", "rpbcol"], w=[Sk])
                                    for h in range(4):
                                        P.pe(lambda e, S_=S_, j=j, rel=rel, h=h, tab=tab, cl=cl: e.matmul(
                                            S_[:, j, h * 128:(h + 1) * 128], lhsT=ident[:], rhs=rmask[:, tab, cl, rel + 3, :],
                                            start=False, stop=False), r=["ident", "rmask"], w=[Sk])
                                    for h in range(4):
                                        P.pe(lambda e, S_=S_, j=j, kof=kof, g=g, h=h, kb_=kb_, qb=qb: e.matmul(
                                            S_[:, j, h * 128:(h + 1) * 128], lhsT=kb_[:, 4 * g + h, kof:kof + 128],
                                            rhs=qb[:, 4 * g + h, :], start=False, stop=(h == 3)), r=[kk, qk], w=[Sk])
                            nb_ = len(bt)
                            P.act(lambda e, S_=S_, p_=p_, nb_=nb_: e.activation(out=p_[:, 0:nb_, :], in_=S_[:, 0:nb_, :], func=AF.Exp),
                                  r=[Sk], w=[pk])
                            for j, rel in enumerate(bt):
                                ko = rel + 3
                                if isA:
                                    P.pe(lambda e, p_=p_, j=j, ko=ko, g=g, vr=vr: e.matmul(
                                        OT[0:65, :], lhsT=vr[:, ko, g * 65:(g + 1) * 65], rhs=p_[:, j, :], start=False, stop=True),
                                        r=[kk, pk], w=["OT"])
                                else:
                                    for h in range(4):
                                        hv = 2 + 4 * g + h
                                        P.pe(lambda e, p_=p_, j=j, ko=ko, hv=hv, h=h, vr=vr: e.matmul(
                                            OT[0:65, h * 128:(h + 1) * 128], lhsT=vr[:, ko, hv * 65:(hv + 1) * 65],
                                            rhs=p_[:, j, h * 128:(h + 1) * 128], start=False, stop=True), r=[kk, pk], w=["OT"])
                            gb += 1
                        o_ = on[jb % 2]
                        ok_ = "on%d" % (jb % 2)
                        if isA:
                            P.dve(lambda e, g=g: e.tensor_tensor(out=den[64:65, :], in0=OT[64:65, :],
                                                                 in1=sinkrow[64:65, 4 * g:4 * g + 4, :], op=ALU.add),
                                  r=["OT", "sinkrow"], w=["den"])
                        else:
                            P.dve(lambda e: e.tensor_copy(out=den[64:65, :], in_=OT[64:65, :]), r=["OT"], w=["den"])
                        P.dve(lambda e: e.reciprocal(out=rinv[64:65, :], in_=den[64:65, :]), r=["den"], w=["rinv"])
                        gsrc = ga if isA else gb_
                        P.dve(lambda e, o_=o_, gsrc=gsrc, g=g: e.tensor_tensor(out=o_[:], in0=OT[0:64, :], in1=gsrc[:, 4 * g:4 * g + 4, :],
                                                                          op=ALU.mult), r=["OT", qk], w=[ok_])
                        P.pe(lambda e: e.matmul(BC[0:64, :], lhsT=ones_f[:, 0:64], rhs=rinv[:, :], start=True, stop=True),
                             r=["ones_f", "rinv"], w=["BC"])
                        hb = (0 if isA else 8) + 4 * g
                        P.dve(lambda e, o_=o_, ys=ys, hb=hb: e.tensor_tensor(out=ys[:, hb:hb + 4, :], in0=o_[:], in1=BC[0:64, :], op=ALU.mult),
                              r=[ok_, "BC"], w=[ysk])
                        jb += 1
                    P.dma("pool", lambda e, ys=ys, ob=ob: e.dma_start(
                        out=YT0.rearrange("(h d) t -> d h t", d=64)[:, :, ob * 128:(ob + 1) * 128], in_=ys[:]), r=[ysk])
                    bi += 1
            P.emit_block()

        if dbg == 1:
            return nc
        _phase345(nc, P, cfg, sb, ps, dbg=dbg, G=dict(
            ident=ident, ones_f=ones_f, zeros_b=zeros_b, gbc=gbc, xin=xin, pin=pin, w_out_ab=w_out_ab, w_in_cd=w_in_cd,
            w_out_cd=w_out_cd, w_ple=w_ple, w_gate=w_gate, qkgain=qkgain, rope=rope, lamv=lamv, subln=subln, qtab=qtab,
            ktab=ktab, diagb=diagb, yout=yout, YT0=YT0, X1=X1, FT1=FT1, VT1=VT1, YT1=YT1,
            load_weight=load_weight, norm_block=norm_block, transpose_to=transpose_to))
    return nc


LAM_INIT1 = 0.8 - 0.6 * math.exp(-0.3 * 1)


def _phase345(nc, P, cfg, sb, ps, G, dbg=0):
    import contextlib
    segs = cfg["segs"]
    ident, ones_f, zeros_b, gbc = G["ident"], G["ones_f"], G["zeros_b"], G["gbc"]
    pin = G["pin"]
    load_weight, norm_block, transpose_to = G["load_weight"], G["norm_block"], G["transpose_to"]
    FT1, VT1, X1, YT1 = G["FT1"], G["VT1"], G["X1"], G["YT1"]

    def out_phase(layer, YT, xsrc_fn, dst_fn, seglist, w_out_d, tok_fn):
        with contextlib.ExitStack() as st:
            wout = sb(st, "wout", [128, 8, D], BF16)
            wg = sb(st, "wg", [128, 8, D], BF16)
            wp = sb(st, "wp", [128, 2, D], BF16)
            wstg = [sb(st, "wstg0", [128, 1024], F32), sb(st, "wstg1", [128, 1024], F32)]
            yts = [sb(st, "yts%d" % i, [128, 8, 512], BF16) for i in range(2)]
            xt = [sb(st, "xt%d" % i, [128, D], F32) for i in range(2)]
            psb = [sb(st, "psb%d" % i, [128, 256], F32) for i in range(2)]
            pb = sb(st, "pb", [128, 256], BF16)
            tmix = sb(st, "tmix", [128, D], F32)
            xa = sb(st, "xa", [128, D], F32)
            xab = sb(st, "xab", [128, D], BF16)
            xaT = sb(st, "xaT", [128, 8, 128], BF16)
            pT = sb(st, "pT", [128, 2, 128], BF16)
            sg = sb(st, "sg", [128, D], F32)
            xo = [sb(st, "xo%d" % i, [128, D], F32) for i in range(2)]
            junk = sb(st, "junk", [128, D], BF16)
            ss = sb(st, "ss", [128, 2], F32)
            mix = ps(st, "mix", [128, D])
            gate = ps(st, "gate", [128, D])
            pp = ps(st, "pp", [128, D])
            tp = ps(st, "tp", [128, 8, 128], BF16)
            tp2 = ps(st, "tp2", [128, 8, 128], BF16)
            load_weight(wstg, wout, w_out_d, 8, D, "wout", colchunk=1024)
            load_weight(wstg, wg, G["w_gate"][layer], 8, D, "wg", colchunk=1024)
            load_weight(wstg, wp, G["w_ple"][layer], 2, D, "wp", colchunk=1024)
            gi = 1 + 2 * layer
            bc = 0
            sc = 0
            for seg in seglist:
                for sti in range(seg["nb"] // 4):
                    y_ = yts[sc % 2]
                    yk = "yts%d" % (sc % 2)
                    ot0 = tok_fn(seg, sti * 4)
                    P.dma("sp", lambda e, y_=y_, ot0=ot0: e.dma_start(
                        out=y_[:], in_=YT.rearrange("(k p) t -> p k t", p=128)[:, :, ot0:ot0 + 512]), w=[yk])
                    sc += 1
                    for b in range(4):
                        i = sti * 4 + b
                        x_ = xt[bc % 2]; xk = "xt%d" % (bc % 2)
                        p_ = psb[bc % 2]; pk = "psb%d" % (bc % 2)
                        o_ = xo[bc % 2]; ok_ = "xo%d" % (bc % 2)
                        xs_ap = xsrc_fn(seg, i)
                        po = (seg["o0"] + i) * 128
                        P.dma("sp", lambda e, x_=x_, xs_ap=xs_ap: e.dma_start(out=x_[:], in_=xs_ap), w=[xk])
                        P.dma("sp", lambda e, p_=p_, po=po: e.dma_start(out=p_[:], in_=pin[layer, po:po + 128, :]), w=[pk])
                        for half in range(2):
                            for k in range(8):
                                P.pe(lambda e, y_=y_, k=k, b=b, half=half: e.matmul(
                                    mix[:, half * 512:(half + 1) * 512], lhsT=y_[:, k, b * 128:(b + 1) * 128],
                                    rhs=wout[:, k, half * 512:(half + 1) * 512], start=(k == 0), stop=(k == 7)),
                                    r=[yk, "wout"], w=["mix"])
                        sx = ss[:, 0:1]
                        P.act(lambda e: e.activation(out=junk[:], in_=mix[:], func=AF.Square, scale=1.0 / math.sqrt(D),
                                                     accum_out=sx), r=["mix"], w=["junk", "ss"])
                        P.dve(lambda e: e.tensor_scalar(out=sx, in0=sx, scalar1=EPS, scalar2=None, op0=ALU.add), r=["ss"], w=["ss"])
                        P.act(lambda e: e.activation(out=sx, in_=sx, func=AF.Sqrt), r=["ss"], w=["ss"])
                        P.dve(lambda e: e.reciprocal(out=sx, in_=sx), r=["ss"], w=["ss"])
                        P.dve(lambda e: e.scalar_tensor_tensor(out=tmix[:], in0=mix[:], scalar=sx, in1=gbc[:, gi, :],
                                                               op0=ALU.mult, op1=ALU.mult), r=["mix", "ss", "gbc"], w=["tmix"])
                        P.pool(lambda e, x_=x_: e.tensor_tensor(out=xa[:], in0=x_[:], in1=tmix[:], op=ALU.add),
                               r=[xk, "tmix"], w=["xa"])
                        P.act(lambda e: e.copy(out=xab[:], in_=xa[:]), r=["xa"], w=["xab"])
                        transpose_to(xab, "xab", 8, tp, "tp", xaT[:], "xaT", False)
                        P.act(lambda e, p_=p_: e.copy(out=pb[:], in_=p_[:]), r=[pk], w=["pb"])
                        transpose_to(pb, "pb", 2, tp2, "tp2", pT[:], "pT", False)
                        for half in range(2):
                            for k in range(8):
                                P.pe(lambda e, k=k, half=half: e.matmul(
                                    gate[:, half * 512:(half + 1) * 512], lhsT=xaT[:, k, :],
                                    rhs=wg[:, k, half * 512:(half + 1) * 512], start=(k == 0), stop=(k == 7)),
                                    r=["xaT", "wg"], w=["gate"])
                            for k in range(2):
                                P.pe(lambda e, k=k, half=half: e.matmul(
                                    pp[:, half * 512:(half + 1) * 512], lhsT=pT[:, k, :],
                                    rhs=wp[:, k, half * 512:(half + 1) * 512], start=(k == 0), stop=(k == 1)),
                                    r=["pT", "wp"], w=["pp"])
                        P.act(lambda e: e.activation(out=sg[:], in_=gate[:], func=AF.Sigmoid), r=["gate"], w=["sg"])
                        P.dve(lambda e: e.tensor_tensor(out=tmix[:], in0=sg[:], in1=pp[:], op=ALU.mult), r=["sg", "pp"], w=["tmix"])
                        P.pool(lambda e, o_=o_: e.tensor_tensor(out=o_[:], in0=xa[:], in1=tmix[:], op=ALU.add),
                               r=["xa", "tmix"], w=[ok_])
                        d_ap = dst_fn(seg, i)
                        P.dma("pool", lambda e, o_=o_, d_ap=d_ap: e.dma_start(out=d_ap, in_=o_[:]), r=[ok_])
                        bc += 1
            P.emit_block()

    xin = G["xin"]
    out_phase(0, G["YT0"],
              lambda seg, i: xin[(seg["e0"] + PAD + i) * 128:(seg["e0"] + PAD + i + 1) * 128, :],
              lambda seg, i: X1[(seg["o0"] + i) * 128:(seg["o0"] + i + 1) * 128, :],
              segs, G["w_out_ab"], lambda seg, i: (seg["o0"] + i) * 128)
    if dbg == 2:
        return

    with contextlib.ExitStack() as st:
        w_in = sb(st, "w_in", [128, 8, 3328], BF16)
        wstg = [sb(st, "wstg0", [128, 1664], F32), sb(st, "wstg1", [128, 1664], F32)]
        xt = [sb(st, "xt%d" % i, [128, D], F32) for i in range(2)]
        hn = [sb(st, "hn%d" % i, [128, D], BF16) for i in range(2)]
        hnT = [sb(st, "hnT%d" % i, [128, 8, 512], BF16) for i in range(2)]
        junk = sb(st, "junk", [128, D], BF16)
        ss = sb(st, "ss", [128, 2], F32)
        rstd = sb(st, "rstd", [128, 2], F32)
        fstage = sb(st, "fstage", [128, 16, 512], BF16)
        qkts = sb(st, "qkts", [128, 5, 512], BF16)
        vst = [sb(st, "vst%d" % i, [128, 642], BF16) for i in range(2)]
        qkf = sb(st, "qkf", [128, 10, 64], F32)
        sqj = sb(st, "sqj", [128, 10, 64], F32)
        ssq = sb(st, "ssq", [128, 10], F32)
        qn = sb(st, "qn", [128, 10, 64], F32)
        ra = sb(st, "ra", [128, 10, 32], F32)
        rb = sb(st, "rb", [128, 10, 32], F32)
        qr = sb(st, "qr", [128, 640], BF16)
        gain = sb(st, "gain", [128, 10, 64], F32)
        rp = [sb(st, "rp%d" % i, [128, 64], F32) for i in range(2)]
        tp = [ps(st, "tp%d" % i, [128, 8, 128], BF16) for i in range(2)]
        fm = [ps(st, "fm%d" % i, [128, 512]) for i in range(2)]
        tqA = ps(st, "tqA", [128, 512])
        tqB = ps(st, "tqB", [128, 256])
        tvD = ps(st, "tvD", [128, 512])
        load_weight(wstg, w_in, G["w_in_cd"], 8, 3328, "w_in")
        P.dma("sp", lambda e: e.dma_start(out=gain[:].rearrange("p h d -> p (h d)"), in_=G["qkgain"][0:1, :].partition_broadcast(128)), w=["gain"])
        P.dve(lambda e: e.tensor_scalar(out=gain[:, 0:8, :], in0=gain[:, 0:8, :], scalar1=0.125, scalar2=None, op0=ALU.mult),
              r=["gain"], w=["gain"])
        for i in range(2):
            P.dve(lambda e, i=i: e.memset(vst[i][:], 1.0), w=["vst%d" % i])
        bc = 0
        sc = 0
        rope = G["rope"]
        for seg in segs:
            pf = seg["name"] == "PF"
            h0 = 8 if pf else 0
            for sti in range(seg["nb"] // 4):
                hT = hnT[sc % 2]; hTk = "hnT%d" % (sc % 2)
                ot0 = (seg["o0"] + sti * 4) * 128
                for b in range(4):
                    ob = seg["o0"] + sti * 4 + b
                    x_ = xt[bc % 2]; xk = "xt%d" % (bc % 2)
                    h_ = hn[bc % 2]; hk = "hn%d" % (bc % 2)
                    P.dma("sp", lambda e, x_=x_, ob=ob: e.dma_start(out=x_[:], in_=X1[ob * 128:(ob + 1) * 128, :]), w=[xk])
                    norm_block(x_[:], xk, 2, h_[:], hk, ss[:, bc % 2:bc % 2 + 1], rstd[:, bc % 2:bc % 2 + 1], junk, str(bc % 2))
                    transpose_to(h_, hk, 8, tp[bc % 2], "tp%d" % (bc % 2), hT[:, :, b * 128:(b + 1) * 128], hTk, bc % 2 == 0)
                    bc += 1
                lst = [("kd", 1792, 4, "k")] if pf else CD_FM
                ci = 0
                for (nm, c0, nch, kind) in lst:
                    for j in range(nch):
                        f0 = c0 + j * 128
                        pf_ = fm[ci % 2]; pk = "fm%d" % (ci % 2)
                        for k in range(8):
                            P.pe(lambda e, pf_=pf_, k=k, f0=f0, hT=hT: e.matmul(pf_[:], lhsT=w_in[:, k, f0:f0 + 128], rhs=hT[:, k, :],
                                                                           start=(k == 0), stop=(k == 7)), r=["w_in", hTk], w=[pk])
                        if kind == "g":
                            P.act(lambda e, pf_=pf_, ci=ci: e.activation(out=fstage[:, ci, :], in_=pf_[:], func=AF.Silu), r=[pk], w=["fstage"])
                        elif kind == "q":
                            P.dve(lambda e, pf_=pf_, ci=ci: e.tensor_scalar(out=fstage[:, ci, :], in0=pf_[:], scalar1=0.125, scalar2=None,
                                                                       op0=ALU.mult), r=[pk], w=["fstage"])
                        else:
                            P.dve(lambda e, pf_=pf_, ci=ci: e.tensor_copy(out=fstage[:, ci, :], in_=pf_[:]), r=[pk], w=["fstage"])
                        ci += 1
                FT1v = FT1.rearrange("(c p) t -> p c t", p=128)
                if pf:
                    P.dma("pool", lambda e, ot0=ot0: e.dma_start(out=FT1v[:, 13:17, ot0:ot0 + 512], in_=fstage[:, 0:4, :]), r=["fstage"])
                else:
                    P.dma("pool", lambda e, ot0=ot0: e.dma_start(out=FT1v[:, 5:21, ot0:ot0 + 512], in_=fstage[:, 0:16, :]), r=["fstage"])
                for b in range(4):
                    ob = seg["o0"] + sti * 4 + b
                    vs = vst[b % 2]; vk = "vst%d" % (b % 2)
                    r_ = rp[b % 2]; rk = "rp%d" % (b % 2)
                    P.dma("sp", lambda e, r_=r_, ob=ob: e.dma_start(out=r_[:], in_=rope[ob * 128:(ob + 1) * 128, :]), w=[rk])
                    if not pf:
                        for k in range(8):
                            P.pe(lambda e, k=k, b=b, hT=hT: e.matmul(tqA[:], lhsT=hT[:, k, b * 128:(b + 1) * 128], rhs=w_in[:, k, 0:512],
                                                               start=(k == 0), stop=(k == 7)), r=["w_in", hTk], w=["tqA"])
                    for k in range(8):
                        P.pe(lambda e, k=k, b=b, hT=hT: e.matmul(tqB[:], lhsT=hT[:, k, b * 128:(b + 1) * 128], rhs=w_in[:, k, 512:768],
                                                           start=(k == 0), stop=(k == 7)), r=["w_in", hTk], w=["tqB"])
                    for k in range(8):
                        P.pe(lambda e, k=k, b=b, hT=hT: e.matmul(tvD[:], lhsT=hT[:, k, b * 128:(b + 1) * 128], rhs=w_in[:, k, 2304:2816],
                                                           start=(k == 0), stop=(k == 7)), r=["w_in", hTk], w=["tvD"])
                    P.dve(lambda e, vs=vs: e.tensor_copy(out=vs[:, 0:130].rearrange("p (h d) -> p h d", d=65)[:, :, 0:64],
                                                         in_=tqB[:, 128:256].rearrange("p (h d) -> p h d", d=64)), r=["tqB"], w=[vk])
                    P.act(lambda e, vs=vs: e.copy(out=vs[:, 130:642], in_=tvD[:]), r=["tvD"], w=[vk])
                    P.dma("pool", lambda e, vs=vs, ob=ob: e.dma_start(out=VT1[ob * 128:(ob + 1) * 128, :], in_=vs[:]), r=[vk])
                    if not pf:
                        P.act(lambda e: e.copy(out=qkf[:, 0:8, :], in_=tqA[:].rearrange("p (h d) -> p h d", d=64)), r=["tqA"], w=["qkf"])
                    P.act(lambda e: e.copy(out=qkf[:, 8:10, :], in_=tqB[:, 0:128].rearrange("p (h d) -> p h d", d=64)), r=["tqB"], w=["qkf"])
                    hs = slice(h0, 10)
                    nh = 10 - h0
                    P.dve(lambda e, hs=hs: e.tensor_tensor(out=sqj[:, hs, :], in0=qkf[:, hs, :], in1=qkf[:, hs, :], op=ALU.mult), r=["qkf"], w=["sqj"])
                    P.dve(lambda e, hs=hs: e.tensor_reduce(out=ssq[:, hs], in_=sqj[:, hs, :], axis=mybir.AxisListType.X, op=ALU.add),
                          r=["sqj"], w=["ssq"])
                    P.dve(lambda e, hs=hs: e.tensor_scalar(out=ssq[:, hs], in0=ssq[:, hs], scalar1=1.0 / 64, scalar2=EPS, op0=ALU.mult, op1=ALU.add),
                          r=["ssq"], w=["ssq"])
                    P.act(lambda e, hs=hs: e.activation(out=ssq[:, hs], in_=ssq[:, hs], func=AF.Sqrt), r=["ssq"], w=["ssq"])
                    P.dve(lambda e, hs=hs: e.reciprocal(out=ssq[:, hs], in_=ssq[:, hs]), r=["ssq"], w=["ssq"])
                    P.dve(lambda e, hs=hs, nh=nh: e.tensor_tensor(out=qn[:, hs, :], in0=qkf[:, hs, :],
                                                                in1=ssq[:, hs].unsqueeze(2).to_broadcast([128, nh, 64]), op=ALU.mult),
                          r=["qkf", "ssq"], w=["qn"])
                    P.dve(lambda e, hs=hs: e.tensor_tensor(out=qn[:, hs, :], in0=qn[:, hs, :], in1=gain[:, hs, :], op=ALU.mult),
                          r=["qn", "gain"], w=["qn"])
                    qv = qn[:].rearrange("p h (j two) -> p h j two", two=2)
                    qrv = qr[:].rearrange("p (h j two) -> p h j two", two=2, j=32)
                    cs = lambda r_=r_, nh=nh: r_[:, 0:32].unsqueeze(1).to_broadcast([128, nh, 32])
                    sn = lambda r_=r_, nh=nh: r_[:, 32:64].unsqueeze(1).to_broadcast([128, nh, 32])
                    P.dve(lambda e, hs=hs, cs=cs: e.tensor_tensor(out=ra[:, hs, :], in0=qv[:, hs, :, 0], in1=cs(), op=ALU.mult), r=["qn", rk], w=["ra"])
                    P.dve(lambda e, hs=hs, sn=sn: e.tensor_tensor(out=rb[:, hs, :], in0=qv[:, hs, :, 1], in1=sn(), op=ALU.mult), r=["qn", rk], w=["rb"])
                    P.dve(lambda e, hs=hs: e.tensor_tensor(out=qrv[:, hs, :, 0], in0=ra[:, hs, :], in1=rb[:, hs, :], op=ALU.subtract),
                          r=["ra", "rb"], w=["qr"])
                    P.dve(lambda e, hs=hs, sn=sn: e.tensor_tensor(out=ra[:, hs, :], in0=qv[:, hs, :, 0], in1=sn(), op=ALU.mult), r=["qn", rk], w=["ra"])
                    P.dve(lambda e, hs=hs, cs=cs: e.tensor_tensor(out=rb[:, hs, :], in0=qv[:, hs, :, 1], in1=cs(), op=ALU.mult), r=["qn", rk], w=["rb"])
                    P.dve(lambda e, hs=hs: e.tensor_tensor(out=qrv[:, hs, :, 1], in0=ra[:, hs, :], in1=rb[:, hs, :], op=ALU.add),
                          r=["ra", "rb"], w=["qr"])
                    c0_ = 4 if pf else 0
                    t_ = tp[b % 2]; tk = "tp%d" % (b % 2)
                    for k in range(c0_, 5):
                        P.pe(lambda e, k=k, t_=t_: e.transpose(out=t_[:, k, :], in_=qr[:, k * 128:(k + 1) * 128], identity=ident[:]),
                             r=["qr", "ident"], w=[tk])
                    P.dve(lambda e, t_=t_, c0_=c0_, b=b: e.tensor_copy(out=qkts[:, c0_:5, b * 128:(b + 1) * 128], in_=t_[:, c0_:5, :]),
                          r=[tk], w=["qkts"])
                c0_ = 4 if pf else 0
                P.dma("pool", lambda e, ot0=ot0, c0_=c0_: e.dma_start(out=FT1v[:, c0_:5, ot0:ot0 + 512], in_=qkts[:, c0_:5, :]), r=["qkts"])
                sc += 1
        P.emit_block()
    if dbg == 3:
        return
    _phase45(nc, P, cfg, sb, ps, G, out_phase)


def _phase45(nc, P, cfg, sb, ps, G, out_phase):
    import contextlib
    segs = cfg["segs"]
    SEG_S, SEG_PO, SEG_PF = segs
    ones_f = G["ones_f"]
    FT1, VT1, X1, YT1 = G["FT1"], G["VT1"], G["X1"], G["YT1"]
    qtab, ktab = G["qtab"], G["ktab"]
    NBS = cfg["NBS"]
    with contextlib.ExitStack() as st:
        nkbmax = max(SEG_S["nb"], SEG_PF["nb"])
        Kb = [sb(st, "Kb%d" % i, [128, nkbmax * 128], BF16) for i in range(2)]
        Vb = sb(st, "Vb", [128, nkbmax * 128], BF16)
        Qb = [sb(st, "Qb%d" % i, [128, 2, 512], BF16) for i in range(2)]
        Gb = [sb(st, "Gb%d" % i, [128, 512], BF16) for i in range(2)]
        PT = [sb(st, "PT%d" % i, [128, 2, 512], BF16) for i in range(3)]
        Mb = [sb(st, "Mb%d" % i, [128, 512], F32) for i in range(2)]
        rinv = sb(st, "rinv", [128, 512], F32)
        rinvz = sb(st, "rinvz", [128, 512], F32)
        on_ = sb(st, "on_", [128, 512], F32)
        o0 = sb(st, "o0", [128, 512], F32)
        od = sb(st, "od", [128, 512], F32)
        sq = sb(st, "sq", [128, 512], F32)
        ybuf = [sb(st, "ybuf%d" % i, [128, 512], BF16) for i in range(2)]
        ones_b = sb(st, "ones_b", [128, 128], BF16)
        lv = sb(st, "lv", [1, 4, 64], F32)
        pr = sb(st, "pr", [1, 2, 64], F32)
        ls = sb(st, "ls", [1, 4], F32)
        nl = sb(st, "nl", [128, 1], F32)
        gsc = sb(st, "gsc", [128, 1], F32)
        SS = [ps(st, "SS%d" % i, [128, 2, 512]) for i in range(3)]
        OTd = [ps(st, "OTd%d" % i, [128, 512]) for i in range(1)]
        LB = [ps(st, "LB%d" % i, [128, 512]) for i in range(1)]

        P.dve(lambda e: e.memset(ones_b[:], 1.0), w=["ones_b"])
        P.dve(lambda e: e.memset(rinvz[:], 0.0), w=["rinvz"])
        for i_ in range(2):
            P.dve(lambda e, i_=i_: e.memset(Qb[i_][64:128], 0.0), w=["Q%d" % i_])
        P.dma("sp", lambda e: e.dma_start(out=lv[:], in_=G["lamv"][:, :, :]), w=["lv"])
        P.dma("sp", lambda e: e.dma_start(out=gsc[:], in_=G["subln"][:, :]), w=["gsc"])
        P.dve(lambda e: e.tensor_scalar(out=gsc[:], in0=gsc[:], scalar1=(1.0 - LAM_INIT1), scalar2=None, op0=ALU.mult), r=["gsc"], w=["gsc"])
        P.dve(lambda e: e.tensor_tensor(out=pr[:, 0, :], in0=lv[:, 0, :], in1=lv[:, 1, :], op=ALU.mult), r=["lv"], w=["pr"])
        P.dve(lambda e: e.tensor_tensor(out=pr[:, 1, :], in0=lv[:, 2, :], in1=lv[:, 3, :], op=ALU.mult), r=["lv"], w=["pr"])
        P.dve(lambda e: e.tensor_reduce(out=ls[:, 0:2], in_=pr[:], axis=mybir.AxisListType.X, op=ALU.add), r=["pr"], w=["ls"])
        P.act(lambda e: e.activation(out=ls[:, 0:2], in_=ls[:, 0:2], func=AF.Exp), r=["ls"], w=["ls"])
        P.dve(lambda e: e.tensor_tensor(out=ls[:, 2:3], in0=ls[:, 1:2], in1=ls[:, 0:1], op=ALU.subtract), r=["ls"], w=["ls"])
        P.dve(lambda e: e.tensor_scalar(out=ls[:, 3:4], in0=ls[:, 2:3], scalar1=-LAM_INIT1, scalar2=None, op0=ALU.add), r=["ls"], w=["ls"])
        P.pe(lambda e: e.matmul(LB[0][:, 0:1], lhsT=ones_f[0:1, 0:128], rhs=ls[0:1, 3:4], start=True, stop=True), r=["ones_f", "ls"], w=["LB0"])
        P.dve(lambda e: e.tensor_copy(out=nl[:], in_=LB[0][:, 0:1]), r=["LB0"], w=["nl"])

        gg = 0
        job = 0
        qj = 0
        gj = 0
        NSS = 3

        def run_units(units, qk, ex, pv):
            nonlocal gg
            n = len(units)
            LA = 2
            for i in range(min(LA, n)):
                qk(units[i], gg + i)
            for i in range(LA, n):
                qk(units[i], gg + i)
                ex(units[i - LA], gg + i - LA)
                pv(units[i - LA], gg + i - LA)
            for i in range(max(0, n - LA), n):
                ex(units[i], gg + i)
                pv(units[i], gg + i)
            gg += n

        for seg, qbase, kseg in ((SEG_S, 0, SEG_S), (SEG_PO, NBS, SEG_PF)):
            nkb = kseg["nb"]
            ko = kseg["o0"] * 128
            nch = seg["nb"] // 4
            static_sign = (seg["name"] == "S")
            P.pool(lambda e: e.memset(Kb[0][64:128, :], 0.0), w=["K0"])
            for kv in range(2):
                K_ = Kb[0]
                r0 = 4 * 128 + kv * 64
                for c0 in range(0, nkb * 128, 2048):
                    c1 = min(nkb * 128, c0 + 2048)
                    P.dma("sp", lambda e, c0=c0, c1=c1, r0=r0, K_=K_, ko=ko: e.dma_start(out=K_[0:64, c0:c1], in_=FT1[r0:r0 + 64, ko + c0:ko + c1]), w=["K0"])
                Vv = Vb[:, 0:nkb * 65].rearrange("p (b f) -> p b f", f=65)
                for b0 in range(0, nkb, 16):
                    b1 = min(nkb, b0 + 16)
                    P.dma("sp", lambda e, b0=b0, b1=b1, Vv=Vv, kv=kv, ko=ko: e.dma_start(
                        out=Vv[:, b0:b1, :], in_=VT1[ko + b0 * 128:ko + b1 * 128, kv * 65:(kv + 1) * 65].rearrange("(b p) f -> p b f", p=128)),
                        w=["V"])
                for h in range(4 * kv, 4 * kv + 4):
                    for ci in range(nch):
                        tok = (seg["o0"] + 4 * ci) * 128
                        qcol = (qbase + 4 * ci) * 128
                        Q_ = Qb[qj % 2]; Qk = "Q%d" % (qj % 2); qj += 1
                        G_ = Gb[gj % 2]; Gk = "G%d" % (gj % 2); gj += 1
                        rq = (h // 2) * 128 + (h % 2) * 64
                        rg = (5 + h // 2) * 128 + (h % 2) * 64
                        P.dma("pool", lambda e, Q_=Q_, rq=rq, tok=tok: e.dma_start(out=Q_[0:64, 0, :], in_=FT1[rq:rq + 64, tok:tok + 512]), w=[Qk])
                        P.dma("pool", lambda e, G_=G_, rg=rg, tok=tok: e.dma_start(out=G_[0:64, :], in_=FT1[rg:rg + 64, tok:tok + 512]), w=[Gk])
                        O_ = OTd[0]; Ok = "OTd0"
                        L_ = LB[0]; Lk = "LB0"

                        def qk(u, gg_, K_=K_, Q_=Q_, Qk=Qk):
                            S_ = SS[gg_ % NSS]
                            for j in range(2):
                                kb = 2 * u + j
                                P.pe(lambda e, S_=S_, j=j, kb=kb, Q_=Q_, K_=K_: e.matmul(S_[:, j, :], lhsT=K_[:, kb * 128:(kb + 1) * 128],
                                                                                    rhs=Q_[:, 0, :], start=True, stop=True),
                                     r=["K0", Qk], w=["SS%d" % (gg_ % NSS)])

                        def ex(u, gg_):
                            S_ = SS[gg_ % NSS]; p_ = PT[gg_ % 3]
                            P.act(lambda e, S_=S_, p_=p_: e.activation(out=p_[:], in_=S_[:], func=AF.Exp),
                                  r=["SS%d" % (gg_ % NSS)], w=["PT%d" % (gg_ % 3)])

                        def pv(u, gg_, O_=O_, Ok=Ok, Vv=Vv, nkb=nkb):
                            p_ = PT[gg_ % 3]
                            for j in range(2):
                                kb = 2 * u + j
                                P.pe(lambda e, p_=p_, j=j, kb=kb, O_=O_, Vv=Vv, nkb=nkb: e.matmul(O_[0:65, :], lhsT=Vv[:, kb, :], rhs=p_[:, j, :],
                                                                                             start=(kb == 0), stop=(kb == nkb - 1)),
                                     r=["V", "PT%d" % (gg_ % 3)], w=[Ok])
                        run_units(list(range(nkb // 2)), qk, ex, pv)
                        y_ = ybuf[job % 2]; yk = "ybuf%d" % (job % 2)
                        P.dve(lambda e, O_=O_: e.reciprocal(out=rinvz[64:65, :], in_=O_[64:65, :]), r=[Ok], w=["rinvz"])
                        P.dve(lambda e, O_=O_, G_=G_: e.tensor_tensor(out=on_[0:64, :], in0=O_[0:64, :], in1=G_[0:64, :], op=ALU.mult),
                              r=[Ok, Gk], w=["on_"])
                        P.pe(lambda e, L_=L_: e.matmul(L_[0:64, :], lhsT=ones_f[:, 0:64], rhs=rinvz[:, :], start=True, stop=True),
                             r=["ones_f", "rinvz"], w=[Lk])
                        P.dve(lambda e, L_=L_, y_=y_: e.tensor_tensor(out=y_[0:64, :], in0=on_[0:64, :], in1=L_[0:64, :], op=ALU.mult),
                              r=["on_", Lk], w=[yk])
                        P.dma("pool", lambda e, y_=y_, h=h, qcol=qcol: e.dma_start(out=YT1[h * 64:(h + 1) * 64, qcol:qcol + 512], in_=y_[0:64, :]), r=[yk])
                        job += 1
            for h in range(4):
                for m in range(2):
                    K_ = Kb[m]
                    r0 = (13 + h) * 128 + m * 64
                    for c0 in range(0, nkb * 128, 2048):
                        c1 = min(nkb * 128, c0 + 2048)
                        P.dma("sp", lambda e, K_=K_, c0=c0, c1=c1, r0=r0, ko=ko: e.dma_start(out=K_[0:64, c0:c1], in_=FT1[r0:r0 + 64, ko + c0:ko + c1]),
                              w=["K%d" % m])
                    P.dma("sp", lambda e, K_=K_, h=h, nkb=nkb, ko=ko: e.dma_start(out=K_[64:68, 0:nkb * 128], in_=ktab[h, :, ko:ko + nkb * 128]), w=["K%d" % m])
                Vv = Vb[:, 0:nkb * 128].rearrange("p (b f) -> p b f", f=128)
                for b0 in range(0, nkb, 16):
                    b1 = min(nkb, b0 + 16)
                    P.dma("sp", lambda e, b0=b0, b1=b1, Vv=Vv, h=h, ko=ko: e.dma_start(
                        out=Vv[:, b0:b1, :], in_=VT1[ko + b0 * 128:ko + b1 * 128, 130 + h * 128:130 + (h + 1) * 128].rearrange("(b p) f -> p b f", p=128)),
                        w=["V"])
                for ci in range(nch):
                    tok = (seg["o0"] + 4 * ci) * 128
                    qcol = (qbase + 4 * ci) * 128
                    G_ = Gb[gj % 2]; Gk = "G%d" % (gj % 2); gj += 1
                    rg = (17 + h) * 128
                    P.dma("pool", lambda e, G_=G_, rg=rg, tok=tok: e.dma_start(out=G_[:, :], in_=FT1[rg:rg + 128, tok:tok + 512]), w=[Gk])
                    if static_sign:
                        units = []
                        for g in range(nkb // 2):
                            kb0 = 2 * g
                            if kb0 + 1 < 4 * ci:
                                units.append(("far", 0, kb0))
                            elif kb0 > 4 * ci + 3:
                                units.append(("far", 1, kb0))
                            else:
                                units.append(("near", kb0))
                                units.append(("near", kb0 + 1))
                    else:
                        units = [("near", kb) for kb in range(nkb)]
                    for m in range(2):
                        K_ = Kb[m]; Kk = "K%d" % m
                        Q_ = Qb[qj % 2]; Qk = "Q%d" % (qj % 2); qj += 1
                        rq = (9 + h) * 128 + m * 64
                        for v in range(2):
                            P.dma("pool", lambda e, Q_=Q_, rq=rq, tok=tok, v=v: e.dma_start(out=Q_[0:64, v, :], in_=FT1[rq:rq + 64, tok:tok + 512]), w=[Qk])
                            P.dma("pool", lambda e, Q_=Q_, h=h, v=v, qcol=qcol: e.dma_start(out=Q_[64:68, v, :], in_=qtab[h, v, :, qcol:qcol + 512]), w=[Qk])
                        O_ = OTd[0]; Ok = "OTd0"
                        L_ = LB[0]; Lk = "LB0"

                        def qk(u, gg_, K_=K_, Kk=Kk, Q_=Q_, Qk=Qk):
                            S_ = SS[gg_ % NSS]
                            if u[0] == "far":
                                lst = [(j, u[1], u[2] + j) for j in range(2)]
                            else:
                                lst = [(v, v, u[1]) for v in range(2)]
                            for (slot, v, kb) in lst:
                                P.pe(lambda e, S_=S_, slot=slot, v=v, kb=kb, Q_=Q_, K_=K_: e.matmul(
                                    S_[:, slot, :], lhsT=K_[0:68, kb * 128:(kb + 1) * 128], rhs=Q_[0:68, v, :], start=True, stop=True),
                                    r=[Kk, Qk], w=["SS%d" % (gg_ % NSS)])

                        def ex(u, gg_):
                            S_ = SS[gg_ % NSS]; p_ = PT[gg_ % 3]; M_ = Mb[gg_ % 2]
                            if u[0] == "far":
                                P.act(lambda e, S_=S_, p_=p_: e.activation(out=p_[:], in_=S_[:], func=AF.Exp),
                                      r=["SS%d" % (gg_ % NSS)], w=["PT%d" % (gg_ % 3)])
                            else:
                                if gg_ % 2 == 0:
                                    P.act(lambda e, S_=S_, M_=M_: e.copy(out=M_[:], in_=S_[:, 0, :]), r=["SS%d" % (gg_ % NSS)], w=["Mb%d" % (gg_ % 2)])
                                else:
                                    P.dve(lambda e, S_=S_, M_=M_: e.tensor_copy(out=M_[:], in_=S_[:, 0, :]), r=["SS%d" % (gg_ % NSS)], w=["Mb%d" % (gg_ % 2)])
                                P.dve(lambda e, S_=S_, M_=M_: e.tensor_tensor(out=M_[:], in0=M_[:], in1=S_[:, 1, :], op=ALU.min),
                                      r=["SS%d" % (gg_ % NSS), "Mb%d" % (gg_ % 2)], w=["Mb%d" % (gg_ % 2)])
                                P.act(lambda e, M_=M_, p_=p_: e.activation(out=p_[:, 0, :], in_=M_[:], func=AF.Exp),
                                      r=["Mb%d" % (gg_ % 2)], w=["PT%d" % (gg_ % 3)])

                        def pv(u, gg_, O_=O_, L_=L_, Ok=Ok, Lk=Lk, Vv=Vv, nkb=nkb):
                            p_ = PT[gg_ % 3]
                            lst = [(j, u[2] + j) for j in range(2)] if u[0] == "far" else [(0, u[1])]
                            for (slot, kb) in lst:
                                P.pe(lambda e, p_=p_, kb=kb, slot=slot, O_=O_, Vv=Vv, nkb=nkb: e.matmul(
                                    O_[:, :], lhsT=Vv[:, kb, :], rhs=p_[:, slot, :], start=(kb == 0), stop=(kb == nkb - 1)),
                                    r=["V", "PT%d" % (gg_ % 3)], w=[Ok])
                                P.pe(lambda e, p_=p_, kb=kb, slot=slot, L_=L_, nkb=nkb: e.matmul(
                                    L_[:, :], lhsT=ones_b[:], rhs=p_[:, slot, :], start=(kb == 0), stop=(kb == nkb - 1)),
                                    r=["ones_b", "PT%d" % (gg_ % 3)], w=[Lk])
                        run_units(units, qk, ex, pv)
                        P.dve(lambda e, L_=L_: e.reciprocal(out=rinv[:], in_=L_[:]), r=[Lk], w=["rinv"])
                        if m == 0:
                            P.dve(lambda e, O_=O_: e.tensor_tensor(out=o0[:], in0=O_[:], in1=rinv[:], op=ALU.mult), r=[Ok, "rinv"], w=["o0"])
                        else:
                            y_ = ybuf[job % 2]; yk = "ybuf%d" % (job % 2)
                            P.dve(lambda e, O_=O_: e.tensor_tensor(out=on_[:], in0=O_[:], in1=rinv[:], op=ALU.mult), r=[Ok, "rinv"], w=["on_"])
                            P.dve(lambda e: e.scalar_tensor_tensor(out=od[:], in0=on_[:], scalar=nl[:, 0:1], in1=o0[:], op0=ALU.mult, op1=ALU.add),
                                  r=["on_", "nl", "o0"], w=["od"])
                            P.dve(lambda e: e.tensor_tensor(out=sq[:], in0=od[:], in1=od[:], op=ALU.mult), r=["od"], w=["sq"])
                            P.pe(lambda e, L_=L_: e.matmul(L_[:, :], lhsT=ones_f[:, :], rhs=sq[:], start=True, stop=True), r=["ones_f", "sq"], w=[Lk])
                            P.dve(lambda e, L_=L_: e.tensor_scalar(out=sq[:], in0=L_[:], scalar1=1.0 / 128, scalar2=EPS, op0=ALU.mult, op1=ALU.add),
                                  r=[Lk], w=["sq"])
                            P.act(lambda e: e.activation(out=sq[:], in_=sq[:], func=AF.Sqrt), r=["sq"], w=["sq"])
                            P.dve(lambda e: e.reciprocal(out=sq[:], in_=sq[:]), r=["sq"], w=["sq"])
                            P.dve(lambda e: e.tensor_tensor(out=od[:], in0=od[:], in1=sq[:], op=ALU.mult), r=["od", "sq"], w=["od"])
                            P.dve(lambda e, y_=y_, G_=G_: e.scalar_tensor_tensor(out=y_[:], in0=od[:], scalar=gsc[:, 0:1], in1=G_[:], op0=ALU.mult, op1=ALU.mult),
                                  r=["od", "gsc", Gk], w=[yk])
                            P.dma("pool", lambda e, y_=y_, h=h, qcol=qcol: e.dma_start(
                                out=YT1[512 + h * 128:512 + (h + 1) * 128, qcol:qcol + 512], in_=y_[:]), r=[yk])
                        job += 1
        P.emit_block()

    yout = G["yout"]

    def qidx(seg, i):
        return (0 if seg["name"] == "S" else NBS) + i
    out_phase(1, YT1,
              lambda seg, i: X1[(seg["o0"] + i) * 128:(seg["o0"] + i + 1) * 128, :],
              lambda seg, i: yout[qidx(seg, i) * 128:(qidx(seg, i) + 1) * 128, :],
              [SEG_S, SEG_PO], G["w_out_cd"], lambda seg, i: qidx(seg, i) * 128)


def host_constants(cfg):
    c = {}
    c["eye"] = np.eye(128, dtype=np.float32)
    colmask, dc = nb_static()
    c["colmask"] = colmask
    k = np.arange(128)[:, None]
    q = np.arange(128)[None, :]
    sl8 = alibi_slopes(8)
    al = np.zeros((128, 3, 8, 128), np.float32)
    for r in range(3):
        dist = np.abs(128 * (r - 1) + k - q).astype(np.float32)
        for h in range(8):
            al[:, r, h, :] = np.where(dist <= 128, -sl8[h] * dist, NEG)
    c["alibia"] = al.astype(NPBF)
    sl4 = alibi_slopes(4)
    dg = np.zeros((128, 4, 128), np.float32)
    for h in range(4):
        dg[:, h, :] = -sl4[h] * np.abs(k - q)
    c["diagb"] = dg.astype(NPBF)
    return c


def seg_positions(cfg, core):
    nbs, nbpo, nbpf = cfg["NBS"], cfg["NBPO"], cfg["NBPF"]
    ps_ = np.arange(nbs * 128)
    ppo = core * nbpo * 128 + np.arange(nbpo * 128)
    ppf = np.arange(nbpf * 128)
    return ps_, ppo, ppf


def prepare_core(cfg, core, inp, consts):
    nbs, nbpo, nbpf = cfg["NBS"], cfg["NBPO"], cfg["NBPF"]
    SS, SP = nbs * 128, nbpf * 128
    pad = PAD * 128
    m = dict(consts)
    xs = inp["x_sample"][core]
    xp = inp["x_prompt"][0]
    z = np.zeros((pad, D), np.float32)
    xpp = np.concatenate([z, xp, z], axis=0)
    lo = core * nbpo * 128
    xin = np.concatenate([z, xs, z, xpp[lo:lo + nbpo * 128 + 2 * pad], xpp], axis=0)
    m["xin"] = np.ascontiguousarray(xin)
    vs = np.concatenate([np.zeros(pad), np.ones(SS), np.zeros(pad)])
    vpf = np.concatenate([np.zeros(pad), np.ones(SP), np.zeros(pad)])
    valid = np.concatenate([vs, vpf[lo:lo + nbpo * 128 + 2 * pad], vpf]).astype(np.float32)
    m["validin"] = np.ascontiguousarray(valid.reshape(-1, 128).T)
    pp = inp["p_prompt"][:, 0]
    m["pin"] = np.ascontiguousarray(np.concatenate(
        [inp["p_sample"][:, core], pp[:, lo:lo + nbpo * 128], pp], axis=1))
    m["gvec"] = np.ascontiguousarray(np.stack([inp["norm_pre"][0], inp["norm_post"][0], inp["norm_pre"][1], inp["norm_post"][1]]))
    m["w_in_ab"] = inp["w_in_ab"][0]; m["w_out_ab"] = inp["w_out_ab"][0]
    m["w_in_cd"] = inp["w_in_cd"][0]; m["w_out_cd"] = inp["w_out_cd"][0]
    m["w_ple"] = inp["w_ple"]; m["w_gate"] = inp["w_ple_gate"]
    m["a_sink"] = inp["a_sink"]
    rpb = inp["b_rpb"][0]
    kr = np.arange(2)[:, None, None, None]; kc = np.arange(64)[None, :, None, None]
    qr = np.arange(2)[None, None, :, None]; qc = np.arange(64)[None, None, None, :]
    dcx = np.broadcast_to(np.clip(kc - qc, -15, 15) + 15, (2, 64, 2, 64)).reshape(128, 128)
    g = np.zeros((128, 7, 8, 128), np.float32)
    for r in range(7):
        drx = np.broadcast_to(np.clip(2 * (r - 3) + kr - qr + 7, 0, 14), (2, 64, 2, 64)).reshape(128, 128)
        for h in range(8):
            g[:, r, h, :] = rpb[h][drx, dcx]
    m["rpbg"] = g
    rm = np.zeros((128, 2, 5, 7, 128), np.float32)
    rows_any = 64
    nblk_any = rows_any // 2
    reps = {0: 0, 1: 1, 2: nblk_any // 2, 3: nblk_any - 2, 4: nblk_any - 1}
    rows_p = nbpf * 2
    for cl in range(5):
        for r in range(7):
            n = reps[cl]
            rm[:, 0, cl, r, :] = nb_rowmask_tile(rows_any, n, n + r - 3)
    seg_po = cfg["segs"][1]
    done = set()
    for i in range(nbpo):
        cl = blk_class(seg_po, i)
        if cl in done:
            continue
        done.add(cl)
        n = core * nbpo + i
        for r in range(7):
            rm[:, 1, cl, r, :] = nb_rowmask_tile(rows_p, n, n + r - 3)
    for cl in range(5):
        if cl not in done:
            rm[:, 1, cl] = NEG
    m["rowmask"] = rm.astype(NPBF)
    m["qkgain"] = np.ascontiguousarray(np.concatenate([np.tile(inp["c_q_norm"][0], 8), np.tile(inp["c_k_norm"][0], 2)])[None, :])
    ps_, ppo, ppf = seg_positions(cfg, core)
    pos = np.concatenate([ps_, ppo, ppf])
    inv = (10000.0 ** (-2.0 * np.arange(16) / 32)).astype(np.float32)
    row = (pos // 64).astype(np.float32); col = (pos % 64).astype(np.float32)
    ang = np.concatenate([row[:, None] * inv, col[:, None] * inv], axis=-1).astype(np.float32)
    m["rope"] = np.ascontiguousarray(np.concatenate([np.cos(ang), np.sin(ang)], axis=-1).astype(np.float32))
    m["lamv"] = np.ascontiguousarray(np.stack([inp["d_lambda_q1"][0], inp["d_lambda_k1"][0], inp["d_lambda_q2"][0], inp["d_lambda_k2"][0]])[None])
    m["subln"] = np.ascontiguousarray(inp["d_subln"][0][:, None])
    sl4 = alibi_slopes(4)
    posq = np.concatenate([ps_, ppo]).astype(np.float32)
    qa_, qb_ = np.floor(posq / 128), np.mod(posq, 128)
    qt = np.zeros((4, 2, 4, posq.size), np.float32)
    kt = np.zeros((4, 4, pos.size), np.float32)
    ka_, kb_ = np.floor(pos / 128).astype(np.float32), np.mod(pos, 128).astype(np.float32)
    for h in range(4):
        s = sl4[h]
        qt[h, 0] = np.stack([-s * 128 * qa_, -s * qb_, np.ones_like(qa_), np.ones_like(qa_)])
        qt[h, 1] = np.stack([s * 128 * qa_, s * qb_, -np.ones_like(qa_), -np.ones_like(qa_)])
        kt[h] = np.stack([np.ones_like(ka_), np.ones_like(ka_), s * 128 * ka_, s * kb_])
    m["qtab"] = qt.astype(NPBF)
    m["ktab"] = kt.astype(NPBF)
    return m


_CACHE = {}


def kernel(**inputs):
    inp = {k: np.asarray(v) for k, v in inputs.items()}
    nbs = inp["x_sample"].shape[1] // 128
    nbpf = inp["x_prompt"].shape[1] // 128
    cfg = cfg_make(nbs, nbpf)
    key = (nbs, nbpf)
    if key not in _CACHE:
        _CACHE[key] = build(cfg)
    nc = _CACHE[key]
    consts = host_constants(cfg)
    maps = [prepare_core(cfg, c, inp, consts) for c in range(NCORE)]
    res = run_bass_kernel_spmd(nc, maps, core_ids=list(range(NCORE)))
    nbpo = cfg["NBPO"]
    ys = np.zeros((NCORE, nbs * 128, D), np.float32)
    yp = np.zeros((1, nbpf * 128, D), np.float32)
    for c in range(NCORE):
        y = np.asarray(res.results[c]["yout"])
        ys[c] = y[:nbs * 128]
        yp[0, c * nbpo * 128:(c + 1) * nbpo * 128] = y[nbs * 128:]
    return (yp, ys)
```

```python
import math
import numpy as np
import ml_dtypes
import concourse.bass as bass
import concourse.mybir as mybir
from concourse.bass_utils import run_bass_kernel_spmd

F32 = mybir.dt.float32
BF16 = mybir.dt.bfloat16
AF = mybir.ActivationFunctionType
ALU = mybir.AluOpType
NPBF = ml_dtypes.bfloat16

NCORE = 8
D = 1024
PAD = 4
EPS = 1e-6
NEG = -30000.0
NSEM_DMA = 8


class Prog:
    ENG = ("pe", "act", "dve", "pool", "sp")

    def __init__(self, nc, sems):
        self.nc = nc
        self.sems = sems
        self.sigcount = {e: 0 for e in self.ENG}
        self.dmacount = {"sp": 0, "pool": 0}
        self.waited = {}
        self.reset()

    def reset(self):
        self.ops = []
        self.lw = {}
        self.rd = {}

    def op(self, eng, fn, r=(), w=(), dma=False):
        idx = len(self.ops)
        deps = set()
        for b in r:
            x = self.lw.get(b)
            if x is not None:
                deps.add(x)
        for b in w:
            x = self.lw.get(b)
            if x is not None:
                deps.add(x)
            deps.update(self.rd.get(b, ()))
        for b in r:
            self.rd.setdefault(b, []).append(idx)
        for b in w:
            self.lw[b] = idx
            self.rd[b] = []
        self.ops.append(dict(eng=eng, fn=fn, deps=deps, dma=dma, need=False))
        return idx

    def pe(self, fn, r=(), w=()): return self.op("pe", fn, r, w)
    def act(self, fn, r=(), w=()): return self.op("act", fn, r, w)
    def dve(self, fn, r=(), w=()): return self.op("dve", fn, r, w)
    def pool(self, fn, r=(), w=()): return self.op("pool", fn, r, w)
    def dma(self, q, fn, r=(), w=()): return self.op(q, fn, r, w, dma=True)

    def emit_block(self, name=None):
        nc = self.nc
        ops = self.ops
        for o in ops:
            for d in o["deps"]:
                p = ops[d]
                if p["eng"] == o["eng"] and o["eng"] == "pe" and not p["dma"]:
                    continue
                p["need"] = True
        for o in ops:
            e = o["eng"]
            if o["dma"]:
                k = self.dmacount[e]
                self.dmacount[e] = k + 1
                o["sig"] = ((e, k % NSEM_DMA), 16 * (k // NSEM_DMA + 1))
                o["pre"] = ((e, k % NSEM_DMA), 16 * (k // NSEM_DMA)) if k >= NSEM_DMA else None
            elif o["need"]:
                self.sigcount[e] += 1
                o["sig"] = (e, self.sigcount[e])
                o["pre"] = None
            else:
                o["sig"] = None
                o["pre"] = None
        per = {e: [] for e in self.ENG}
        for o in ops:
            e = o["eng"]
            waits = []
            cand = {}
            for d in o["deps"]:
                p = ops[d]
                if p["eng"] == e and e == "pe" and not p["dma"]:
                    continue
                s, v = p["sig"]
                cand[s] = max(cand.get(s, 0), v)
            if o["pre"] is not None:
                s, v = o["pre"]
                cand[s] = max(cand.get(s, 0), v)
            for s, v in cand.items():
                if self.waited.get((e, s), 0) >= v:
                    continue
                self.waited[(e, s)] = v
                waits.append((s, v))
            per[e].append((o, waits))
        tails = {}
        for q in ("sp", "pool"):
            k = self.dmacount[q]
            tl = []
            for i in range(NSEM_DMA):
                n = (k - i + NSEM_DMA - 1) // NSEM_DMA if k > i else 0
                if n > 0 and self.waited.get((q, (q, i)), 0) < 16 * n:
                    self.waited[(q, (q, i))] = 16 * n
                    tl.append(((q, i), 16 * n))
            tails[q] = tl
        sems = self.sems

        def run(eh, e):
            for o, waits in per[e]:
                for s, v in waits:
                    eh.wait_ge(sems[s], v)
                ins = o["fn"](eh)
                if o["sig"] is not None:
                    s, v = o["sig"]
                    ins.then_inc(sems[s], 16 if o["dma"] else 1)
            for s, v in tails.get(e, ()):
                eh.wait_ge(sems[s], v)

        with nc.Block() as block:
            @block.tensor
            def _(eh): run(eh, "pe")

            @block.scalar
            def _(eh): run(eh, "act")

            @block.vector
            def _(eh): run(eh, "dve")

            @block.gpsimd
            def _(eh): run(eh, "pool")

            @block.sync
            def _(eh): run(eh, "sp")
        self.reset()


def alibi_slopes(n):
    return (2.0 ** (-8.0 * np.arange(1, n + 1) / n)).astype(np.float32)


def nb_rowmask_tile(rows, n, kb):
    m = np.full((2, 64, 2, 64), NEG, np.float32)
    nblk = rows // 2
    if n < 0 or n >= nblk or kb < 0 or kb >= nblk:
        return m.reshape(128, 128)
    for qr in range(2):
        r = 2 * n + qr
        rs = min(max(r - 4, 0), rows - 8)
        for kr in range(2):
            kk = 2 * kb + kr
            if rs <= kk < rs + 8:
                m[kr, :, qr, :] = 0.0
    return m.reshape(128, 128)


def nb_static():
    kc = np.arange(64)[:, None]
    qc = np.arange(64)[None, :]
    ws = np.clip(qc - 8, 0, 48)
    colok = (kc >= ws) & (kc < ws + 16)
    colmask = np.where(colok, 0.0, NEG).astype(np.float32)
    colmask = np.broadcast_to(colmask[None, :, None, :], (2, 64, 2, 64)).reshape(128, 128)
    dc = np.clip(kc - qc, -15, 15) + 15
    return colmask, dc


def cfg_make(nbs, nbpf):
    c = dict(NBS=nbs, NBPF=nbpf, NBPO=nbpf // NCORE)
    segs = []
    e0 = 0
    o0 = 0
    for name, nb in (("S", nbs), ("PO", nbpf // NCORE), ("PF", nbpf)):
        segs.append(dict(name=name, nb=nb, e0=e0, o0=o0, ne=nb + 2 * PAD))
        e0 += nb + 2 * PAD
        o0 += nb
    c["segs"] = segs
    c["NE"] = e0
    c["NO"] = o0
    c["NQ"] = nbs + nbpf // NCORE
    return c


def blk_class(seg, i):
    nb = seg["nb"]
    if i == 0: return 0
    if i == 1: return 1
    if i == nb - 2: return 3
    if i == nb - 1: return 4
    return 2


def blk_rels(seg, i):
    cl = blk_class(seg, i)
    if seg["name"] == "PO":
        return {0: list(range(-2, 4)), 1: list(range(-2, 3)), 2: list(range(-2, 3)),
                3: list(range(-2, 3)), 4: list(range(-3, 3))}[cl]
    return {0: [0, 1, 2, 3], 1: [-1, 0, 1, 2], 2: [-2, -1, 0, 1, 2], 3: [-2, -1, 0, 1], 4: [-3, -2, -1, 0]}[cl]


AB_FM = [("qa", 0, 4, "q"), ("ka", 512, 1, "k"), ("ga", 768, 4, "g"),
         ("qb", 1280, 4, "q"), ("kb", 1792, 4, "k"), ("gb", 2816, 4, "g")]
CD_FM = [("gc", 768, 4, "g"), ("qd", 1280, 4, "q"), ("kd", 1792, 4, "k"), ("gd", 2816, 4, "g")]


def build(cfg, dbg=0):
    nc = bass.Bass("TRN2", target_bir_lowering=False)
    NE, NO, NQ = cfg["NE"], cfg["NO"], cfg["NQ"]
    segs = cfg["segs"]
    TE, TO, TQ = NE * 128, NO * 128, NQ * 128

    def din(name, shape, dt=F32):
        return nc.dram_tensor(name, list(shape), dt, kind="ExternalInput").ap()

    def dscr(name, shape, dt):
        kind = "ExternalOutput" if (dbg and name in ("YT0", "FT0", "VT0", "X1", "FT1", "VT1", "YT1")) else "Internal"
        return nc.dram_tensor(name, list(shape), dt, kind=kind).ap()

    xin = din("xin", [TE, D])
    validin = din("validin", [128, NE])
    pin = din("pin", [2, TO, 256])
    gvec = din("gvec", [4, D])
    w_in_ab = din("w_in_ab", [D, 3328]); w_out_ab = din("w_out_ab", [D, D])
    w_in_cd = din("w_in_cd", [D, 3328]); w_out_cd = din("w_out_cd", [D, D])
    w_ple = din("w_ple", [2, 256, D]); w_gate = din("w_gate", [2, D, D])
    a_sink = din("a_sink", [1, 8])
    rpbg = din("rpbg", [128, 7, 8, 128])
    colmask = din("colmask", [128, 128])
    alibia = din("alibia", [128, 3, 8, 128], BF16)
    rowmask = din("rowmask", [128, 2, 5, 7, 128], BF16)
    qkgain = din("qkgain", [1, 640])
    rope = din("rope", [TO, 64])
    lamv = din("lamv", [1, 4, 64])
    subln = din("subln", [128, 1])
    qtab = din("qtab", [4, 2, 4, TQ], BF16)
    ktab = din("ktab", [4, 4, TO], BF16)
    diagb = din("diagb", [128, 4, 128], BF16)
    yout = nc.dram_tensor("yout", [TQ, D], F32, kind="ExternalOutput").ap()

    FT0 = dscr("FT0", [21 * 128, TE], BF16)
    VT0 = dscr("VT0", [TE, 650], BF16)
    YT0 = dscr("YT0", [D, TO], BF16)
    X1 = dscr("X1", [TO, D], F32)
    FT1 = dscr("FT1", [21 * 128, TO], BF16)
    VT1 = dscr("VT1", [TO, 642], BF16)
    YT1 = dscr("YT1", [D, TQ], BF16)

    import contextlib
    es = contextlib.ExitStack()
    with es:
        sems = {}
        for e in Prog.ENG:
            sems[e] = es.enter_context(nc.semaphore("s_" + e))
        for q in ("sp", "pool"):
            for i in range(NSEM_DMA):
                sems[(q, i)] = es.enter_context(nc.semaphore("d_%s%d" % (q, i)))
        P = Prog(nc, sems)

        ucnt = [0]

        def sb(stack, name, shape, dt):
            ucnt[0] += 1
            return stack.enter_context(nc.sbuf_tensor("%s_u%d" % (name, ucnt[0]), list(shape), dt))

        def ps(stack, name, shape, dt=F32):
            ucnt[0] += 1
            return stack.enter_context(nc.psum_tensor("%s_u%d" % (name, ucnt[0]), list(shape), dt))

        ident = sb(es, "ident", [128, 128], BF16)
        identf = sb(es, "identf", [128, 128], F32)
        ones_f = sb(es, "ones_f", [128, 128], F32)
        zeros_b = sb(es, "zeros_b", [128, 512], BF16)
        gbc = sb(es, "gbc", [128, 4, D], F32)
        valid_sb = sb(es, "valid_sb", [128, NE], F32)
        ones10 = sb(es, "ones10", [128, 10], F32)

        eye = din("eye", [128, 128])
        P.dma("sp", lambda e: e.dma_start(out=identf[:], in_=eye[:, :]), w=["identf"])
        P.dve(lambda e: e.tensor_copy(out=ident[:], in_=identf[:]), r=["identf"], w=["ident"])
        P.dve(lambda e: e.memset(ones_f[:], 1.0), w=["ones_f"])
        P.dve(lambda e: e.memset(zeros_b[:], 0.0), w=["zeros_b"])
        P.dve(lambda e: e.memset(ones10[:], 1.0), w=["ones10"])
        P.dma("sp", lambda e: e.dma_start(out=valid_sb[:], in_=validin[:, :]), w=["valid"])
        for i in range(4):
            P.dma("sp", lambda e, i=i: e.dma_start(out=gbc[:, i, :], in_=gvec[i:i + 1, :].partition_broadcast(128)),
                  w=["gbc"])
        P.emit_block()

        def load_weight(stack_bufs, wdst, wsrc, kch, ncols, key, colchunk=1664):
            stg = stack_bufs
            j = 0
            for k in range(kch):
                for c0 in range(0, ncols, colchunk):
                    c1 = min(ncols, c0 + colchunk)
                    s = stg[j % 2]
                    sk = "wstg%d" % (j % 2)
                    P.dma("sp", lambda e, s=s, k=k, c0=c0, c1=c1: e.dma_start(
                        out=s[:, 0:c1 - c0], in_=wsrc[k * 128:(k + 1) * 128, c0:c1]), w=[sk])
                    if j % 2 == 0:
                        P.dve(lambda e, s=s, k=k, c0=c0, c1=c1: e.tensor_copy(out=wdst[:, k, c0:c1], in_=s[:, 0:c1 - c0]),
                              r=[sk], w=[key])
                    else:
                        P.pool(lambda e, s=s, k=k, c0=c0, c1=c1: e.tensor_copy(out=wdst[:, k, c0:c1], in_=s[:, 0:c1 - c0]),
                               r=[sk], w=[key])
                    j += 1

        def norm_block(xt, xk, gi, hn, hnk, ss, rstd, junk, tagk):
            P.act(lambda e: e.activation(out=junk[:], in_=xt, func=AF.Square, scale=1.0 / math.sqrt(D), accum_out=ss[:, 0:1]),
                  r=[xk], w=["junk", "ss" + tagk])
            P.dve(lambda e: e.tensor_scalar(out=rstd[:, 0:1], in0=ss[:, 0:1], scalar1=EPS, scalar2=None,
                                            op0=ALU.add), r=["ss" + tagk], w=["rstd" + tagk])
            P.act(lambda e: e.activation(out=rstd[:, 0:1], in_=rstd[:, 0:1], func=AF.Sqrt), r=["rstd" + tagk], w=["rstd" + tagk])
            P.dve(lambda e: e.reciprocal(out=rstd[:, 0:1], in_=rstd[:, 0:1]), r=["rstd" + tagk], w=["rstd" + tagk])
            P.dve(lambda e: e.scalar_tensor_tensor(out=hn, in0=xt, scalar=rstd[:, 0:1], in1=gbc[:, gi, :],
                                                   op0=ALU.mult, op1=ALU.mult),
                  r=[xk, "rstd" + tagk, "gbc"], w=[hnk])

        def transpose_to(src, srck, nk, tp, tpk, dst, dstk, use_act):
            for k in range(nk):
                P.pe(lambda e, k=k: e.transpose(out=tp[:, k, :], in_=src[:, k * 128:(k + 1) * 128], identity=ident[:]),
                     r=[srck, "ident"], w=[tpk])
            if use_act:
                P.act(lambda e: e.copy(out=dst, in_=tp[:, 0:nk, :]), r=[tpk], w=[dstk])
            else:
                P.dve(lambda e: e.tensor_copy(out=dst, in_=tp[:, 0:nk, :]), r=[tpk], w=[dstk])

        with contextlib.ExitStack() as st:
            w_in = sb(st, "w_in", [128, 8, 3328], BF16)
            wstg = [sb(st, "wstg0", [128, 1664], F32), sb(st, "wstg1", [128, 1664], F32)]
            xt = [sb(st, "xt%d" % i, [128, D], F32) for i in range(2)]
            hn = [sb(st, "hn%d" % i, [128, D], BF16) for i in range(2)]
            hnT = [sb(st, "hnT%d" % i, [128, 8, 512], BF16) for i in range(2)]
            junk = sb(st, "junk", [128, D], BF16)
            ss = sb(st, "ss", [128, 2], F32)
            rstd = sb(st, "rstd", [128, 2], F32)
            fstage = [sb(st, "fstage%d" % i, [128, 21, 512], BF16) for i in range(2)]
            vstage = [sb(st, "vstage%d" % i, [128, 10, 65], BF16) for i in range(2)]
            tp = [ps(st, "tp%d" % i, [128, 8, 128], BF16) for i in range(2)]
            fm = [ps(st, "fm%d" % i, [128, 512]) for i in range(2)]
            tmA = [ps(st, "tmA%d" % i, [128, 512]) for i in range(1)]
            tmB = [ps(st, "tmB%d" % i, [128, 128]) for i in range(1)]

            load_weight(wstg, w_in, w_in_ab, 8, 3328, "w_in")
            nst = NE // 4
            bc = 0
            for sti in range(nst):
                hT = hnT[sti % 2]
                hTk = "hnT%d" % (sti % 2)
                for b in range(4):
                    eb = sti * 4 + b
                    x_ = xt[bc % 2]
                    xk = "xt%d" % (bc % 2)
                    h_ = hn[bc % 2]
                    hk = "hn%d" % (bc % 2)
                    P.dma("sp", lambda e, x_=x_, eb=eb: e.dma_start(out=x_[:], in_=xin[eb * 128:(eb + 1) * 128, :]), w=[xk])
                    sx = ss[:, bc % 2:bc % 2 + 1]
                    rx = rstd[:, bc % 2:bc % 2 + 1]
                    norm_block(x_[:], xk, 0, h_[:], hk, sx, rx, junk, str(bc % 2))
                    t_ = tp[bc % 2]
                    transpose_to(h_, hk, 8, t_, "tp%d" % (bc % 2), hT[:, :, b * 128:(b + 1) * 128], hTk, bc % 2 == 0)
                    bc += 1
                fs = fstage[sti % 2]
                fsk = "fstage%d" % (sti % 2)
                ci = 0
                for (nm, c0, nch, kind) in AB_FM:
                    for j in range(nch):
                        f0 = c0 + j * 128
                        pf = fm[ci % 2]
                        pk = "fm%d" % (ci % 2)
                        for k in range(8):
                            P.pe(lambda e, pf=pf, k=k, f0=f0, hT=hT: e.matmul(pf[:], lhsT=w_in[:, k, f0:f0 + 128], rhs=hT[:, k, :],
                                                                         start=(k == 0), stop=(k == 7)),
                                 r=["w_in", hTk], w=[pk])
                        if kind == "g":
                            P.act(lambda e, pf=pf, ci=ci, fs=fs: e.activation(out=fs[:, ci, :], in_=pf[:], func=AF.Silu),
                                  r=[pk], w=[fsk])
                        elif kind == "q":
                            P.dve(lambda e, pf=pf, ci=ci, fs=fs: e.tensor_scalar(out=fs[:, ci, :], in0=pf[:], scalar1=0.125,
                                                                           scalar2=None, op0=ALU.mult),
                                  r=[pk], w=[fsk])
                        else:
                            P.dve(lambda e, pf=pf, ci=ci, fs=fs: e.tensor_copy(out=fs[:, ci, :], in_=pf[:]), r=[pk], w=[fsk])
                        ci += 1
                P.dma("pool", lambda e, fs=fs, sti=sti: e.dma_start(
                    out=FT0.rearrange("(c p) t -> p c t", p=128)[:, :, sti * 512:(sti + 1) * 512], in_=fs[:]), r=[fsk])
                for b in range(4):
                    eb = sti * 4 + b
                    vs = vstage[b % 2]
                    vk = "vstage%d" % (b % 2)
                    for k in range(8):
                        P.pe(lambda e, k=k, b=b, hT=hT: e.matmul(tmA[0][:], lhsT=hT[:, k, b * 128:(b + 1) * 128],
                                                           rhs=w_in[:, k, 2304:2816], start=(k == 0), stop=(k == 7)),
                             r=["w_in", hTk], w=["tmA"])
                    for k in range(8):
                        P.pe(lambda e, k=k, b=b, hT=hT: e.matmul(tmB[0][:], lhsT=hT[:, k, b * 128:(b + 1) * 128],
                                                           rhs=w_in[:, k, 640:768], start=(k == 0), stop=(k == 7)),
                             r=["w_in", hTk], w=["tmB"])
                    P.dve(lambda e, vs=vs: e.tensor_copy(out=vs[:, 2:10, 0:64], in_=tmA[0][:].rearrange("p (h d) -> p h d", d=64)),
                          r=["tmA"], w=[vk])
                    P.act(lambda e, vs=vs: e.copy(out=vs[:, 0:2, 0:64], in_=tmB[0][:].rearrange("p (h d) -> p h d", d=64)),
                          r=["tmB"], w=[vk])
                    P.dve(lambda e, vs=vs, eb=eb: e.tensor_scalar(out=vs[:, :, 64], in0=ones10[:], scalar1=valid_sb[:, eb:eb + 1],
                                                             scalar2=None, op0=ALU.mult), r=["valid", "ones10"], w=[vk])
                    P.dma("pool", lambda e, vs=vs, eb=eb: e.dma_start(out=VT0[eb * 128:(eb + 1) * 128, :],
                                                                 in_=vs[:].rearrange("p h d -> p (h d)")), r=[vk])
            P.emit_block()

        with contextlib.ExitStack() as st:
            rpbcol = sb(st, "rpbcol", [128, 7, 8, 128], BF16)
            rpbstg = sb(st, "rpbstg", [128, 7, 8, 128], F32)
            cmask = sb(st, "cmask", [128, 128], F32)
            alib = sb(st, "alib", [128, 3, 8, 128], BF16)
            rmask = sb(st, "rmask", [128, 2, 5, 7, 128], BF16)
            sinkrow = sb(st, "sinkrow", [65, 8, 128], F32)
            sinkv = sb(st, "sinkv", [65, 8], F32)
            KAb = [sb(st, "KA%d" % i, [128, 2, 7 * 128], BF16) for i in range(2)]
            KBb = [sb(st, "KB%d" % i, [128, 8, 7 * 128], BF16) for i in range(2)]
            VRb = [sb(st, "VR%d" % i, [128, 7, 650], BF16) for i in range(2)]
            QAb = [sb(st, "QA%d" % i, [128, 8, 128], BF16) for i in range(2)]
            QBb = [sb(st, "QB%d" % i, [128, 8, 128], BF16) for i in range(2)]
            GAb = [sb(st, "GA%d" % i, [64, 8, 128], BF16) for i in range(2)]
            GBb = [sb(st, "GB%d" % i, [64, 8, 128], BF16) for i in range(2)]
            PT = [sb(st, "PT%d" % i, [128, 3, 512], BF16) for i in range(2)]
            den2 = [sb(st, "den%d" % i, [65, 512], F32) for i in range(2)]
            rinv2 = [sb(st, "rinv%d" % i, [128, 512], F32) for i in range(2)]
            on = [sb(st, "on%d" % i, [64, 512], F32) for i in range(2)]
            ystage = [sb(st, "ystage%d" % i, [64, 16, 128], BF16) for i in range(2)]
            ST = [ps(st, "ST%d" % i, [128, 3, 512]) for i in range(2)]
            OT2 = [ps(st, "OT%d" % i, [128, 512]) for i in range(2)]

            for i_ in range(2):
                P.dve(lambda e, i_=i_: e.memset(rinv2[i_][:], 0.0), w=["rinv%d" % i_])
            for i_ in range(2):
                P.dve(lambda e, i_=i_: e.memset(KAb[i_][64:128], 0.0), w=["kv%d" % i_])
                P.pool(lambda e, i_=i_: e.memset(KBb[i_][64:128], 0.0), w=["kv%d" % i_])
                P.dve(lambda e, i_=i_: e.memset(QAb[i_][64:128], 0.0), w=["qg%d" % i_])
                P.dve(lambda e, i_=i_: e.memset(QBb[i_][64:128], 0.0), w=["qg%d" % i_])
            P.dma("sp", lambda e: e.dma_start(out=rpbstg[:], in_=rpbg[:, :, :, :]), w=["rpbstg"])
            P.dma("sp", lambda e: e.dma_start(out=cmask[:], in_=colmask[:, :]), w=["cmask"])
            P.dma("sp", lambda e: e.dma_start(out=alib[:], in_=alibia[:, :, :, :]), w=["alib"])
            P.dma("sp", lambda e: e.dma_start(out=rmask[:], in_=rowmask[:, :, :, :, :]), w=["rmask"])
            P.dma("sp", lambda e: e.dma_start(out=sinkv[64:65, :], in_=a_sink[0:1, :]), w=["sinkv"])
            for r_ in range(7):
                for h in range(8):
                    P.dve(lambda e, r_=r_, h=h: e.tensor_tensor(out=rpbcol[:, r_, h, :], in0=rpbstg[:, r_, h, :], in1=cmask[:],
                                                             op=ALU.add), r=["rpbstg", "cmask"], w=["rpbcol"])
            P.act(lambda e: e.activation(out=sinkv[64:65, :], in_=sinkv[64:65, :], func=AF.Exp), r=["sinkv"], w=["sinkv"])
            P.dve(lambda e: e.tensor_copy(out=sinkrow[64:65, :, :], in_=sinkv[64:65, :].unsqueeze(2).to_broadcast([1, 8, 128])),
                  r=["sinkv"], w=["sinkrow"])

            FT0v = FT0.rearrange("(c h d) t -> d c h t", h=2, d=64)

            def block_gen(seg, i, bi):
                s2 = bi % 2
                tab = 1 if seg["name"] == "PO" else 0
                eb = seg["e0"] + PAD + i
                ob = seg["o0"] + i
                ka, kb_, vr = KAb[s2], KBb[s2], VRb[s2]
                qa, qb, ga, gb_ = QAb[s2], QBb[s2], GAb[s2], GBb[s2]
                S_ = ST[s2]; Sk = "ST%d" % s2
                p_ = PT[s2]; pk = "PT%d" % s2
                O_ = OT2[s2]; Ok = "OT%d" % s2
                o_ = on[s2]; ok_ = "on%d" % s2
                dn = den2[s2]; dk = "den%d" % s2
                rv = rinv2[s2]; rk_ = "rinv%d" % s2
                ys = ystage[s2]; ysk = "ystage%d" % s2
                t0 = (eb - 3) * 128
                t1 = (eb + 4) * 128
                kk = "kv%d" % s2
                qk = "qg%d" % s2
                P.dma("sp", lambda e: e.dma_start(out=ka[0:64], in_=FT0v[:, 4, :, t0:t1]), w=[kk])
                for hh in range(4):
                    P.dma("sp", lambda e, hh=hh: e.dma_start(out=kb_[0:64, 2 * hh:2 * hh + 2, :], in_=FT0v[:, 13 + hh, :, t0:t1]), w=[kk])
                P.dma("sp", lambda e: e.dma_start(out=vr[:], in_=VT0[t0:t1, :].rearrange("(b p) f -> p b f", p=128)), w=[kk])
                q0 = eb * 128
                for hh in range(4):
                    P.dma("pool", lambda e, hh=hh: e.dma_start(out=qa[0:64, 2 * hh:2 * hh + 2, :], in_=FT0v[:, 0 + hh, :, q0:q0 + 128]), w=[qk])
                    P.dma("pool", lambda e, hh=hh: e.dma_start(out=ga[:, 2 * hh:2 * hh + 2, :], in_=FT0v[:, 5 + hh, :, q0:q0 + 128]), w=[qk])
                    P.dma("pool", lambda e, hh=hh: e.dma_start(out=qb[0:64, 2 * hh:2 * hh + 2, :], in_=FT0v[:, 9 + hh, :, q0:q0 + 128]), w=[qk])
                    P.dma("pool", lambda e, hh=hh: e.dma_start(out=gb_[:, 2 * hh:2 * hh + 2, :], in_=FT0v[:, 17 + hh, :, q0:q0 + 128]), w=[qk])
                yield
                cl = blk_class(seg, i)
                rels_b = blk_rels(seg, i)
                for job in range(4):
                    isA = job < 2
                    g = job % 2
                    rels = [-1, 0, 1] if isA else rels_b
                    batches = [rels[j:j + 3] for j in range(0, len(rels), 3)]
                    P.pe(lambda e: e.matmul(O_[0:65, :], lhsT=zeros_b[:, 0:65], rhs=zeros_b[:, :], start=True, stop=True),
                         r=["zeros_b"], w=[Ok])
                    for bt in batches:
                        for j, rel in enumerate(bt):
                            kof = (rel + 3) * 128
                            if isA:
                                P.pe(lambda e, g=g, j=j, rel=rel: e.matmul(S_[:, j, :], lhsT=ident[:], rhs=alib[:, rel + 1, 4 * g:4 * g + 4, :],
                                                                      start=True, stop=False), r=["ident", "alib"], w=[Sk])
                                P.pe(lambda e, g=g, j=j, kof=kof: e.matmul(S_[:, j, :], lhsT=ka[:, g, kof:kof + 128], rhs=qa[:, 4 * g:4 * g + 4, :],
                                                                      start=False, stop=True), r=[kk, qk], w=[Sk])
                            else:
                                P.pe(lambda e, g=g, j=j, rel=rel: e.matmul(S_[:, j, :], lhsT=ident[:], rhs=rpbcol[:, rel + 3, 4 * g:4 * g + 4, :],
                                                                      start=True, stop=False), r=["ident", "rpbcol"], w=[Sk])
                                for h in range(4):
                                    P.pe(lambda e, g=g, j=j, rel=rel, h=h: e.matmul(S_[:, j, h * 128:(h + 1) * 128], lhsT=ident[:],
                                                                               rhs=rmask[:, tab, cl, rel + 3, :], start=False, stop=False),
                                         r=["ident", "rmask"], w=[Sk])
                                for h in range(4):
                                    P.pe(lambda e, g=g, j=j, kof=kof, h=h: e.matmul(S_[:, j, h * 128:(h + 1) * 128], lhsT=kb_[:, 4 * g + h, kof:kof + 128],
                                                                               rhs=qb[:, 4 * g + h, :], start=False, stop=(h == 3)),
                                         r=[kk, qk], w=[Sk])
                        yield
                        nb_ = len(bt)
                        P.act(lambda e, g=g, nb_=nb_: e.activation(out=p_[:, 0:nb_, :], in_=S_[:, 0:nb_, :], func=AF.Exp), r=[Sk], w=[pk])
                        for j, rel in enumerate(bt):
                            ko = rel + 3
                            if isA:
                                P.pe(lambda e, g=g, j=j, ko=ko: e.matmul(O_[0:65, :], lhsT=vr[:, ko, g * 65:(g + 1) * 65], rhs=p_[:, j, :],
                                                                    start=False, stop=True), r=[kk, pk], w=[Ok])
                            else:
                                for h in range(4):
                                    hv = 2 + 4 * g + h
                                    P.pe(lambda e, g=g, j=j, ko=ko, hv=hv, h=h: e.matmul(O_[0:65, h * 128:(h + 1) * 128], lhsT=vr[:, ko, hv * 65:(hv + 1) * 65],
                                                                                    rhs=p_[:, j, h * 128:(h + 1) * 128], start=False, stop=True),
                                         r=[kk, pk], w=[Ok])
                        yield
                    if isA:
                        P.dve(lambda e, g=g: e.tensor_tensor(out=dn[64:65, :], in0=O_[64:65, :], in1=sinkrow[64:65, 4 * g:4 * g + 4, :], op=ALU.add),
                              r=[Ok, "sinkrow"], w=[dk])
                    else:
                        P.dve(lambda e: e.tensor_copy(out=dn[64:65, :], in_=O_[64:65, :]), r=[Ok], w=[dk])
                    P.dve(lambda e: e.reciprocal(out=rv[64:65, :], in_=dn[64:65, :]), r=[dk], w=[rk_])
                    gsrc = ga if isA else gb_
                    P.dve(lambda e, g=g, gsrc=gsrc: e.tensor_tensor(out=o_[:], in0=O_[0:64, :], in1=gsrc[:, 4 * g:4 * g + 4, :], op=ALU.mult),
                          r=[Ok, qk], w=[ok_])
                    yield
                    P.pe(lambda e: e.matmul(S_[0:64, 0, :], lhsT=ones_f[:, 0:64], rhs=rv[:, :], start=True, stop=True),
                         r=["ones_f", rk_], w=[Sk])
                    hb = (0 if isA else 8) + 4 * g
                    P.dve(lambda e, g=g, hb=hb: e.tensor_tensor(out=ys[:, hb:hb + 4, :], in0=o_[:], in1=S_[0:64, 0, :], op=ALU.mult),
                          r=[ok_, Sk], w=[ysk])
                    yield
                P.dma("pool", lambda e: e.dma_start(out=YT0.rearrange("(h d) t -> d h t", d=64)[:, :, ob * 128:(ob + 1) * 128], in_=ys[:]), r=[ysk])

            blocks = [(seg, i) for seg in segs for i in range(seg["nb"])]
            for p0 in range(0, len(blocks), 2):
                live = [block_gen(blocks[p0 + s_][0], blocks[p0 + s_][1], p0 + s_) for s_ in range(2) if p0 + s_ < len(blocks)]
                while live:
                    for g_ in list(live):
                        try:
                            next(g_)
                        except StopIteration:
                            live.remove(g_)
            P.emit_block()

        if dbg == 1:
            return nc
        _phase345(nc, P, cfg, sb, ps, dbg=dbg, G=dict(
            ident=ident, ones_f=ones_f, zeros_b=zeros_b, gbc=gbc, xin=xin, pin=pin, w_out_ab=w_out_ab, w_in_cd=w_in_cd,
            w_out_cd=w_out_cd, w_ple=w_ple, w_gate=w_gate, qkgain=qkgain, rope=rope, lamv=lamv, subln=subln, qtab=qtab,
            ktab=ktab, diagb=diagb, yout=yout, YT0=YT0, X1=X1, FT1=FT1, VT1=VT1, YT1=YT1,
            load_weight=load_weight, norm_block=norm_block, transpose_to=transpose_to))
    return nc


LAM_INIT1 = 0.8 - 0.6 * math.exp(-0.3 * 1)


def _phase345(nc, P, cfg, sb, ps, G, dbg=0):
    import contextlib
    segs = cfg["segs"]
    ident, ones_f, zeros_b, gbc = G["ident"], G["ones_f"], G["zeros_b"], G["gbc"]
    pin = G["pin"]
    load_weight, norm_block, transpose_to = G["load_weight"], G["norm_block"], G["transpose_to"]
    FT1, VT1, X1, YT1 = G["FT1"], G["VT1"], G["X1"], G["YT1"]

    def out_phase(layer, YT, xsrc_fn, dst_fn, seglist, w_out_d, tok_fn):
        with contextlib.ExitStack() as st:
            wout = sb(st, "wout", [128, 8, D], BF16)
            wg = sb(st, "wg", [128, 8, D], BF16)
            wp = sb(st, "wp", [128, 2, D], BF16)
            wstg = [sb(st, "wstg0", [128, 1024], F32), sb(st, "wstg1", [128, 1024], F32)]
            yts = [sb(st, "yts%d" % i, [128, 8, 512], BF16) for i in range(2)]
            xt = [sb(st, "xt%d" % i, [128, D], F32) for i in range(2)]
            psb = [sb(st, "psb%d" % i, [128, 256], F32) for i in range(2)]
            pb = sb(st, "pb", [128, 256], BF16)
            tmix = sb(st, "tmix", [128, D], F32)
            xa = sb(st, "xa", [128, D], F32)
            xab = sb(st, "xab", [128, D], BF16)
            xaT = sb(st, "xaT", [128, 8, 128], BF16)
            pT = sb(st, "pT", [128, 2, 128], BF16)
            sg = sb(st, "sg", [128, D], F32)
            xo = [sb(st, "xo%d" % i, [128, D], F32) for i in range(2)]
            junk = sb(st, "junk", [128, D], BF16)
            ss = sb(st, "ss", [128, 2], F32)
            mix = ps(st, "mix", [128, D])
            gate = ps(st, "gate", [128, D])
            pp = ps(st, "pp", [128, D])
            tp = ps(st, "tp", [128, 8, 128], BF16)
            tp2 = ps(st, "tp2", [128, 8, 128], BF16)
            load_weight(wstg, wout, w_out_d, 8, D, "wout", colchunk=1024)
            load_weight(wstg, wg, G["w_gate"][layer], 8, D, "wg", colchunk=1024)
            load_weight(wstg, wp, G["w_ple"][layer], 2, D, "wp", colchunk=1024)
            gi = 1 + 2 * layer
            bc = 0
            sc = 0
            for seg in seglist:
                for sti in range(seg["nb"] // 4):
                    y_ = yts[sc % 2]
                    yk = "yts%d" % (sc % 2)
                    ot0 = tok_fn(seg, sti * 4)
                    P.dma("sp", lambda e, y_=y_, ot0=ot0: e.dma_start(
                        out=y_[:], in_=YT.rearrange("(k p) t -> p k t", p=128)[:, :, ot0:ot0 + 512]), w=[yk])
                    sc += 1
                    for b in range(4):
                        i = sti * 4 + b
                        x_ = xt[bc % 2]; xk = "xt%d" % (bc % 2)
                        p_ = psb[bc % 2]; pk = "psb%d" % (bc % 2)
                        o_ = xo[bc % 2]; ok_ = "xo%d" % (bc % 2)
                        xs_ap = xsrc_fn(seg, i)
                        po = (seg["o0"] + i) * 128
                        P.dma("sp", lambda e, x_=x_, xs_ap=xs_ap: e.dma_start(out=x_[:], in_=xs_ap), w=[xk])
                        P.dma("sp", lambda e, p_=p_, po=po: e.dma_start(out=p_[:], in_=pin[layer, po:po + 128, :]), w=[pk])
                        for half in range(2):
                            for k in range(8):
                                P.pe(lambda e, y_=y_, k=k, b=b, half=half: e.matmul(
                                    mix[:, half * 512:(half + 1) * 512], lhsT=y_[:, k, b * 128:(b + 1) * 128],
                                    rhs=wout[:, k, half * 512:(half + 1) * 512], start=(k == 0), stop=(k == 7)),
                                    r=[yk, "wout"], w=["mix"])
                        sx = ss[:, 0:1]
                        P.act(lambda e: e.activation(out=junk[:], in_=mix[:], func=AF.Square, scale=1.0 / math.sqrt(D),
                                                     accum_out=sx), r=["mix"], w=["junk", "ss"])
                        P.dve(lambda e: e.tensor_scalar(out=sx, in0=sx, scalar1=EPS, scalar2=None, op0=ALU.add), r=["ss"], w=["ss"])
                        P.act(lambda e: e.activation(out=sx, in_=sx, func=AF.Sqrt), r=["ss"], w=["ss"])
                        P.dve(lambda e: e.reciprocal(out=sx, in_=sx), r=["ss"], w=["ss"])
                        P.dve(lambda e: e.scalar_tensor_tensor(out=tmix[:], in0=mix[:], scalar=sx, in1=gbc[:, gi, :],
                                                               op0=ALU.mult, op1=ALU.mult), r=["mix", "ss", "gbc"], w=["tmix"])
                        P.pool(lambda e, x_=x_: e.tensor_tensor(out=xa[:], in0=x_[:], in1=tmix[:], op=ALU.add),
                               r=[xk, "tmix"], w=["xa"])
                        P.act(lambda e: e.copy(out=xab[:], in_=xa[:]), r=["xa"], w=["xab"])
                        transpose_to(xab, "xab", 8, tp, "tp", xaT[:], "xaT", False)
                        P.act(lambda e, p_=p_: e.copy(out=pb[:], in_=p_[:]), r=[pk], w=["pb"])
                        transpose_to(pb, "pb", 2, tp2, "tp2", pT[:], "pT", False)
                        for half in range(2):
                            for k in range(8):
                                P.pe(lambda e, k=k, half=half: e.matmul(
                                    gate[:, half * 512:(half + 1) * 512], lhsT=xaT[:, k, :],
                                    rhs=wg[:, k, half * 512:(half + 1) * 512], start=(k == 0), stop=(k == 7)),
                                    r=["xaT", "wg"], w=["gate"])
                            for k in range(2):
                                P.pe(lambda e, k=k, half=half: e.matmul(
                                    pp[:, half * 512:(half + 1) * 512], lhsT=pT[:, k, :],
                                    rhs=wp[:, k, half * 512:(half + 1) * 512], start=(k == 0), stop=(k == 1)),
                                    r=["pT", "wp"], w=["pp"])
                        P.act(lambda e: e.activation(out=sg[:], in_=gate[:], func=AF.Sigmoid), r=["gate"], w=["sg"])
                        P.dve(lambda e: e.tensor_tensor(out=tmix[:], in0=sg[:], in1=pp[:], op=ALU.mult), r=["sg", "pp"], w=["tmix"])
                        P.pool(lambda e, o_=o_: e.tensor_tensor(out=o_[:], in0=xa[:], in1=tmix[:], op=ALU.add),
                               r=["xa", "tmix"], w=[ok_])
                        d_ap = dst_fn(seg, i)
                        P.dma("pool", lambda e, o_=o_, d_ap=d_ap: e.dma_start(out=d_ap, in_=o_[:]), r=[ok_])
                        bc += 1
            P.emit_block()

    xin = G["xin"]
    out_phase(0, G["YT0"],
              lambda seg, i: xin[(seg["e0"] + PAD + i) * 128:(seg["e0"] + PAD + i + 1) * 128, :],
              lambda seg, i: X1[(seg["o0"] + i) * 128:(seg["o0"] + i + 1) * 128, :],
              segs, G["w_out_ab"], lambda seg, i: (seg["o0"] + i) * 128)
    if dbg == 2:
        return

    with contextlib.ExitStack() as st:
        w_in = sb(st, "w_in", [128, 8, 3328], BF16)
        wstg = [sb(st, "wstg0", [128, 1664], F32), sb(st, "wstg1", [128, 1664], F32)]
        xt = [sb(st, "xt%d" % i, [128, D], F32) for i in range(2)]
        hn = [sb(st, "hn%d" % i, [128, D], BF16) for i in range(2)]
        hnT = [sb(st, "hnT%d" % i, [128, 8, 512], BF16) for i in range(2)]
        junk = sb(st, "junk", [128, D], BF16)
        ss = sb(st, "ss", [128, 2], F32)
        rstd = sb(st, "rstd", [128, 2], F32)
        fstage = sb(st, "fstage", [128, 16, 512], BF16)
        qkts = sb(st, "qkts", [128, 5, 512], BF16)
        vst = [sb(st, "vst%d" % i, [128, 642], BF16) for i in range(2)]
        qkf = sb(st, "qkf", [128, 10, 64], F32)
        sqj = sb(st, "sqj", [128, 10, 64], F32)
        ssq = sb(st, "ssq", [128, 10], F32)
        qn = sb(st, "qn", [128, 10, 64], F32)
        ra = sb(st, "ra", [128, 10, 32], F32)
        rb = sb(st, "rb", [128, 10, 32], F32)
        qr = sb(st, "qr", [128, 640], BF16)
        gain = sb(st, "gain", [128, 10, 64], F32)
        rp = [sb(st, "rp%d" % i, [128, 64], F32) for i in range(2)]
        tp = [ps(st, "tp%d" % i, [128, 8, 128], BF16) for i in range(2)]
        fm = [ps(st, "fm%d" % i, [128, 512]) for i in range(2)]
        tqA = ps(st, "tqA", [128, 512])
        tqB = ps(st, "tqB", [128, 256])
        tvD = ps(st, "tvD", [128, 512])
        load_weight(wstg, w_in, G["w_in_cd"], 8, 3328, "w_in")
        P.dma("sp", lambda e: e.dma_start(out=gain[:].rearrange("p h d -> p (h d)"), in_=G["qkgain"][0:1, :].partition_broadcast(128)), w=["gain"])
        P.dve(lambda e: e.tensor_scalar(out=gain[:, 0:8, :], in0=gain[:, 0:8, :], scalar1=0.125, scalar2=None, op0=ALU.mult),
              r=["gain"], w=["gain"])
        for i in range(2):
            P.dve(lambda e, i=i: e.memset(vst[i][:], 1.0), w=["vst%d" % i])
        bc = 0
        sc = 0
        rope = G["rope"]
        for seg in segs:
            pf = seg["name"] == "PF"
            h0 = 8 if pf else 0
            for sti in range(seg["nb"] // 4):
                hT = hnT[sc % 2]; hTk = "hnT%d" % (sc % 2)
                ot0 = (seg["o0"] + sti * 4) * 128
                for b in range(4):
                    ob = seg["o0"] + sti * 4 + b
                    x_ = xt[bc % 2]; xk = "xt%d" % (bc % 2)
                    h_ = hn[bc % 2]; hk = "hn%d" % (bc % 2)
                    P.dma("sp", lambda e, x_=x_, ob=ob: e.dma_start(out=x_[:], in_=X1[ob * 128:(ob + 1) * 128, :]), w=[xk])
                    norm_block(x_[:], xk, 2, h_[:], hk, ss[:, bc % 2:bc % 2 + 1], rstd[:, bc % 2:bc % 2 + 1], junk, str(bc % 2))
                    transpose_to(h_, hk, 8, tp[bc % 2], "tp%d" % (bc % 2), hT[:, :, b * 128:(b + 1) * 128], hTk, bc % 2 == 0)
                    bc += 1
                lst = [("kd", 1792, 4, "k")] if pf else CD_FM
                ci = 0
                for (nm, c0, nch, kind) in lst:
                    for j in range(nch):
                        f0 = c0 + j * 128
                        pf_ = fm[ci % 2]; pk = "fm%d" % (ci % 2)
                        for k in range(8):
                            P.pe(lambda e, pf_=pf_, k=k, f0=f0, hT=hT: e.matmul(pf_[:], lhsT=w_in[:, k, f0:f0 + 128], rhs=hT[:, k, :],
                                                                           start=(k == 0), stop=(k == 7)), r=["w_in", hTk], w=[pk])
                        if kind == "g":
                            P.act(lambda e, pf_=pf_, ci=ci: e.activation(out=fstage[:, ci, :], in_=pf_[:], func=AF.Silu), r=[pk], w=["fstage"])
                        elif kind == "q":
                            P.dve(lambda e, pf_=pf_, ci=ci: e.tensor_scalar(out=fstage[:, ci, :], in0=pf_[:], scalar1=0.125, scalar2=None,
                                                                       op0=ALU.mult), r=[pk], w=["fstage"])
                        else:
                            P.dve(lambda e, pf_=pf_, ci=ci: e.tensor_copy(out=fstage[:, ci, :], in_=pf_[:]), r=[pk], w=["fstage"])
                        ci += 1
                FT1v = FT1.rearrange("(c p) t -> p c t", p=128)
                if pf:
                    P.dma("pool", lambda e, ot0=ot0: e.dma_start(out=FT1v[:, 13:17, ot0:ot0 + 512], in_=fstage[:, 0:4, :]), r=["fstage"])
                else:
                    P.dma("pool", lambda e, ot0=ot0: e.dma_start(out=FT1v[:, 5:21, ot0:ot0 + 512], in_=fstage[:, 0:16, :]), r=["fstage"])
                for b in range(4):
                    ob = seg["o0"] + sti * 4 + b
                    vs = vst[b % 2]; vk = "vst%d" % (b % 2)
                    r_ = rp[b % 2]; rk = "rp%d" % (b % 2)
                    P.dma("sp", lambda e, r_=r_, ob=ob: e.dma_start(out=r_[:], in_=rope[ob * 128:(ob + 1) * 128, :]), w=[rk])
                    if not pf:
                        for k in range(8):
                            P.pe(lambda e, k=k, b=b, hT=hT: e.matmul(tqA[:], lhsT=hT[:, k, b * 128:(b + 1) * 128], rhs=w_in[:, k, 0:512],
                                                               start=(k == 0), stop=(k == 7)), r=["w_in", hTk], w=["tqA"])
                    for k in range(8):
                        P.pe(lambda e, k=k, b=b, hT=hT: e.matmul(tqB[:], lhsT=hT[:, k, b * 128:(b + 1) * 128], rhs=w_in[:, k, 512:768],
                                                           start=(k == 0), stop=(k == 7)), r=["w_in", hTk], w=["tqB"])
                    for k in range(8):
                        P.pe(lambda e, k=k, b=b, hT=hT: e.matmul(tvD[:], lhsT=hT[:, k, b * 128:(b + 1) * 128], rhs=w_in[:, k, 2304:2816],
                                                           start=(k == 0), stop=(k == 7)), r=["w_in", hTk], w=["tvD"])
                    P.dve(lambda e, vs=vs: e.tensor_copy(out=vs[:, 0:130].rearrange("p (h d) -> p h d", d=65)[:, :, 0:64],
                                                         in_=tqB[:, 128:256].rearrange("p (h d) -> p h d", d=64)), r=["tqB"], w=[vk])
                    P.act(lambda e, vs=vs: e.copy(out=vs[:, 130:642], in_=tvD[:]), r=["tvD"], w=[vk])
                    P.dma("pool", lambda e, vs=vs, ob=ob: e.dma_start(out=VT1[ob * 128:(ob + 1) * 128, :], in_=vs[:]), r=[vk])
                    if not pf:
                        P.act(lambda e: e.copy(out=qkf[:, 0:8, :], in_=tqA[:].rearrange("p (h d) -> p h d", d=64)), r=["tqA"], w=["qkf"])
                    P.act(lambda e: e.copy(out=qkf[:, 8:10, :], in_=tqB[:, 0:128].rearrange("p (h d) -> p h d", d=64)), r=["tqB"], w=["qkf"])
                    hs = slice(h0, 10)
                    nh = 10 - h0
                    P.dve(lambda e, hs=hs: e.tensor_tensor(out=sqj[:, hs, :], in0=qkf[:, hs, :], in1=qkf[:, hs, :], op=ALU.mult), r=["qkf"], w=["sqj"])
                    P.dve(lambda e, hs=hs: e.tensor_reduce(out=ssq[:, hs], in_=sqj[:, hs, :], axis=mybir.AxisListType.X, op=ALU.add),
                          r=["sqj"], w=["ssq"])
                    P.dve(lambda e, hs=hs: e.tensor_scalar(out=ssq[:, hs], in0=ssq[:, hs], scalar1=1.0 / 64, scalar2=EPS, op0=ALU.mult, op1=ALU.add),
                          r=["ssq"], w=["ssq"])
                    P.act(lambda e, hs=hs: e.activation(out=ssq[:, hs], in_=ssq[:, hs], func=AF.Sqrt), r=["ssq"], w=["ssq"])
                    P.dve(lambda e, hs=hs: e.reciprocal(out=ssq[:, hs], in_=ssq[:, hs]), r=["ssq"], w=["ssq"])
                    P.dve(lambda e, hs=hs, nh=nh: e.tensor_tensor(out=qn[:, hs, :], in0=qkf[:, hs, :],
                                                                in1=ssq[:, hs].unsqueeze(2).to_broadcast([128, nh, 64]), op=ALU.mult),
                          r=["qkf", "ssq"], w=["qn"])
                    P.dve(lambda e, hs=hs: e.tensor_tensor(out=qn[:, hs, :], in0=qn[:, hs, :], in1=gain[:, hs, :], op=ALU.mult),
                          r=["qn", "gain"], w=["qn"])
                    qv = qn[:].rearrange("p h (j two) -> p h j two", two=2)
                    qrv = qr[:].rearrange("p (h j two) -> p h j two", two=2, j=32)
                    cs = lambda r_=r_, nh=nh: r_[:, 0:32].unsqueeze(1).to_broadcast([128, nh, 32])
                    sn = lambda r_=r_, nh=nh: r_[:, 32:64].unsqueeze(1).to_broadcast([128, nh, 32])
                    P.dve(lambda e, hs=hs, cs=cs: e.tensor_tensor(out=ra[:, hs, :], in0=qv[:, hs, :, 0], in1=cs(), op=ALU.mult), r=["qn", rk], w=["ra"])
                    P.dve(lambda e, hs=hs, sn=sn: e.tensor_tensor(out=rb[:, hs, :], in0=qv[:, hs, :, 1], in1=sn(), op=ALU.mult), r=["qn", rk], w=["rb"])
                    P.dve(lambda e, hs=hs: e.tensor_tensor(out=qrv[:, hs, :, 0], in0=ra[:, hs, :], in1=rb[:, hs, :], op=ALU.subtract),
                          r=["ra", "rb"], w=["qr"])
                    P.dve(lambda e, hs=hs, sn=sn: e.tensor_tensor(out=ra[:, hs, :], in0=qv[:, hs, :, 0], in1=sn(), op=ALU.mult), r=["qn", rk], w=["ra"])
                    P.dve(lambda e, hs=hs, cs=cs: e.tensor_tensor(out=rb[:, hs, :], in0=qv[:, hs, :, 1], in1=cs(), op=ALU.mult), r=["qn", rk], w=["rb"])
                    P.dve(lambda e, hs=hs: e.tensor_tensor(out=qrv[:, hs, :, 1], in0=ra[:, hs, :], in1=rb[:, hs, :], op=ALU.add),
                          r=["ra", "rb"], w=["qr"])
                    c0_ = 4 if pf else 0
                    t_ = tp[b % 2]; tk = "tp%d" % (b % 2)
                    for k in range(c0_, 5):
                        P.pe(lambda e, k=k, t_=t_: e.transpose(out=t_[:, k, :], in_=qr[:, k * 128:(k + 1) * 128], identity=ident[:]),
                             r=["qr", "ident"], w=[tk])
                    P.dve(lambda e, t_=t_, c0_=c0_, b=b: e.tensor_copy(out=qkts[:, c0_:5, b * 128:(b + 1) * 128], in_=t_[:, c0_:5, :]),
                          r=[tk], w=["qkts"])
                c0_ = 4 if pf else 0
                P.dma("pool", lambda e, ot0=ot0, c0_=c0_: e.dma_start(out=FT1v[:, c0_:5, ot0:ot0 + 512], in_=qkts[:, c0_:5, :]), r=["qkts"])
                sc += 1
        P.emit_block()
    if dbg == 3:
        return
    _phase45(nc, P, cfg, sb, ps, G, out_phase)


def _phase45(nc, P, cfg, sb, ps, G, out_phase):
    import contextlib
    segs = cfg["segs"]
    SEG_S, SEG_PO, SEG_PF = segs
    ones_f = G["ones_f"]
    FT1, VT1, X1, YT1 = G["FT1"], G["VT1"], G["X1"], G["YT1"]
    qtab, ktab = G["qtab"], G["ktab"]
    NBS = cfg["NBS"]
    with contextlib.ExitStack() as st:
        nkbmax = max(SEG_S["nb"], SEG_PF["nb"])
        Kb = [sb(st, "Kb%d" % i, [128, nkbmax * 128], BF16) for i in range(2)]
        Vb = sb(st, "Vb", [128, nkbmax * 128], BF16)
        Qb = [sb(st, "Qb%d" % i, [128, 2, 512], BF16) for i in range(2)]
        Gb = [sb(st, "Gb%d" % i, [128, 512], BF16) for i in range(2)]
        PT = [sb(st, "PT%d" % i, [128, 2, 512], BF16) for i in range(3)]
        Mb = [sb(st, "Mb%d" % i, [128, 512], F32) for i in range(2)]
        rinv = sb(st, "rinv", [128, 512], F32)
        rinvz = sb(st, "rinvz", [128, 512], F32)
        acc = sb(st, "acc", [128, 2, 512], F32)
        on_ = sb(st, "on_", [128, 512], F32)
        o0 = sb(st, "o0", [128, 512], F32)
        od = sb(st, "od", [128, 512], F32)
        sq = sb(st, "sq", [128, 512], F32)
        ybuf = [sb(st, "ybuf%d" % i, [128, 512], BF16) for i in range(2)]
        ones_b = sb(st, "ones_b", [128, 128], BF16)
        lv = sb(st, "lv", [1, 4, 64], F32)
        pr = sb(st, "pr", [1, 2, 64], F32)
        ls = sb(st, "ls", [1, 4], F32)
        nl = sb(st, "nl", [128, 1], F32)
        gsc = sb(st, "gsc", [128, 1], F32)
        SS = [ps(st, "SS%d" % i, [128, 2, 512]) for i in range(3)]
        OTd = [ps(st, "OTd%d" % i, [128, 512]) for i in range(1)]
        LB = [ps(st, "LB%d" % i, [128, 512]) for i in range(1)]

        P.dve(lambda e: e.memset(ones_b[:], 1.0), w=["ones_b"])
        P.dve(lambda e: e.memset(rinvz[:], 0.0), w=["rinvz"])
        for i_ in range(2):
            P.dve(lambda e, i_=i_: e.memset(Qb[i_][64:128], 0.0), w=["Q%d" % i_])
        P.dma("sp", lambda e: e.dma_start(out=lv[:], in_=G["lamv"][:, :, :]), w=["lv"])
        P.dma("sp", lambda e: e.dma_start(out=gsc[:], in_=G["subln"][:, :]), w=["gsc"])
        P.dve(lambda e: e.tensor_scalar(out=gsc[:], in0=gsc[:], scalar1=(1.0 - LAM_INIT1), scalar2=None, op0=ALU.mult), r=["gsc"], w=["gsc"])
        P.dve(lambda e: e.tensor_tensor(out=pr[:, 0, :], in0=lv[:, 0, :], in1=lv[:, 1, :], op=ALU.mult), r=["lv"], w=["pr"])
        P.dve(lambda e: e.tensor_tensor(out=pr[:, 1, :], in0=lv[:, 2, :], in1=lv[:, 3, :], op=ALU.mult), r=["lv"], w=["pr"])
        P.dve(lambda e: e.tensor_reduce(out=ls[:, 0:2], in_=pr[:], axis=mybir.AxisListType.X, op=ALU.add), r=["pr"], w=["ls"])
        P.act(lambda e: e.activation(out=ls[:, 0:2], in_=ls[:, 0:2], func=AF.Exp), r=["ls"], w=["ls"])
        P.dve(lambda e: e.tensor_tensor(out=ls[:, 2:3], in0=ls[:, 1:2], in1=ls[:, 0:1], op=ALU.subtract), r=["ls"], w=["ls"])
        P.dve(lambda e: e.tensor_scalar(out=ls[:, 3:4], in0=ls[:, 2:3], scalar1=-LAM_INIT1, scalar2=None, op0=ALU.add), r=["ls"], w=["ls"])
        P.pe(lambda e: e.matmul(LB[0][:, 0:1], lhsT=ones_f[0:1, 0:128], rhs=ls[0:1, 3:4], start=True, stop=True), r=["ones_f", "ls"], w=["LB0"])
        P.dve(lambda e: e.tensor_copy(out=nl[:], in_=LB[0][:, 0:1]), r=["LB0"], w=["nl"])

        gg = 0
        job = 0
        qj = 0
        gj = 0
        NSS = 3

        def run_units(units, qk, ex, pv):
            nonlocal gg
            n = len(units)
            LA = 2
            for i in range(min(LA, n)):
                qk(units[i], gg + i)
            for i in range(LA, n):
                qk(units[i], gg + i)
                ex(units[i - LA], gg + i - LA)
                pv(units[i - LA], gg + i - LA)
            for i in range(max(0, n - LA), n):
                ex(units[i], gg + i)
                pv(units[i], gg + i)
            gg += n

        for seg, qbase, kseg in ((SEG_S, 0, SEG_S), (SEG_PO, NBS, SEG_PF)):
            nkb = kseg["nb"]
            ko = kseg["o0"] * 128
            nch = seg["nb"] // 4
            static_sign = (seg["name"] == "S")
            P.pool(lambda e: e.memset(Kb[0][64:128, :], 0.0), w=["K0"])
            for kv in range(2):
                K_ = Kb[0]
                r0 = 4 * 128 + kv * 64
                for c0 in range(0, nkb * 128, 2048):
                    c1 = min(nkb * 128, c0 + 2048)
                    P.dma("sp", lambda e, c0=c0, c1=c1, r0=r0, K_=K_, ko=ko: e.dma_start(out=K_[0:64, c0:c1], in_=FT1[r0:r0 + 64, ko + c0:ko + c1]), w=["K0"])
                Vv = Vb[:, 0:nkb * 65].rearrange("p (b f) -> p b f", f=65)
                for b0 in range(0, nkb, 16):
                    b1 = min(nkb, b0 + 16)
                    P.dma("sp", lambda e, b0=b0, b1=b1, Vv=Vv, kv=kv, ko=ko: e.dma_start(
                        out=Vv[:, b0:b1, :], in_=VT1[ko + b0 * 128:ko + b1 * 128, kv * 65:(kv + 1) * 65].rearrange("(b p) f -> p b f", p=128)),
                        w=["V"])
                for h in range(4 * kv, 4 * kv + 4):
                    for ci in range(nch):
                        tok = (seg["o0"] + 4 * ci) * 128
                        qcol = (qbase + 4 * ci) * 128
                        Q_ = Qb[qj % 2]; Qk = "Q%d" % (qj % 2); qj += 1
                        G_ = Gb[gj % 2]; Gk = "G%d" % (gj % 2); gj += 1
                        rq = (h // 2) * 128 + (h % 2) * 64
                        rg = (5 + h // 2) * 128 + (h % 2) * 64
                        P.dma("pool", lambda e, Q_=Q_, rq=rq, tok=tok: e.dma_start(out=Q_[0:64, 0, :], in_=FT1[rq:rq + 64, tok:tok + 512]), w=[Qk])
                        P.dma("pool", lambda e, G_=G_, rg=rg, tok=tok: e.dma_start(out=G_[0:64, :], in_=FT1[rg:rg + 64, tok:tok + 512]), w=[Gk])
                        O_ = OTd[0]; Ok = "OTd0"
                        L_ = LB[0]; Lk = "LB0"

                        def qk(u, gg_, K_=K_, Q_=Q_, Qk=Qk):
                            S_ = SS[gg_ % NSS]
                            for j in range(2):
                                kb = 2 * u + j
                                P.pe(lambda e, S_=S_, j=j, kb=kb, Q_=Q_, K_=K_: e.matmul(S_[:, j, :], lhsT=K_[:, kb * 128:(kb + 1) * 128],
                                                                                    rhs=Q_[:, 0, :], start=True, stop=True),
                                     r=["K0", Qk], w=["SS%d" % (gg_ % NSS)])

                        def ex(u, gg_):
                            S_ = SS[gg_ % NSS]; p_ = PT[gg_ % 3]
                            P.act(lambda e, S_=S_, p_=p_: e.activation(out=p_[:], in_=S_[:], func=AF.Exp),
                                  r=["SS%d" % (gg_ % NSS)], w=["PT%d" % (gg_ % 3)])

                        def pv(u, gg_, O_=O_, Ok=Ok, Vv=Vv, nkb=nkb):
                            p_ = PT[gg_ % 3]
                            for j in range(2):
                                kb = 2 * u + j
                                P.pe(lambda e, p_=p_, j=j, kb=kb, O_=O_, Vv=Vv, nkb=nkb: e.matmul(O_[0:65, :], lhsT=Vv[:, kb, :], rhs=p_[:, j, :],
                                                                                             start=(kb == 0), stop=(kb == nkb - 1)),
                                     r=["V", "PT%d" % (gg_ % 3)], w=[Ok])
                        run_units(list(range(nkb // 2)), qk, ex, pv)
                        y_ = ybuf[job % 2]; yk = "ybuf%d" % (job % 2)
                        P.dve(lambda e, O_=O_: e.reciprocal(out=rinvz[64:65, :], in_=O_[64:65, :]), r=[Ok], w=["rinvz"])
                        P.dve(lambda e, O_=O_, G_=G_: e.tensor_tensor(out=on_[0:64, :], in0=O_[0:64, :], in1=G_[0:64, :], op=ALU.mult),
                              r=[Ok, Gk], w=["on_"])
                        P.pe(lambda e, L_=L_: e.matmul(L_[0:64, :], lhsT=ones_f[:, 0:64], rhs=rinvz[:, :], start=True, stop=True),
                             r=["ones_f", "rinvz"], w=[Lk])
                        P.dve(lambda e, L_=L_, y_=y_: e.tensor_tensor(out=y_[0:64, :], in0=on_[0:64, :], in1=L_[0:64, :], op=ALU.mult),
                              r=["on_", Lk], w=[yk])
                        P.dma("pool", lambda e, y_=y_, h=h, qcol=qcol: e.dma_start(out=YT1[h * 64:(h + 1) * 64, qcol:qcol + 512], in_=y_[0:64, :]), r=[yk])
                        job += 1
            for h in range(4):
                for m in range(2):
                    K_ = Kb[m]
                    r0 = (13 + h) * 128 + m * 64
                    for c0 in range(0, nkb * 128, 2048):
                        c1 = min(nkb * 128, c0 + 2048)
                        P.dma("sp", lambda e, K_=K_, c0=c0, c1=c1, r0=r0, ko=ko: e.dma_start(out=K_[0:64, c0:c1], in_=FT1[r0:r0 + 64, ko + c0:ko + c1]),
                              w=["K%d" % m])
                    P.dma("sp", lambda e, K_=K_, h=h, nkb=nkb, ko=ko: e.dma_start(out=K_[64:68, 0:nkb * 128], in_=ktab[h, :, ko:ko + nkb * 128]), w=["K%d" % m])
                Vv = Vb[:, 0:nkb * 128].rearrange("p (b f) -> p b f", f=128)
                for b0 in range(0, nkb, 16):
                    b1 = min(nkb, b0 + 16)
                    P.dma("sp", lambda e, b0=b0, b1=b1, Vv=Vv, h=h, ko=ko: e.dma_start(
                        out=Vv[:, b0:b1, :], in_=VT1[ko + b0 * 128:ko + b1 * 128, 130 + h * 128:130 + (h + 1) * 128].rearrange("(b p) f -> p b f", p=128)),
                        w=["V"])
                for ci in range(nch):
                    tok = (seg["o0"] + 4 * ci) * 128
                    qcol = (qbase + 4 * ci) * 128
                    G_ = Gb[gj % 2]; Gk = "G%d" % (gj % 2); gj += 1
                    rg = (17 + h) * 128
                    P.dma("pool", lambda e, G_=G_, rg=rg, tok=tok: e.dma_start(out=G_[:, :], in_=FT1[rg:rg + 128, tok:tok + 512]), w=[Gk])
                    if static_sign:
                        units = []
                        for g in range(nkb // 2):
                            kb0 = 2 * g
                            if kb0 + 1 < 4 * ci:
                                units.append(("far", 0, kb0))
                            elif kb0 > 4 * ci + 3:
                                units.append(("far", 1, kb0))
                            else:
                                units.append(("near", kb0))
                                units.append(("near", kb0 + 1))
                    else:
                        units = [("near", kb) for kb in range(nkb)]
                    for m in range(2):
                        K_ = Kb[m]; Kk = "K%d" % m
                        Q_ = Qb[qj % 2]; Qk = "Q%d" % (qj % 2); qj += 1
                        rq = (9 + h) * 128 + m * 64
                        for v in range(2):
                            P.dma("pool", lambda e, Q_=Q_, rq=rq, tok=tok, v=v: e.dma_start(out=Q_[0:64, v, :], in_=FT1[rq:rq + 64, tok:tok + 512]), w=[Qk])
                            P.dma("pool", lambda e, Q_=Q_, h=h, v=v, qcol=qcol: e.dma_start(out=Q_[64:68, v, :], in_=qtab[h, v, :, qcol:qcol + 512]), w=[Qk])
                        O_ = OTd[0]; Ok = "OTd0"
                        L_ = LB[0]; Lk = "LB0"

                        def qk(u, gg_, K_=K_, Kk=Kk, Q_=Q_, Qk=Qk):
                            S_ = SS[gg_ % NSS]
                            if u[0] == "far":
                                lst = [(j, u[1], u[2] + j) for j in range(2)]
                            else:
                                lst = [(v, v, u[1]) for v in range(2)]
                            for (slot, v, kb) in lst:
                                P.pe(lambda e, S_=S_, slot=slot, v=v, kb=kb, Q_=Q_, K_=K_: e.matmul(
                                    S_[:, slot, :], lhsT=K_[0:68, kb * 128:(kb + 1) * 128], rhs=Q_[0:68, v, :], start=True, stop=True),
                                    r=[Kk, Qk], w=["SS%d" % (gg_ % NSS)])

                        def ex(u, gg_):
                            S_ = SS[gg_ % NSS]; p_ = PT[gg_ % 3]; M_ = Mb[gg_ % 2]
                            if u[0] == "far":
                                P.act(lambda e, S_=S_, p_=p_: e.activation(out=p_[:], in_=S_[:], func=AF.Exp),
                                      r=["SS%d" % (gg_ % NSS)], w=["PT%d" % (gg_ % 3)])
                            else:
                                if gg_ % 2 == 0:
                                    P.act(lambda e, S_=S_, M_=M_: e.copy(out=M_[:], in_=S_[:, 0, :]), r=["SS%d" % (gg_ % NSS)], w=["Mb%d" % (gg_ % 2)])
                                else:
                                    P.dve(lambda e, S_=S_, M_=M_: e.tensor_copy(out=M_[:], in_=S_[:, 0, :]), r=["SS%d" % (gg_ % NSS)], w=["Mb%d" % (gg_ % 2)])
                                P.dve(lambda e, S_=S_, M_=M_: e.tensor_tensor(out=M_[:], in0=M_[:], in1=S_[:, 1, :], op=ALU.min),
                                      r=["SS%d" % (gg_ % NSS), "Mb%d" % (gg_ % 2)], w=["Mb%d" % (gg_ % 2)])
                                P.act(lambda e, M_=M_, p_=p_: e.activation(out=p_[:, 0, :], in_=M_[:], func=AF.Exp),
                                      r=["Mb%d" % (gg_ % 2)], w=["PT%d" % (gg_ % 3)])

                        def pv(u, gg_, O_=O_, L_=L_, Ok=Ok, Lk=Lk, Vv=Vv, nkb=nkb):
                            p_ = PT[gg_ % 3]
                            lst = [(j, u[2] + j) for j in range(2)] if u[0] == "far" else [(0, u[1])]
                            for (slot, kb) in lst:
                                P.pe(lambda e, p_=p_, kb=kb, slot=slot, O_=O_, Vv=Vv, nkb=nkb: e.matmul(
                                    O_[:, :], lhsT=Vv[:, kb, :], rhs=p_[:, slot, :], start=(kb == 0), stop=(kb == nkb - 1)),
                                    r=["V", "PT%d" % (gg_ % 3)], w=[Ok])
                            accop = P.dve if static_sign else P.pool
                            if u[0] == "far":
                                accop(lambda e, p_=p_: e.tensor_tensor(out=acc[:], in0=acc[:], in1=p_[:], op=ALU.add),
                                      r=["acc", "PT%d" % (gg_ % 3)], w=["acc"])
                            else:
                                accop(lambda e, p_=p_: e.tensor_tensor(out=acc[:, 0, :], in0=acc[:, 0, :], in1=p_[:, 0, :], op=ALU.add),
                                      r=["acc", "PT%d" % (gg_ % 3)], w=["acc"])
                        (P.dve if static_sign else P.pool)(lambda e: e.memset(acc[:], 0.0), w=["acc"])
                        run_units(units, qk, ex, pv)
                        for j_ in range(2):
                            P.pe(lambda e, L_=L_, j_=j_: e.matmul(L_[:, :], lhsT=ones_f[:, :], rhs=acc[:, j_, :], start=(j_ == 0), stop=(j_ == 1)),
                                 r=["ones_f", "acc"], w=[Lk])
                        P.dve(lambda e, L_=L_: e.reciprocal(out=rinv[:], in_=L_[:]), r=[Lk], w=["rinv"])
                        if m == 0:
                            P.dve(lambda e, O_=O_: e.tensor_tensor(out=o0[:], in0=O_[:], in1=rinv[:], op=ALU.mult), r=[Ok, "rinv"], w=["o0"])
                        else:
                            y_ = ybuf[job % 2]; yk = "ybuf%d" % (job % 2)
                            P.dve(lambda e, O_=O_: e.tensor_tensor(out=on_[:], in0=O_[:], in1=rinv[:], op=ALU.mult), r=[Ok, "rinv"], w=["on_"])
                            P.dve(lambda e: e.scalar_tensor_tensor(out=od[:], in0=on_[:], scalar=nl[:, 0:1], in1=o0[:], op0=ALU.mult, op1=ALU.add),
                                  r=["on_", "nl", "o0"], w=["od"])
                            P.dve(lambda e: e.tensor_tensor(out=sq[:], in0=od[:], in1=od[:], op=ALU.mult), r=["od"], w=["sq"])
                            P.pe(lambda e, L_=L_: e.matmul(L_[:, :], lhsT=ones_f[:, :], rhs=sq[:], start=True, stop=True), r=["ones_f", "sq"], w=[Lk])
                            P.dve(lambda e, L_=L_: e.tensor_scalar(out=sq[:], in0=L_[:], scalar1=1.0 / 128, scalar2=EPS, op0=ALU.mult, op1=ALU.add),
                                  r=[Lk], w=["sq"])
                            P.act(lambda e: e.activation(out=sq[:], in_=sq[:], func=AF.Sqrt), r=["sq"], w=["sq"])
                            P.dve(lambda e: e.reciprocal(out=sq[:], in_=sq[:]), r=["sq"], w=["sq"])
                            P.dve(lambda e: e.tensor_tensor(out=od[:], in0=od[:], in1=sq[:], op=ALU.mult), r=["od", "sq"], w=["od"])
                            P.dve(lambda e, y_=y_, G_=G_: e.scalar_tensor_tensor(out=y_[:], in0=od[:], scalar=gsc[:, 0:1], in1=G_[:], op0=ALU.mult, op1=ALU.mult),
                                  r=["od", "gsc", Gk], w=[yk])
                            P.dma("pool", lambda e, y_=y_, h=h, qcol=qcol: e.dma_start(
                                out=YT1[512 + h * 128:512 + (h + 1) * 128, qcol:qcol + 512], in_=y_[:]), r=[yk])
                        job += 1
        P.emit_block()

    yout = G["yout"]

    def qidx(seg, i):
        return (0 if seg["name"] == "S" else NBS) + i
    out_phase(1, YT1,
              lambda seg, i: X1[(seg["o0"] + i) * 128:(seg["o0"] + i + 1) * 128, :],
              lambda seg, i: yout[qidx(seg, i) * 128:(qidx(seg, i) + 1) * 128, :],
              [SEG_S, SEG_PO], G["w_out_cd"], lambda seg, i: qidx(seg, i) * 128)


def host_constants(cfg):
    c = {}
    c["eye"] = np.eye(128, dtype=np.float32)
    colmask, dc = nb_static()
    c["colmask"] = colmask
    k = np.arange(128)[:, None]
    q = np.arange(128)[None, :]
    sl8 = alibi_slopes(8)
    al = np.zeros((128, 3, 8, 128), np.float32)
    for r in range(3):
        dist = np.abs(128 * (r - 1) + k - q).astype(np.float32)
        for h in range(8):
            al[:, r, h, :] = np.where(dist <= 128, -sl8[h] * dist, NEG)
    c["alibia"] = al.astype(NPBF)
    sl4 = alibi_slopes(4)
    dg = np.zeros((128, 4, 128), np.float32)
    for h in range(4):
        dg[:, h, :] = -sl4[h] * np.abs(k - q)
    c["diagb"] = dg.astype(NPBF)
    return c


def seg_positions(cfg, core):
    nbs, nbpo, nbpf = cfg["NBS"], cfg["NBPO"], cfg["NBPF"]
    ps_ = np.arange(nbs * 128)
    ppo = core * nbpo * 128 + np.arange(nbpo * 128)
    ppf = np.arange(nbpf * 128)
    return ps_, ppo, ppf


def prepare_core(cfg, core, inp, consts):
    nbs, nbpo, nbpf = cfg["NBS"], cfg["NBPO"], cfg["NBPF"]
    SS, SP = nbs * 128, nbpf * 128
    pad = PAD * 128
    m = dict(consts)
    xs = inp["x_sample"][core]
    xp = inp["x_prompt"][0]
    z = np.zeros((pad, D), np.float32)
    xpp = np.concatenate([z, xp, z], axis=0)
    lo = core * nbpo * 128
    xin = np.concatenate([z, xs, z, xpp[lo:lo + nbpo * 128 + 2 * pad], xpp], axis=0)
    m["xin"] = np.ascontiguousarray(xin)
    vs = np.concatenate([np.zeros(pad), np.ones(SS), np.zeros(pad)])
    vpf = np.concatenate([np.zeros(pad), np.ones(SP), np.zeros(pad)])
    valid = np.concatenate([vs, vpf[lo:lo + nbpo * 128 + 2 * pad], vpf]).astype(np.float32)
    m["validin"] = np.ascontiguousarray(valid.reshape(-1, 128).T)
    pp = inp["p_prompt"][:, 0]
    m["pin"] = np.ascontiguousarray(np.concatenate(
        [inp["p_sample"][:, core], pp[:, lo:lo + nbpo * 128], pp], axis=1))
    m["gvec"] = np.ascontiguousarray(np.stack([inp["norm_pre"][0], inp["norm_post"][0], inp["norm_pre"][1], inp["norm_post"][1]]))
    m["w_in_ab"] = inp["w_in_ab"][0]; m["w_out_ab"] = inp["w_out_ab"][0]
    m["w_in_cd"] = inp["w_in_cd"][0]; m["w_out_cd"] = inp["w_out_cd"][0]
    m["w_ple"] = inp["w_ple"]; m["w_gate"] = inp["w_ple_gate"]
    m["a_sink"] = inp["a_sink"]
    rpb = inp["b_rpb"][0]
    kr = np.arange(2)[:, None, None, None]; kc = np.arange(64)[None, :, None, None]
    qr = np.arange(2)[None, None, :, None]; qc = np.arange(64)[None, None, None, :]
    dcx = np.broadcast_to(np.clip(kc - qc, -15, 15) + 15, (2, 64, 2, 64)).reshape(128, 128)
    g = np.zeros((128, 7, 8, 128), np.float32)
    for r in range(7):
        drx = np.broadcast_to(np.clip(2 * (r - 3) + kr - qr + 7, 0, 14), (2, 64, 2, 64)).reshape(128, 128)
        for h in range(8):
            g[:, r, h, :] = rpb[h][drx, dcx]
    m["rpbg"] = g
    rm = np.zeros((128, 2, 5, 7, 128), np.float32)
    rows_any = 64
    nblk_any = rows_any // 2
    reps = {0: 0, 1: 1, 2: nblk_any // 2, 3: nblk_any - 2, 4: nblk_any - 1}
    rows_p = nbpf * 2
    for cl in range(5):
        for r in range(7):
            n = reps[cl]
            rm[:, 0, cl, r, :] = nb_rowmask_tile(rows_any, n, n + r - 3)
    seg_po = cfg["segs"][1]
    done = set()
    for i in range(nbpo):
        cl = blk_class(seg_po, i)
        if cl in done:
            continue
        done.add(cl)
        n = core * nbpo + i
        for r in range(7):
            rm[:, 1, cl, r, :] = nb_rowmask_tile(rows_p, n, n + r - 3)
    for cl in range(5):
        if cl not in done:
            rm[:, 1, cl] = NEG
    m["rowmask"] = rm.astype(NPBF)
    m["qkgain"] = np.ascontiguousarray(np.concatenate([np.tile(inp["c_q_norm"][0], 8), np.tile(inp["c_k_norm"][0], 2)])[None, :])
    ps_, ppo, ppf = seg_positions(cfg, core)
    pos = np.concatenate([ps_, ppo, ppf])
    inv = (10000.0 ** (-2.0 * np.arange(16) / 32)).astype(np.float32)
    row = (pos // 64).astype(np.float32); col = (pos % 64).astype(np.float32)
    ang = np.concatenate([row[:, None] * inv, col[:, None] * inv], axis=-1).astype(np.float32)
    m["rope"] = np.ascontiguousarray(np.concatenate([np.cos(ang), np.sin(ang)], axis=-1).astype(np.float32))
    m["lamv"] = np.ascontiguousarray(np.stack([inp["d_lambda_q1"][0], inp["d_lambda_k1"][0], inp["d_lambda_q2"][0], inp["d_lambda_k2"][0]])[None])
    m["subln"] = np.ascontiguousarray(inp["d_subln"][0][:, None])
    sl4 = alibi_slopes(4)
    posq = np.concatenate([ps_, ppo]).astype(np.float32)
    qa_, qb_ = np.floor(posq / 128), np.mod(posq, 128)
    qt = np.zeros((4, 2, 4, posq.size), np.float32)
    kt = np.zeros((4, 4, pos.size), np.float32)
    ka_, kb_ = np.floor(pos / 128).astype(np.float32), np.mod(pos, 128).astype(np.float32)
    for h in range(4):
        s = sl4[h]
        qt[h, 0] = np.stack([-s * 128 * qa_, -s * qb_, np.ones_like(qa_), np.ones_like(qa_)])
        qt[h, 1] = np.stack([s * 128 * qa_, s * qb_, -np.ones_like(qa_), -np.ones_like(qa_)])
        kt[h] = np.stack([np.ones_like(ka_), np.ones_like(ka_), s * 128 * ka_, s * kb_])
    m["qtab"] = qt.astype(NPBF)
    m["ktab"] = kt.astype(NPBF)
    return m


_CACHE = {}


def kernel(**inputs):
    inp = {k: np.asarray(v) for k, v in inputs.items()}
    nbs = inp["x_sample"].shape[1] // 128
    nbpf = inp["x_prompt"].shape[1] // 128
    cfg = cfg_make(nbs, nbpf)
    key = (nbs, nbpf)
    if key not in _CACHE:
        _CACHE[key] = build(cfg)
    nc = _CACHE[key]
    consts = host_constants(cfg)
    maps = [prepare_core(cfg, c, inp, consts) for c in range(NCORE)]
    res = run_bass_kernel_spmd(nc, maps, core_ids=list(range(NCORE)))
    nbpo = cfg["NBPO"]
    ys = np.zeros((NCORE, nbs * 128, D), np.float32)
    yp = np.zeros((1, nbpf * 128, D), np.float32)
    for c in range(NCORE):
        y = np.asarray(res.results[c]["yout"])
        ys[c] = y[:nbs * 128]
        yp[0, c * nbpo * 128:(c + 1) * nbpo * 128] = y[nbs * 128:]
    return (yp, ys)
```

```python
import math
import numpy as np
import ml_dtypes
import concourse.bass as bass
import concourse.mybir as mybir
from concourse.bass_utils import run_bass_kernel_spmd

F32 = mybir.dt.float32
BF16 = mybir.dt.bfloat16
AF = mybir.ActivationFunctionType
ALU = mybir.AluOpType
NPBF = ml_dtypes.bfloat16

NCORE = 8
D = 1024
PAD = 4
EPS = 1e-6
NEG = -30000.0
NSEM_DMA = 8


class Prog:
    ENG = ("pe", "act", "dve", "pool", "sp")

    def __init__(self, nc, sems):
        self.nc = nc
        self.sems = sems
        self.sigcount = {e: 0 for e in self.ENG}
        self.dmacount = {"sp": 0, "pool": 0}
        self.waited = {}
        self.reset()

    def reset(self):
        self.ops = []
        self.lw = {}
        self.rd = {}

    def op(self, eng, fn, r=(), w=(), dma=False):
        idx = len(self.ops)
        deps = set()
        for b in r:
            x = self.lw.get(b)
            if x is not None:
                deps.add(x)
        for b in w:
            x = self.lw.get(b)
            if x is not None:
                deps.add(x)
            deps.update(self.rd.get(b, ()))
        for b in r:
            self.rd.setdefault(b, []).append(idx)
        for b in w:
            self.lw[b] = idx
            self.rd[b] = []
        self.ops.append(dict(eng=eng, fn=fn, deps=deps, dma=dma, need=False))
        return idx

    def pe(self, fn, r=(), w=()): return self.op("pe", fn, r, w)
    def act(self, fn, r=(), w=()): return self.op("act", fn, r, w)
    def dve(self, fn, r=(), w=()): return self.op("dve", fn, r, w)
    def pool(self, fn, r=(), w=()): return self.op("pool", fn, r, w)
    def dma(self, q, fn, r=(), w=()): return self.op(q, fn, r, w, dma=True)

    def emit_block(self, name=None):
        nc = self.nc
        ops = self.ops
        for o in ops:
            for d in o["deps"]:
                p = ops[d]
                if p["eng"] == o["eng"] and o["eng"] == "pe" and not p["dma"]:
                    continue
                p["need"] = True
        for o in ops:
            e = o["eng"]
            if o["dma"]:
                k = self.dmacount[e]
                self.dmacount[e] = k + 1
                o["sig"] = ((e, k % NSEM_DMA), 16 * (k // NSEM_DMA + 1))
                o["pre"] = ((e, k % NSEM_DMA), 16 * (k // NSEM_DMA)) if k >= NSEM_DMA else None
            elif o["need"]:
                self.sigcount[e] += 1
                o["sig"] = (e, self.sigcount[e])
                o["pre"] = None
            else:
                o["sig"] = None
                o["pre"] = None
        per = {e: [] for e in self.ENG}
        for o in ops:
            e = o["eng"]
            waits = []
            cand = {}
            for d in o["deps"]:
                p = ops[d]
                if p["eng"] == e and e == "pe" and not p["dma"]:
                    continue
                s, v = p["sig"]
                cand[s] = max(cand.get(s, 0), v)
            if o["pre"] is not None:
                s, v = o["pre"]
                cand[s] = max(cand.get(s, 0), v)
            for s, v in cand.items():
                if self.waited.get((e, s), 0) >= v:
                    continue
                self.waited[(e, s)] = v
                waits.append((s, v))
            per[e].append((o, waits))
        tails = {}
        for q in ("sp", "pool"):
            k = self.dmacount[q]
            tl = []
            for i in range(NSEM_DMA):
                n = (k - i + NSEM_DMA - 1) // NSEM_DMA if k > i else 0
                if n > 0 and self.waited.get((q, (q, i)), 0) < 16 * n:
                    self.waited[(q, (q, i))] = 16 * n
                    tl.append(((q, i), 16 * n))
            tails[q] = tl
        sems = self.sems

        def run(eh, e):
            for o, waits in per[e]:
                for s, v in waits:
                    eh.wait_ge(sems[s], v)
                ins = o["fn"](eh)
                if o["sig"] is not None:
                    s, v = o["sig"]
                    ins.then_inc(sems[s], 16 if o["dma"] else 1)
            for s, v in tails.get(e, ()):
                eh.wait_ge(sems[s], v)

        with nc.Block() as block:
            @block.tensor
            def _(eh): run(eh, "pe")

            @block.scalar
            def _(eh): run(eh, "act")

            @block.vector
            def _(eh): run(eh, "dve")

            @block.gpsimd
            def _(eh): run(eh, "pool")

            @block.sync
            def _(eh): run(eh, "sp")
        self.reset()


def alibi_slopes(n):
    return (2.0 ** (-8.0 * np.arange(1, n + 1) / n)).astype(np.float32)


def nb_rowmask_tile(rows, n, kb):
    m = np.full((2, 64, 2, 64), NEG, np.float32)
    nblk = rows // 2
    if n < 0 or n >= nblk or kb < 0 or kb >= nblk:
        return m.reshape(128, 128)
    for qr in range(2):
        r = 2 * n + qr
        rs = min(max(r - 4, 0), rows - 8)
        for kr in range(2):
            kk = 2 * kb + kr
            if rs <= kk < rs + 8:
                m[kr, :, qr, :] = 0.0
    return m.reshape(128, 128)


def nb_static():
    kc = np.arange(64)[:, None]
    qc = np.arange(64)[None, :]
    ws = np.clip(qc - 8, 0, 48)
    colok = (kc >= ws) & (kc < ws + 16)
    colmask = np.where(colok, 0.0, NEG).astype(np.float32)
    colmask = np.broadcast_to(colmask[None, :, None, :], (2, 64, 2, 64)).reshape(128, 128)
    dc = np.clip(kc - qc, -15, 15) + 15
    return colmask, dc


def cfg_make(nbs, nbpf):
    c = dict(NBS=nbs, NBPF=nbpf, NBPO=nbpf // NCORE)
    segs = []
    e0 = 0
    o0 = 0
    for name, nb in (("S", nbs), ("PO", nbpf // NCORE), ("PF", nbpf)):
        segs.append(dict(name=name, nb=nb, e0=e0, o0=o0, ne=nb + 2 * PAD))
        e0 += nb + 2 * PAD
        o0 += nb
    c["segs"] = segs
    c["NE"] = e0
    c["NO"] = o0
    c["NQ"] = nbs + nbpf // NCORE
    return c


def blk_class(seg, i):
    nb = seg["nb"]
    if i == 0: return 0
    if i == 1: return 1
    if i == nb - 2: return 3
    if i == nb - 1: return 4
    return 2


def blk_rels(seg, i):
    cl = blk_class(seg, i)
    if seg["name"] == "PO":
        return {0: list(range(-2, 4)), 1: list(range(-2, 3)), 2: list(range(-2, 3)),
                3: list(range(-2, 3)), 4: list(range(-3, 3))}[cl]
    return {0: [0, 1, 2, 3], 1: [-1, 0, 1, 2], 2: [-2, -1, 0, 1, 2], 3: [-2, -1, 0, 1], 4: [-3, -2, -1, 0]}[cl]


AB_FM = [("qa", 0, 4, "q"), ("ka", 512, 1, "k"), ("ga", 768, 4, "g"),
         ("qb", 1280, 4, "q"), ("kb", 1792, 4, "k"), ("gb", 2816, 4, "g")]
CD_FM = [("gc", 768, 4, "g"), ("qd", 1280, 4, "q"), ("kd", 1792, 4, "k"), ("gd", 2816, 4, "g")]


def build(cfg, dbg=0):
    nc = bass.Bass("TRN2", target_bir_lowering=False)
    NE, NO, NQ = cfg["NE"], cfg["NO"], cfg["NQ"]
    segs = cfg["segs"]
    TE, TO, TQ = NE * 128, NO * 128, NQ * 128

    def din(name, shape, dt=F32):
        return nc.dram_tensor(name, list(shape), dt, kind="ExternalInput").ap()

    def dscr(name, shape, dt):
        kind = "ExternalOutput" if (dbg and name in ("YT0", "FT0", "VT0", "X1", "FT1", "VT1", "YT1")) else "Internal"
        return nc.dram_tensor(name, list(shape), dt, kind=kind).ap()

    xin = din("xin", [TE, D])
    validin = din("validin", [128, NE])
    pin = din("pin", [2, TO, 256])
    gvec = din("gvec", [4, D])
    w_in_ab = din("w_in_ab", [D, 3328]); w_out_ab = din("w_out_ab", [D, D])
    w_in_cd = din("w_in_cd", [D, 3328]); w_out_cd = din("w_out_cd", [D, D])
    w_ple = din("w_ple", [2, 256, D]); w_gate = din("w_gate", [2, D, D])
    a_sink = din("a_sink", [1, 8])
    rpbg = din("rpbg", [128, 7, 8, 128])
    colmask = din("colmask", [128, 128])
    alibia = din("alibia", [128, 3, 8, 128], BF16)
    rowmask = din("rowmask", [128, 2, 5, 7, 128], BF16)
    qkgain = din("qkgain", [1, 640])
    rope = din("rope", [TO, 64])
    lamv = din("lamv", [1, 4, 64])
    subln = din("subln", [128, 1])
    qtab = din("qtab", [4, 2, 4, TQ], BF16)
    ktab = din("ktab", [4, 4, TO], BF16)
    diagb = din("diagb", [128, 4, 128], BF16)
    yout = nc.dram_tensor("yout", [TQ, D], F32, kind="ExternalOutput").ap()

    FT0 = dscr("FT0", [21 * 128, TE], BF16)
    VT0 = dscr("VT0", [TE, 650], BF16)
    YT0 = dscr("YT0", [D, TO], BF16)
    X1 = dscr("X1", [TO, D], F32)
    FT1 = dscr("FT1", [21 * 128, TO], BF16)
    VT1 = dscr("VT1", [TO, 642], BF16)
    YT1 = dscr("YT1", [D, TQ], BF16)

    import contextlib
    es = contextlib.ExitStack()
    with es:
        sems = {}
        for e in Prog.ENG:
            sems[e] = es.enter_context(nc.semaphore("s_" + e))
        for q in ("sp", "pool"):
            for i in range(NSEM_DMA):
                sems[(q, i)] = es.enter_context(nc.semaphore("d_%s%d" % (q, i)))
        P = Prog(nc, sems)

        ucnt = [0]

        def sb(stack, name, shape, dt):
            ucnt[0] += 1
            return stack.enter_context(nc.sbuf_tensor("%s_u%d" % (name, ucnt[0]), list(shape), dt))

        def ps(stack, name, shape, dt=F32):
            ucnt[0] += 1
            return stack.enter_context(nc.psum_tensor("%s_u%d" % (name, ucnt[0]), list(shape), dt))

        ident = sb(es, "ident", [128, 128], BF16)
        identf = sb(es, "identf", [128, 128], F32)
        ones_f = sb(es, "ones_f", [128, 128], F32)
        zeros_b = sb(es, "zeros_b", [128, 512], BF16)
        gbc = sb(es, "gbc", [128, 4, D], F32)
        valid_sb = sb(es, "valid_sb", [128, NE], F32)
        ones10 = sb(es, "ones10", [128, 10], F32)

        eye = din("eye", [128, 128])
        P.dma("sp", lambda e: e.dma_start(out=identf[:], in_=eye[:, :]), w=["identf"])
        P.dve(lambda e: e.tensor_copy(out=ident[:], in_=identf[:]), r=["identf"], w=["ident"])
        P.dve(lambda e: e.memset(ones_f[:], 1.0), w=["ones_f"])
        P.dve(lambda e: e.memset(zeros_b[:], 0.0), w=["zeros_b"])
        P.dve(lambda e: e.memset(ones10[:], 1.0), w=["ones10"])
        P.dma("sp", lambda e: e.dma_start(out=valid_sb[:], in_=validin[:, :]), w=["valid"])
        for i in range(4):
            P.dma("sp", lambda e, i=i: e.dma_start(out=gbc[:, i, :], in_=gvec[i:i + 1, :].partition_broadcast(128)),
                  w=["gbc"])
        P.emit_block()

        def load_weight(stack_bufs, wdst, wsrc, kch, ncols, key, colchunk=1664):
            stg = stack_bufs
            j = 0
            for k in range(kch):
                for c0 in range(0, ncols, colchunk):
                    c1 = min(ncols, c0 + colchunk)
                    s = stg[j % 2]
                    sk = "wstg%d" % (j % 2)
                    P.dma("sp", lambda e, s=s, k=k, c0=c0, c1=c1: e.dma_start(
                        out=s[:, 0:c1 - c0], in_=wsrc[k * 128:(k + 1) * 128, c0:c1]), w=[sk])
                    if j % 2 == 0:
                        P.dve(lambda e, s=s, k=k, c0=c0, c1=c1: e.tensor_copy(out=wdst[:, k, c0:c1], in_=s[:, 0:c1 - c0]),
                              r=[sk], w=[key])
                    else:
                        P.pool(lambda e, s=s, k=k, c0=c0, c1=c1: e.tensor_copy(out=wdst[:, k, c0:c1], in_=s[:, 0:c1 - c0]),
                               r=[sk], w=[key])
                    j += 1

        def norm_block(xt, xk, gi, hn, hnk, ss, rstd, junk, tagk):
            P.act(lambda e: e.activation(out=junk[:], in_=xt, func=AF.Square, scale=1.0 / math.sqrt(D), accum_out=ss[:, 0:1]),
                  r=[xk], w=["junk", "ss" + tagk])
            P.dve(lambda e: e.tensor_scalar(out=rstd[:, 0:1], in0=ss[:, 0:1], scalar1=EPS, scalar2=None,
                                            op0=ALU.add), r=["ss" + tagk], w=["rstd" + tagk])
            P.act(lambda e: e.activation(out=rstd[:, 0:1], in_=rstd[:, 0:1], func=AF.Sqrt), r=["rstd" + tagk], w=["rstd" + tagk])
            P.dve(lambda e: e.reciprocal(out=rstd[:, 0:1], in_=rstd[:, 0:1]), r=["rstd" + tagk], w=["rstd" + tagk])
            P.dve(lambda e: e.scalar_tensor_tensor(out=hn, in0=xt, scalar=rstd[:, 0:1], in1=gbc[:, gi, :],
                                                   op0=ALU.mult, op1=ALU.mult),
                  r=[xk, "rstd" + tagk, "gbc"], w=[hnk])

        def transpose_to(src, srck, nk, tp, tpk, dst, dstk, use_act):
            for k in range(nk):
                P.pe(lambda e, k=k: e.transpose(out=tp[:, k, :], in_=src[:, k * 128:(k + 1) * 128], identity=ident[:]),
                     r=[srck, "ident"], w=[tpk])
            if use_act:
                P.act(lambda e: e.copy(out=dst, in_=tp[:, 0:nk, :]), r=[tpk], w=[dstk])
            else:
                P.dve(lambda e: e.tensor_copy(out=dst, in_=tp[:, 0:nk, :]), r=[tpk], w=[dstk])

        with contextlib.ExitStack() as st:
            w_in = sb(st, "w_in", [128, 8, 3328], BF16)
            wstg = [sb(st, "wstg0", [128, 1664], F32), sb(st, "wstg1", [128, 1664], F32)]
            xt = [sb(st, "xt%d" % i, [128, D], F32) for i in range(2)]
            hn = [sb(st, "hn%d" % i, [128, D], BF16) for i in range(2)]
            hnT = [sb(st, "hnT%d" % i, [128, 8, 512], BF16) for i in range(2)]
            junk = sb(st, "junk", [128, D], BF16)
            ss = sb(st, "ss", [128, 2], F32)
            rstd = sb(st, "rstd", [128, 2], F32)
            fstage = [sb(st, "fstage%d" % i, [128, 21, 512], BF16) for i in range(2)]
            vstage = [sb(st, "vstage%d" % i, [128, 10, 65], BF16) for i in range(2)]
            tp = [ps(st, "tp%d" % i, [128, 8, 128], BF16) for i in range(2)]
            fm = [ps(st, "fm%d" % i, [128, 512]) for i in range(2)]
            tmA = [ps(st, "tmA%d" % i, [128, 512]) for i in range(1)]
            tmB = [ps(st, "tmB%d" % i, [128, 128]) for i in range(1)]

            load_weight(wstg, w_in, w_in_ab, 8, 3328, "w_in")
            nst = NE // 4
            bc = 0
            for sti in range(nst):
                hT = hnT[sti % 2]
                hTk = "hnT%d" % (sti % 2)
                for b in range(4):
                    eb = sti * 4 + b
                    x_ = xt[bc % 2]
                    xk = "xt%d" % (bc % 2)
                    h_ = hn[bc % 2]
                    hk = "hn%d" % (bc % 2)
                    P.dma("sp", lambda e, x_=x_, eb=eb: e.dma_start(out=x_[:], in_=xin[eb * 128:(eb + 1) * 128, :]), w=[xk])
                    sx = ss[:, bc % 2:bc % 2 + 1]
                    rx = rstd[:, bc % 2:bc % 2 + 1]
                    norm_block(x_[:], xk, 0, h_[:], hk, sx, rx, junk, str(bc % 2))
                    t_ = tp[bc % 2]
                    transpose_to(h_, hk, 8, t_, "tp%d" % (bc % 2), hT[:, :, b * 128:(b + 1) * 128], hTk, bc % 2 == 0)
                    bc += 1
                fs = fstage[sti % 2]
                fsk = "fstage%d" % (sti % 2)
                ci = 0
                for (nm, c0, nch, kind) in AB_FM:
                    for j in range(nch):
                        f0 = c0 + j * 128
                        pf = fm[ci % 2]
                        pk = "fm%d" % (ci % 2)
                        for k in range(8):
                            P.pe(lambda e, pf=pf, k=k, f0=f0, hT=hT: e.matmul(pf[:], lhsT=w_in[:, k, f0:f0 + 128], rhs=hT[:, k, :],
                                                                         start=(k == 0), stop=(k == 7)),
                                 r=["w_in", hTk], w=[pk])
                        if kind == "g":
                            P.act(lambda e, pf=pf, ci=ci, fs=fs: e.activation(out=fs[:, ci, :], in_=pf[:], func=AF.Silu),
                                  r=[pk], w=[fsk])
                        elif kind == "q":
                            P.dve(lambda e, pf=pf, ci=ci, fs=fs: e.tensor_scalar(out=fs[:, ci, :], in0=pf[:], scalar1=0.125,
                                                                           scalar2=None, op0=ALU.mult),
                                  r=[pk], w=[fsk])
                        else:
                            P.dve(lambda e, pf=pf, ci=ci, fs=fs: e.tensor_copy(out=fs[:, ci, :], in_=pf[:]), r=[pk], w=[fsk])
                        ci += 1
                P.dma("pool", lambda e, fs=fs, sti=sti: e.dma_start(
                    out=FT0.rearrange("(c p) t -> p c t", p=128)[:, :, sti * 512:(sti + 1) * 512], in_=fs[:]), r=[fsk])
                for b in range(4):
                    eb = sti * 4 + b
                    vs = vstage[b % 2]
                    vk = "vstage%d" % (b % 2)
                    for k in range(8):
                        P.pe(lambda e, k=k, b=b, hT=hT: e.matmul(tmA[0][:], lhsT=hT[:, k, b * 128:(b + 1) * 128],
                                                           rhs=w_in[:, k, 2304:2816], start=(k == 0), stop=(k == 7)),
                             r=["w_in", hTk], w=["tmA"])
                    for k in range(8):
                        P.pe(lambda e, k=k, b=b, hT=hT: e.matmul(tmB[0][:], lhsT=hT[:, k, b * 128:(b + 1) * 128],
                                                           rhs=w_in[:, k, 640:768], start=(k == 0), stop=(k == 7)),
                             r=["w_in", hTk], w=["tmB"])
                    P.dve(lambda e, vs=vs: e.tensor_copy(out=vs[:, 2:10, 0:64], in_=tmA[0][:].rearrange("p (h d) -> p h d", d=64)),
                          r=["tmA"], w=[vk])
                    P.act(lambda e, vs=vs: e.copy(out=vs[:, 0:2, 0:64], in_=tmB[0][:].rearrange("p (h d) -> p h d", d=64)),
                          r=["tmB"], w=[vk])
                    P.dve(lambda e, vs=vs, eb=eb: e.tensor_scalar(out=vs[:, :, 64], in0=ones10[:], scalar1=valid_sb[:, eb:eb + 1],
                                                             scalar2=None, op0=ALU.mult), r=["valid", "ones10"], w=[vk])
                    P.dma("pool", lambda e, vs=vs, eb=eb: e.dma_start(out=VT0[eb * 128:(eb + 1) * 128, :],
                                                                 in_=vs[:].rearrange("p h d -> p (h d)")), r=[vk])
            P.emit_block()

        with contextlib.ExitStack() as st:
            rpbcol = sb(st, "rpbcol", [128, 7, 8, 128], BF16)
            rpbstg = sb(st, "rpbstg", [128, 7, 8, 128], F32)
            cmask = sb(st, "cmask", [128, 128], F32)
            alib = sb(st, "alib", [128, 3, 8, 128], BF16)
            rmask = sb(st, "rmask", [128, 2, 5, 7, 128], BF16)
            sinkrow = sb(st, "sinkrow", [65, 8, 128], F32)
            sinkv = sb(st, "sinkv", [65, 8], F32)
            KAb = [sb(st, "KA%d" % i, [128, 2, 7 * 128], BF16) for i in range(2)]
            KBb = [sb(st, "KB%d" % i, [128, 8, 7 * 128], BF16) for i in range(2)]
            VRb = [sb(st, "VR%d" % i, [128, 7, 650], BF16) for i in range(2)]
            QAb = [sb(st, "QA%d" % i, [128, 8, 128], BF16) for i in range(2)]
            QBb = [sb(st, "QB%d" % i, [128, 8, 128], BF16) for i in range(2)]
            GAb = [sb(st, "GA%d" % i, [64, 8, 128], BF16) for i in range(2)]
            GBb = [sb(st, "GB%d" % i, [64, 8, 128], BF16) for i in range(2)]
            PT = [sb(st, "PT%d" % i, [128, 3, 512], BF16) for i in range(2)]
            den2 = [sb(st, "den%d" % i, [65, 512], F32) for i in range(2)]
            rinv2 = [sb(st, "rinv%d" % i, [128, 512], F32) for i in range(2)]
            on = [sb(st, "on%d" % i, [64, 512], F32) for i in range(2)]
            ystage = [sb(st, "ystage%d" % i, [64, 16, 128], BF16) for i in range(2)]
            ST = [ps(st, "ST%d" % i, [128, 3, 512]) for i in range(2)]
            OT2 = [ps(st, "OT%d" % i, [128, 512]) for i in range(2)]

            for i_ in range(2):
                P.dve(lambda e, i_=i_: e.memset(rinv2[i_][:], 0.0), w=["rinv%d" % i_])
            for i_ in range(2):
                P.dve(lambda e, i_=i_: e.memset(KAb[i_][64:128], 0.0), w=["kv%d" % i_])
                P.pool(lambda e, i_=i_: e.memset(KBb[i_][64:128], 0.0), w=["kv%d" % i_])
                P.dve(lambda e, i_=i_: e.memset(QAb[i_][64:128], 0.0), w=["qg%d" % i_])
                P.dve(lambda e, i_=i_: e.memset(QBb[i_][64:128], 0.0), w=["qg%d" % i_])
            P.dma("sp", lambda e: e.dma_start(out=rpbstg[:], in_=rpbg[:, :, :, :]), w=["rpbstg"])
            P.dma("sp", lambda e: e.dma_start(out=cmask[:], in_=colmask[:, :]), w=["cmask"])
            P.dma("sp", lambda e: e.dma_start(out=alib[:], in_=alibia[:, :, :, :]), w=["alib"])
            P.dma("sp", lambda e: e.dma_start(out=rmask[:], in_=rowmask[:, :, :, :, :]), w=["rmask"])
            P.dma("sp", lambda e: e.dma_start(out=sinkv[64:65, :], in_=a_sink[0:1, :]), w=["sinkv"])
            for r_ in range(7):
                for h in range(8):
                    P.dve(lambda e, r_=r_, h=h: e.tensor_tensor(out=rpbcol[:, r_, h, :], in0=rpbstg[:, r_, h, :], in1=cmask[:],
                                                             op=ALU.add), r=["rpbstg", "cmask"], w=["rpbcol"])
            P.act(lambda e: e.activation(out=sinkv[64:65, :], in_=sinkv[64:65, :], func=AF.Exp), r=["sinkv"], w=["sinkv"])
            P.dve(lambda e: e.tensor_copy(out=sinkrow[64:65, :, :], in_=sinkv[64:65, :].unsqueeze(2).to_broadcast([1, 8, 128])),
                  r=["sinkv"], w=["sinkrow"])

            FT0v = FT0.rearrange("(c h d) t -> d c h t", h=2, d=64)

            def block_gen(seg, i, bi):
                s2 = bi % 2
                tab = 1 if seg["name"] == "PO" else 0
                eb = seg["e0"] + PAD + i
                ob = seg["o0"] + i
                ka, kb_, vr = KAb[s2], KBb[s2], VRb[s2]
                qa, qb, ga, gb_ = QAb[s2], QBb[s2], GAb[s2], GBb[s2]
                S_ = ST[s2]; Sk = "ST%d" % s2
                p_ = PT[s2]; pk = "PT%d" % s2
                O_ = OT2[s2]; Ok = "OT%d" % s2
                o_ = on[s2]; ok_ = "on%d" % s2
                dn = den2[s2]; dk = "den%d" % s2
                rv = rinv2[s2]; rk_ = "rinv%d" % s2
                ys = ystage[s2]; ysk = "ystage%d" % s2
                t0 = (eb - 3) * 128
                t1 = (eb + 4) * 128
                kk = "kv%d" % s2
                qk = "qg%d" % s2
                P.dma("sp", lambda e: e.dma_start(out=ka[0:64], in_=FT0v[:, 4, :, t0:t1]), w=[kk])
                P.dma("sp", lambda e: e.dma_start(out=kb_[0:64].rearrange("d (c h) t -> d c h t", h=2), in_=FT0v[:, 13:17, :, t0:t1]), w=[kk])
                P.dma("sp", lambda e: e.dma_start(out=vr[:], in_=VT0[t0:t1, :].rearrange("(b p) f -> p b f", p=128)), w=[kk])
                q0 = eb * 128
                P.dma("pool", lambda e: e.dma_start(out=qa[0:64].rearrange("d (c h) t -> d c h t", h=2), in_=FT0v[:, 0:4, :, q0:q0 + 128]), w=[qk])
                P.dma("sp", lambda e: e.dma_start(out=ga[:].rearrange("d (c h) t -> d c h t", h=2), in_=FT0v[:, 5:9, :, q0:q0 + 128]), w=[qk])
                P.dma("pool", lambda e: e.dma_start(out=qb[0:64].rearrange("d (c h) t -> d c h t", h=2), in_=FT0v[:, 9:13, :, q0:q0 + 128]), w=[qk])
                P.dma("sp", lambda e: e.dma_start(out=gb_[:].rearrange("d (c h) t -> d c h t", h=2), in_=FT0v[:, 17:21, :, q0:q0 + 128]), w=[qk])
                yield
                cl = blk_class(seg, i)
                rels_b = blk_rels(seg, i)
                for job in range(4):
                    isA = job < 2
                    g = job % 2
                    rels = [-1, 0, 1] if isA else rels_b
                    batches = [rels[j:j + 3] for j in range(0, len(rels), 3)]
                    P.pe(lambda e: e.matmul(O_[0:65, :], lhsT=zeros_b[:, 0:65], rhs=zeros_b[:, :], start=True, stop=True),
                         r=["zeros_b"], w=[Ok])
                    for bt in batches:
                        for j, rel in enumerate(bt):
                            kof = (rel + 3) * 128
                            if isA:
                                P.pe(lambda e, g=g, j=j, rel=rel: e.matmul(S_[:, j, :], lhsT=ident[:], rhs=alib[:, rel + 1, 4 * g:4 * g + 4, :],
                                                                      start=True, stop=False), r=["ident", "alib"], w=[Sk])
                                P.pe(lambda e, g=g, j=j, kof=kof: e.matmul(S_[:, j, :], lhsT=ka[:, g, kof:kof + 128], rhs=qa[:, 4 * g:4 * g + 4, :],
                                                                      start=False, stop=True), r=[kk, qk], w=[Sk])
                            else:
                                P.pe(lambda e, g=g, j=j, rel=rel: e.matmul(S_[:, j, :], lhsT=ident[:], rhs=rpbcol[:, rel + 3, 4 * g:4 * g + 4, :],
                                                                      start=True, stop=False), r=["ident", "rpbcol"], w=[Sk])
                                for h in range(4):
                                    P.pe(lambda e, g=g, j=j, rel=rel, h=h: e.matmul(S_[:, j, h * 128:(h + 1) * 128], lhsT=ident[:],
                                                                               rhs=rmask[:, tab, cl, rel + 3, :], start=False, stop=False),
                                         r=["ident", "rmask"], w=[Sk])
                                for h in range(4):
                                    P.pe(lambda e, g=g, j=j, kof=kof, h=h: e.matmul(S_[:, j, h * 128:(h + 1) * 128], lhsT=kb_[:, 4 * g + h, kof:kof + 128],
                                                                               rhs=qb[:, 4 * g + h, :], start=False, stop=(h == 3)),
                                         r=[kk, qk], w=[Sk])
                        yield
                        nb_ = len(bt)
                        P.act(lambda e, g=g, nb_=nb_: e.activation(out=p_[:, 0:nb_, :], in_=S_[:, 0:nb_, :], func=AF.Exp), r=[Sk], w=[pk])
                        for j, rel in enumerate(bt):
                            ko = rel + 3
                            if isA:
                                P.pe(lambda e, g=g, j=j, ko=ko: e.matmul(O_[0:65, :], lhsT=vr[:, ko, g * 65:(g + 1) * 65], rhs=p_[:, j, :],
                                                                    start=False, stop=True), r=[kk, pk], w=[Ok])
                            else:
                                for h in range(4):
                                    hv = 2 + 4 * g + h
                                    P.pe(lambda e, g=g, j=j, ko=ko, hv=hv, h=h: e.matmul(O_[0:65, h * 128:(h + 1) * 128], lhsT=vr[:, ko, hv * 65:(hv + 1) * 65],
                                                                                    rhs=p_[:, j, h * 128:(h + 1) * 128], start=False, stop=True),
                                         r=[kk, pk], w=[Ok])
                        yield
                    if isA:
                        P.dve(lambda e, g=g: e.tensor_tensor(out=dn[64:65, :], in0=O_[64:65, :], in1=sinkrow[64:65, 4 * g:4 * g + 4, :], op=ALU.add),
                              r=[Ok, "sinkrow"], w=[dk])
                    else:
                        P.dve(lambda e: e.tensor_copy(out=dn[64:65, :], in_=O_[64:65, :]), r=[Ok], w=[dk])
                    P.dve(lambda e: e.reciprocal(out=rv[64:65, :], in_=dn[64:65, :]), r=[dk], w=[rk_])
                    gsrc = ga if isA else gb_
                    P.dve(lambda e, g=g, gsrc=gsrc: e.tensor_tensor(out=o_[:], in0=O_[0:64, :], in1=gsrc[:, 4 * g:4 * g + 4, :], op=ALU.mult),
                          r=[Ok, qk], w=[ok_])
                    yield
                    P.pe(lambda e: e.matmul(S_[0:64, 0, :], lhsT=ones_f[:, 0:64], rhs=rv[:, :], start=True, stop=True),
                         r=["ones_f", rk_], w=[Sk])
                    hb = (0 if isA else 8) + 4 * g
                    P.dve(lambda e, g=g, hb=hb: e.tensor_tensor(out=ys[:, hb:hb + 4, :], in0=o_[:], in1=S_[0:64, 0, :], op=ALU.mult),
                          r=[ok_, Sk], w=[ysk])
                    yield
                P.dma("pool", lambda e: e.dma_start(out=YT0.rearrange("(h d) t -> d h t", d=64)[:, :, ob * 128:(ob + 1) * 128], in_=ys[:]), r=[ysk])

            blocks = [(seg, i) for seg in segs for i in range(seg["nb"])]
            for p0 in range(0, len(blocks), 2):
                live = [block_gen(blocks[p0 + s_][0], blocks[p0 + s_][1], p0 + s_) for s_ in range(2) if p0 + s_ < len(blocks)]
                while live:
                    for g_ in list(live):
                        try:
                            next(g_)
                        except StopIteration:
                            live.remove(g_)
            P.emit_block()

        if dbg == 1:
            return nc
        _phase345(nc, P, cfg, sb, ps, dbg=dbg, G=dict(
            ident=ident, ones_f=ones_f, zeros_b=zeros_b, gbc=gbc, xin=xin, pin=pin, w_out_ab=w_out_ab, w_in_cd=w_in_cd,
            w_out_cd=w_out_cd, w_ple=w_ple, w_gate=w_gate, qkgain=qkgain, rope=rope, lamv=lamv, subln=subln, qtab=qtab,
            ktab=ktab, diagb=diagb, yout=yout, YT0=YT0, X1=X1, FT1=FT1, VT1=VT1, YT1=YT1,
            load_weight=load_weight, norm_block=norm_block, transpose_to=transpose_to))
    return nc


LAM_INIT1 = 0.8 - 0.6 * math.exp(-0.3 * 1)


def _phase345(nc, P, cfg, sb, ps, G, dbg=0):
    import contextlib
    segs = cfg["segs"]
    ident, ones_f, zeros_b, gbc = G["ident"], G["ones_f"], G["zeros_b"], G["gbc"]
    pin = G["pin"]
    load_weight, norm_block, transpose_to = G["load_weight"], G["norm_block"], G["transpose_to"]
    FT1, VT1, X1, YT1 = G["FT1"], G["VT1"], G["X1"], G["YT1"]

    def out_phase(layer, YT, xsrc_fn, dst_fn, seglist, w_out_d, tok_fn):
        with contextlib.ExitStack() as st:
            wout = sb(st, "wout", [128, 8, D], BF16)
            wg = sb(st, "wg", [128, 8, D], BF16)
            wp = sb(st, "wp", [128, 2, D], BF16)
            wstg = [sb(st, "wstg0", [128, 1024], F32), sb(st, "wstg1", [128, 1024], F32)]
            yts = [sb(st, "yts%d" % i, [128, 8, 512], BF16) for i in range(2)]
            xt = [sb(st, "xt%d" % i, [128, D], F32) for i in range(2)]
            psb = [sb(st, "psb%d" % i, [128, 256], F32) for i in range(2)]
            pb = sb(st, "pb", [128, 256], BF16)
            tmix = sb(st, "tmix", [128, D], F32)
            xa = sb(st, "xa", [128, D], F32)
            xab = sb(st, "xab", [128, D], BF16)
            xaT = sb(st, "xaT", [128, 8, 128], BF16)
            pT = sb(st, "pT", [128, 2, 128], BF16)
            sg = sb(st, "sg", [128, D], F32)
            xo = [sb(st, "xo%d" % i, [128, D], F32) for i in range(2)]
            junk = sb(st, "junk", [128, D], BF16)
            ss = sb(st, "ss", [128, 2], F32)
            mix = ps(st, "mix", [128, D])
            gate = ps(st, "gate", [128, D])
            pp = ps(st, "pp", [128, D])
            tp = ps(st, "tp", [128, 8, 128], BF16)
            tp2 = ps(st, "tp2", [128, 8, 128], BF16)
            load_weight(wstg, wout, w_out_d, 8, D, "wout", colchunk=1024)
            load_weight(wstg, wg, G["w_gate"][layer], 8, D, "wg", colchunk=1024)
            load_weight(wstg, wp, G["w_ple"][layer], 2, D, "wp", colchunk=1024)
            gi = 1 + 2 * layer
            bc = 0
            sc = 0
            for seg in seglist:
                for sti in range(seg["nb"] // 4):
                    y_ = yts[sc % 2]
                    yk = "yts%d" % (sc % 2)
                    ot0 = tok_fn(seg, sti * 4)
                    P.dma("sp", lambda e, y_=y_, ot0=ot0: e.dma_start(
                        out=y_[:], in_=YT.rearrange("(k p) t -> p k t", p=128)[:, :, ot0:ot0 + 512]), w=[yk])
                    sc += 1
                    for b in range(4):
                        i = sti * 4 + b
                        x_ = xt[bc % 2]; xk = "xt%d" % (bc % 2)
                        p_ = psb[bc % 2]; pk = "psb%d" % (bc % 2)
                        o_ = xo[bc % 2]; ok_ = "xo%d" % (bc % 2)
                        xs_ap = xsrc_fn(seg, i)
                        po = (seg["o0"] + i) * 128
                        P.dma("sp", lambda e, x_=x_, xs_ap=xs_ap: e.dma_start(out=x_[:], in_=xs_ap), w=[xk])
                        P.dma("sp", lambda e, p_=p_, po=po: e.dma_start(out=p_[:], in_=pin[layer, po:po + 128, :]), w=[pk])
                        for half in range(2):
                            for k in range(8):
                                P.pe(lambda e, y_=y_, k=k, b=b, half=half: e.matmul(
                                    mix[:, half * 512:(half + 1) * 512], lhsT=y_[:, k, b * 128:(b + 1) * 128],
                                    rhs=wout[:, k, half * 512:(half + 1) * 512], start=(k == 0), stop=(k == 7)),
                                    r=[yk, "wout"], w=["mix"])
                        sx = ss[:, 0:1]
                        P.act(lambda e: e.activation(out=junk[:], in_=mix[:], func=AF.Square, scale=1.0 / math.sqrt(D),
                                                     accum_out=sx), r=["mix"], w=["junk", "ss"])
                        P.dve(lambda e: e.tensor_scalar(out=sx, in0=sx, scalar1=EPS, scalar2=None, op0=ALU.add), r=["ss"], w=["ss"])
                        P.act(lambda e: e.activation(out=sx, in_=sx, func=AF.Sqrt), r=["ss"], w=["ss"])
                        P.dve(lambda e: e.reciprocal(out=sx, in_=sx), r=["ss"], w=["ss"])
                        P.dve(lambda e: e.scalar_tensor_tensor(out=tmix[:], in0=mix[:], scalar=sx, in1=gbc[:, gi, :],
                                                               op0=ALU.mult, op1=ALU.mult), r=["mix", "ss", "gbc"], w=["tmix"])
                        P.pool(lambda e, x_=x_: e.tensor_tensor(out=xa[:], in0=x_[:], in1=tmix[:], op=ALU.add),
                               r=[xk, "tmix"], w=["xa"])
                        P.act(lambda e: e.copy(out=xab[:], in_=xa[:]), r=["xa"], w=["xab"])
                        transpose_to(xab, "xab", 8, tp, "tp", xaT[:], "xaT", False)
                        P.act(lambda e, p_=p_: e.copy(out=pb[:], in_=p_[:]), r=[pk], w=["pb"])
                        transpose_to(pb, "pb", 2, tp2, "tp2", pT[:], "pT", False)
                        for half in range(2):
                            for k in range(8):
                                P.pe(lambda e, k=k, half=half: e.matmul(
                                    gate[:, half * 512:(half + 1) * 512], lhsT=xaT[:, k, :],
                                    rhs=wg[:, k, half * 512:(half + 1) * 512], start=(k == 0), stop=(k == 7)),
                                    r=["xaT", "wg"], w=["gate"])
                            for k in range(2):
                                P.pe(lambda e, k=k, half=half: e.matmul(
                                    pp[:, half * 512:(half + 1) * 512], lhsT=pT[:, k, :],
                                    rhs=wp[:, k, half * 512:(half + 1) * 512], start=(k == 0), stop=(k == 1)),
                                    r=["pT", "wp"], w=["pp"])
                        P.act(lambda e: e.activation(out=sg[:], in_=gate[:], func=AF.Sigmoid), r=["gate"], w=["sg"])
                        P.dve(lambda e: e.tensor_tensor(out=tmix[:], in0=sg[:], in1=pp[:], op=ALU.mult), r=["sg", "pp"], w=["tmix"])
                        P.pool(lambda e, o_=o_: e.tensor_tensor(out=o_[:], in0=xa[:], in1=tmix[:], op=ALU.add),
                               r=["xa", "tmix"], w=[ok_])
                        d_ap = dst_fn(seg, i)
                        P.dma("pool", lambda e, o_=o_, d_ap=d_ap: e.dma_start(out=d_ap, in_=o_[:]), r=[ok_])
                        bc += 1
            P.emit_block()

    xin = G["xin"]
    out_phase(0, G["YT0"],
              lambda seg, i: xin[(seg["e0"] + PAD + i) * 128:(seg["e0"] + PAD + i + 1) * 128, :],
              lambda seg, i: X1[(seg["o0"] + i) * 128:(seg["o0"] + i + 1) * 128, :],
              segs, G["w_out_ab"], lambda seg, i: (seg["o0"] + i) * 128)
    if dbg == 2:
        return

    with contextlib.ExitStack() as st:
        w_in = sb(st, "w_in", [128, 8, 3328], BF16)
        wstg = [sb(st, "wstg0", [128, 1664], F32), sb(st, "wstg1", [128, 1664], F32)]
        xt = [sb(st, "xt%d" % i, [128, D], F32) for i in range(2)]
        hn = [sb(st, "hn%d" % i, [128, D], BF16) for i in range(2)]
        hnT = [sb(st, "hnT%d" % i, [128, 8, 512], BF16) for i in range(2)]
        junk = sb(st, "junk", [128, D], BF16)
        ss = sb(st, "ss", [128, 2], F32)
        rstd = sb(st, "rstd", [128, 2], F32)
        fstage = sb(st, "fstage", [128, 16, 512], BF16)
        qkts = sb(st, "qkts", [128, 5, 512], BF16)
        vst = [sb(st, "vst%d" % i, [128, 642], BF16) for i in range(2)]
        qkf = sb(st, "qkf", [128, 10, 64], F32)
        sqj = sb(st, "sqj", [128, 10, 64], F32)
        ssq = sb(st, "ssq", [128, 10], F32)
        qn = sb(st, "qn", [128, 10, 64], F32)
        ra = sb(st, "ra", [128, 10, 32], F32)
        rb = sb(st, "rb", [128, 10, 32], F32)
        qr = sb(st, "qr", [128, 640], BF16)
        gain = sb(st, "gain", [128, 10, 64], F32)
        rp = [sb(st, "rp%d" % i, [128, 64], F32) for i in range(2)]
        tp = [ps(st, "tp%d" % i, [128, 8, 128], BF16) for i in range(2)]
        fm = [ps(st, "fm%d" % i, [128, 512]) for i in range(2)]
        tqA = ps(st, "tqA", [128, 512])
        tqB = ps(st, "tqB", [128, 256])
        tvD = ps(st, "tvD", [128, 512])
        load_weight(wstg, w_in, G["w_in_cd"], 8, 3328, "w_in")
        P.dma("sp", lambda e: e.dma_start(out=gain[:].rearrange("p h d -> p (h d)"), in_=G["qkgain"][0:1, :].partition_broadcast(128)), w=["gain"])
        P.dve(lambda e: e.tensor_scalar(out=gain[:, 0:8, :], in0=gain[:, 0:8, :], scalar1=0.125, scalar2=None, op0=ALU.mult),
              r=["gain"], w=["gain"])
        for i in range(2):
            P.dve(lambda e, i=i: e.memset(vst[i][:], 1.0), w=["vst%d" % i])
        bc = 0
        sc = 0
        rope = G["rope"]
        for seg in segs:
            pf = seg["name"] == "PF"
            h0 = 8 if pf else 0
            for sti in range(seg["nb"] // 4):
                hT = hnT[sc % 2]; hTk = "hnT%d" % (sc % 2)
                ot0 = (seg["o0"] + sti * 4) * 128
                for b in range(4):
                    ob = seg["o0"] + sti * 4 + b
                    x_ = xt[bc % 2]; xk = "xt%d" % (bc % 2)
                    h_ = hn[bc % 2]; hk = "hn%d" % (bc % 2)
                    P.dma("sp", lambda e, x_=x_, ob=ob: e.dma_start(out=x_[:], in_=X1[ob * 128:(ob + 1) * 128, :]), w=[xk])
                    norm_block(x_[:], xk, 2, h_[:], hk, ss[:, bc % 2:bc % 2 + 1], rstd[:, bc % 2:bc % 2 + 1], junk, str(bc % 2))
                    transpose_to(h_, hk, 8, tp[bc % 2], "tp%d" % (bc % 2), hT[:, :, b * 128:(b + 1) * 128], hTk, bc % 2 == 0)
                    bc += 1
                lst = [("kd", 1792, 4, "k")] if pf else CD_FM
                ci = 0
                for (nm, c0, nch, kind) in lst:
                    for j in range(nch):
                        f0 = c0 + j * 128
                        pf_ = fm[ci % 2]; pk = "fm%d" % (ci % 2)
                        for k in range(8):
                            P.pe(lambda e, pf_=pf_, k=k, f0=f0, hT=hT: e.matmul(pf_[:], lhsT=w_in[:, k, f0:f0 + 128], rhs=hT[:, k, :],
                                                                           start=(k == 0), stop=(k == 7)), r=["w_in", hTk], w=[pk])
                        if kind == "g":
                            P.act(lambda e, pf_=pf_, ci=ci: e.activation(out=fstage[:, ci, :], in_=pf_[:], func=AF.Silu), r=[pk], w=["fstage"])
                        elif kind == "q":
                            P.dve(lambda e, pf_=pf_, ci=ci: e.tensor_scalar(out=fstage[:, ci, :], in0=pf_[:], scalar1=0.125, scalar2=None,
                                                                       op0=ALU.mult), r=[pk], w=["fstage"])
                        else:
                            P.dve(lambda e, pf_=pf_, ci=ci: e.tensor_copy(out=fstage[:, ci, :], in_=pf_[:]), r=[pk], w=["fstage"])
                        ci += 1
                FT1v = FT1.rearrange("(c p) t -> p c t", p=128)
                if pf:
                    P.dma("pool", lambda e, ot0=ot0: e.dma_start(out=FT1v[:, 13:17, ot0:ot0 + 512], in_=fstage[:, 0:4, :]), r=["fstage"])
                else:
                    P.dma("pool", lambda e, ot0=ot0: e.dma_start(out=FT1v[:, 5:21, ot0:ot0 + 512], in_=fstage[:, 0:16, :]), r=["fstage"])
                for b in range(4):
                    ob = seg["o0"] + sti * 4 + b
                    vs = vst[b % 2]; vk = "vst%d" % (b % 2)
                    r_ = rp[b % 2]; rk = "rp%d" % (b % 2)
                    P.dma("sp", lambda e, r_=r_, ob=ob: e.dma_start(out=r_[:], in_=rope[ob * 128:(ob + 1) * 128, :]), w=[rk])
                    if not pf:
                        for k in range(8):
                            P.pe(lambda e, k=k, b=b, hT=hT: e.matmul(tqA[:], lhsT=hT[:, k, b * 128:(b + 1) * 128], rhs=w_in[:, k, 0:512],
                                                               start=(k == 0), stop=(k == 7)), r=["w_in", hTk], w=["tqA"])
                    for k in range(8):
                        P.pe(lambda e, k=k, b=b, hT=hT: e.matmul(tqB[:], lhsT=hT[:, k, b * 128:(b + 1) * 128], rhs=w_in[:, k, 512:768],
                                                           start=(k == 0), stop=(k == 7)), r=["w_in", hTk], w=["tqB"])
                    for k in range(8):
                        P.pe(lambda e, k=k, b=b, hT=hT: e.matmul(tvD[:], lhsT=hT[:, k, b * 128:(b + 1) * 128], rhs=w_in[:, k, 2304:2816],
                                                           start=(k == 0), stop=(k == 7)), r=["w_in", hTk], w=["tvD"])
                    P.dve(lambda e, vs=vs: e.tensor_copy(out=vs[:, 0:130].rearrange("p (h d) -> p h d", d=65)[:, :, 0:64],
                                                         in_=tqB[:, 128:256].rearrange("p (h d) -> p h d", d=64)), r=["tqB"], w=[vk])
                    P.act(lambda e, vs=vs: e.copy(out=vs[:, 130:642], in_=tvD[:]), r=["tvD"], w=[vk])
                    P.dma("pool", lambda e, vs=vs, ob=ob: e.dma_start(out=VT1[ob * 128:(ob + 1) * 128, :], in_=vs[:]), r=[vk])
                    if not pf:
                        P.act(lambda e: e.copy(out=qkf[:, 0:8, :], in_=tqA[:].rearrange("p (h d) -> p h d", d=64)), r=["tqA"], w=["qkf"])
                    P.act(lambda e: e.copy(out=qkf[:, 8:10, :], in_=tqB[:, 0:128].rearrange("p (h d) -> p h d", d=64)), r=["tqB"], w=["qkf"])
                    hs = slice(h0, 10)
                    nh = 10 - h0
                    P.dve(lambda e, hs=hs: e.tensor_tensor(out=sqj[:, hs, :], in0=qkf[:, hs, :], in1=qkf[:, hs, :], op=ALU.mult), r=["qkf"], w=["sqj"])
                    P.dve(lambda e, hs=hs: e.tensor_reduce(out=ssq[:, hs], in_=sqj[:, hs, :], axis=mybir.AxisListType.X, op=ALU.add),
                          r=["sqj"], w=["ssq"])
                    P.dve(lambda e, hs=hs: e.tensor_scalar(out=ssq[:, hs], in0=ssq[:, hs], scalar1=1.0 / 64, scalar2=EPS, op0=ALU.mult, op1=ALU.add),
                          r=["ssq"], w=["ssq"])
                    P.act(lambda e, hs=hs: e.activation(out=ssq[:, hs], in_=ssq[:, hs], func=AF.Sqrt), r=["ssq"], w=["ssq"])
                    P.dve(lambda e, hs=hs: e.reciprocal(out=ssq[:, hs], in_=ssq[:, hs]), r=["ssq"], w=["ssq"])
                    P.dve(lambda e, hs=hs, nh=nh: e.tensor_tensor(out=qn[:, hs, :], in0=qkf[:, hs, :],
                                                                in1=ssq[:, hs].unsqueeze(2).to_broadcast([128, nh, 64]), op=ALU.mult),
                          r=["qkf", "ssq"], w=["qn"])
                    P.dve(lambda e, hs=hs: e.tensor_tensor(out=qn[:, hs, :], in0=qn[:, hs, :], in1=gain[:, hs, :], op=ALU.mult),
                          r=["qn", "gain"], w=["qn"])
                    qv = qn[:].rearrange("p h (j two) -> p h j two", two=2)
                    qrv = qr[:].rearrange("p (h j two) -> p h j two", two=2, j=32)
                    cs = lambda r_=r_, nh=nh: r_[:, 0:32].unsqueeze(1).to_broadcast([128, nh, 32])
                    sn = lambda r_=r_, nh=nh: r_[:, 32:64].unsqueeze(1).to_broadcast([128, nh, 32])
                    P.dve(lambda e, hs=hs, cs=cs: e.tensor_tensor(out=ra[:, hs, :], in0=qv[:, hs, :, 0], in1=cs(), op=ALU.mult), r=["qn", rk], w=["ra"])
                    P.dve(lambda e, hs=hs, sn=sn: e.tensor_tensor(out=rb[:, hs, :], in0=qv[:, hs, :, 1], in1=sn(), op=ALU.mult), r=["qn", rk], w=["rb"])
                    P.dve(lambda e, hs=hs: e.tensor_tensor(out=qrv[:, hs, :, 0], in0=ra[:, hs, :], in1=rb[:, hs, :], op=ALU.subtract),
                          r=["ra", "rb"], w=["qr"])
                    P.dve(lambda e, hs=hs, sn=sn: e.tensor_tensor(out=ra[:, hs, :], in0=qv[:, hs, :, 0], in1=sn(), op=ALU.mult), r=["qn", rk], w=["ra"])
                    P.dve(lambda e, hs=hs, cs=cs: e.tensor_tensor(out=rb[:, hs, :], in0=qv[:, hs, :, 1], in1=cs(), op=ALU.mult), r=["qn", rk], w=["rb"])
                    P.dve(lambda e, hs=hs: e.tensor_tensor(out=qrv[:, hs, :, 1], in0=ra[:, hs, :], in1=rb[:, hs, :], op=ALU.add),
                          r=["ra", "rb"], w=["qr"])
                    c0_ = 4 if pf else 0
                    t_ = tp[b % 2]; tk = "tp%d" % (b % 2)
                    for k in range(c0_, 5):
                        P.pe(lambda e, k=k, t_=t_: e.transpose(out=t_[:, k, :], in_=qr[:, k * 128:(k + 1) * 128], identity=ident[:]),
                             r=["qr", "ident"], w=[tk])
                    P.dve(lambda e, t_=t_, c0_=c0_, b=b: e.tensor_copy(out=qkts[:, c0_:5, b * 128:(b + 1) * 128], in_=t_[:, c0_:5, :]),
                          r=[tk], w=["qkts"])
                c0_ = 4 if pf else 0
                P.dma("pool", lambda e, ot0=ot0, c0_=c0_: e.dma_start(out=FT1v[:, c0_:5, ot0:ot0 + 512], in_=qkts[:, c0_:5, :]), r=["qkts"])
                sc += 1
        P.emit_block()
    if dbg == 3:
        return
    _phase45(nc, P, cfg, sb, ps, G, out_phase)


def _phase45(nc, P, cfg, sb, ps, G, out_phase):
    import contextlib
    segs = cfg["segs"]
    SEG_S, SEG_PO, SEG_PF = segs
    ones_f = G["ones_f"]
    FT1, VT1, X1, YT1 = G["FT1"], G["VT1"], G["X1"], G["YT1"]
    qtab, ktab = G["qtab"], G["ktab"]
    NBS = cfg["NBS"]
    with contextlib.ExitStack() as st:
        nkbmax = max(SEG_S["nb"], SEG_PF["nb"])
        Kb = [sb(st, "Kb%d" % i, [128, nkbmax * 128], BF16) for i in range(2)]
        Vb = sb(st, "Vb", [128, nkbmax * 128], BF16)
        Qb = [sb(st, "Qb%d" % i, [128, 2, 512], BF16) for i in range(2)]
        Gb = [sb(st, "Gb%d" % i, [128, 512], BF16) for i in range(2)]
        PT = [sb(st, "PT%d" % i, [128, 2, 512], BF16) for i in range(3)]
        Mb = [sb(st, "Mb%d" % i, [128, 512], F32) for i in range(2)]
        rinv = sb(st, "rinv", [128, 512], F32)
        rinvz = sb(st, "rinvz", [128, 512], F32)
        acc = sb(st, "acc", [128, 2, 512], F32)
        on_ = sb(st, "on_", [128, 512], F32)
        o0 = sb(st, "o0", [128, 512], F32)
        od = sb(st, "od", [128, 512], F32)
        sq = sb(st, "sq", [128, 512], F32)
        ybuf = [sb(st, "ybuf%d" % i, [128, 512], BF16) for i in range(2)]
        ones_b = sb(st, "ones_b", [128, 128], BF16)
        lv = sb(st, "lv", [1, 4, 64], F32)
        pr = sb(st, "pr", [1, 2, 64], F32)
        ls = sb(st, "ls", [1, 4], F32)
        nl = sb(st, "nl", [128, 1], F32)
        gsc = sb(st, "gsc", [128, 1], F32)
        SS = [ps(st, "SS%d" % i, [128, 2, 512]) for i in range(3)]
        OTd = [ps(st, "OTd%d" % i, [128, 512]) for i in range(1)]
        LB = [ps(st, "LB%d" % i, [128, 512]) for i in range(1)]

        P.dve(lambda e: e.memset(ones_b[:], 1.0), w=["ones_b"])
        P.dve(lambda e: e.memset(rinvz[:], 0.0), w=["rinvz"])
        for i_ in range(2):
            P.dve(lambda e, i_=i_: e.memset(Qb[i_][64:128], 0.0), w=["Q%d" % i_])
        P.dma("sp", lambda e: e.dma_start(out=lv[:], in_=G["lamv"][:, :, :]), w=["lv"])
        P.dma("sp", lambda e: e.dma_start(out=gsc[:], in_=G["subln"][:, :]), w=["gsc"])
        P.dve(lambda e: e.tensor_scalar(out=gsc[:], in0=gsc[:], scalar1=(1.0 - LAM_INIT1), scalar2=None, op0=ALU.mult), r=["gsc"], w=["gsc"])
        P.dve(lambda e: e.tensor_tensor(out=pr[:, 0, :], in0=lv[:, 0, :], in1=lv[:, 1, :], op=ALU.mult), r=["lv"], w=["pr"])
        P.dve(lambda e: e.tensor_tensor(out=pr[:, 1, :], in0=lv[:, 2, :], in1=lv[:, 3, :], op=ALU.mult), r=["lv"], w=["pr"])
        P.dve(lambda e: e.tensor_reduce(out=ls[:, 0:2], in_=pr[:], axis=mybir.AxisListType.X, op=ALU.add), r=["pr"], w=["ls"])
        P.act(lambda e: e.activation(out=ls[:, 0:2], in_=ls[:, 0:2], func=AF.Exp), r=["ls"], w=["ls"])
        P.dve(lambda e: e.tensor_tensor(out=ls[:, 2:3], in0=ls[:, 1:2], in1=ls[:, 0:1], op=ALU.subtract), r=["ls"], w=["ls"])
        P.dve(lambda e: e.tensor_scalar(out=ls[:, 3:4], in0=ls[:, 2:3], scalar1=-LAM_INIT1, scalar2=None, op0=ALU.add), r=["ls"], w=["ls"])
        P.pe(lambda e: e.matmul(LB[0][:, 0:1], lhsT=ones_f[0:1, 0:128], rhs=ls[0:1, 3:4], start=True, stop=True), r=["ones_f", "ls"], w=["LB0"])
        P.dve(lambda e: e.tensor_copy(out=nl[:], in_=LB[0][:, 0:1]), r=["LB0"], w=["nl"])

        gg = 0
        job = 0
        qj = 0
        gj = 0
        NSS = 3

        def run_units(units, qk, ex, pv):
            nonlocal gg
            n = len(units)
            LA = 2
            for i in range(min(LA, n)):
                qk(units[i], gg + i)
            for i in range(LA, n):
                qk(units[i], gg + i)
                ex(units[i - LA], gg + i - LA)
                pv(units[i - LA], gg + i - LA)
            for i in range(max(0, n - LA), n):
                ex(units[i], gg + i)
                pv(units[i], gg + i)
            gg += n

        for seg, qbase, kseg in ((SEG_S, 0, SEG_S), (SEG_PO, NBS, SEG_PF)):
            nkb = kseg["nb"]
            ko = kseg["o0"] * 128
            nch = seg["nb"] // 4
            static_sign = (seg["name"] == "S")
            P.pool(lambda e: e.memset(Kb[0][64:128, :], 0.0), w=["K0"])
            for kv in range(2):
                K_ = Kb[0]
                r0 = 4 * 128 + kv * 64
                for c0 in range(0, nkb * 128, 2048):
                    c1 = min(nkb * 128, c0 + 2048)
                    P.dma("sp", lambda e, c0=c0, c1=c1, r0=r0, K_=K_, ko=ko: e.dma_start(out=K_[0:64, c0:c1], in_=FT1[r0:r0 + 64, ko + c0:ko + c1]), w=["K0"])
                Vv = Vb[:, 0:nkb * 65].rearrange("p (b f) -> p b f", f=65)
                for b0 in range(0, nkb, 16):
                    b1 = min(nkb, b0 + 16)
                    P.dma("sp", lambda e, b0=b0, b1=b1, Vv=Vv, kv=kv, ko=ko: e.dma_start(
                        out=Vv[:, b0:b1, :], in_=VT1[ko + b0 * 128:ko + b1 * 128, kv * 65:(kv + 1) * 65].rearrange("(b p) f -> p b f", p=128)),
                        w=["V"])
                for h in range(4 * kv, 4 * kv + 4):
                    for ci in range(nch):
                        tok = (seg["o0"] + 4 * ci) * 128
                        qcol = (qbase + 4 * ci) * 128
                        Q_ = Qb[qj % 2]; Qk = "Q%d" % (qj % 2); qj += 1
                        G_ = Gb[gj % 2]; Gk = "G%d" % (gj % 2); gj += 1
                        rq = (h // 2) * 128 + (h % 2) * 64
                        rg = (5 + h // 2) * 128 + (h % 2) * 64
                        P.dma("pool", lambda e, Q_=Q_, rq=rq, tok=tok: e.dma_start(out=Q_[0:64, 0, :], in_=FT1[rq:rq + 64, tok:tok + 512]), w=[Qk])
                        P.dma("pool", lambda e, G_=G_, rg=rg, tok=tok: e.dma_start(out=G_[0:64, :], in_=FT1[rg:rg + 64, tok:tok + 512]), w=[Gk])
                        O_ = OTd[0]; Ok = "OTd0"
                        L_ = LB[0]; Lk = "LB0"

                        def qk(u, gg_, K_=K_, Q_=Q_, Qk=Qk):
                            S_ = SS[gg_ % NSS]
                            for j in range(2):
                                kb = 2 * u + j
                                P.pe(lambda e, S_=S_, j=j, kb=kb, Q_=Q_, K_=K_: e.matmul(S_[:, j, :], lhsT=K_[:, kb * 128:(kb + 1) * 128],
                                                                                    rhs=Q_[:, 0, :], start=True, stop=True),
                                     r=["K0", Qk], w=["SS%d" % (gg_ % NSS)])

                        def ex(u, gg_):
                            S_ = SS[gg_ % NSS]; p_ = PT[gg_ % 3]
                            P.act(lambda e, S_=S_, p_=p_: e.activation(out=p_[:], in_=S_[:], func=AF.Exp),
                                  r=["SS%d" % (gg_ % NSS)], w=["PT%d" % (gg_ % 3)])

                        def pv(u, gg_, O_=O_, Ok=Ok, Vv=Vv, nkb=nkb):
                            p_ = PT[gg_ % 3]
                            for j in range(2):
                                kb = 2 * u + j
                                P.pe(lambda e, p_=p_, j=j, kb=kb, O_=O_, Vv=Vv, nkb=nkb: e.matmul(O_[0:65, :], lhsT=Vv[:, kb, :], rhs=p_[:, j, :],
                                                                                             start=(kb == 0), stop=(kb == nkb - 1)),
                                     r=["V", "PT%d" % (gg_ % 3)], w=[Ok])
                        run_units(list(range(nkb // 2)), qk, ex, pv)
                        y_ = ybuf[job % 2]; yk = "ybuf%d" % (job % 2)
                        P.dve(lambda e, O_=O_: e.reciprocal(out=rinvz[64:65, :], in_=O_[64:65, :]), r=[Ok], w=["rinvz"])
                        P.dve(lambda e, O_=O_, G_=G_: e.tensor_tensor(out=on_[0:64, :], in0=O_[0:64, :], in1=G_[0:64, :], op=ALU.mult),
                              r=[Ok, Gk], w=["on_"])
                        P.pe(lambda e, L_=L_: e.matmul(L_[0:64, :], lhsT=ones_f[:, 0:64], rhs=rinvz[:, :], start=True, stop=True),
                             r=["ones_f", "rinvz"], w=[Lk])
                        P.dve(lambda e, L_=L_, y_=y_: e.tensor_tensor(out=y_[0:64, :], in0=on_[0:64, :], in1=L_[0:64, :], op=ALU.mult),
                              r=["on_", Lk], w=[yk])
                        P.dma("pool", lambda e, y_=y_, h=h, qcol=qcol: e.dma_start(out=YT1[h * 64:(h + 1) * 64, qcol:qcol + 512], in_=y_[0:64, :]), r=[yk])
                        job += 1
            for h in range(4):
                for m in range(2):
                    K_ = Kb[m]
                    r0 = (13 + h) * 128 + m * 64
                    for c0 in range(0, nkb * 128, 2048):
                        c1 = min(nkb * 128, c0 + 2048)
                        P.dma("sp", lambda e, K_=K_, c0=c0, c1=c1, r0=r0, ko=ko: e.dma_start(out=K_[0:64, c0:c1], in_=FT1[r0:r0 + 64, ko + c0:ko + c1]),
                              w=["K%d" % m])
                    P.dma("sp", lambda e, K_=K_, h=h, nkb=nkb, ko=ko: e.dma_start(out=K_[64:68, 0:nkb * 128], in_=ktab[h, :, ko:ko + nkb * 128]), w=["K%d" % m])
                Vv = Vb[:, 0:nkb * 128].rearrange("p (b f) -> p b f", f=128)
                for b0 in range(0, nkb, 16):
                    b1 = min(nkb, b0 + 16)
                    P.dma("sp", lambda e, b0=b0, b1=b1, Vv=Vv, h=h, ko=ko: e.dma_start(
                        out=Vv[:, b0:b1, :], in_=VT1[ko + b0 * 128:ko + b1 * 128, 130 + h * 128:130 + (h + 1) * 128].rearrange("(b p) f -> p b f", p=128)),
                        w=["V"])
                for ci in range(nch):
                    tok = (seg["o0"] + 4 * ci) * 128
                    qcol = (qbase + 4 * ci) * 128
                    G_ = Gb[gj % 2]; Gk = "G%d" % (gj % 2); gj += 1
                    rg = (17 + h) * 128
                    P.dma("pool", lambda e, G_=G_, rg=rg, tok=tok: e.dma_start(out=G_[:, :], in_=FT1[rg:rg + 128, tok:tok + 512]), w=[Gk])
                    if static_sign:
                        units = []
                        for g in range(nkb // 2):
                            kb0 = 2 * g
                            if kb0 + 1 < 4 * ci:
                                units.append(("far", 0, kb0))
                            elif kb0 > 4 * ci + 3:
                                units.append(("far", 1, kb0))
                            else:
                                units.append(("near", kb0))
                                units.append(("near", kb0 + 1))
                    else:
                        units = [("near", kb) for kb in range(nkb)]
                    for m in range(2):
                        K_ = Kb[m]; Kk = "K%d" % m
                        Q_ = Qb[qj % 2]; Qk = "Q%d" % (qj % 2); qj += 1
                        rq = (9 + h) * 128 + m * 64
                        for v in range(2):
                            P.dma("pool", lambda e, Q_=Q_, rq=rq, tok=tok, v=v: e.dma_start(out=Q_[0:64, v, :], in_=FT1[rq:rq + 64, tok:tok + 512]), w=[Qk])
                            P.dma("pool", lambda e, Q_=Q_, h=h, v=v, qcol=qcol: e.dma_start(out=Q_[64:68, v, :], in_=qtab[h, v, :, qcol:qcol + 512]), w=[Qk])
                        O_ = OTd[0]; Ok = "OTd0"
                        L_ = LB[0]; Lk = "LB0"

                        def qk(u, gg_, K_=K_, Kk=Kk, Q_=Q_, Qk=Qk):
                            S_ = SS[gg_ % NSS]
                            if u[0] == "far":
                                lst = [(j, u[1], u[2] + j) for j in range(2)]
                            else:
                                lst = [(v, v, u[1]) for v in range(2)]
                            for (slot, v, kb) in lst:
                                P.pe(lambda e, S_=S_, slot=slot, v=v, kb=kb, Q_=Q_, K_=K_: e.matmul(
                                    S_[:, slot, :], lhsT=K_[0:68, kb * 128:(kb + 1) * 128], rhs=Q_[0:68, v, :], start=True, stop=True),
                                    r=[Kk, Qk], w=["SS%d" % (gg_ % NSS)])

                        def ex(u, gg_):
                            S_ = SS[gg_ % NSS]; p_ = PT[gg_ % 3]; M_ = Mb[gg_ % 2]
                            if u[0] == "far":
                                P.act(lambda e, S_=S_, p_=p_: e.activation(out=p_[:], in_=S_[:], func=AF.Exp),
                                      r=["SS%d" % (gg_ % NSS)], w=["PT%d" % (gg_ % 3)])
                            else:
                                if gg_ % 2 == 0:
                                    P.act(lambda e, S_=S_, M_=M_: e.copy(out=M_[:], in_=S_[:, 0, :]), r=["SS%d" % (gg_ % NSS)], w=["Mb%d" % (gg_ % 2)])
                                else:
                                    P.dve(lambda e, S_=S_, M_=M_: e.tensor_copy(out=M_[:], in_=S_[:, 0, :]), r=["SS%d" % (gg_ % NSS)], w=["Mb%d" % (gg_ % 2)])
                                P.dve(lambda e, S_=S_, M_=M_: e.tensor_tensor(out=M_[:], in0=M_[:], in1=S_[:, 1, :], op=ALU.min),
                                      r=["SS%d" % (gg_ % NSS), "Mb%d" % (gg_ % 2)], w=["Mb%d" % (gg_ % 2)])
                                P.act(lambda e, M_=M_, p_=p_: e.activation(out=p_[:, 0, :], in_=M_[:], func=AF.Exp),
                                      r=["Mb%d" % (gg_ % 2)], w=["PT%d" % (gg_ % 3)])

                        def pv(u, gg_, O_=O_, L_=L_, Ok=Ok, Lk=Lk, Vv=Vv, nkb=nkb):
                            p_ = PT[gg_ % 3]
                            lst = [(j, u[2] + j) for j in range(2)] if u[0] == "far" else [(0, u[1])]
                            for (slot, kb) in lst:
                                P.pe(lambda e, p_=p_, kb=kb, slot=slot, O_=O_, Vv=Vv, nkb=nkb: e.matmul(
                                    O_[:, :], lhsT=Vv[:, kb, :], rhs=p_[:, slot, :], start=(kb == 0), stop=(kb == nkb - 1)),
                                    r=["V", "PT%d" % (gg_ % 3)], w=[Ok])
                            accop = P.dve if static_sign else P.pool
                            if u[0] == "far":
                                accop(lambda e, p_=p_: e.tensor_tensor(out=acc[:], in0=acc[:], in1=p_[:], op=ALU.add),
                                      r=["acc", "PT%d" % (gg_ % 3)], w=["acc"])
                            else:
                                accop(lambda e, p_=p_: e.tensor_tensor(out=acc[:, 0, :], in0=acc[:, 0, :], in1=p_[:, 0, :], op=ALU.add),
                                      r=["acc", "PT%d" % (gg_ % 3)], w=["acc"])
                        (P.dve if static_sign else P.pool)(lambda e: e.memset(acc[:], 0.0), w=["acc"])
                        run_units(units, qk, ex, pv)
                        for j_ in range(2):
                            P.pe(lambda e, L_=L_, j_=j_: e.matmul(L_[:, :], lhsT=ones_f[:, :], rhs=acc[:, j_, :], start=(j_ == 0), stop=(j_ == 1)),
                                 r=["ones_f", "acc"], w=[Lk])
                        P.dve(lambda e, L_=L_: e.reciprocal(out=rinv[:], in_=L_[:]), r=[Lk], w=["rinv"])
                        if m == 0:
                            P.dve(lambda e, O_=O_: e.tensor_tensor(out=o0[:], in0=O_[:], in1=rinv[:], op=ALU.mult), r=[Ok, "rinv"], w=["o0"])
                        else:
                            y_ = ybuf[job % 2]; yk = "ybuf%d" % (job % 2)
                            P.dve(lambda e, O_=O_: e.tensor_tensor(out=on_[:], in0=O_[:], in1=rinv[:], op=ALU.mult), r=[Ok, "rinv"], w=["on_"])
                            P.dve(lambda e: e.scalar_tensor_tensor(out=od[:], in0=on_[:], scalar=nl[:, 0:1], in1=o0[:], op0=ALU.mult, op1=ALU.add),
                                  r=["on_", "nl", "o0"], w=["od"])
                            P.dve(lambda e: e.tensor_tensor(out=sq[:], in0=od[:], in1=od[:], op=ALU.mult), r=["od"], w=["sq"])
                            P.pe(lambda e, L_=L_: e.matmul(L_[:, :], lhsT=ones_f[:, :], rhs=sq[:], start=True, stop=True), r=["ones_f", "sq"], w=[Lk])
                            P.dve(lambda e, L_=L_: e.tensor_scalar(out=sq[:], in0=L_[:], scalar1=1.0 / 128, scalar2=EPS, op0=ALU.mult, op1=ALU.add),
                                  r=[Lk], w=["sq"])
                            P.act(lambda e: e.activation(out=sq[:], in_=sq[:], func=AF.Sqrt), r=["sq"], w=["sq"])
                            P.dve(lambda e: e.reciprocal(out=sq[:], in_=sq[:]), r=["sq"], w=["sq"])
                            P.dve(lambda e: e.tensor_tensor(out=od[:], in0=od[:], in1=sq[:], op=ALU.mult), r=["od", "sq"], w=["od"])
                            P.dve(lambda e, y_=y_, G_=G_: e.scalar_tensor_tensor(out=y_[:], in0=od[:], scalar=gsc[:, 0:1], in1=G_[:], op0=ALU.mult, op1=ALU.mult),
                                  r=["od", "gsc", Gk], w=[yk])
                            P.dma("pool", lambda e, y_=y_, h=h, qcol=qcol: e.dma_start(
                                out=YT1[512 + h * 128:512 + (h + 1) * 128, qcol:qcol + 512], in_=y_[:]), r=[yk])
                        job += 1
        P.emit_block()

    yout = G["yout"]

    def qidx(seg, i):
        return (0 if seg["name"] == "S" else NBS) + i
    out_phase(1, YT1,
              lambda seg, i: X1[(seg["o0"] + i) * 128:(seg["o0"] + i + 1) * 128, :],
              lambda seg, i: yout[qidx(seg, i) * 128:(qidx(seg, i) + 1) * 128, :],
              [SEG_S, SEG_PO], G["w_out_cd"], lambda seg, i: qidx(seg, i) * 128)


def host_constants(cfg):
    c = {}
    c["eye"] = np.eye(128, dtype=np.float32)
    colmask, dc = nb_static()
    c["colmask"] = colmask
    k = np.arange(128)[:, None]
    q = np.arange(128)[None, :]
    sl8 = alibi_slopes(8)
    al = np.zeros((128, 3, 8, 128), np.float32)
    for r in range(3):
        dist = np.abs(128 * (r - 1) + k - q).astype(np.float32)
        for h in range(8):
            al[:, r, h, :] = np.where(dist <= 128, -sl8[h] * dist, NEG)
    c["alibia"] = al.astype(NPBF)
    sl4 = alibi_slopes(4)
    dg = np.zeros((128, 4, 128), np.float32)
    for h in range(4):
        dg[:, h, :] = -sl4[h] * np.abs(k - q)
    c["diagb"] = dg.astype(NPBF)
    return c


def seg_positions(cfg, core):
    nbs, nbpo, nbpf = cfg["NBS"], cfg["NBPO"], cfg["NBPF"]
    ps_ = np.arange(nbs * 128)
    ppo = core * nbpo * 128 + np.arange(nbpo * 128)
    ppf = np.arange(nbpf * 128)
    return ps_, ppo, ppf


def prepare_core(cfg, core, inp, consts):
    nbs, nbpo, nbpf = cfg["NBS"], cfg["NBPO"], cfg["NBPF"]
    SS, SP = nbs * 128, nbpf * 128
    pad = PAD * 128
    m = dict(consts)
    xs = inp["x_sample"][core]
    xp = inp["x_prompt"][0]
    z = np.zeros((pad, D), np.float32)
    xpp = np.concatenate([z, xp, z], axis=0)
    lo = core * nbpo * 128
    xin = np.concatenate([z, xs, z, xpp[lo:lo + nbpo * 128 + 2 * pad], xpp], axis=0)
    m["xin"] = np.ascontiguousarray(xin)
    vs = np.concatenate([np.zeros(pad), np.ones(SS), np.zeros(pad)])
    vpf = np.concatenate([np.zeros(pad), np.ones(SP), np.zeros(pad)])
    valid = np.concatenate([vs, vpf[lo:lo + nbpo * 128 + 2 * pad], vpf]).astype(np.float32)
    m["validin"] = np.ascontiguousarray(valid.reshape(-1, 128).T)
    pp = inp["p_prompt"][:, 0]
    m["pin"] = np.ascontiguousarray(np.concatenate(
        [inp["p_sample"][:, core], pp[:, lo:lo + nbpo * 128], pp], axis=1))
    m["gvec"] = np.ascontiguousarray(np.stack([inp["norm_pre"][0], inp["norm_post"][0], inp["norm_pre"][1], inp["norm_post"][1]]))
    m["w_in_ab"] = inp["w_in_ab"][0]; m["w_out_ab"] = inp["w_out_ab"][0]
    m["w_in_cd"] = inp["w_in_cd"][0]; m["w_out_cd"] = inp["w_out_cd"][0]
    m["w_ple"] = inp["w_ple"]; m["w_gate"] = inp["w_ple_gate"]
    m["a_sink"] = inp["a_sink"]
    rpb = inp["b_rpb"][0]
    kr = np.arange(2)[:, None, None, None]; kc = np.arange(64)[None, :, None, None]
    qr = np.arange(2)[None, None, :, None]; qc = np.arange(64)[None, None, None, :]
    dcx = np.broadcast_to(np.clip(kc - qc, -15, 15) + 15, (2, 64, 2, 64)).reshape(128, 128)
    g = np.zeros((128, 7, 8, 128), np.float32)
    for r in range(7):
        drx = np.broadcast_to(np.clip(2 * (r - 3) + kr - qr + 7, 0, 14), (2, 64, 2, 64)).reshape(128, 128)
        for h in range(8):
            g[:, r, h, :] = rpb[h][drx, dcx]
    m["rpbg"] = g
    rm = np.zeros((128, 2, 5, 7, 128), np.float32)
    rows_any = 64
    nblk_any = rows_any // 2
    reps = {0: 0, 1: 1, 2: nblk_any // 2, 3: nblk_any - 2, 4: nblk_any - 1}
    rows_p = nbpf * 2
    for cl in range(5):
        for r in range(7):
            n = reps[cl]
            rm[:, 0, cl, r, :] = nb_rowmask_tile(rows_any, n, n + r - 3)
    seg_po = cfg["segs"][1]
    done = set()
    for i in range(nbpo):
        cl = blk_class(seg_po, i)
        if cl in done:
            continue
        done.add(cl)
        n = core * nbpo + i
        for r in range(7):
            rm[:, 1, cl, r, :] = nb_rowmask_tile(rows_p, n, n + r - 3)
    for cl in range(5):
        if cl not in done:
            rm[:, 1, cl] = NEG
    m["rowmask"] = rm.astype(NPBF)
    m["qkgain"] = np.ascontiguousarray(np.concatenate([np.tile(inp["c_q_norm"][0], 8), np.tile(inp["c_k_norm"][0], 2)])[None, :])
    ps_, ppo, ppf = seg_positions(cfg, core)
    pos = np.concatenate([ps_, ppo, ppf])
    inv = (10000.0 ** (-2.0 * np.arange(16) / 32)).astype(np.float32)
    row = (pos // 64).astype(np.float32); col = (pos % 64).astype(np.float32)
    ang = np.concatenate([row[:, None] * inv, col[:, None] * inv], axis=-1).astype(np.float32)
    m["rope"] = np.ascontiguousarray(np.concatenate([np.cos(ang), np.sin(ang)], axis=-1).astype(np.float32))
    m["lamv"] = np.ascontiguousarray(np.stack([inp["d_lambda_q1"][0], inp["d_lambda_k1"][0], inp["d_lambda_q2"][0], inp["d_lambda_k2"][0]])[None])
    m["subln"] = np.ascontiguousarray(inp["d_subln"][0][:, None])
    sl4 = alibi_slopes(4)
    posq = np.concatenate([ps_, ppo]).astype(np.float32)
    qa_, qb_ = np.floor(posq / 128), np.mod(posq, 128)
    qt = np.zeros((4, 2, 4, posq.size), np.float32)
    kt = np.zeros((4, 4, pos.size), np.float32)
    ka_, kb_ = np.floor(pos / 128).astype(np.float32), np.mod(pos, 128).astype(np.float32)
    for h in range(4):
        s = sl4[h]
        qt[h, 0] = np.stack([-s * 128 * qa_, -s * qb_, np.ones_like(qa_), np.ones_like(qa_)])
        qt[h, 1] = np.stack([s * 128 * qa_, s * qb_, -np.ones_like(qa_), -np.ones_like(qa_)])
        kt[h] = np.stack([np.ones_like(ka_), np.ones_like(ka_), s * 128 * ka_, s * kb_])
    m["qtab"] = qt.astype(NPBF)
    m["ktab"] = kt.astype(NPBF)
    return m


_CACHE = {}


def kernel(**inputs):
    inp = {k: np.asarray(v) for k, v in inputs.items()}
    nbs = inp["x_sample"].shape[1] // 128
    nbpf = inp["x_prompt"].shape[1] // 128
    cfg = cfg_make(nbs, nbpf)
    key = (nbs, nbpf)
    if key not in _CACHE:
        _CACHE[key] = build(cfg)
    nc = _CACHE[key]
    consts = host_constants(cfg)
    maps = [prepare_core(cfg, c, inp, consts) for c in range(NCORE)]
    res = run_bass_kernel_spmd(nc, maps, core_ids=list(range(NCORE)))
    nbpo = cfg["NBPO"]
    ys = np.zeros((NCORE, nbs * 128, D), np.float32)
    yp = np.zeros((1, nbpf * 128, D), np.float32)
    for c in range(NCORE):
        y = np.asarray(res.results[c]["yout"])
        ys[c] = y[:nbs * 128]
        yp[0, c * nbpo * 128:(c + 1) * nbpo * 128] = y[nbs * 128:]
    return (yp, ys)
```

```python
import math
import numpy as np
import ml_dtypes
import concourse.bass as bass
import concourse.mybir as mybir
from concourse.bass_utils import run_bass_kernel_spmd

F32 = mybir.dt.float32
BF16 = mybir.dt.bfloat16
AF = mybir.ActivationFunctionType
ALU = mybir.AluOpType
NPBF = ml_dtypes.bfloat16

NCORE = 8
D = 1024
PAD = 4
EPS = 1e-6
NEG = -30000.0
NSEM_DMA = 8


class Prog:
    ENG = ("pe", "act", "dve", "pool", "sp")

    def __init__(self, nc, sems):
        self.nc = nc
        self.sems = sems
        self.sigcount = {e: 0 for e in self.ENG}
        self.dmacount = {"sp": 0, "pool": 0}
        self.waited = {}
        self.reset()

    def reset(self):
        self.ops = []
        self.lw = {}
        self.rd = {}

    def op(self, eng, fn, r=(), w=(), dma=False):
        idx = len(self.ops)
        deps = set()
        for b in r:
            x = self.lw.get(b)
            if x is not None:
                deps.add(x)
        for b in w:
            x = self.lw.get(b)
            if x is not None:
                deps.add(x)
            deps.update(self.rd.get(b, ()))
        for b in r:
            self.rd.setdefault(b, []).append(idx)
        for b in w:
            self.lw[b] = idx
            self.rd[b] = []
        self.ops.append(dict(eng=eng, fn=fn, deps=deps, dma=dma, need=False))
        return idx

    def pe(self, fn, r=(), w=()): return self.op("pe", fn, r, w)
    def act(self, fn, r=(), w=()): return self.op("act", fn, r, w)
    def dve(self, fn, r=(), w=()): return self.op("dve", fn, r, w)
    def pool(self, fn, r=(), w=()): return self.op("pool", fn, r, w)
    def dma(self, q, fn, r=(), w=()): return self.op(q, fn, r, w, dma=True)

    def emit_block(self, name=None):
        nc = self.nc
        ops = self.ops
        for o in ops:
            for d in o["deps"]:
                p = ops[d]
                if p["eng"] == o["eng"] and o["eng"] == "pe" and not p["dma"]:
                    continue
                p["need"] = True
        for o in ops:
            e = o["eng"]
            if o["dma"]:
                k = self.dmacount[e]
                self.dmacount[e] = k + 1
                o["sig"] = ((e, k % NSEM_DMA), 16 * (k // NSEM_DMA + 1))
                o["pre"] = ((e, k % NSEM_DMA), 16 * (k // NSEM_DMA)) if k >= NSEM_DMA else None
            elif o["need"]:
                self.sigcount[e] += 1
                o["sig"] = (e, self.sigcount[e])
                o["pre"] = None
            else:
                o["sig"] = None
                o["pre"] = None
        per = {e: [] for e in self.ENG}
        for o in ops:
            e = o["eng"]
            waits = []
            cand = {}
            for d in o["deps"]:
                p = ops[d]
                if p["eng"] == e and e == "pe" and not p["dma"]:
                    continue
                s, v = p["sig"]
                cand[s] = max(cand.get(s, 0), v)
            if o["pre"] is not None:
                s, v = o["pre"]
                cand[s] = max(cand.get(s, 0), v)
            for s, v in cand.items():
                if self.waited.get((e, s), 0) >= v:
                    continue
                self.waited[(e, s)] = v
                waits.append((s, v))
            per[e].append((o, waits))
        tails = {}
        for q in ("sp", "pool"):
            k = self.dmacount[q]
            tl = []
            for i in range(NSEM_DMA):
                n = (k - i + NSEM_DMA - 1) // NSEM_DMA if k > i else 0
                if n > 0 and self.waited.get((q, (q, i)), 0) < 16 * n:
                    self.waited[(q, (q, i))] = 16 * n
                    tl.append(((q, i), 16 * n))
            tails[q] = tl
        sems = self.sems

        def run(eh, e):
            for o, waits in per[e]:
                for s, v in waits:
                    eh.wait_ge(sems[s], v)
                ins = o["fn"](eh)
                if o["sig"] is not None:
                    s, v = o["sig"]
                    ins.then_inc(sems[s], 16 if o["dma"] else 1)
            for s, v in tails.get(e, ()):
                eh.wait_ge(sems[s], v)

        with nc.Block() as block:
            @block.tensor
            def _(eh): run(eh, "pe")

            @block.scalar
            def _(eh): run(eh, "act")

            @block.vector
            def _(eh): run(eh, "dve")

            @block.gpsimd
            def _(eh): run(eh, "pool")

            @block.sync
            def _(eh): run(eh, "sp")
        self.reset()


def alibi_slopes(n):
    return (2.0 ** (-8.0 * np.arange(1, n + 1) / n)).astype(np.float32)


def nb_rowmask_tile(rows, n, kb):
    m = np.full((2, 64, 2, 64), NEG, np.float32)
    nblk = rows // 2
    if n < 0 or n >= nblk or kb < 0 or kb >= nblk:
        return m.reshape(128, 128)
    for qr in range(2):
        r = 2 * n + qr
        rs = min(max(r - 4, 0), rows - 8)
        for kr in range(2):
            kk = 2 * kb + kr
            if rs <= kk < rs + 8:
                m[kr, :, qr, :] = 0.0
    return m.reshape(128, 128)


def nb_static():
    kc = np.arange(64)[:, None]
    qc = np.arange(64)[None, :]
    ws = np.clip(qc - 8, 0, 48)
    colok = (kc >= ws) & (kc < ws + 16)
    colmask = np.where(colok, 0.0, NEG).astype(np.float32)
    colmask = np.broadcast_to(colmask[None, :, None, :], (2, 64, 2, 64)).reshape(128, 128)
    dc = np.clip(kc - qc, -15, 15) + 15
    return colmask, dc


def cfg_make(nbs, nbpf):
    c = dict(NBS=nbs, NBPF=nbpf, NBPO=nbpf // NCORE)
    segs = []
    e0 = 0
    o0 = 0
    for name, nb in (("S", nbs), ("PO", nbpf // NCORE), ("PF", nbpf)):
        segs.append(dict(name=name, nb=nb, e0=e0, o0=o0, ne=nb + 2 * PAD))
        e0 += nb + 2 * PAD
        o0 += nb
    c["segs"] = segs
    c["NE"] = e0
    c["NO"] = o0
    c["NQ"] = nbs + nbpf // NCORE
    return c


def blk_class(seg, i):
    nb = seg["nb"]
    if i == 0: return 0
    if i == 1: return 1
    if i == nb - 2: return 3
    if i == nb - 1: return 4
    return 2


def blk_rels(seg, i):
    cl = blk_class(seg, i)
    if seg["name"] == "PO":
        return {0: list(range(-2, 4)), 1: list(range(-2, 3)), 2: list(range(-2, 3)),
                3: list(range(-2, 3)), 4: list(range(-3, 3))}[cl]
    return {0: [0, 1, 2, 3], 1: [-1, 0, 1, 2], 2: [-2, -1, 0, 1, 2], 3: [-2, -1, 0, 1], 4: [-3, -2, -1, 0]}[cl]


AB_FM = [("qa", 0, 4, "q"), ("ka", 512, 1, "k"), ("ga", 768, 4, "g"),
         ("qb", 1280, 4, "q"), ("kb", 1792, 4, "k"), ("gb", 2816, 4, "g")]
CD_FM = [("gc", 768, 4, "g"), ("qd", 1280, 4, "q"), ("kd", 1792, 4, "k"), ("gd", 2816, 4, "g")]


def build(cfg, dbg=0):
    nc = bass.Bass("TRN2", target_bir_lowering=False)
    NE, NO, NQ = cfg["NE"], cfg["NO"], cfg["NQ"]
    segs = cfg["segs"]
    TE, TO, TQ = NE * 128, NO * 128, NQ * 128

    def din(name, shape, dt=F32):
        return nc.dram_tensor(name, list(shape), dt, kind="ExternalInput").ap()

    def dscr(name, shape, dt):
        kind = "ExternalOutput" if (dbg and name in ("YT0", "FT0", "VT0", "X1", "FT1", "VT1", "YT1")) else "Internal"
        return nc.dram_tensor(name, list(shape), dt, kind=kind).ap()

    xin = din("xin", [TE, D])
    validin = din("validin", [128, NE])
    pin = din("pin", [2, TO, 256])
    gvec = din("gvec", [4, D])
    w_in_ab = din("w_in_ab", [D, 3328]); w_out_ab = din("w_out_ab", [D, D])
    w_in_cd = din("w_in_cd", [D, 3328]); w_out_cd = din("w_out_cd", [D, D])
    w_ple = din("w_ple", [2, 256, D]); w_gate = din("w_gate", [2, D, D])
    a_sink = din("a_sink", [1, 8])
    rpbg = din("rpbg", [128, 7, 8, 128])
    colmask = din("colmask", [128, 128])
    alibia = din("alibia", [128, 3, 8, 128], BF16)
    rowmask = din("rowmask", [128, 2, 5, 7, 128], BF16)
    qkgain = din("qkgain", [1, 640])
    rope = din("rope", [TO, 64])
    lamv = din("lamv", [1, 4, 64])
    subln = din("subln", [128, 1])
    qtab = din("qtab", [4, 2, 4, TQ], BF16)
    ktab = din("ktab", [4, 4, TO], BF16)
    diagb = din("diagb", [128, 4, 128], BF16)
    yout = nc.dram_tensor("yout", [TQ, D], F32, kind="ExternalOutput").ap()

    FT0 = dscr("FT0", [21 * 128, TE], BF16)
    VT0 = dscr("VT0", [TE, 650], BF16)
    YT0 = dscr("YT0", [D, TO], BF16)
    X1 = dscr("X1", [TO, D], F32)
    FT1 = dscr("FT1", [21 * 128, TO], BF16)
    VT1 = dscr("VT1", [TO, 642], BF16)
    YT1 = dscr("YT1", [D, TQ], BF16)

    import contextlib
    es = contextlib.ExitStack()
    with es:
        sems = {}
        for e in Prog.ENG:
            sems[e] = es.enter_context(nc.semaphore("s_" + e))
        for q in ("sp", "pool"):
            for i in range(NSEM_DMA):
                sems[(q, i)] = es.enter_context(nc.semaphore("d_%s%d" % (q, i)))
        P = Prog(nc, sems)

        ucnt = [0]

        def sb(stack, name, shape, dt):
            ucnt[0] += 1
            return stack.enter_context(nc.sbuf_tensor("%s_u%d" % (name, ucnt[0]), list(shape), dt))

        def ps(stack, name, shape, dt=F32):
            ucnt[0] += 1
            return stack.enter_context(nc.psum_tensor("%s_u%d" % (name, ucnt[0]), list(shape), dt))

        ident = sb(es, "ident", [128, 128], BF16)
        identf = sb(es, "identf", [128, 128], F32)
        ones_f = sb(es, "ones_f", [128, 128], F32)
        zeros_b = sb(es, "zeros_b", [128, 512], BF16)
        gbc = sb(es, "gbc", [128, 4, D], F32)
        valid_sb = sb(es, "valid_sb", [128, NE], F32)
        ones10 = sb(es, "ones10", [128, 10], F32)

        eye = din("eye", [128, 128])
        P.dma("sp", lambda e: e.dma_start(out=identf[:], in_=eye[:, :]), w=["identf"])
        P.dve(lambda e: e.tensor_copy(out=ident[:], in_=identf[:]), r=["identf"], w=["ident"])
        P.dve(lambda e: e.memset(ones_f[:], 1.0), w=["ones_f"])
        P.dve(lambda e: e.memset(zeros_b[:], 0.0), w=["zeros_b"])
        P.dve(lambda e: e.memset(ones10[:], 1.0), w=["ones10"])
        P.dma("sp", lambda e: e.dma_start(out=valid_sb[:], in_=validin[:, :]), w=["valid"])
        for i in range(4):
            P.dma("sp", lambda e, i=i: e.dma_start(out=gbc[:, i, :], in_=gvec[i:i + 1, :].partition_broadcast(128)),
                  w=["gbc"])
        P.emit_block()

        def load_weight(stack_bufs, wdst, wsrc, kch, ncols, key, colchunk=1664):
            stg = stack_bufs
            j = 0
            for k in range(kch):
                for c0 in range(0, ncols, colchunk):
                    c1 = min(ncols, c0 + colchunk)
                    s = stg[j % 2]
                    sk = "wstg%d" % (j % 2)
                    P.dma("sp", lambda e, s=s, k=k, c0=c0, c1=c1: e.dma_start(
                        out=s[:, 0:c1 - c0], in_=wsrc[k * 128:(k + 1) * 128, c0:c1]), w=[sk])
                    if j % 2 == 0:
                        P.dve(lambda e, s=s, k=k, c0=c0, c1=c1: e.tensor_copy(out=wdst[:, k, c0:c1], in_=s[:, 0:c1 - c0]),
                              r=[sk], w=[key])
                    else:
                        P.pool(lambda e, s=s, k=k, c0=c0, c1=c1: e.tensor_copy(out=wdst[:, k, c0:c1], in_=s[:, 0:c1 - c0]),
                               r=[sk], w=[key])
                    j += 1

        def norm_block(xt, xk, gi, hn, hnk, ss, rstd, junk, tagk):
            P.act(lambda e: e.activation(out=junk[:], in_=xt, func=AF.Square, scale=1.0 / math.sqrt(D), accum_out=ss[:, 0:1]),
                  r=[xk], w=["junk", "ss" + tagk])
            P.dve(lambda e: e.tensor_scalar(out=rstd[:, 0:1], in0=ss[:, 0:1], scalar1=EPS, scalar2=None,
                                            op0=ALU.add), r=["ss" + tagk], w=["rstd" + tagk])
            P.act(lambda e: e.activation(out=rstd[:, 0:1], in_=rstd[:, 0:1], func=AF.Sqrt), r=["rstd" + tagk], w=["rstd" + tagk])
            P.dve(lambda e: e.reciprocal(out=rstd[:, 0:1], in_=rstd[:, 0:1]), r=["rstd" + tagk], w=["rstd" + tagk])
            P.dve(lambda e: e.scalar_tensor_tensor(out=hn, in0=xt, scalar=rstd[:, 0:1], in1=gbc[:, gi, :],
                                                   op0=ALU.mult, op1=ALU.mult),
                  r=[xk, "rstd" + tagk, "gbc"], w=[hnk])

        def transpose_to(src, srck, nk, tp, tpk, dst, dstk, use_act):
            for k in range(nk):
                P.pe(lambda e, k=k: e.transpose(out=tp[:, k, :], in_=src[:, k * 128:(k + 1) * 128], identity=ident[:]),
                     r=[srck, "ident"], w=[tpk])
            if use_act:
                P.act(lambda e: e.copy(out=dst, in_=tp[:, 0:nk, :]), r=[tpk], w=[dstk])
            else:
                P.dve(lambda e: e.tensor_copy(out=dst, in_=tp[:, 0:nk, :]), r=[tpk], w=[dstk])

        with contextlib.ExitStack() as st:
            w_in = sb(st, "w_in", [128, 8, 3328], BF16)
            wstg = [sb(st, "wstg0", [128, 1664], F32), sb(st, "wstg1", [128, 1664], F32)]
            xt = [sb(st, "xt%d" % i, [128, D], F32) for i in range(2)]
            hn = [sb(st, "hn%d" % i, [128, D], BF16) for i in range(2)]
            hnT = [sb(st, "hnT%d" % i, [128, 8, 512], BF16) for i in range(2)]
            junk = sb(st, "junk", [128, D], BF16)
            ss = sb(st, "ss", [128, 2], F32)
            rstd = sb(st, "rstd", [128, 2], F32)
            fstage = [sb(st, "fstage%d" % i, [128, 21, 512], BF16) for i in range(2)]
            vstage = [sb(st, "vstage%d" % i, [128, 10, 65], BF16) for i in range(2)]
            tp = [ps(st, "tp%d" % i, [128, 8, 128], BF16) for i in range(2)]
            fm = [ps(st, "fm%d" % i, [128, 512]) for i in range(2)]
            tmA = [ps(st, "tmA%d" % i, [128, 512]) for i in range(1)]
            tmB = [ps(st, "tmB%d" % i, [128, 128]) for i in range(1)]

            load_weight(wstg, w_in, w_in_ab, 8, 3328, "w_in")
            nst = NE // 4
            bc = 0
            for sti in range(nst):
                hT = hnT[sti % 2]
                hTk = "hnT%d" % (sti % 2)
                for b in range(4):
                    eb = sti * 4 + b
                    x_ = xt[bc % 2]
                    xk = "xt%d" % (bc % 2)
                    h_ = hn[bc % 2]
                    hk = "hn%d" % (bc % 2)
                    P.dma("sp", lambda e, x_=x_, eb=eb: e.dma_start(out=x_[:], in_=xin[eb * 128:(eb + 1) * 128, :]), w=[xk])
                    sx = ss[:, bc % 2:bc % 2 + 1]
                    rx = rstd[:, bc % 2:bc % 2 + 1]
                    norm_block(x_[:], xk, 0, h_[:], hk, sx, rx, junk, str(bc % 2))
                    t_ = tp[bc % 2]
                    transpose_to(h_, hk, 8, t_, "tp%d" % (bc % 2), hT[:, :, b * 128:(b + 1) * 128], hTk, bc % 2 == 0)
                    bc += 1
                fs = fstage[sti % 2]
                fsk = "fstage%d" % (sti % 2)
                ci = 0
                for (nm, c0, nch, kind) in AB_FM:
                    for j in range(nch):
                        f0 = c0 + j * 128
                        pf = fm[ci % 2]
                        pk = "fm%d" % (ci % 2)
                        for k in range(8):
                            P.pe(lambda e, pf=pf, k=k, f0=f0, hT=hT: e.matmul(pf[:], lhsT=w_in[:, k, f0:f0 + 128], rhs=hT[:, k, :],
                                                                         start=(k == 0), stop=(k == 7)),
                                 r=["w_in", hTk], w=[pk])
                        if kind == "g":
                            P.act(lambda e, pf=pf, ci=ci, fs=fs: e.activation(out=fs[:, ci, :], in_=pf[:], func=AF.Silu),
                                  r=[pk], w=[fsk])
                        elif kind == "q":
                            P.dve(lambda e, pf=pf, ci=ci, fs=fs: e.tensor_scalar(out=fs[:, ci, :], in0=pf[:], scalar1=0.125,
                                                                           scalar2=None, op0=ALU.mult),
                                  r=[pk], w=[fsk])
                        else:
                            P.dve(lambda e, pf=pf, ci=ci, fs=fs: e.tensor_copy(out=fs[:, ci, :], in_=pf[:]), r=[pk], w=[fsk])
                        ci += 1
                P.dma("pool", lambda e, fs=fs, sti=sti: e.dma_start(
                    out=FT0.rearrange("(c p) t -> p c t", p=128)[:, :, sti * 512:(sti + 1) * 512], in_=fs[:]), r=[fsk])
                for b in range(4):
                    eb = sti * 4 + b
                    vs = vstage[b % 2]
                    vk = "vstage%d" % (b % 2)
                    for k in range(8):
                        P.pe(lambda e, k=k, b=b, hT=hT: e.matmul(tmA[0][:], lhsT=hT[:, k, b * 128:(b + 1) * 128],
                                                           rhs=w_in[:, k, 2304:2816], start=(k == 0), stop=(k == 7)),
                             r=["w_in", hTk], w=["tmA"])
                    for k in range(8):
                        P.pe(lambda e, k=k, b=b, hT=hT: e.matmul(tmB[0][:], lhsT=hT[:, k, b * 128:(b + 1) * 128],
                                                           rhs=w_in[:, k, 640:768], start=(k == 0), stop=(k == 7)),
                             r=["w_in", hTk], w=["tmB"])
                    P.dve(lambda e, vs=vs: e.tensor_copy(out=vs[:, 2:10, 0:64], in_=tmA[0][:].rearrange("p (h d) -> p h d", d=64)),
                          r=["tmA"], w=[vk])
                    P.act(lambda e, vs=vs: e.copy(out=vs[:, 0:2, 0:64], in_=tmB[0][:].rearrange("p (h d) -> p h d", d=64)),
                          r=["tmB"], w=[vk])
                    P.dve(lambda e, vs=vs, eb=eb: e.tensor_scalar(out=vs[:, :, 64], in0=ones10[:], scalar1=valid_sb[:, eb:eb + 1],
                                                             scalar2=None, op0=ALU.mult), r=["valid", "ones10"], w=[vk])
                    P.dma("pool", lambda e, vs=vs, eb=eb: e.dma_start(out=VT0[eb * 128:(eb + 1) * 128, :],
                                                                 in_=vs[:].rearrange("p h d -> p (h d)")), r=[vk])
            P.emit_block()

        with contextlib.ExitStack() as st:
            rpbcol = sb(st, "rpbcol", [128, 7, 8, 128], BF16)
            rpbstg = sb(st, "rpbstg", [128, 7, 8, 128], F32)
            cmask = sb(st, "cmask", [128, 128], F32)
            alib = sb(st, "alib", [128, 3, 8, 128], BF16)
            rmask = sb(st, "rmask", [128, 2, 5, 7, 128], BF16)
            sinkrow = sb(st, "sinkrow", [65, 8, 128], F32)
            sinkv = sb(st, "sinkv", [65, 8], F32)
            KAb = [sb(st, "KA%d" % i, [128, 2, 7 * 128], BF16) for i in range(2)]
            KBb = [sb(st, "KB%d" % i, [128, 8, 7 * 128], BF16) for i in range(2)]
            VRb = [sb(st, "VR%d" % i, [128, 7, 650], BF16) for i in range(2)]
            QAb = [sb(st, "QA%d" % i, [128, 8, 128], BF16) for i in range(2)]
            QBb = [sb(st, "QB%d" % i, [128, 8, 128], BF16) for i in range(2)]
            GAb = [sb(st, "GA%d" % i, [64, 8, 128], BF16) for i in range(2)]
            GBb = [sb(st, "GB%d" % i, [64, 8, 128], BF16) for i in range(2)]
            PT = [sb(st, "PT%d" % i, [128, 3, 512], BF16) for i in range(2)]
            den2 = [sb(st, "den%d" % i, [65, 512], F32) for i in range(2)]
            rinv2 = [sb(st, "rinv%d" % i, [128, 512], F32) for i in range(2)]
            on = [sb(st, "on%d" % i, [64, 512], F32) for i in range(2)]
            ystage = [sb(st, "ystage%d" % i, [64, 16, 128], BF16) for i in range(2)]
            ST = [ps(st, "ST%d" % i, [128, 3, 512]) for i in range(2)]
            OT2 = [ps(st, "OT%d" % i, [128, 512]) for i in range(2)]

            for i_ in range(2):
                P.dve(lambda e, i_=i_: e.memset(rinv2[i_][:], 0.0), w=["rinv%d" % i_])
            for i_ in range(2):
                P.dve(lambda e, i_=i_: e.memset(KAb[i_][64:128], 0.0), w=["kv%d" % i_])
                P.pool(lambda e, i_=i_: e.memset(KBb[i_][64:128], 0.0), w=["kv%d" % i_])
                P.dve(lambda e, i_=i_: e.memset(QAb[i_][64:128], 0.0), w=["qg%d" % i_])
                P.dve(lambda e, i_=i_: e.memset(QBb[i_][64:128], 0.0), w=["qg%d" % i_])
            P.dma("sp", lambda e: e.dma_start(out=rpbstg[:], in_=rpbg[:, :, :, :]), w=["rpbstg"])
            P.dma("sp", lambda e: e.dma_start(out=cmask[:], in_=colmask[:, :]), w=["cmask"])
            P.dma("sp", lambda e: e.dma_start(out=alib[:], in_=alibia[:, :, :, :]), w=["alib"])
            P.dma("sp", lambda e: e.dma_start(out=rmask[:], in_=rowmask[:, :, :, :, :]), w=["rmask"])
            P.dma("sp", lambda e: e.dma_start(out=sinkv[64:65, :], in_=a_sink[0:1, :]), w=["sinkv"])
            for r_ in range(7):
                for h in range(8):
                    P.dve(lambda e, r_=r_, h=h: e.tensor_tensor(out=rpbcol[:, r_, h, :], in0=rpbstg[:, r_, h, :], in1=cmask[:],
                                                             op=ALU.add), r=["rpbstg", "cmask"], w=["rpbcol"])
            P.act(lambda e: e.activation(out=sinkv[64:65, :], in_=sinkv[64:65, :], func=AF.Exp), r=["sinkv"], w=["sinkv"])
            P.dve(lambda e: e.tensor_copy(out=sinkrow[64:65, :, :], in_=sinkv[64:65, :].unsqueeze(2).to_broadcast([1, 8, 128])),
                  r=["sinkv"], w=["sinkrow"])

            FT0v = FT0.rearrange("(c h d) t -> d c h t", h=2, d=64)

            def block_gen(seg, i, bi):
                s2 = bi % 2
                tab = 1 if seg["name"] == "PO" else 0
                eb = seg["e0"] + PAD + i
                ob = seg["o0"] + i
                ka, kb_, vr = KAb[s2], KBb[s2], VRb[s2]
                qa, qb, ga, gb_ = QAb[s2], QBb[s2], GAb[s2], GBb[s2]
                S_ = ST[s2]; Sk = "ST%d" % s2
                p_ = PT[s2]; pk = "PT%d" % s2
                O_ = OT2[s2]; Ok = "OT%d" % s2
                o_ = on[s2]; ok_ = "on%d" % s2
                dn = den2[s2]; dk = "den%d" % s2
                rv = rinv2[s2]; rk_ = "rinv%d" % s2
                ys = ystage[s2]; ysk = "ystage%d" % s2
                t0 = (eb - 3) * 128
                t1 = (eb + 4) * 128
                kk = "kv%d" % s2
                qk = "qg%d" % s2
                P.dma("sp", lambda e: e.dma_start(out=ka[0:64], in_=FT0v[:, 4, :, t0:t1]), w=[kk])
                P.dma("sp", lambda e: e.dma_start(out=kb_[0:64].rearrange("d (c h) t -> d c h t", h=2), in_=FT0v[:, 13:17, :, t0:t1]), w=[kk])
                P.dma("sp", lambda e: e.dma_start(out=vr[:], in_=VT0[t0:t1, :].rearrange("(b p) f -> p b f", p=128)), w=[kk])
                q0 = eb * 128
                P.dma("pool", lambda e: e.dma_start(out=qa[0:64].rearrange("d (c h) t -> d c h t", h=2), in_=FT0v[:, 0:4, :, q0:q0 + 128]), w=[qk])
                P.dma("sp", lambda e: e.dma_start(out=ga[:].rearrange("d (c h) t -> d c h t", h=2), in_=FT0v[:, 5:9, :, q0:q0 + 128]), w=[qk])
                P.dma("pool", lambda e: e.dma_start(out=qb[0:64].rearrange("d (c h) t -> d c h t", h=2), in_=FT0v[:, 9:13, :, q0:q0 + 128]), w=[qk])
                P.dma("sp", lambda e: e.dma_start(out=gb_[:].rearrange("d (c h) t -> d c h t", h=2), in_=FT0v[:, 17:21, :, q0:q0 + 128]), w=[qk])
                yield
                cl = blk_class(seg, i)
                rels_b = blk_rels(seg, i)
                for job in range(4):
                    isA = job < 2
                    g = job % 2
                    rels = [-1, 0, 1] if isA else rels_b
                    batches = [rels[j:j + 3] for j in range(0, len(rels), 3)]
                    P.pe(lambda e: e.matmul(O_[0:65, :], lhsT=zeros_b[:, 0:65], rhs=zeros_b[:, :], start=True, stop=True),
                         r=["zeros_b"], w=[Ok])
                    for bt in batches:
                        for j, rel in enumerate(bt):
                            kof = (rel + 3) * 128
                            if isA:
                                P.pe(lambda e, g=g, j=j, rel=rel: e.matmul(S_[:, j, :], lhsT=ident[:], rhs=alib[:, rel + 1, 4 * g:4 * g + 4, :],
                                                                      start=True, stop=False), r=["ident", "alib"], w=[Sk])
                                P.pe(lambda e, g=g, j=j, kof=kof: e.matmul(S_[:, j, :], lhsT=ka[:, g, kof:kof + 128], rhs=qa[:, 4 * g:4 * g + 4, :],
                                                                      start=False, stop=True), r=[kk, qk], w=[Sk])
                            else:
                                P.pe(lambda e, g=g, j=j, rel=rel: e.matmul(S_[:, j, :], lhsT=ident[:], rhs=rpbcol[:, rel + 3, 4 * g:4 * g + 4, :],
                                                                      start=True, stop=False), r=["ident", "rpbcol"], w=[Sk])
                                P.pe(lambda e, g=g, j=j, rel=rel: e.matmul(S_[:, j, :], lhsT=ident[:],
                                                                           rhs=rmask[:, tab, cl, rel + 3, :].unsqueeze(1).to_broadcast([128, 4, 128]),
                                                                           start=False, stop=False),
                                     r=["ident", "rmask"], w=[Sk])
                                for h in range(4):
                                    P.pe(lambda e, g=g, j=j, kof=kof, h=h: e.matmul(S_[:, j, h * 128:(h + 1) * 128], lhsT=kb_[:, 4 * g + h, kof:kof + 128],
                                                                               rhs=qb[:, 4 * g + h, :], start=False, stop=(h == 3)),
                                         r=[kk, qk], w=[Sk])
                        yield
                        nb_ = len(bt)
                        P.act(lambda e, g=g, nb_=nb_: e.activation(out=p_[:, 0:nb_, :], in_=S_[:, 0:nb_, :], func=AF.Exp), r=[Sk], w=[pk])
                        for j, rel in enumerate(bt):
                            ko = rel + 3
                            if isA:
                                P.pe(lambda e, g=g, j=j, ko=ko: e.matmul(O_[0:65, :], lhsT=vr[:, ko, g * 65:(g + 1) * 65], rhs=p_[:, j, :],
                                                                    start=False, stop=True), r=[kk, pk], w=[Ok])
                            else:
                                for h in range(4):
                                    hv = 2 + 4 * g + h
                                    P.pe(lambda e, g=g, j=j, ko=ko, hv=hv, h=h: e.matmul(O_[0:65, h * 128:(h + 1) * 128], lhsT=vr[:, ko, hv * 65:(hv + 1) * 65],
                                                                                    rhs=p_[:, j, h * 128:(h + 1) * 128], start=False, stop=True),
                                         r=[kk, pk], w=[Ok])
                        yield
                    if isA:
                        P.dve(lambda e, g=g: e.tensor_tensor(out=dn[64:65, :], in0=O_[64:65, :], in1=sinkrow[64:65, 4 * g:4 * g + 4, :], op=ALU.add),
                              r=[Ok, "sinkrow"], w=[dk])
                    else:
                        P.dve(lambda e: e.tensor_copy(out=dn[64:65, :], in_=O_[64:65, :]), r=[Ok], w=[dk])
                    P.dve(lambda e: e.reciprocal(out=rv[64:65, :], in_=dn[64:65, :]), r=[dk], w=[rk_])
                    gsrc = ga if isA else gb_
                    P.dve(lambda e, g=g, gsrc=gsrc: e.tensor_tensor(out=o_[:], in0=O_[0:64, :], in1=gsrc[:, 4 * g:4 * g + 4, :], op=ALU.mult),
                          r=[Ok, qk], w=[ok_])
                    yield
                    P.pe(lambda e: e.matmul(S_[0:64, 0, :], lhsT=ones_f[:, 0:64], rhs=rv[:, :], start=True, stop=True),
                         r=["ones_f", rk_], w=[Sk])
                    hb = (0 if isA else 8) + 4 * g
                    P.dve(lambda e, g=g, hb=hb: e.tensor_tensor(out=ys[:, hb:hb + 4, :], in0=o_[:], in1=S_[0:64, 0, :], op=ALU.mult),
                          r=[ok_, Sk], w=[ysk])
                    yield
                P.dma("pool", lambda e: e.dma_start(out=YT0.rearrange("(h d) t -> d h t", d=64)[:, :, ob * 128:(ob + 1) * 128], in_=ys[:]), r=[ysk])

            blocks = [(seg, i) for seg in segs for i in range(seg["nb"])]
            for p0 in range(0, len(blocks), 2):
                live = [block_gen(blocks[p0 + s_][0], blocks[p0 + s_][1], p0 + s_) for s_ in range(2) if p0 + s_ < len(blocks)]
                while live:
                    for g_ in list(live):
                        try:
                            next(g_)
                        except StopIteration:
                            live.remove(g_)
            P.emit_block()

        if dbg == 1:
            return nc
        _phase345(nc, P, cfg, sb, ps, dbg=dbg, G=dict(
            ident=ident, ones_f=ones_f, zeros_b=zeros_b, gbc=gbc, xin=xin, pin=pin, w_out_ab=w_out_ab, w_in_cd=w_in_cd,
            w_out_cd=w_out_cd, w_ple=w_ple, w_gate=w_gate, qkgain=qkgain, rope=rope, lamv=lamv, subln=subln, qtab=qtab,
            ktab=ktab, diagb=diagb, yout=yout, YT0=YT0, X1=X1, FT1=FT1, VT1=VT1, YT1=YT1,
            load_weight=load_weight, norm_block=norm_block, transpose_to=transpose_to))
    return nc


LAM_INIT1 = 0.8 - 0.6 * math.exp(-0.3 * 1)


def _phase345(nc, P, cfg, sb, ps, G, dbg=0):
    import contextlib
    segs = cfg["segs"]
    ident, ones_f, zeros_b, gbc = G["ident"], G["ones_f"], G["zeros_b"], G["gbc"]
    pin = G["pin"]
    load_weight, norm_block, transpose_to = G["load_weight"], G["norm_block"], G["transpose_to"]
    FT1, VT1, X1, YT1 = G["FT1"], G["VT1"], G["X1"], G["YT1"]

    def out_phase(layer, YT, xsrc_fn, dst_fn, seglist, w_out_d, tok_fn):
        with contextlib.ExitStack() as st:
            wout = sb(st, "wout", [128, 8, D], BF16)
            wg = sb(st, "wg", [128, 8, D], BF16)
            wp = sb(st, "wp", [128, 2, D], BF16)
            wstg = [sb(st, "wstg0", [128, 1024], F32), sb(st, "wstg1", [128, 1024], F32)]
            yts = [sb(st, "yts%d" % i, [128, 8, 512], BF16) for i in range(2)]
            xt = [sb(st, "xt%d" % i, [128, D], F32) for i in range(2)]
            psb = [sb(st, "psb%d" % i, [128, 256], F32) for i in range(2)]
            pb = sb(st, "pb", [128, 256], BF16)
            tmix = sb(st, "tmix", [128, D], F32)
            xa = sb(st, "xa", [128, D], F32)
            xab = sb(st, "xab", [128, D], BF16)
            xaT = sb(st, "xaT", [128, 8, 128], BF16)
            pT = sb(st, "pT", [128, 2, 128], BF16)
            sg = sb(st, "sg", [128, D], F32)
            xo = [sb(st, "xo%d" % i, [128, D], F32) for i in range(2)]
            junk = sb(st, "junk", [128, D], BF16)
            ss = sb(st, "ss", [128, 2], F32)
            mix = ps(st, "mix", [128, D])
            gate = ps(st, "gate", [128, D])
            pp = ps(st, "pp", [128, D])
            tp = ps(st, "tp", [128, 8, 128], BF16)
            tp2 = ps(st, "tp2", [128, 8, 128], BF16)
            load_weight(wstg, wout, w_out_d, 8, D, "wout", colchunk=1024)
            load_weight(wstg, wg, G["w_gate"][layer], 8, D, "wg", colchunk=1024)
            load_weight(wstg, wp, G["w_ple"][layer], 2, D, "wp", colchunk=1024)
            gi = 1 + 2 * layer
            bc = 0
            sc = 0
            for seg in seglist:
                for sti in range(seg["nb"] // 4):
                    y_ = yts[sc % 2]
                    yk = "yts%d" % (sc % 2)
                    ot0 = tok_fn(seg, sti * 4)
                    P.dma("sp", lambda e, y_=y_, ot0=ot0: e.dma_start(
                        out=y_[:], in_=YT.rearrange("(k p) t -> p k t", p=128)[:, :, ot0:ot0 + 512]), w=[yk])
                    sc += 1
                    for b in range(4):
                        i = sti * 4 + b
                        x_ = xt[bc % 2]; xk = "xt%d" % (bc % 2)
                        p_ = psb[bc % 2]; pk = "psb%d" % (bc % 2)
                        o_ = xo[bc % 2]; ok_ = "xo%d" % (bc % 2)
                        xs_ap = xsrc_fn(seg, i)
                        po = (seg["o0"] + i) * 128
                        P.dma("sp", lambda e, x_=x_, xs_ap=xs_ap: e.dma_start(out=x_[:], in_=xs_ap), w=[xk])
                        P.dma("sp", lambda e, p_=p_, po=po: e.dma_start(out=p_[:], in_=pin[layer, po:po + 128, :]), w=[pk])
                        for half in range(2):
                            for k in range(8):
                                P.pe(lambda e, y_=y_, k=k, b=b, half=half: e.matmul(
                                    mix[:, half * 512:(half + 1) * 512], lhsT=y_[:, k, b * 128:(b + 1) * 128],
                                    rhs=wout[:, k, half * 512:(half + 1) * 512], start=(k == 0), stop=(k == 7)),
                                    r=[yk, "wout"], w=["mix"])
                        sx = ss[:, 0:1]
                        P.act(lambda e: e.activation(out=junk[:], in_=mix[:], func=AF.Square, scale=1.0 / math.sqrt(D),
                                                     accum_out=sx), r=["mix"], w=["junk", "ss"])
                        P.dve(lambda e: e.tensor_scalar(out=sx, in0=sx, scalar1=EPS, scalar2=None, op0=ALU.add), r=["ss"], w=["ss"])
                        P.act(lambda e: e.activation(out=sx, in_=sx, func=AF.Sqrt), r=["ss"], w=["ss"])
                        P.dve(lambda e: e.reciprocal(out=sx, in_=sx), r=["ss"], w=["ss"])
                        P.dve(lambda e: e.scalar_tensor_tensor(out=tmix[:], in0=mix[:], scalar=sx, in1=gbc[:, gi, :],
                                                               op0=ALU.mult, op1=ALU.mult), r=["mix", "ss", "gbc"], w=["tmix"])
                        P.pool(lambda e, x_=x_: e.tensor_tensor(out=xa[:], in0=x_[:], in1=tmix[:], op=ALU.add),
                               r=[xk, "tmix"], w=["xa"])
                        P.act(lambda e: e.copy(out=xab[:], in_=xa[:]), r=["xa"], w=["xab"])
                        transpose_to(xab, "xab", 8, tp, "tp", xaT[:], "xaT", False)
                        P.act(lambda e, p_=p_: e.copy(out=pb[:], in_=p_[:]), r=[pk], w=["pb"])
                        transpose_to(pb, "pb", 2, tp2, "tp2", pT[:], "pT", False)
                        for half in range(2):
                            for k in range(8):
                                P.pe(lambda e, k=k, half=half: e.matmul(
                                    gate[:, half * 512:(half + 1) * 512], lhsT=xaT[:, k, :],
                                    rhs=wg[:, k, half * 512:(half + 1) * 512], start=(k == 0), stop=(k == 7)),
                                    r=["xaT", "wg"], w=["gate"])
                            for k in range(2):
                                P.pe(lambda e, k=k, half=half: e.matmul(
                                    pp[:, half * 512:(half + 1) * 512], lhsT=pT[:, k, :],
                                    rhs=wp[:, k, half * 512:(half + 1) * 512], start=(k == 0), stop=(k == 1)),
                                    r=["pT", "wp"], w=["pp"])
                        P.act(lambda e: e.activation(out=sg[:], in_=gate[:], func=AF.Sigmoid), r=["gate"], w=["sg"])
                        P.dve(lambda e: e.tensor_tensor(out=tmix[:], in0=sg[:], in1=pp[:], op=ALU.mult), r=["sg", "pp"], w=["tmix"])
                        P.pool(lambda e, o_=o_: e.tensor_tensor(out=o_[:], in0=xa[:], in1=tmix[:], op=ALU.add),
                               r=["xa", "tmix"], w=[ok_])
                        d_ap = dst_fn(seg, i)
                        P.dma("pool", lambda e, o_=o_, d_ap=d_ap: e.dma_start(out=d_ap, in_=o_[:]), r=[ok_])
                        bc += 1
            P.emit_block()

    xin = G["xin"]
    out_phase(0, G["YT0"],
              lambda seg, i: xin[(seg["e0"] + PAD + i) * 128:(seg["e0"] + PAD + i + 1) * 128, :],
              lambda seg, i: X1[(seg["o0"] + i) * 128:(seg["o0"] + i + 1) * 128, :],
              segs, G["w_out_ab"], lambda seg, i: (seg["o0"] + i) * 128)
    if dbg == 2:
        return

    with contextlib.ExitStack() as st:
        w_in = sb(st, "w_in", [128, 8, 3328], BF16)
        wstg = [sb(st, "wstg0", [128, 1664], F32), sb(st, "wstg1", [128, 1664], F32)]
        xt = [sb(st, "xt%d" % i, [128, D], F32) for i in range(2)]
        hn = [sb(st, "hn%d" % i, [128, D], BF16) for i in range(2)]
        hnT = [sb(st, "hnT%d" % i, [128, 8, 512], BF16) for i in range(2)]
        junk = sb(st, "junk", [128, D], BF16)
        ss = sb(st, "ss", [128, 2], F32)
        rstd = sb(st, "rstd", [128, 2], F32)
        fstage = sb(st, "fstage", [128, 16, 512], BF16)
        qkts = sb(st, "qkts", [128, 5, 512], BF16)
        vst = [sb(st, "vst%d" % i, [128, 642], BF16) for i in range(2)]
        qkf = sb(st, "qkf", [128, 10, 64], F32)
        sqj = sb(st, "sqj", [128, 10, 64], F32)
        ssq = sb(st, "ssq", [128, 10], F32)
        qn = sb(st, "qn", [128, 10, 64], F32)
        ra = sb(st, "ra", [128, 10, 32], F32)
        rb = sb(st, "rb", [128, 10, 32], F32)
        qr = sb(st, "qr", [128, 640], BF16)
        gain = sb(st, "gain", [128, 10, 64], F32)
        rp = [sb(st, "rp%d" % i, [128, 64], F32) for i in range(2)]
        tp = [ps(st, "tp%d" % i, [128, 8, 128], BF16) for i in range(2)]
        fm = [ps(st, "fm%d" % i, [128, 512]) for i in range(2)]
        tqA = ps(st, "tqA", [128, 512])
        tqB = ps(st, "tqB", [128, 256])
        tvD = ps(st, "tvD", [128, 512])
        load_weight(wstg, w_in, G["w_in_cd"], 8, 3328, "w_in")
        P.dma("sp", lambda e: e.dma_start(out=gain[:].rearrange("p h d -> p (h d)"), in_=G["qkgain"][0:1, :].partition_broadcast(128)), w=["gain"])
        P.dve(lambda e: e.tensor_scalar(out=gain[:, 0:8, :], in0=gain[:, 0:8, :], scalar1=0.125, scalar2=None, op0=ALU.mult),
              r=["gain"], w=["gain"])
        for i in range(2):
            P.dve(lambda e, i=i: e.memset(vst[i][:], 1.0), w=["vst%d" % i])
        bc = 0
        sc = 0
        rope = G["rope"]
        for seg in segs:
            pf = seg["name"] == "PF"
            h0 = 8 if pf else 0
            for sti in range(seg["nb"] // 4):
                hT = hnT[sc % 2]; hTk = "hnT%d" % (sc % 2)
                ot0 = (seg["o0"] + sti * 4) * 128
                for b in range(4):
                    ob = seg["o0"] + sti * 4 + b
                    x_ = xt[bc % 2]; xk = "xt%d" % (bc % 2)
                    h_ = hn[bc % 2]; hk = "hn%d" % (bc % 2)
                    P.dma("sp", lambda e, x_=x_, ob=ob: e.dma_start(out=x_[:], in_=X1[ob * 128:(ob + 1) * 128, :]), w=[xk])
                    norm_block(x_[:], xk, 2, h_[:], hk, ss[:, bc % 2:bc % 2 + 1], rstd[:, bc % 2:bc % 2 + 1], junk, str(bc % 2))
                    transpose_to(h_, hk, 8, tp[bc % 2], "tp%d" % (bc % 2), hT[:, :, b * 128:(b + 1) * 128], hTk, bc % 2 == 0)
                    bc += 1
                lst = [("kd", 1792, 4, "k")] if pf else CD_FM
                ci = 0
                for (nm, c0, nch, kind) in lst:
                    for j in range(nch):
                        f0 = c0 + j * 128
                        pf_ = fm[ci % 2]; pk = "fm%d" % (ci % 2)
                        for k in range(8):
                            P.pe(lambda e, pf_=pf_, k=k, f0=f0, hT=hT: e.matmul(pf_[:], lhsT=w_in[:, k, f0:f0 + 128], rhs=hT[:, k, :],
                                                                           start=(k == 0), stop=(k == 7)), r=["w_in", hTk], w=[pk])
                        if kind == "g":
                            P.act(lambda e, pf_=pf_, ci=ci: e.activation(out=fstage[:, ci, :], in_=pf_[:], func=AF.Silu), r=[pk], w=["fstage"])
                        elif kind == "q":
                            P.dve(lambda e, pf_=pf_, ci=ci: e.tensor_scalar(out=fstage[:, ci, :], in0=pf_[:], scalar1=0.125, scalar2=None,
                                                                       op0=ALU.mult), r=[pk], w=["fstage"])
                        else:
                            P.dve(lambda e, pf_=pf_, ci=ci: e.tensor_copy(out=fstage[:, ci, :], in_=pf_[:]), r=[pk], w=["fstage"])
                        ci += 1
                FT1v = FT1.rearrange("(c p) t -> p c t", p=128)
                if pf:
                    P.dma("pool", lambda e, ot0=ot0: e.dma_start(out=FT1v[:, 13:17, ot0:ot0 + 512], in_=fstage[:, 0:4, :]), r=["fstage"])
                else:
                    P.dma("pool", lambda e, ot0=ot0: e.dma_start(out=FT1v[:, 5:21, ot0:ot0 + 512], in_=fstage[:, 0:16, :]), r=["fstage"])
                for b in range(4):
                    ob = seg["o0"] + sti * 4 + b
                    vs = vst[b % 2]; vk = "vst%d" % (b % 2)
                    r_ = rp[b % 2]; rk = "rp%d" % (b % 2)
                    P.dma("sp", lambda e, r_=r_, ob=ob: e.dma_start(out=r_[:], in_=rope[ob * 128:(ob + 1) * 128, :]), w=[rk])
                    if not pf:
                        for k in range(8):
                            P.pe(lambda e, k=k, b=b, hT=hT: e.matmul(tqA[:], lhsT=hT[:, k, b * 128:(b + 1) * 128], rhs=w_in[:, k, 0:512],
                                                               start=(k == 0), stop=(k == 7)), r=["w_in", hTk], w=["tqA"])
                    for k in range(8):
                        P.pe(lambda e, k=k, b=b, hT=hT: e.matmul(tqB[:], lhsT=hT[:, k, b * 128:(b + 1) * 128], rhs=w_in[:, k, 512:768],
                                                           start=(k == 0), stop=(k == 7)), r=["w_in", hTk], w=["tqB"])
                    for k in range(8):
                        P.pe(lambda e, k=k, b=b, hT=hT: e.matmul(tvD[:], lhsT=hT[:, k, b * 128:(b + 1) * 128], rhs=w_in[:, k, 2304:2816],
                                                           start=(k == 0), stop=(k == 7)), r=["w_in", hTk], w=["tvD"])
                    P.dve(lambda e, vs=vs: e.tensor_copy(out=vs[:, 0:130].rearrange("p (h d) -> p h d", d=65)[:, :, 0:64],
                                                         in_=tqB[:, 128:256].rearrange("p (h d) -> p h d", d=64)), r=["tqB"], w=[vk])
                    P.act(lambda e, vs=vs: e.copy(out=vs[:, 130:642], in_=tvD[:]), r=["tvD"], w=[vk])
                    P.dma("pool", lambda e, vs=vs, ob=ob: e.dma_start(out=VT1[ob * 128:(ob + 1) * 128, :], in_=vs[:]), r=[vk])
                    if not pf:
                        P.act(lambda e: e.copy(out=qkf[:, 0:8, :], in_=tqA[:].rearrange("p (h d) -> p h d", d=64)), r=["tqA"], w=["qkf"])
                    P.act(lambda e: e.copy(out=qkf[:, 8:10, :], in_=tqB[:, 0:128].rearrange("p (h d) -> p h d", d=64)), r=["tqB"], w=["qkf"])
                    hs = slice(h0, 10)
                    nh = 10 - h0
                    P.dve(lambda e, hs=hs: e.tensor_tensor(out=sqj[:, hs, :], in0=qkf[:, hs, :], in1=qkf[:, hs, :], op=ALU.mult), r=["qkf"], w=["sqj"])
                    P.dve(lambda e, hs=hs: e.tensor_reduce(out=ssq[:, hs], in_=sqj[:, hs, :], axis=mybir.AxisListType.X, op=ALU.add),
                          r=["sqj"], w=["ssq"])
                    P.dve(lambda e, hs=hs: e.tensor_scalar(out=ssq[:, hs], in0=ssq[:, hs], scalar1=1.0 / 64, scalar2=EPS, op0=ALU.mult, op1=ALU.add),
                          r=["ssq"], w=["ssq"])
                    P.act(lambda e, hs=hs: e.activation(out=ssq[:, hs], in_=ssq[:, hs], func=AF.Sqrt), r=["ssq"], w=["ssq"])
                    P.dve(lambda e, hs=hs: e.reciprocal(out=ssq[:, hs], in_=ssq[:, hs]), r=["ssq"], w=["ssq"])
                    P.dve(lambda e, hs=hs, nh=nh: e.tensor_tensor(out=qn[:, hs, :], in0=qkf[:, hs, :],
                                                                in1=ssq[:, hs].unsqueeze(2).to_broadcast([128, nh, 64]), op=ALU.mult),
                          r=["qkf", "ssq"], w=["qn"])
                    P.dve(lambda e, hs=hs: e.tensor_tensor(out=qn[:, hs, :], in0=qn[:, hs, :], in1=gain[:, hs, :], op=ALU.mult),
                          r=["qn", "gain"], w=["qn"])
                    qv = qn[:].rearrange("p h (j two) -> p h j two", two=2)
                    qrv = qr[:].rearrange("p (h j two) -> p h j two", two=2, j=32)
                    cs = lambda r_=r_, nh=nh: r_[:, 0:32].unsqueeze(1).to_broadcast([128, nh, 32])
                    sn = lambda r_=r_, nh=nh: r_[:, 32:64].unsqueeze(1).to_broadcast([128, nh, 32])
                    P.dve(lambda e, hs=hs, cs=cs: e.tensor_tensor(out=ra[:, hs, :], in0=qv[:, hs, :, 0], in1=cs(), op=ALU.mult), r=["qn", rk], w=["ra"])
                    P.dve(lambda e, hs=hs, sn=sn: e.tensor_tensor(out=rb[:, hs, :], in0=qv[:, hs, :, 1], in1=sn(), op=ALU.mult), r=["qn", rk], w=["rb"])
                    P.dve(lambda e, hs=hs: e.tensor_tensor(out=qrv[:, hs, :, 0], in0=ra[:, hs, :], in1=rb[:, hs, :], op=ALU.subtract),
                          r=["ra", "rb"], w=["qr"])
                    P.dve(lambda e, hs=hs, sn=sn: e.tensor_tensor(out=ra[:, hs, :], in0=qv[:, hs, :, 0], in1=sn(), op=ALU.mult), r=["qn", rk], w=["ra"])
                    P.dve(lambda e, hs=hs, cs=cs: e.tensor_tensor(out=rb[:, hs, :], in0=qv[:, hs, :, 1], in1=cs(), op=ALU.mult), r=["qn", rk], w=["rb"])
                    P.dve(lambda e, hs=hs: e.tensor_tensor(out=qrv[:, hs, :, 1], in0=ra[:, hs, :], in1=rb[:, hs, :], op=ALU.add),
                          r=["ra", "rb"], w=["qr"])
                    c0_ = 4 if pf else 0
                    t_ = tp[b % 2]; tk = "tp%d" % (b % 2)
                    for k in range(c0_, 5):
                        P.pe(lambda e, k=k, t_=t_: e.transpose(out=t_[:, k, :], in_=qr[:, k * 128:(k + 1) * 128], identity=ident[:]),
                             r=["qr", "ident"], w=[tk])
                    P.dve(lambda e, t_=t_, c0_=c0_, b=b: e.tensor_copy(out=qkts[:, c0_:5, b * 128:(b + 1) * 128], in_=t_[:, c0_:5, :]),
                          r=[tk], w=["qkts"])
                c0_ = 4 if pf else 0
                P.dma("pool", lambda e, ot0=ot0, c0_=c0_: e.dma_start(out=FT1v[:, c0_:5, ot0:ot0 + 512], in_=qkts[:, c0_:5, :]), r=["qkts"])
                sc += 1
        P.emit_block()
    if dbg == 3:
        return
    _phase45(nc, P, cfg, sb, ps, G, out_phase)


def _phase45(nc, P, cfg, sb, ps, G, out_phase):
    import contextlib
    segs = cfg["segs"]
    SEG_S, SEG_PO, SEG_PF = segs
    ones_f = G["ones_f"]
    FT1, VT1, X1, YT1 = G["FT1"], G["VT1"], G["X1"], G["YT1"]
    qtab, ktab = G["qtab"], G["ktab"]
    NBS = cfg["NBS"]
    with contextlib.ExitStack() as st:
        nkbmax = max(SEG_S["nb"], SEG_PF["nb"])
        Kb = [sb(st, "Kb%d" % i, [128, nkbmax * 128], BF16) for i in range(2)]
        Vb = sb(st, "Vb", [128, nkbmax * 128], BF16)
        Qb = [sb(st, "Qb%d" % i, [128, 2, 512], BF16) for i in range(2)]
        Gb = [sb(st, "Gb%d" % i, [128, 512], BF16) for i in range(2)]
        PT = [sb(st, "PT%d" % i, [128, 2, 512], BF16) for i in range(3)]
        Mb = [sb(st, "Mb%d" % i, [128, 512], F32) for i in range(2)]
        rinv = sb(st, "rinv", [128, 512], F32)
        rinvz = sb(st, "rinvz", [128, 512], F32)
        acc = sb(st, "acc", [128, 2, 512], F32)
        on_ = sb(st, "on_", [128, 512], F32)
        o0 = sb(st, "o0", [128, 512], F32)
        od = sb(st, "od", [128, 512], F32)
        sq = sb(st, "sq", [128, 512], F32)
        ybuf = [sb(st, "ybuf%d" % i, [128, 512], BF16) for i in range(2)]
        ones_b = sb(st, "ones_b", [128, 128], BF16)
        lv = sb(st, "lv", [1, 4, 64], F32)
        pr = sb(st, "pr", [1, 2, 64], F32)
        ls = sb(st, "ls", [1, 4], F32)
        nl = sb(st, "nl", [128, 1], F32)
        gsc = sb(st, "gsc", [128, 1], F32)
        SS = [ps(st, "SS%d" % i, [128, 2, 512]) for i in range(3)]
        OTd = [ps(st, "OTd%d" % i, [128, 512]) for i in range(1)]
        LB = [ps(st, "LB%d" % i, [128, 512]) for i in range(1)]

        P.dve(lambda e: e.memset(ones_b[:], 1.0), w=["ones_b"])
        P.dve(lambda e: e.memset(rinvz[:], 0.0), w=["rinvz"])
        for i_ in range(2):
            P.dve(lambda e, i_=i_: e.memset(Qb[i_][64:128], 0.0), w=["Q%d" % i_])
        P.dma("sp", lambda e: e.dma_start(out=lv[:], in_=G["lamv"][:, :, :]), w=["lv"])
        P.dma("sp", lambda e: e.dma_start(out=gsc[:], in_=G["subln"][:, :]), w=["gsc"])
        P.dve(lambda e: e.tensor_scalar(out=gsc[:], in0=gsc[:], scalar1=(1.0 - LAM_INIT1), scalar2=None, op0=ALU.mult), r=["gsc"], w=["gsc"])
        P.dve(lambda e: e.tensor_tensor(out=pr[:, 0, :], in0=lv[:, 0, :], in1=lv[:, 1, :], op=ALU.mult), r=["lv"], w=["pr"])
        P.dve(lambda e: e.tensor_tensor(out=pr[:, 1, :], in0=lv[:, 2, :], in1=lv[:, 3, :], op=ALU.mult), r=["lv"], w=["pr"])
        P.dve(lambda e: e.tensor_reduce(out=ls[:, 0:2], in_=pr[:], axis=mybir.AxisListType.X, op=ALU.add), r=["pr"], w=["ls"])
        P.act(lambda e: e.activation(out=ls[:, 0:2], in_=ls[:, 0:2], func=AF.Exp), r=["ls"], w=["ls"])
        P.dve(lambda e: e.tensor_tensor(out=ls[:, 2:3], in0=ls[:, 1:2], in1=ls[:, 0:1], op=ALU.subtract), r=["ls"], w=["ls"])
        P.dve(lambda e: e.tensor_scalar(out=ls[:, 3:4], in0=ls[:, 2:3], scalar1=-LAM_INIT1, scalar2=None, op0=ALU.add), r=["ls"], w=["ls"])
        P.pe(lambda e: e.matmul(LB[0][:, 0:1], lhsT=ones_f[0:1, 0:128], rhs=ls[0:1, 3:4], start=True, stop=True), r=["ones_f", "ls"], w=["LB0"])
        P.dve(lambda e: e.tensor_copy(out=nl[:], in_=LB[0][:, 0:1]), r=["LB0"], w=["nl"])

        gg = 0
        job = 0
        qj = 0
        gj = 0
        NSS = 3

        def run_units(units, qk, ex, pv):
            nonlocal gg
            n = len(units)
            LA = 2
            for i in range(min(LA, n)):
                qk(units[i], gg + i)
            for i in range(LA, n):
                qk(units[i], gg + i)
                ex(units[i - LA], gg + i - LA)
                pv(units[i - LA], gg + i - LA)
            for i in range(max(0, n - LA), n):
                ex(units[i], gg + i)
                pv(units[i], gg + i)
            gg += n

        for seg, qbase, kseg in ((SEG_S, 0, SEG_S), (SEG_PO, NBS, SEG_PF)):
            nkb = kseg["nb"]
            ko = kseg["o0"] * 128
            nch = seg["nb"] // 4
            static_sign = (seg["name"] == "S")
            P.pool(lambda e: e.memset(Kb[0][64:128, :], 0.0), w=["K0"])
            for kv in range(2):
                K_ = Kb[0]
                r0 = 4 * 128 + kv * 64
                for c0 in range(0, nkb * 128, 2048):
                    c1 = min(nkb * 128, c0 + 2048)
                    P.dma("sp", lambda e, c0=c0, c1=c1, r0=r0, K_=K_, ko=ko: e.dma_start(out=K_[0:64, c0:c1], in_=FT1[r0:r0 + 64, ko + c0:ko + c1]), w=["K0"])
                Vv = Vb[:, 0:nkb * 65].rearrange("p (b f) -> p b f", f=65)
                for b0 in range(0, nkb, 16):
                    b1 = min(nkb, b0 + 16)
                    P.dma("sp", lambda e, b0=b0, b1=b1, Vv=Vv, kv=kv, ko=ko: e.dma_start(
                        out=Vv[:, b0:b1, :], in_=VT1[ko + b0 * 128:ko + b1 * 128, kv * 65:(kv + 1) * 65].rearrange("(b p) f -> p b f", p=128)),
                        w=["V"])
                for h in range(4 * kv, 4 * kv + 4):
                    for ci in range(nch):
                        tok = (seg["o0"] + 4 * ci) * 128
                        qcol = (qbase + 4 * ci) * 128
                        Q_ = Qb[qj % 2]; Qk = "Q%d" % (qj % 2); qj += 1
                        G_ = Gb[gj % 2]; Gk = "G%d" % (gj % 2); gj += 1
                        rq = (h // 2) * 128 + (h % 2) * 64
                        rg = (5 + h // 2) * 128 + (h % 2) * 64
                        P.dma("pool", lambda e, Q_=Q_, rq=rq, tok=tok: e.dma_start(out=Q_[0:64, 0, :], in_=FT1[rq:rq + 64, tok:tok + 512]), w=[Qk])
                        P.dma("pool", lambda e, G_=G_, rg=rg, tok=tok: e.dma_start(out=G_[0:64, :], in_=FT1[rg:rg + 64, tok:tok + 512]), w=[Gk])
                        O_ = OTd[0]; Ok = "OTd0"
                        L_ = LB[0]; Lk = "LB0"

                        def qk(u, gg_, K_=K_, Q_=Q_, Qk=Qk):
                            S_ = SS[gg_ % NSS]
                            for j in range(2):
                                kb = 2 * u + j
                                P.pe(lambda e, S_=S_, j=j, kb=kb, Q_=Q_, K_=K_: e.matmul(S_[:, j, :], lhsT=K_[:, kb * 128:(kb + 1) * 128],
                                                                                    rhs=Q_[:, 0, :], start=True, stop=True),
                                     r=["K0", Qk], w=["SS%d" % (gg_ % NSS)])

                        def ex(u, gg_):
                            S_ = SS[gg_ % NSS]; p_ = PT[gg_ % 3]
                            P.act(lambda e, S_=S_, p_=p_: e.activation(out=p_[:], in_=S_[:], func=AF.Exp),
                                  r=["SS%d" % (gg_ % NSS)], w=["PT%d" % (gg_ % 3)])

                        def pv(u, gg_, O_=O_, Ok=Ok, Vv=Vv, nkb=nkb):
                            p_ = PT[gg_ % 3]
                            for j in range(2):
                                kb = 2 * u + j
                                P.pe(lambda e, p_=p_, j=j, kb=kb, O_=O_, Vv=Vv, nkb=nkb: e.matmul(O_[0:65, :], lhsT=Vv[:, kb, :], rhs=p_[:, j, :],
                                                                                             start=(kb == 0), stop=(kb == nkb - 1)),
                                     r=["V", "PT%d" % (gg_ % 3)], w=[Ok])
                        run_units(list(range(nkb // 2)), qk, ex, pv)
                        y_ = ybuf[job % 2]; yk = "ybuf%d" % (job % 2)
                        P.dve(lambda e, O_=O_: e.reciprocal(out=rinvz[64:65, :], in_=O_[64:65, :]), r=[Ok], w=["rinvz"])
                        P.dve(lambda e, O_=O_, G_=G_: e.tensor_tensor(out=on_[0:64, :], in0=O_[0:64, :], in1=G_[0:64, :], op=ALU.mult),
                              r=[Ok, Gk], w=["on_"])
                        P.pe(lambda e, L_=L_: e.matmul(L_[0:64, :], lhsT=ones_f[:, 0:64], rhs=rinvz[:, :], start=True, stop=True),
                             r=["ones_f", "rinvz"], w=[Lk])
                        P.dve(lambda e, L_=L_, y_=y_: e.tensor_tensor(out=y_[0:64, :], in0=on_[0:64, :], in1=L_[0:64, :], op=ALU.mult),
                              r=["on_", Lk], w=[yk])
                        P.dma("pool", lambda e, y_=y_, h=h, qcol=qcol: e.dma_start(out=YT1[h * 64:(h + 1) * 64, qcol:qcol + 512], in_=y_[0:64, :]), r=[yk])
                        job += 1
            for h in range(4):
                for m in range(2):
                    K_ = Kb[m]
                    r0 = (13 + h) * 128 + m * 64
                    for c0 in range(0, nkb * 128, 2048):
                        c1 = min(nkb * 128, c0 + 2048)
                        P.dma("sp", lambda e, K_=K_, c0=c0, c1=c1, r0=r0, ko=ko: e.dma_start(out=K_[0:64, c0:c1], in_=FT1[r0:r0 + 64, ko + c0:ko + c1]),
                              w=["K%d" % m])
                    P.dma("sp", lambda e, K_=K_, h=h, nkb=nkb, ko=ko: e.dma_start(out=K_[64:68, 0:nkb * 128], in_=ktab[h, :, ko:ko + nkb * 128]), w=["K%d" % m])
                Vv = Vb[:, 0:nkb * 128].rearrange("p (b f) -> p b f", f=128)
                for b0 in range(0, nkb, 16):
                    b1 = min(nkb, b0 + 16)
                    P.dma("sp", lambda e, b0=b0, b1=b1, Vv=Vv, h=h, ko=ko: e.dma_start(
                        out=Vv[:, b0:b1, :], in_=VT1[ko + b0 * 128:ko + b1 * 128, 130 + h * 128:130 + (h + 1) * 128].rearrange("(b p) f -> p b f", p=128)),
                        w=["V"])
                for ci in range(nch):
                    tok = (seg["o0"] + 4 * ci) * 128
                    qcol = (qbase + 4 * ci) * 128
                    G_ = Gb[gj % 2]; Gk = "G%d" % (gj % 2); gj += 1
                    rg = (17 + h) * 128
                    P.dma("pool", lambda e, G_=G_, rg=rg, tok=tok: e.dma_start(out=G_[:, :], in_=FT1[rg:rg + 128, tok:tok + 512]), w=[Gk])
                    if static_sign:
                        units = []
                        for g in range(nkb // 2):
                            kb0 = 2 * g
                            if kb0 + 1 < 4 * ci:
                                units.append(("far", 0, kb0))
                            elif kb0 > 4 * ci + 3:
                                units.append(("far", 1, kb0))
                            else:
                                units.append(("near", kb0))
                                units.append(("near", kb0 + 1))
                    else:
                        units = [("near", kb) for kb in range(nkb)]
                    for m in range(2):
                        K_ = Kb[m]; Kk = "K%d" % m
                        Q_ = Qb[qj % 2]; Qk = "Q%d" % (qj % 2); qj += 1
                        rq = (9 + h) * 128 + m * 64
                        for v in range(2):
                            P.dma("pool", lambda e, Q_=Q_, rq=rq, tok=tok, v=v: e.dma_start(out=Q_[0:64, v, :], in_=FT1[rq:rq + 64, tok:tok + 512]), w=[Qk])
                            P.dma("pool", lambda e, Q_=Q_, h=h, v=v, qcol=qcol: e.dma_start(out=Q_[64:68, v, :], in_=qtab[h, v, :, qcol:qcol + 512]), w=[Qk])
                        O_ = OTd[0]; Ok = "OTd0"
                        L_ = LB[0]; Lk = "LB0"

                        def qk(u, gg_, K_=K_, Kk=Kk, Q_=Q_, Qk=Qk):
                            S_ = SS[gg_ % NSS]
                            if u[0] == "far":
                                lst = [(j, u[1], u[2] + j) for j in range(2)]
                            else:
                                lst = [(v, v, u[1]) for v in range(2)]
                            for (slot, v, kb) in lst:
                                P.pe(lambda e, S_=S_, slot=slot, v=v, kb=kb, Q_=Q_, K_=K_: e.matmul(
                                    S_[:, slot, :], lhsT=K_[0:68, kb * 128:(kb + 1) * 128], rhs=Q_[0:68, v, :], start=True, stop=True),
                                    r=[Kk, Qk], w=["SS%d" % (gg_ % NSS)])

                        def ex(u, gg_):
                            S_ = SS[gg_ % NSS]; p_ = PT[gg_ % 3]; M_ = Mb[gg_ % 2]
                            if u[0] == "far":
                                P.act(lambda e, S_=S_, p_=p_: e.activation(out=p_[:], in_=S_[:], func=AF.Exp),
                                      r=["SS%d" % (gg_ % NSS)], w=["PT%d" % (gg_ % 3)])
                            else:
                                if gg_ % 2 == 0:
                                    P.act(lambda e, S_=S_, M_=M_: e.copy(out=M_[:], in_=S_[:, 0, :]), r=["SS%d" % (gg_ % NSS)], w=["Mb%d" % (gg_ % 2)])
                                else:
                                    P.dve(lambda e, S_=S_, M_=M_: e.tensor_copy(out=M_[:], in_=S_[:, 0, :]), r=["SS%d" % (gg_ % NSS)], w=["Mb%d" % (gg_ % 2)])
                                P.dve(lambda e, S_=S_, M_=M_: e.tensor_tensor(out=M_[:], in0=M_[:], in1=S_[:, 1, :], op=ALU.min),
                                      r=["SS%d" % (gg_ % NSS), "Mb%d" % (gg_ % 2)], w=["Mb%d" % (gg_ % 2)])
                                P.act(lambda e, M_=M_, p_=p_: e.activation(out=p_[:, 0, :], in_=M_[:], func=AF.Exp),
                                      r=["Mb%d" % (gg_ % 2)], w=["PT%d" % (gg_ % 3)])

                        def pv(u, gg_, O_=O_, L_=L_, Ok=Ok, Lk=Lk, Vv=Vv, nkb=nkb):
                            p_ = PT[gg_ % 3]
                            lst = [(j, u[2] + j) for j in range(2)] if u[0] == "far" else [(0, u[1])]
                            for (slot, kb) in lst:
                                P.pe(lambda e, p_=p_, kb=kb, slot=slot, O_=O_, Vv=Vv, nkb=nkb: e.matmul(
                                    O_[:, :], lhsT=Vv[:, kb, :], rhs=p_[:, slot, :], start=(kb == 0), stop=(kb == nkb - 1)),
                                    r=["V", "PT%d" % (gg_ % 3)], w=[Ok])
                            accop = P.dve if static_sign else P.pool
                            if u[0] == "far":
                                accop(lambda e, p_=p_: e.tensor_tensor(out=acc[:], in0=acc[:], in1=p_[:], op=ALU.add),
                                      r=["acc", "PT%d" % (gg_ % 3)], w=["acc"])
                            else:
                                accop(lambda e, p_=p_: e.tensor_tensor(out=acc[:, 0, :], in0=acc[:, 0, :], in1=p_[:, 0, :], op=ALU.add),
                                      r=["acc", "PT%d" % (gg_ % 3)], w=["acc"])
                        (P.dve if static_sign else P.pool)(lambda e: e.memset(acc[:], 0.0), w=["acc"])
                        run_units(units, qk, ex, pv)
                        for j_ in range(2):
                            P.pe(lambda e, L_=L_, j_=j_: e.matmul(L_[:, :], lhsT=ones_f[:, :], rhs=acc[:, j_, :], start=(j_ == 0), stop=(j_ == 1)),
                                 r=["ones_f", "acc"], w=[Lk])
                        P.dve(lambda e, L_=L_: e.reciprocal(out=rinv[:], in_=L_[:]), r=[Lk], w=["rinv"])
                        if m == 0:
                            P.dve(lambda e, O_=O_: e.tensor_tensor(out=o0[:], in0=O_[:], in1=rinv[:], op=ALU.mult), r=[Ok, "rinv"], w=["o0"])
                        else:
                            y_ = ybuf[job % 2]; yk = "ybuf%d" % (job % 2)
                            P.dve(lambda e, O_=O_: e.tensor_tensor(out=on_[:], in0=O_[:], in1=rinv[:], op=ALU.mult), r=[Ok, "rinv"], w=["on_"])
                            P.dve(lambda e: e.scalar_tensor_tensor(out=od[:], in0=on_[:], scalar=nl[:, 0:1], in1=o0[:], op0=ALU.mult, op1=ALU.add),
                                  r=["on_", "nl", "o0"], w=["od"])
                            P.dve(lambda e: e.tensor_tensor(out=sq[:], in0=od[:], in1=od[:], op=ALU.mult), r=["od"], w=["sq"])
                            P.pe(lambda e, L_=L_: e.matmul(L_[:, :], lhsT=ones_f[:, :], rhs=sq[:], start=True, stop=True), r=["ones_f", "sq"], w=[Lk])
                            P.dve(lambda e, L_=L_: e.tensor_scalar(out=sq[:], in0=L_[:], scalar1=1.0 / 128, scalar2=EPS, op0=ALU.mult, op1=ALU.add),
                                  r=[Lk], w=["sq"])
                            P.act(lambda e: e.activation(out=sq[:], in_=sq[:], func=AF.Sqrt), r=["sq"], w=["sq"])
                            P.dve(lambda e: e.reciprocal(out=sq[:], in_=sq[:]), r=["sq"], w=["sq"])
                            P.dve(lambda e: e.tensor_tensor(out=od[:], in0=od[:], in1=sq[:], op=ALU.mult), r=["od", "sq"], w=["od"])
                            P.dve(lambda e, y_=y_, G_=G_: e.scalar_tensor_tensor(out=y_[:], in0=od[:], scalar=gsc[:, 0:1], in1=G_[:], op0=ALU.mult, op1=ALU.mult),
                                  r=["od", "gsc", Gk], w=[yk])
                            P.dma("pool", lambda e, y_=y_, h=h, qcol=qcol: e.dma_start(
                                out=YT1[512 + h * 128:512 + (h + 1) * 128, qcol:qcol + 512], in_=y_[:]), r=[yk])
                        job += 1
        P.emit_block()

    yout = G["yout"]

    def qidx(seg, i):
        return (0 if seg["name"] == "S" else NBS) + i
    out_phase(1, YT1,
              lambda seg, i: X1[(seg["o0"] + i) * 128:(seg["o0"] + i + 1) * 128, :],
              lambda seg, i: yout[qidx(seg, i) * 128:(qidx(seg, i) + 1) * 128, :],
              [SEG_S, SEG_PO], G["w_out_cd"], lambda seg, i: qidx(seg, i) * 128)


def host_constants(cfg):
    c = {}
    c["eye"] = np.eye(128, dtype=np.float32)
    colmask, dc = nb_static()
    c["colmask"] = colmask
    k = np.arange(128)[:, None]
    q = np.arange(128)[None, :]
    sl8 = alibi_slopes(8)
    al = np.zeros((128, 3, 8, 128), np.float32)
    for r in range(3):
        dist = np.abs(128 * (r - 1) + k - q).astype(np.float32)
        for h in range(8):
            al[:, r, h, :] = np.where(dist <= 128, -sl8[h] * dist, NEG)
    c["alibia"] = al.astype(NPBF)
    sl4 = alibi_slopes(4)
    dg = np.zeros((128, 4, 128), np.float32)
    for h in range(4):
        dg[:, h, :] = -sl4[h] * np.abs(k - q)
    c["diagb"] = dg.astype(NPBF)
    return c


def seg_positions(cfg, core):
    nbs, nbpo, nbpf = cfg["NBS"], cfg["NBPO"], cfg["NBPF"]
    ps_ = np.arange(nbs * 128)
    ppo = core * nbpo * 128 + np.arange(nbpo * 128)
    ppf = np.arange(nbpf * 128)
    return ps_, ppo, ppf


def prepare_core(cfg, core, inp, consts):
    nbs, nbpo, nbpf = cfg["NBS"], cfg["NBPO"], cfg["NBPF"]
    SS, SP = nbs * 128, nbpf * 128
    pad = PAD * 128
    m = dict(consts)
    xs = inp["x_sample"][core]
    xp = inp["x_prompt"][0]
    z = np.zeros((pad, D), np.float32)
    xpp = np.concatenate([z, xp, z], axis=0)
    lo = core * nbpo * 128
    xin = np.concatenate([z, xs, z, xpp[lo:lo + nbpo * 128 + 2 * pad], xpp], axis=0)
    m["xin"] = np.ascontiguousarray(xin)
    vs = np.concatenate([np.zeros(pad), np.ones(SS), np.zeros(pad)])
    vpf = np.concatenate([np.zeros(pad), np.ones(SP), np.zeros(pad)])
    valid = np.concatenate([vs, vpf[lo:lo + nbpo * 128 + 2 * pad], vpf]).astype(np.float32)
    m["validin"] = np.ascontiguousarray(valid.reshape(-1, 128).T)
    pp = inp["p_prompt"][:, 0]
    m["pin"] = np.ascontiguousarray(np.concatenate(
        [inp["p_sample"][:, core], pp[:, lo:lo + nbpo * 128], pp], axis=1))
    m["gvec"] = np.ascontiguousarray(np.stack([inp["norm_pre"][0], inp["norm_post"][0], inp["norm_pre"][1], inp["norm_post"][1]]))
    m["w_in_ab"] = inp["w_in_ab"][0]; m["w_out_ab"] = inp["w_out_ab"][0]
    m["w_in_cd"] = inp["w_in_cd"][0]; m["w_out_cd"] = inp["w_out_cd"][0]
    m["w_ple"] = inp["w_ple"]; m["w_gate"] = inp["w_ple_gate"]
    m["a_sink"] = inp["a_sink"]
    rpb = inp["b_rpb"][0]
    kr = np.arange(2)[:, None, None, None]; kc = np.arange(64)[None, :, None, None]
    qr = np.arange(2)[None, None, :, None]; qc = np.arange(64)[None, None, None, :]
    dcx = np.broadcast_to(np.clip(kc - qc, -15, 15) + 15, (2, 64, 2, 64)).reshape(128, 128)
    g = np.zeros((128, 7, 8, 128), np.float32)
    for r in range(7):
        drx = np.broadcast_to(np.clip(2 * (r - 3) + kr - qr + 7, 0, 14), (2, 64, 2, 64)).reshape(128, 128)
        for h in range(8):
            g[:, r, h, :] = rpb[h][drx, dcx]
    m["rpbg"] = g
    rm = np.zeros((128, 2, 5, 7, 128), np.float32)
    rows_any = 64
    nblk_any = rows_any // 2
    reps = {0: 0, 1: 1, 2: nblk_any // 2, 3: nblk_any - 2, 4: nblk_any - 1}
    rows_p = nbpf * 2
    for cl in range(5):
        for r in range(7):
            n = reps[cl]
            rm[:, 0, cl, r, :] = nb_rowmask_tile(rows_any, n, n + r - 3)
    seg_po = cfg["segs"][1]
    done = set()
    for i in range(nbpo):
        cl = blk_class(seg_po, i)
        if cl in done:
            continue
        done.add(cl)
        n = core * nbpo + i
        for r in range(7):
            rm[:, 1, cl, r, :] = nb_rowmask_tile(rows_p, n, n + r - 3)
    for cl in range(5):
        if cl not in done:
            rm[:, 1, cl] = NEG
    m["rowmask"] = rm.astype(NPBF)
    m["qkgain"] = np.ascontiguousarray(np.concatenate([np.tile(inp["c_q_norm"][0], 8), np.tile(inp["c_k_norm"][0], 2)])[None, :])
    ps_, ppo, ppf = seg_positions(cfg, core)
    pos = np.concatenate([ps_, ppo, ppf])
    inv = (10000.0 ** (-2.0 * np.arange(16) / 32)).astype(np.float32)
    row = (pos // 64).astype(np.float32); col = (pos % 64).astype(np.float32)
    ang = np.concatenate([row[:, None] * inv, col[:, None] * inv], axis=-1).astype(np.float32)
    m["rope"] = np.ascontiguousarray(np.concatenate([np.cos(ang), np.sin(ang)], axis=-1).astype(np.float32))
    m["lamv"] = np.ascontiguousarray(np.stack([inp["d_lambda_q1"][0], inp["d_lambda_k1"][0], inp["d_lambda_q2"][0], inp["d_lambda_k2"][0]])[None])
    m["subln"] = np.ascontiguousarray(inp["d_subln"][0][:, None])
    sl4 = alibi_slopes(4)
    posq = np.concatenate([ps_, ppo]).astype(np.float32)
    qa_, qb_ = np.floor(posq / 128), np.mod(posq, 128)
    qt = np.zeros((4, 2, 4, posq.size), np.float32)
    kt = np.zeros((4, 4, pos.size), np.float32)
    ka_, kb_ = np.floor(pos / 128).astype(np.float32), np.mod(pos, 128).astype(np.float32)
    for h in range(4):
        s = sl4[h]
        qt[h, 0] = np.stack([-s * 128 * qa_, -s * qb_, np.ones_like(qa_), np.ones_like(qa_)])
        qt[h, 1] = np.stack([s * 128 * qa_, s * qb_, -np.ones_like(qa_), -np.ones_like(qa_)])
        kt[h] = np.stack([np.ones_like(ka_), np.ones_like(ka_), s * 128 * ka_, s * kb_])
    m["qtab"] = qt.astype(NPBF)
    m["ktab"] = kt.astype(NPBF)
    return m


_CACHE = {}


def kernel(**inputs):
    inp = {k: np.asarray(v) for k, v in inputs.items()}
    nbs = inp["x_sample"].shape[1] // 128
    nbpf = inp["x_prompt"].shape[1] // 128
    cfg = cfg_make(nbs, nbpf)
    key = (nbs, nbpf)
    if key not in _CACHE:
        _CACHE[key] = build(cfg)
    nc = _CACHE[key]
    consts = host_constants(cfg)
    maps = [prepare_core(cfg, c, inp, consts) for c in range(NCORE)]
    res = run_bass_kernel_spmd(nc, maps, core_ids=list(range(NCORE)))
    nbpo = cfg["NBPO"]
    ys = np.zeros((NCORE, nbs * 128, D), np.float32)
    yp = np.zeros((1, nbpf * 128, D), np.float32)
    for c in range(NCORE):
        y = np.asarray(res.results[c]["yout"])
        ys[c] = y[:nbs * 128]
        yp[0, c * nbpo * 128:(c + 1) * nbpo * 128] = y[nbs * 128:]
    return (yp, ys)
```

```python
import math
import numpy as np
import ml_dtypes
import concourse.bass as bass
import concourse.mybir as mybir
from concourse.bass_utils import run_bass_kernel_spmd

F32 = mybir.dt.float32
BF16 = mybir.dt.bfloat16
AF = mybir.ActivationFunctionType
ALU = mybir.AluOpType
NPBF = ml_dtypes.bfloat16

NCORE = 8
D = 1024
PAD = 4
EPS = 1e-6
NEG = -30000.0
NSEM_DMA = 8


class Prog:
    ENG = ("pe", "act", "dve", "pool", "sp")

    def __init__(self, nc, sems):
        self.nc = nc
        self.sems = sems
        self.sigcount = {e: 0 for e in self.ENG}
        self.dmacount = {"sp": 0, "pool": 0}
        self.waited = {}
        self.reset()

    def reset(self):
        self.ops = []
        self.lw = {}
        self.rd = {}

    def op(self, eng, fn, r=(), w=(), dma=False):
        idx = len(self.ops)
        deps = set()
        for b in r:
            x = self.lw.get(b)
            if x is not None:
                deps.add(x)
        for b in w:
            x = self.lw.get(b)
            if x is not None:
                deps.add(x)
            deps.update(self.rd.get(b, ()))
        for b in r:
            self.rd.setdefault(b, []).append(idx)
        for b in w:
            self.lw[b] = idx
            self.rd[b] = []
        self.ops.append(dict(eng=eng, fn=fn, deps=deps, dma=dma, need=False))
        return idx

    def pe(self, fn, r=(), w=()): return self.op("pe", fn, r, w)
    def act(self, fn, r=(), w=()): return self.op("act", fn, r, w)
    def dve(self, fn, r=(), w=()): return self.op("dve", fn, r, w)
    def pool(self, fn, r=(), w=()): return self.op("pool", fn, r, w)
    def dma(self, q, fn, r=(), w=()): return self.op(q, fn, r, w, dma=True)

    def emit_block(self, name=None):
        nc = self.nc
        ops = self.ops
        for o in ops:
            for d in o["deps"]:
                p = ops[d]
                if p["eng"] == o["eng"] and o["eng"] == "pe" and not p["dma"]:
                    continue
                p["need"] = True
        for o in ops:
            e = o["eng"]
            if o["dma"]:
                k = self.dmacount[e]
                self.dmacount[e] = k + 1
                o["sig"] = ((e, k % NSEM_DMA), 16 * (k // NSEM_DMA + 1))
                o["pre"] = ((e, k % NSEM_DMA), 16 * (k // NSEM_DMA)) if k >= NSEM_DMA else None
            elif o["need"]:
                self.sigcount[e] += 1
                o["sig"] = (e, self.sigcount[e])
                o["pre"] = None
            else:
                o["sig"] = None
                o["pre"] = None
        per = {e: [] for e in self.ENG}
        for o in ops:
            e = o["eng"]
            waits = []
            cand = {}
            for d in o["deps"]:
                p = ops[d]
                if p["eng"] == e and e == "pe" and not p["dma"]:
                    continue
                s, v = p["sig"]
                cand[s] = max(cand.get(s, 0), v)
            if o["pre"] is not None:
                s, v = o["pre"]
                cand[s] = max(cand.get(s, 0), v)
            for s, v in cand.items():
                if self.waited.get((e, s), 0) >= v:
                    continue
                self.waited[(e, s)] = v
                waits.append((s, v))
            per[e].append((o, waits))
        tails = {}
        for q in ("sp", "pool"):
            k = self.dmacount[q]
            tl = []
            for i in range(NSEM_DMA):
                n = (k - i + NSEM_DMA - 1) // NSEM_DMA if k > i else 0
                if n > 0 and self.waited.get((q, (q, i)), 0) < 16 * n:
                    self.waited[(q, (q, i))] = 16 * n
                    tl.append(((q, i), 16 * n))
            tails[q] = tl
        sems = self.sems

        def run(eh, e):
            for o, waits in per[e]:
                for s, v in waits:
                    eh.wait_ge(sems[s], v)
                ins = o["fn"](eh)
                if o["sig"] is not None:
                    s, v = o["sig"]
                    ins.then_inc(sems[s], 16 if o["dma"] else 1)
            for s, v in tails.get(e, ()):
                eh.wait_ge(sems[s], v)

        with nc.Block() as block:
            @block.tensor
            def _(eh): run(eh, "pe")

            @block.scalar
            def _(eh): run(eh, "act")

            @block.vector
            def _(eh): run(eh, "dve")

            @block.gpsimd
            def _(eh): run(eh, "pool")

            @block.sync
            def _(eh): run(eh, "sp")
        self.reset()


def alibi_slopes(n):
    return (2.0 ** (-8.0 * np.arange(1, n + 1) / n)).astype(np.float32)


def nb_rowmask_tile(rows, n, kb):
    m = np.full((2, 64, 2, 64), NEG, np.float32)
    nblk = rows // 2
    if n < 0 or n >= nblk or kb < 0 or kb >= nblk:
        return m.reshape(128, 128)
    for qr in range(2):
        r = 2 * n + qr
        rs = min(max(r - 4, 0), rows - 8)
        for kr in range(2):
            kk = 2 * kb + kr
            if rs <= kk < rs + 8:
                m[kr, :, qr, :] = 0.0
    return m.reshape(128, 128)


def nb_static():
    kc = np.arange(64)[:, None]
    qc = np.arange(64)[None, :]
    ws = np.clip(qc - 8, 0, 48)
    colok = (kc >= ws) & (kc < ws + 16)
    colmask = np.where(colok, 0.0, NEG).astype(np.float32)
    colmask = np.broadcast_to(colmask[None, :, None, :], (2, 64, 2, 64)).reshape(128, 128)
    dc = np.clip(kc - qc, -15, 15) + 15
    return colmask, dc


def cfg_make(nbs, nbpf):
    c = dict(NBS=nbs, NBPF=nbpf, NBPO=nbpf // NCORE)
    segs = []
    e0 = 0
    o0 = 0
    for name, nb in (("S", nbs), ("PO", nbpf // NCORE), ("PF", nbpf)):
        segs.append(dict(name=name, nb=nb, e0=e0, o0=o0, ne=nb + 2 * PAD))
        e0 += nb + 2 * PAD
        o0 += nb
    c["segs"] = segs
    c["NE"] = e0
    c["NO"] = o0
    c["NQ"] = nbs + nbpf // NCORE
    return c


def blk_class(seg, i):
    nb = seg["nb"]
    if i == 0: return 0
    if i == 1: return 1
    if i == nb - 2: return 3
    if i == nb - 1: return 4
    return 2


def blk_rels(seg, i):
    cl = blk_class(seg, i)
    if seg["name"] == "PO":
        return {0: list(range(-2, 4)), 1: list(range(-2, 3)), 2: list(range(-2, 3)),
                3: list(range(-2, 3)), 4: list(range(-3, 3))}[cl]
    return {0: [0, 1, 2, 3], 1: [-1, 0, 1, 2], 2: [-2, -1, 0, 1, 2], 3: [-2, -1, 0, 1], 4: [-3, -2, -1, 0]}[cl]


AB_FM = [("qa", 0, 4, "q"), ("ka", 512, 1, "k"), ("ga", 768, 4, "g"),
         ("qb", 1280, 4, "q"), ("kb", 1792, 4, "k"), ("gb", 2816, 4, "g")]
CD_FM = [("gc", 768, 4, "g"), ("qd", 1280, 4, "q"), ("kd", 1792, 4, "k"), ("gd", 2816, 4, "g")]


def build(cfg, dbg=0):
    nc = bass.Bass("TRN2", target_bir_lowering=False)
    NE, NO, NQ = cfg["NE"], cfg["NO"], cfg["NQ"]
    segs = cfg["segs"]
    TE, TO, TQ = NE * 128, NO * 128, NQ * 128

    def din(name, shape, dt=F32):
        return nc.dram_tensor(name, list(shape), dt, kind="ExternalInput").ap()

    def dscr(name, shape, dt):
        kind = "ExternalOutput" if (dbg and name in ("YT0", "FT0", "VT0", "X1", "FT1", "VT1", "YT1")) else "Internal"
        return nc.dram_tensor(name, list(shape), dt, kind=kind).ap()

    xin = din("xin", [TE, D])
    validin = din("validin", [128, NE])
    pin = din("pin", [2, TO, 256])
    gvec = din("gvec", [4, D])
    w_in_ab = din("w_in_ab", [D, 3328]); w_out_ab = din("w_out_ab", [D, D])
    w_in_cd = din("w_in_cd", [D, 3328]); w_out_cd = din("w_out_cd", [D, D])
    w_ple = din("w_ple", [2, 256, D]); w_gate = din("w_gate", [2, D, D])
    a_sink = din("a_sink", [1, 8])
    rpbg = din("rpbg", [128, 7, 8, 128])
    colmask = din("colmask", [128, 128])
    alibia = din("alibia", [128, 3, 8, 128], BF16)
    rowmask = din("rowmask", [128, 2, 5, 7, 128], BF16)
    qkgain = din("qkgain", [1, 640])
    rope = din("rope", [TO, 64])
    lamv = din("lamv", [1, 4, 64])
    subln = din("subln", [128, 1])
    qtab = din("qtab", [4, 2, 4, TQ], BF16)
    ktab = din("ktab", [4, 4, TO], BF16)
    diagb = din("diagb", [128, 4, 128], BF16)
    yout = nc.dram_tensor("yout", [TQ, D], F32, kind="ExternalOutput").ap()

    FT0 = dscr("FT0", [21 * 128, TE], BF16)
    VT0 = dscr("VT0", [TE, 650], BF16)
    YT0 = dscr("YT0", [D, TO], BF16)
    X1 = dscr("X1", [TO, D], F32)
    FT1 = dscr("FT1", [21 * 128, TO], BF16)
    VT1 = dscr("VT1", [TO, 642], BF16)
    YT1 = dscr("YT1", [D, TQ], BF16)

    import contextlib
    es = contextlib.ExitStack()
    with es:
        sems = {}
        for e in Prog.ENG:
            sems[e] = es.enter_context(nc.semaphore("s_" + e))
        for q in ("sp", "pool"):
            for i in range(NSEM_DMA):
                sems[(q, i)] = es.enter_context(nc.semaphore("d_%s%d" % (q, i)))
        P = Prog(nc, sems)

        ucnt = [0]

        def sb(stack, name, shape, dt):
            ucnt[0] += 1
            return stack.enter_context(nc.sbuf_tensor("%s_u%d" % (name, ucnt[0]), list(shape), dt))

        def ps(stack, name, shape, dt=F32):
            ucnt[0] += 1
            return stack.enter_context(nc.psum_tensor("%s_u%d" % (name, ucnt[0]), list(shape), dt))

        ident = sb(es, "ident", [128, 128], BF16)
        identf = sb(es, "identf", [128, 128], F32)
        ones_f = sb(es, "ones_f", [128, 128], F32)
        zeros_b = sb(es, "zeros_b", [128, 512], BF16)
        gbc = sb(es, "gbc", [128, 4, D], F32)
        valid_sb = sb(es, "valid_sb", [128, NE], F32)
        ones10 = sb(es, "ones10", [128, 10], F32)

        eye = din("eye", [128, 128])
        P.dma("sp", lambda e: e.dma_start(out=identf[:], in_=eye[:, :]), w=["identf"])
        P.dve(lambda e: e.tensor_copy(out=ident[:], in_=identf[:]), r=["identf"], w=["ident"])
        P.dve(lambda e: e.memset(ones_f[:], 1.0), w=["ones_f"])
        P.dve(lambda e: e.memset(zeros_b[:], 0.0), w=["zeros_b"])
        P.dve(lambda e: e.memset(ones10[:], 1.0), w=["ones10"])
        P.dma("sp", lambda e: e.dma_start(out=valid_sb[:], in_=validin[:, :]), w=["valid"])
        for i in range(4):
            P.dma("sp", lambda e, i=i: e.dma_start(out=gbc[:, i, :], in_=gvec[i:i + 1, :].partition_broadcast(128)),
                  w=["gbc"])
        P.emit_block()

        def load_weight(stack_bufs, wdst, wsrc, kch, ncols, key, colchunk=1664):
            stg = stack_bufs
            j = 0
            for k in range(kch):
                for c0 in range(0, ncols, colchunk):
                    c1 = min(ncols, c0 + colchunk)
                    s = stg[j % 2]
                    sk = "wstg%d" % (j % 2)
                    P.dma("sp", lambda e, s=s, k=k, c0=c0, c1=c1: e.dma_start(
                        out=s[:, 0:c1 - c0], in_=wsrc[k * 128:(k + 1) * 128, c0:c1]), w=[sk])
                    if j % 2 == 0:
                        P.dve(lambda e, s=s, k=k, c0=c0, c1=c1: e.tensor_copy(out=wdst[:, k, c0:c1], in_=s[:, 0:c1 - c0]),
                              r=[sk], w=[key])
                    else:
                        P.pool(lambda e, s=s, k=k, c0=c0, c1=c1: e.tensor_copy(out=wdst[:, k, c0:c1], in_=s[:, 0:c1 - c0]),
                               r=[sk], w=[key])
                    j += 1

        def norm_block(xt, xk, gi, hn, hnk, ss, rstd, junk, tagk):
            P.act(lambda e: e.activation(out=junk[:], in_=xt, func=AF.Square, scale=1.0 / math.sqrt(D), accum_out=ss[:, 0:1]),
                  r=[xk], w=["junk", "ss" + tagk])
            P.dve(lambda e: e.tensor_scalar(out=rstd[:, 0:1], in0=ss[:, 0:1], scalar1=EPS, scalar2=None,
                                            op0=ALU.add), r=["ss" + tagk], w=["rstd" + tagk])
            P.act(lambda e: e.activation(out=rstd[:, 0:1], in_=rstd[:, 0:1], func=AF.Sqrt), r=["rstd" + tagk], w=["rstd" + tagk])
            P.dve(lambda e: e.reciprocal(out=rstd[:, 0:1], in_=rstd[:, 0:1]), r=["rstd" + tagk], w=["rstd" + tagk])
            P.dve(lambda e: e.scalar_tensor_tensor(out=hn, in0=xt, scalar=rstd[:, 0:1], in1=gbc[:, gi, :],
                                                   op0=ALU.mult, op1=ALU.mult),
                  r=[xk, "rstd" + tagk, "gbc"], w=[hnk])

        def transpose_to(src, srck, nk, tp, tpk, dst, dstk, use_act):
            for k in range(nk):
                P.pe(lambda e, k=k: e.transpose(out=tp[:, k, :], in_=src[:, k * 128:(k + 1) * 128], identity=ident[:]),
                     r=[srck, "ident"], w=[tpk])
            if use_act:
                P.act(lambda e: e.copy(out=dst, in_=tp[:, 0:nk, :]), r=[tpk], w=[dstk])
            else:
                P.dve(lambda e: e.tensor_copy(out=dst, in_=tp[:, 0:nk, :]), r=[tpk], w=[dstk])

        with contextlib.ExitStack() as st:
            w_in = sb(st, "w_in", [128, 8, 3328], BF16)
            wstg = [sb(st, "wstg0", [128, 1664], F32), sb(st, "wstg1", [128, 1664], F32)]
            xt = [sb(st, "xt%d" % i, [128, D], F32) for i in range(2)]
            hn = [sb(st, "hn%d" % i, [128, D], BF16) for i in range(2)]
            hnT = [sb(st, "hnT%d" % i, [128, 8, 512], BF16) for i in range(2)]
            junk = sb(st, "junk", [128, D], BF16)
            ss = sb(st, "ss", [128, 2], F32)
            rstd = sb(st, "rstd", [128, 2], F32)
            fstage = [sb(st, "fstage%d" % i, [128, 21, 512], BF16) for i in range(2)]
            vstage = [sb(st, "vstage%d" % i, [128, 10, 65], BF16) for i in range(2)]
            tp = [ps(st, "tp%d" % i, [128, 8, 128], BF16) for i in range(2)]
            fm = [ps(st, "fm%d" % i, [128, 512]) for i in range(2)]
            tmA = [ps(st, "tmA%d" % i, [128, 512]) for i in range(1)]
            tmB = [ps(st, "tmB%d" % i, [128, 128]) for i in range(1)]

            load_weight(wstg, w_in, w_in_ab, 8, 3328, "w_in")
            nst = NE // 4
            bc = 0
            for sti in range(nst):
                hT = hnT[sti % 2]
                hTk = "hnT%d" % (sti % 2)
                for b in range(4):
                    eb = sti * 4 + b
                    x_ = xt[bc % 2]
                    xk = "xt%d" % (bc % 2)
                    h_ = hn[bc % 2]
                    hk = "hn%d" % (bc % 2)
                    P.dma("sp", lambda e, x_=x_, eb=eb: e.dma_start(out=x_[:], in_=xin[eb * 128:(eb + 1) * 128, :]), w=[xk])
                    sx = ss[:, bc % 2:bc % 2 + 1]
                    rx = rstd[:, bc % 2:bc % 2 + 1]
                    norm_block(x_[:], xk, 0, h_[:], hk, sx, rx, junk, str(bc % 2))
                    t_ = tp[bc % 2]
                    transpose_to(h_, hk, 8, t_, "tp%d" % (bc % 2), hT[:, :, b * 128:(b + 1) * 128], hTk, bc % 2 == 0)
                    bc += 1
                fs = fstage[sti % 2]
                fsk = "fstage%d" % (sti % 2)
                ci = 0
                for (nm, c0, nch, kind) in AB_FM:
                    for j in range(nch):
                        f0 = c0 + j * 128
                        pf = fm[ci % 2]
                        pk = "fm%d" % (ci % 2)
                        for k in range(8):
                            P.pe(lambda e, pf=pf, k=k, f0=f0, hT=hT: e.matmul(pf[:], lhsT=w_in[:, k, f0:f0 + 128], rhs=hT[:, k, :],
                                                                         start=(k == 0), stop=(k == 7)),
                                 r=["w_in", hTk], w=[pk])
                        if kind == "g":
                            P.act(lambda e, pf=pf, ci=ci, fs=fs: e.activation(out=fs[:, ci, :], in_=pf[:], func=AF.Silu),
                                  r=[pk], w=[fsk])
                        elif kind == "q":
                            P.dve(lambda e, pf=pf, ci=ci, fs=fs: e.tensor_scalar(out=fs[:, ci, :], in0=pf[:], scalar1=0.125,
                                                                           scalar2=None, op0=ALU.mult),
                                  r=[pk], w=[fsk])
                        else:
                            P.dve(lambda e, pf=pf, ci=ci, fs=fs: e.tensor_copy(out=fs[:, ci, :], in_=pf[:]), r=[pk], w=[fsk])
                        ci += 1
                P.dma("pool", lambda e, fs=fs, sti=sti: e.dma_start(
                    out=FT0.rearrange("(c p) t -> p c t", p=128)[:, :, sti * 512:(sti + 1) * 512], in_=fs[:]), r=[fsk])
                for b in range(4):
                    eb = sti * 4 + b
                    vs = vstage[b % 2]
                    vk = "vstage%d" % (b % 2)
                    for k in range(8):
                        P.pe(lambda e, k=k, b=b, hT=hT: e.matmul(tmA[0][:], lhsT=hT[:, k, b * 128:(b + 1) * 128],
                                                           rhs=w_in[:, k, 2304:2816], start=(k == 0), stop=(k == 7)),
                             r=["w_in", hTk], w=["tmA"])
                    for k in range(8):
                        P.pe(lambda e, k=k, b=b, hT=hT: e.matmul(tmB[0][:], lhsT=hT[:, k, b * 128:(b + 1) * 128],
                                                           rhs=w_in[:, k, 640:768], start=(k == 0), stop=(k == 7)),
                             r=["w_in", hTk], w=["tmB"])
                    P.dve(lambda e, vs=vs: e.tensor_copy(out=vs[:, 2:10, 0:64], in_=tmA[0][:].rearrange("p (h d) -> p h d", d=64)),
                          r=["tmA"], w=[vk])
                    P.act(lambda e, vs=vs: e.copy(out=vs[:, 0:2, 0:64], in_=tmB[0][:].rearrange("p (h d) -> p h d", d=64)),
                          r=["tmB"], w=[vk])
                    P.dve(lambda e, vs=vs, eb=eb: e.tensor_scalar(out=vs[:, :, 64], in0=ones10[:], scalar1=valid_sb[:, eb:eb + 1],
                                                             scalar2=None, op0=ALU.mult), r=["valid", "ones10"], w=[vk])
                    P.dma("pool", lambda e, vs=vs, eb=eb: e.dma_start(out=VT0[eb * 128:(eb + 1) * 128, :],
                                                                 in_=vs[:].rearrange("p h d -> p (h d)")), r=[vk])
            P.emit_block()

        with contextlib.ExitStack() as st:
            rpbcol = sb(st, "rpbcol", [128, 7, 8, 128], BF16)
            rpbstg = sb(st, "rpbstg", [128, 7, 8, 128], F32)
            cmask = sb(st, "cmask", [128, 128], F32)
            alib = sb(st, "alib", [128, 3, 8, 128], BF16)
            rmask = sb(st, "rmask", [128, 2, 5, 7, 128], BF16)
            sinkrow = sb(st, "sinkrow", [65, 8, 128], F32)
            sinkv = sb(st, "sinkv", [65, 8], F32)
            KAb = [sb(st, "KA%d" % i, [128, 2, 7 * 128], BF16) for i in range(2)]
            KBb = [sb(st, "KB%d" % i, [128, 8, 7 * 128], BF16) for i in range(2)]
            VRb = [sb(st, "VR%d" % i, [128, 7, 650], BF16) for i in range(2)]
            QAb = [sb(st, "QA%d" % i, [128, 8, 128], BF16) for i in range(2)]
            QBb = [sb(st, "QB%d" % i, [128, 8, 128], BF16) for i in range(2)]
            GAb = [sb(st, "GA%d" % i, [64, 8, 128], BF16) for i in range(2)]
            GBb = [sb(st, "GB%d" % i, [64, 8, 128], BF16) for i in range(2)]
            PT = [sb(st, "PT%d" % i, [128, 3, 512], BF16) for i in range(2)]
            den2 = [sb(st, "den%d" % i, [65, 512], F32) for i in range(2)]
            rinv2 = [sb(st, "rinv%d" % i, [128, 512], F32) for i in range(2)]
            on = [sb(st, "on%d" % i, [64, 512], F32) for i in range(2)]
            ystage = [sb(st, "ystage%d" % i, [64, 16, 128], BF16) for i in range(2)]
            ST = [ps(st, "ST%d" % i, [128, 3, 512]) for i in range(2)]
            OT2 = [ps(st, "OT%d" % i, [128, 512]) for i in range(2)]

            for i_ in range(2):
                P.dve(lambda e, i_=i_: e.memset(rinv2[i_][:], 0.0), w=["rinv%d" % i_])
            for i_ in range(2):
                P.dve(lambda e, i_=i_: e.memset(KAb[i_][64:128], 0.0), w=["kv%d" % i_])
                P.pool(lambda e, i_=i_: e.memset(KBb[i_][64:128], 0.0), w=["kv%d" % i_])
                P.dve(lambda e, i_=i_: e.memset(QAb[i_][64:128], 0.0), w=["qg%d" % i_])
                P.dve(lambda e, i_=i_: e.memset(QBb[i_][64:128], 0.0), w=["qg%d" % i_])
            P.dma("sp", lambda e: e.dma_start(out=rpbstg[:], in_=rpbg[:, :, :, :]), w=["rpbstg"])
            P.dma("sp", lambda e: e.dma_start(out=cmask[:], in_=colmask[:, :]), w=["cmask"])
            P.dma("sp", lambda e: e.dma_start(out=alib[:], in_=alibia[:, :, :, :]), w=["alib"])
            P.dma("sp", lambda e: e.dma_start(out=rmask[:], in_=rowmask[:, :, :, :, :]), w=["rmask"])
            P.dma("sp", lambda e: e.dma_start(out=sinkv[64:65, :], in_=a_sink[0:1, :]), w=["sinkv"])
            for r_ in range(7):
                for h in range(8):
                    P.dve(lambda e, r_=r_, h=h: e.tensor_tensor(out=rpbcol[:, r_, h, :], in0=rpbstg[:, r_, h, :], in1=cmask[:],
                                                             op=ALU.add), r=["rpbstg", "cmask"], w=["rpbcol"])
            P.act(lambda e: e.activation(out=sinkv[64:65, :], in_=sinkv[64:65, :], func=AF.Exp), r=["sinkv"], w=["sinkv"])
            P.dve(lambda e: e.tensor_copy(out=sinkrow[64:65, :, :], in_=sinkv[64:65, :].unsqueeze(2).to_broadcast([1, 8, 128])),
                  r=["sinkv"], w=["sinkrow"])

            FT0v = FT0.rearrange("(c h d) t -> d c h t", h=2, d=64)

            def block_gen(seg, i, bi):
                s2 = bi % 2
                tab = 1 if seg["name"] == "PO" else 0
                eb = seg["e0"] + PAD + i
                ob = seg["o0"] + i
                ka, kb_, vr = KAb[s2], KBb[s2], VRb[s2]
                qa, qb, ga, gb_ = QAb[s2], QBb[s2], GAb[s2], GBb[s2]
                S_ = ST[s2]; Sk = "ST%d" % s2
                p_ = PT[s2]; pk = "PT%d" % s2
                O_ = OT2[s2]; Ok = "OT%d" % s2
                o_ = on[s2]; ok_ = "on%d" % s2
                dn = den2[s2]; dk = "den%d" % s2
                rv = rinv2[s2]; rk_ = "rinv%d" % s2
                ys = ystage[s2]; ysk = "ystage%d" % s2
                t0 = (eb - 3) * 128
                t1 = (eb + 4) * 128
                kk = "kv%d" % s2
                qk = "qg%d" % s2
                P.dma("sp", lambda e: e.dma_start(out=ka[0:64], in_=FT0v[:, 4, :, t0:t1]), w=[kk])
                P.dma("sp", lambda e: e.dma_start(out=kb_[0:64].rearrange("d (c h) t -> d c h t", h=2), in_=FT0v[:, 13:17, :, t0:t1]), w=[kk])
                P.dma("sp", lambda e: e.dma_start(out=vr[:], in_=VT0[t0:t1, :].rearrange("(b p) f -> p b f", p=128)), w=[kk])
                q0 = eb * 128
                P.dma("pool", lambda e: e.dma_start(out=qa[0:64].rearrange("d (c h) t -> d c h t", h=2), in_=FT0v[:, 0:4, :, q0:q0 + 128]), w=[qk])
                P.dma("sp", lambda e: e.dma_start(out=ga[:].rearrange("d (c h) t -> d c h t", h=2), in_=FT0v[:, 5:9, :, q0:q0 + 128]), w=[qk])
                P.dma("pool", lambda e: e.dma_start(out=qb[0:64].rearrange("d (c h) t -> d c h t", h=2), in_=FT0v[:, 9:13, :, q0:q0 + 128]), w=[qk])
                P.dma("sp", lambda e: e.dma_start(out=gb_[:].rearrange("d (c h) t -> d c h t", h=2), in_=FT0v[:, 17:21, :, q0:q0 + 128]), w=[qk])
                yield
                cl = blk_class(seg, i)
                rels_b = blk_rels(seg, i)
                for job in range(4):
                    isA = job < 2
                    g = job % 2
                    rels = [-1, 0, 1] if isA else rels_b
                    batches = [rels[j:j + 3] for j in range(0, len(rels), 3)]
                    P.pe(lambda e: e.matmul(O_[0:65, :], lhsT=zeros_b[:, 0:65], rhs=zeros_b[:, :], start=True, stop=True),
                         r=["zeros_b"], w=[Ok])
                    for bt in batches:
                        for j, rel in enumerate(bt):
                            kof = (rel + 3) * 128
                            if isA:
                                P.pe(lambda e, g=g, j=j, rel=rel: e.matmul(S_[:, j, :], lhsT=ident[:], rhs=alib[:, rel + 1, 4 * g:4 * g + 4, :],
                                                                      start=True, stop=False), r=["ident", "alib"], w=[Sk])
                                P.pe(lambda e, g=g, j=j, kof=kof: e.matmul(S_[:, j, :], lhsT=ka[:, g, kof:kof + 128], rhs=qa[:, 4 * g:4 * g + 4, :],
                                                                      start=False, stop=True), r=[kk, qk], w=[Sk])
                            else:
                                P.pe(lambda e, g=g, j=j, rel=rel: e.matmul(S_[:, j, :], lhsT=ident[:], rhs=rpbcol[:, rel + 3, 4 * g:4 * g + 4, :],
                                                                      start=True, stop=False), r=["ident", "rpbcol"], w=[Sk])
                                P.pe(lambda e, g=g, j=j, rel=rel: e.matmul(S_[:, j, :], lhsT=ident[:],
                                                                           rhs=rmask[:, tab, cl, rel + 3, :].unsqueeze(1).to_broadcast([128, 4, 128]),
                                                                           start=False, stop=False),
                                     r=["ident", "rmask"], w=[Sk])
                                for h in range(4):
                                    P.pe(lambda e, g=g, j=j, kof=kof, h=h: e.matmul(S_[:, j, h * 128:(h + 1) * 128], lhsT=kb_[:, 4 * g + h, kof:kof + 128],
                                                                               rhs=qb[:, 4 * g + h, :], start=False, stop=(h == 3)),
                                         r=[kk, qk], w=[Sk])
                        yield
                        nb_ = len(bt)
                        P.act(lambda e, g=g, nb_=nb_: e.activation(out=p_[:, 0:nb_, :], in_=S_[:, 0:nb_, :], func=AF.Exp), r=[Sk], w=[pk])
                        for j, rel in enumerate(bt):
                            ko = rel + 3
                            if isA:
                                P.pe(lambda e, g=g, j=j, ko=ko: e.matmul(O_[0:65, :], lhsT=vr[:, ko, g * 65:(g + 1) * 65], rhs=p_[:, j, :],
                                                                    start=False, stop=True), r=[kk, pk], w=[Ok])
                            else:
                                for h in range(4):
                                    hv = 2 + 4 * g + h
                                    P.pe(lambda e, g=g, j=j, ko=ko, hv=hv, h=h: e.matmul(O_[0:65, h * 128:(h + 1) * 128], lhsT=vr[:, ko, hv * 65:(hv + 1) * 65],
                                                                                    rhs=p_[:, j, h * 128:(h + 1) * 128], start=False, stop=True),
                                         r=[kk, pk], w=[Ok])
                        yield
                    if isA:
                        P.dve(lambda e, g=g: e.tensor_tensor(out=dn[64:65, :], in0=O_[64:65, :], in1=sinkrow[64:65, 4 * g:4 * g + 4, :], op=ALU.add),
                              r=[Ok, "sinkrow"], w=[dk])
                    else:
                        P.dve(lambda e: e.tensor_copy(out=dn[64:65, :], in_=O_[64:65, :]), r=[Ok], w=[dk])
                    P.dve(lambda e: e.reciprocal(out=rv[64:65, :], in_=dn[64:65, :]), r=[dk], w=[rk_])
                    gsrc = ga if isA else gb_
                    P.dve(lambda e, g=g, gsrc=gsrc: e.tensor_tensor(out=o_[:], in0=O_[0:64, :], in1=gsrc[:, 4 * g:4 * g + 4, :], op=ALU.mult),
                          r=[Ok, qk], w=[ok_])
                    yield
                    P.pe(lambda e: e.matmul(S_[0:64, 0, :], lhsT=ones_f[:, 0:64], rhs=rv[:, :], start=True, stop=True),
                         r=["ones_f", rk_], w=[Sk])
                    hb = (0 if isA else 8) + 4 * g
                    P.dve(lambda e, g=g, hb=hb: e.tensor_tensor(out=ys[:, hb:hb + 4, :], in0=o_[:], in1=S_[0:64, 0, :], op=ALU.mult),
                          r=[ok_, Sk], w=[ysk])
                    yield
                P.dma("pool", lambda e: e.dma_start(out=YT0.rearrange("(h d) t -> d h t", d=64)[:, :, ob * 128:(ob + 1) * 128], in_=ys[:]), r=[ysk])

            blocks = [(seg, i) for seg in segs for i in range(seg["nb"])]
            for p0 in range(0, len(blocks), 2):
                live = [block_gen(blocks[p0 + s_][0], blocks[p0 + s_][1], p0 + s_) for s_ in range(2) if p0 + s_ < len(blocks)]
                while live:
                    for g_ in list(live):
                        try:
                            next(g_)
                        except StopIteration:
                            live.remove(g_)
            P.emit_block()

        if dbg == 1:
            return nc
        _phase345(nc, P, cfg, sb, ps, dbg=dbg, G=dict(
            ident=ident, ones_f=ones_f, zeros_b=zeros_b, gbc=gbc, xin=xin, pin=pin, w_out_ab=w_out_ab, w_in_cd=w_in_cd,
            w_out_cd=w_out_cd, w_ple=w_ple, w_gate=w_gate, qkgain=qkgain, rope=rope, lamv=lamv, subln=subln, qtab=qtab,
            ktab=ktab, diagb=diagb, yout=yout, YT0=YT0, X1=X1, FT1=FT1, VT1=VT1, YT1=YT1,
            load_weight=load_weight, norm_block=norm_block, transpose_to=transpose_to))
    return nc


LAM_INIT1 = 0.8 - 0.6 * math.exp(-0.3 * 1)


def _phase345(nc, P, cfg, sb, ps, G, dbg=0):
    import contextlib
    segs = cfg["segs"]
    ident, ones_f, zeros_b, gbc = G["ident"], G["ones_f"], G["zeros_b"], G["gbc"]
    pin = G["pin"]
    load_weight, norm_block, transpose_to = G["load_weight"], G["norm_block"], G["transpose_to"]
    FT1, VT1, X1, YT1 = G["FT1"], G["VT1"], G["X1"], G["YT1"]

    def out_phase(layer, YT, xsrc_fn, dst_fn, seglist, w_out_d, tok_fn):
        with contextlib.ExitStack() as st:
            wout = sb(st, "wout", [128, 8, D], BF16)
            wg = sb(st, "wg", [128, 8, D], BF16)
            wp = sb(st, "wp", [128, 2, D], BF16)
            wstg = [sb(st, "wstg0", [128, 1024], F32), sb(st, "wstg1", [128, 1024], F32)]
            yts = [sb(st, "yts%d" % i, [128, 8, 512], BF16) for i in range(2)]
            xt = [sb(st, "xt%d" % i, [128, D], F32) for i in range(2)]
            psb = [sb(st, "psb%d" % i, [128, 256], F32) for i in range(2)]
            pb = sb(st, "pb", [128, 256], BF16)
            tmix = sb(st, "tmix", [128, D], F32)
            xa = sb(st, "xa", [128, D], F32)
            xab = sb(st, "xab", [128, D], BF16)
            xaT = sb(st, "xaT", [128, 8, 128], BF16)
            pT = sb(st, "pT", [128, 2, 128], BF16)
            sg = sb(st, "sg", [128, D], F32)
            xo = [sb(st, "xo%d" % i, [128, D], F32) for i in range(2)]
            junk = sb(st, "junk", [128, D], BF16)
            ss = sb(st, "ss", [128, 2], F32)
            mix = ps(st, "mix", [128, D])
            gate = ps(st, "gate", [128, D])
            pp = ps(st, "pp", [128, D])
            tp = ps(st, "tp", [128, 8, 128], BF16)
            tp2 = ps(st, "tp2", [128, 8, 128], BF16)
            load_weight(wstg, wout, w_out_d, 8, D, "wout", colchunk=1024)
            load_weight(wstg, wg, G["w_gate"][layer], 8, D, "wg", colchunk=1024)
            load_weight(wstg, wp, G["w_ple"][layer], 2, D, "wp", colchunk=1024)
            gi = 1 + 2 * layer
            bc = 0
            sc = 0
            for seg in seglist:
                for sti in range(seg["nb"] // 4):
                    y_ = yts[sc % 2]
                    yk = "yts%d" % (sc % 2)
                    ot0 = tok_fn(seg, sti * 4)
                    P.dma("sp", lambda e, y_=y_, ot0=ot0: e.dma_start(
                        out=y_[:], in_=YT.rearrange("(k p) t -> p k t", p=128)[:, :, ot0:ot0 + 512]), w=[yk])
                    sc += 1
                    for b in range(4):
                        i = sti * 4 + b
                        x_ = xt[bc % 2]; xk = "xt%d" % (bc % 2)
                        p_ = psb[bc % 2]; pk = "psb%d" % (bc % 2)
                        o_ = xo[bc % 2]; ok_ = "xo%d" % (bc % 2)
                        xs_ap = xsrc_fn(seg, i)
                        po = (seg["o0"] + i) * 128
                        P.dma("sp", lambda e, x_=x_, xs_ap=xs_ap: e.dma_start(out=x_[:], in_=xs_ap), w=[xk])
                        P.dma("sp", lambda e, p_=p_, po=po: e.dma_start(out=p_[:], in_=pin[layer, po:po + 128, :]), w=[pk])
                        for half in range(2):
                            for k in range(8):
                                P.pe(lambda e, y_=y_, k=k, b=b, half=half: e.matmul(
                                    mix[:, half * 512:(half + 1) * 512], lhsT=y_[:, k, b * 128:(b + 1) * 128],
                                    rhs=wout[:, k, half * 512:(half + 1) * 512], start=(k == 0), stop=(k == 7)),
                                    r=[yk, "wout"], w=["mix"])
                        sx = ss[:, 0:1]
                        P.act(lambda e: e.activation(out=junk[:], in_=mix[:], func=AF.Square, scale=1.0 / math.sqrt(D),
                                                     accum_out=sx), r=["mix"], w=["junk", "ss"])
                        P.dve(lambda e: e.tensor_scalar(out=sx, in0=sx, scalar1=EPS, scalar2=None, op0=ALU.add), r=["ss"], w=["ss"])
                        P.act(lambda e: e.activation(out=sx, in_=sx, func=AF.Sqrt), r=["ss"], w=["ss"])
                        P.dve(lambda e: e.reciprocal(out=sx, in_=sx), r=["ss"], w=["ss"])
                        P.dve(lambda e: e.scalar_tensor_tensor(out=tmix[:], in0=mix[:], scalar=sx, in1=gbc[:, gi, :],
                                                               op0=ALU.mult, op1=ALU.mult), r=["mix", "ss", "gbc"], w=["tmix"])
                        P.dve(lambda e, x_=x_: e.tensor_tensor(out=xa[:], in0=x_[:], in1=tmix[:], op=ALU.add),
                               r=[xk, "tmix"], w=["xa"])
                        P.act(lambda e: e.copy(out=xab[:], in_=xa[:]), r=["xa"], w=["xab"])
                        transpose_to(xab, "xab", 8, tp, "tp", xaT[:], "xaT", False)
                        P.act(lambda e, p_=p_: e.copy(out=pb[:], in_=p_[:]), r=[pk], w=["pb"])
                        transpose_to(pb, "pb", 2, tp2, "tp2", pT[:], "pT", False)
                        for half in range(2):
                            for k in range(8):
                                P.pe(lambda e, k=k, half=half: e.matmul(
                                    gate[:, half * 512:(half + 1) * 512], lhsT=xaT[:, k, :],
                                    rhs=wg[:, k, half * 512:(half + 1) * 512], start=(k == 0), stop=(k == 7)),
                                    r=["xaT", "wg"], w=["gate"])
                            for k in range(2):
                                P.pe(lambda e, k=k, half=half: e.matmul(
                                    pp[:, half * 512:(half + 1) * 512], lhsT=pT[:, k, :],
                                    rhs=wp[:, k, half * 512:(half + 1) * 512], start=(k == 0), stop=(k == 1)),
                                    r=["pT", "wp"], w=["pp"])
                        P.act(lambda e: e.activation(out=sg[:], in_=gate[:], func=AF.Sigmoid), r=["gate"], w=["sg"])
                        P.dve(lambda e: e.tensor_tensor(out=tmix[:], in0=sg[:], in1=pp[:], op=ALU.mult), r=["sg", "pp"], w=["tmix"])
                        P.dve(lambda e, o_=o_: e.tensor_tensor(out=o_[:], in0=xa[:], in1=tmix[:], op=ALU.add),
                               r=["xa", "tmix"], w=[ok_])
                        d_ap = dst_fn(seg, i)
                        P.dma("pool", lambda e, o_=o_, d_ap=d_ap: e.dma_start(out=d_ap, in_=o_[:]), r=[ok_])
                        bc += 1
            P.emit_block()

    xin = G["xin"]
    out_phase(0, G["YT0"],
              lambda seg, i: xin[(seg["e0"] + PAD + i) * 128:(seg["e0"] + PAD + i + 1) * 128, :],
              lambda seg, i: X1[(seg["o0"] + i) * 128:(seg["o0"] + i + 1) * 128, :],
              segs, G["w_out_ab"], lambda seg, i: (seg["o0"] + i) * 128)
    if dbg == 2:
        return

    with contextlib.ExitStack() as st:
        w_in = sb(st, "w_in", [128, 8, 3328], BF16)
        wstg = [sb(st, "wstg0", [128, 1664], F32), sb(st, "wstg1", [128, 1664], F32)]
        xt = [sb(st, "xt%d" % i, [128, D], F32) for i in range(2)]
        hn = [sb(st, "hn%d" % i, [128, D], BF16) for i in range(2)]
        hnT = [sb(st, "hnT%d" % i, [128, 8, 512], BF16) for i in range(2)]
        junk = sb(st, "junk", [128, D], BF16)
        ss = sb(st, "ss", [128, 2], F32)
        rstd = sb(st, "rstd", [128, 2], F32)
        fstage = sb(st, "fstage", [128, 16, 512], BF16)
        qkts = sb(st, "qkts", [128, 5, 512], BF16)
        vst = [sb(st, "vst%d" % i, [128, 642], BF16) for i in range(2)]
        qkf = sb(st, "qkf", [128, 10, 64], F32)
        sqj = sb(st, "sqj", [128, 10, 64], F32)
        ssq = sb(st, "ssq", [128, 10], F32)
        qn = sb(st, "qn", [128, 10, 64], F32)
        ra = sb(st, "ra", [128, 10, 32], F32)
        rb = sb(st, "rb", [128, 10, 32], F32)
        qr = sb(st, "qr", [128, 640], BF16)
        gain = sb(st, "gain", [128, 10, 64], F32)
        rp = [sb(st, "rp%d" % i, [128, 64], F32) for i in range(2)]
        tp = [ps(st, "tp%d" % i, [128, 8, 128], BF16) for i in range(2)]
        fm = [ps(st, "fm%d" % i, [128, 512]) for i in range(2)]
        tqA = ps(st, "tqA", [128, 512])
        tqB = ps(st, "tqB", [128, 256])
        tvD = ps(st, "tvD", [128, 512])
        load_weight(wstg, w_in, G["w_in_cd"], 8, 3328, "w_in")
        P.dma("sp", lambda e: e.dma_start(out=gain[:].rearrange("p h d -> p (h d)"), in_=G["qkgain"][0:1, :].partition_broadcast(128)), w=["gain"])
        P.dve(lambda e: e.tensor_scalar(out=gain[:, 0:8, :], in0=gain[:, 0:8, :], scalar1=0.125, scalar2=None, op0=ALU.mult),
              r=["gain"], w=["gain"])
        for i in range(2):
            P.dve(lambda e, i=i: e.memset(vst[i][:], 1.0), w=["vst%d" % i])
        bc = 0
        sc = 0
        rope = G["rope"]
        for seg in segs:
            pf = seg["name"] == "PF"
            h0 = 8 if pf else 0
            for sti in range(seg["nb"] // 4):
                hT = hnT[sc % 2]; hTk = "hnT%d" % (sc % 2)
                ot0 = (seg["o0"] + sti * 4) * 128
                for b in range(4):
                    ob = seg["o0"] + sti * 4 + b
                    x_ = xt[bc % 2]; xk = "xt%d" % (bc % 2)
                    h_ = hn[bc % 2]; hk = "hn%d" % (bc % 2)
                    P.dma("sp", lambda e, x_=x_, ob=ob: e.dma_start(out=x_[:], in_=X1[ob * 128:(ob + 1) * 128, :]), w=[xk])
                    norm_block(x_[:], xk, 2, h_[:], hk, ss[:, bc % 2:bc % 2 + 1], rstd[:, bc % 2:bc % 2 + 1], junk, str(bc % 2))
                    transpose_to(h_, hk, 8, tp[bc % 2], "tp%d" % (bc % 2), hT[:, :, b * 128:(b + 1) * 128], hTk, bc % 2 == 0)
                    bc += 1
                lst = [("kd", 1792, 4, "k")] if pf else CD_FM
                ci = 0
                for (nm, c0, nch, kind) in lst:
                    for j in range(nch):
                        f0 = c0 + j * 128
                        pf_ = fm[ci % 2]; pk = "fm%d" % (ci % 2)
                        for k in range(8):
                            P.pe(lambda e, pf_=pf_, k=k, f0=f0, hT=hT: e.matmul(pf_[:], lhsT=w_in[:, k, f0:f0 + 128], rhs=hT[:, k, :],
                                                                           start=(k == 0), stop=(k == 7)), r=["w_in", hTk], w=[pk])
                        if kind == "g":
                            P.act(lambda e, pf_=pf_, ci=ci: e.activation(out=fstage[:, ci, :], in_=pf_[:], func=AF.Silu), r=[pk], w=["fstage"])
                        elif kind == "q":
                            P.dve(lambda e, pf_=pf_, ci=ci: e.tensor_scalar(out=fstage[:, ci, :], in0=pf_[:], scalar1=0.125, scalar2=None,
                                                                       op0=ALU.mult), r=[pk], w=["fstage"])
                        else:
                            P.dve(lambda e, pf_=pf_, ci=ci: e.tensor_copy(out=fstage[:, ci, :], in_=pf_[:]), r=[pk], w=["fstage"])
                        ci += 1
                FT1v = FT1.rearrange("(c p) t -> p c t", p=128)
                if pf:
                    P.dma("pool", lambda e, ot0=ot0: e.dma_start(out=FT1v[:, 13:17, ot0:ot0 + 512], in_=fstage[:, 0:4, :]), r=["fstage"])
                else:
                    P.dma("pool", lambda e, ot0=ot0: e.dma_start(out=FT1v[:, 5:21, ot0:ot0 + 512], in_=fstage[:, 0:16, :]), r=["fstage"])
                for b in range(4):
                    ob = seg["o0"] + sti * 4 + b
                    vs = vst[b % 2]; vk = "vst%d" % (b % 2)
                    r_ = rp[b % 2]; rk = "rp%d" % (b % 2)
                    P.dma("sp", lambda e, r_=r_, ob=ob: e.dma_start(out=r_[:], in_=rope[ob * 128:(ob + 1) * 128, :]), w=[rk])
                    if not pf:
                        for k in range(8):
                            P.pe(lambda e, k=k, b=b, hT=hT: e.matmul(tqA[:], lhsT=hT[:, k, b * 128:(b + 1) * 128], rhs=w_in[:, k, 0:512],
                                                               start=(k == 0), stop=(k == 7)), r=["w_in", hTk], w=["tqA"])
                    for k in range(8):
                        P.pe(lambda e, k=k, b=b, hT=hT: e.matmul(tqB[:], lhsT=hT[:, k, b * 128:(b + 1) * 128], rhs=w_in[:, k, 512:768],
                                                           start=(k == 0), stop=(k == 7)), r=["w_in", hTk], w=["tqB"])
                    for k in range(8):
                        P.pe(lambda e, k=k, b=b, hT=hT: e.matmul(tvD[:], lhsT=hT[:, k, b * 128:(b + 1) * 128], rhs=w_in[:, k, 2304:2816],
                                                           start=(k == 0), stop=(k == 7)), r=["w_in", hTk], w=["tvD"])
                    P.dve(lambda e, vs=vs: e.tensor_copy(out=vs[:, 0:130].rearrange("p (h d) -> p h d", d=65)[:, :, 0:64],
                                                         in_=tqB[:, 128:256].rearrange("p (h d) -> p h d", d=64)), r=["tqB"], w=[vk])
                    P.act(lambda e, vs=vs: e.copy(out=vs[:, 130:642], in_=tvD[:]), r=["tvD"], w=[vk])
                    P.dma("pool", lambda e, vs=vs, ob=ob: e.dma_start(out=VT1[ob * 128:(ob + 1) * 128, :], in_=vs[:]), r=[vk])
                    if not pf:
                        P.act(lambda e: e.copy(out=qkf[:, 0:8, :], in_=tqA[:].rearrange("p (h d) -> p h d", d=64)), r=["tqA"], w=["qkf"])
                    P.act(lambda e: e.copy(out=qkf[:, 8:10, :], in_=tqB[:, 0:128].rearrange("p (h d) -> p h d", d=64)), r=["tqB"], w=["qkf"])
                    hs = slice(h0, 10)
                    nh = 10 - h0
                    P.dve(lambda e, hs=hs: e.tensor_tensor(out=sqj[:, hs, :], in0=qkf[:, hs, :], in1=qkf[:, hs, :], op=ALU.mult), r=["qkf"], w=["sqj"])
                    P.dve(lambda e, hs=hs: e.tensor_reduce(out=ssq[:, hs], in_=sqj[:, hs, :], axis=mybir.AxisListType.X, op=ALU.add),
                          r=["sqj"], w=["ssq"])
                    P.dve(lambda e, hs=hs: e.tensor_scalar(out=ssq[:, hs], in0=ssq[:, hs], scalar1=1.0 / 64, scalar2=EPS, op0=ALU.mult, op1=ALU.add),
                          r=["ssq"], w=["ssq"])
                    P.act(lambda e, hs=hs: e.activation(out=ssq[:, hs], in_=ssq[:, hs], func=AF.Sqrt), r=["ssq"], w=["ssq"])
                    P.dve(lambda e, hs=hs: e.reciprocal(out=ssq[:, hs], in_=ssq[:, hs]), r=["ssq"], w=["ssq"])
                    P.dve(lambda e, hs=hs, nh=nh: e.tensor_tensor(out=qn[:, hs, :], in0=qkf[:, hs, :],
                                                                in1=ssq[:, hs].unsqueeze(2).to_broadcast([128, nh, 64]), op=ALU.mult),
                          r=["qkf", "ssq"], w=["qn"])
                    P.dve(lambda e, hs=hs: e.tensor_tensor(out=qn[:, hs, :], in0=qn[:, hs, :], in1=gain[:, hs, :], op=ALU.mult),
                          r=["qn", "gain"], w=["qn"])
                    qv = qn[:].rearrange("p h (j two) -> p h j two", two=2)
                    qrv = qr[:].rearrange("p (h j two) -> p h j two", two=2, j=32)
                    cs = lambda r_=r_, nh=nh: r_[:, 0:32].unsqueeze(1).to_broadcast([128, nh, 32])
                    sn = lambda r_=r_, nh=nh: r_[:, 32:64].unsqueeze(1).to_broadcast([128, nh, 32])
                    P.dve(lambda e, hs=hs, cs=cs: e.tensor_tensor(out=ra[:, hs, :], in0=qv[:, hs, :, 0], in1=cs(), op=ALU.mult), r=["qn", rk], w=["ra"])
                    P.dve(lambda e, hs=hs, sn=sn: e.tensor_tensor(out=rb[:, hs, :], in0=qv[:, hs, :, 1], in1=sn(), op=ALU.mult), r=["qn", rk], w=["rb"])
                    P.dve(lambda e, hs=hs: e.tensor_tensor(out=qrv[:, hs, :, 0], in0=ra[:, hs, :], in1=rb[:, hs, :], op=ALU.subtract),
                          r=["ra", "rb"], w=["qr"])
                    P.dve(lambda e, hs=hs, sn=sn: e.tensor_tensor(out=ra[:, hs, :], in0=qv[:, hs, :, 0], in1=sn(), op=ALU.mult), r=["qn", rk], w=["ra"])
                    P.dve(lambda e, hs=hs, cs=cs: e.tensor_tensor(out=rb[:, hs, :], in0=qv[:, hs, :, 1], in1=cs(), op=ALU.mult), r=["qn", rk], w=["rb"])
                    P.dve(lambda e, hs=hs: e.tensor_tensor(out=qrv[:, hs, :, 1], in0=ra[:, hs, :], in1=rb[:, hs, :], op=ALU.add),
                          r=["ra", "rb"], w=["qr"])
                    c0_ = 4 if pf else 0
                    t_ = tp[b % 2]; tk = "tp%d" % (b % 2)
                    for k in range(c0_, 5):
                        P.pe(lambda e, k=k, t_=t_: e.transpose(out=t_[:, k, :], in_=qr[:, k * 128:(k + 1) * 128], identity=ident[:]),
                             r=["qr", "ident"], w=[tk])
                    P.dve(lambda e, t_=t_, c0_=c0_, b=b: e.tensor_copy(out=qkts[:, c0_:5, b * 128:(b + 1) * 128], in_=t_[:, c0_:5, :]),
                          r=[tk], w=["qkts"])
                c0_ = 4 if pf else 0
                P.dma("pool", lambda e, ot0=ot0, c0_=c0_: e.dma_start(out=FT1v[:, c0_:5, ot0:ot0 + 512], in_=qkts[:, c0_:5, :]), r=["qkts"])
                sc += 1
        P.emit_block()
    if dbg == 3:
        return
    _phase45(nc, P, cfg, sb, ps, G, out_phase)


def _phase45(nc, P, cfg, sb, ps, G, out_phase):
    import contextlib
    segs = cfg["segs"]
    SEG_S, SEG_PO, SEG_PF = segs
    ones_f = G["ones_f"]
    FT1, VT1, X1, YT1 = G["FT1"], G["VT1"], G["X1"], G["YT1"]
    qtab, ktab = G["qtab"], G["ktab"]
    NBS = cfg["NBS"]
    with contextlib.ExitStack() as st:
        nkbmax = max(SEG_S["nb"], SEG_PF["nb"])
        Kb = [sb(st, "Kb%d" % i, [128, nkbmax * 128], BF16) for i in range(2)]
        Vb = sb(st, "Vb", [128, nkbmax * 128], BF16)
        Qb = [sb(st, "Qb%d" % i, [128, 2, 512], BF16) for i in range(2)]
        Gb = [sb(st, "Gb%d" % i, [128, 512], BF16) for i in range(2)]
        PT = [sb(st, "PT%d" % i, [128, 2, 512], BF16) for i in range(3)]
        Mb = [sb(st, "Mb%d" % i, [128, 512], F32) for i in range(2)]
        rinv = sb(st, "rinv", [128, 512], F32)
        rinvz = sb(st, "rinvz", [128, 512], F32)
        acc = sb(st, "acc", [128, 2, 512], F32)
        on_ = sb(st, "on_", [128, 512], F32)
        o0 = sb(st, "o0", [128, 512], F32)
        od = sb(st, "od", [128, 512], F32)
        sq = sb(st, "sq", [128, 512], F32)
        ybuf = [sb(st, "ybuf%d" % i, [128, 512], BF16) for i in range(2)]
        ones_b = sb(st, "ones_b", [128, 128], BF16)
        lv = sb(st, "lv", [1, 4, 64], F32)
        pr = sb(st, "pr", [1, 2, 64], F32)
        ls = sb(st, "ls", [1, 4], F32)
        nl = sb(st, "nl", [128, 1], F32)
        gsc = sb(st, "gsc", [128, 1], F32)
        SS = [ps(st, "SS%d" % i, [128, 2, 512]) for i in range(3)]
        OTd = [ps(st, "OTd%d" % i, [128, 512]) for i in range(1)]
        LB = [ps(st, "LB%d" % i, [128, 512]) for i in range(1)]

        P.dve(lambda e: e.memset(ones_b[:], 1.0), w=["ones_b"])
        P.dve(lambda e: e.memset(rinvz[:], 0.0), w=["rinvz"])
        for i_ in range(2):
            P.dve(lambda e, i_=i_: e.memset(Qb[i_][64:128], 0.0), w=["Q%d" % i_])
        P.dma("sp", lambda e: e.dma_start(out=lv[:], in_=G["lamv"][:, :, :]), w=["lv"])
        P.dma("sp", lambda e: e.dma_start(out=gsc[:], in_=G["subln"][:, :]), w=["gsc"])
        P.dve(lambda e: e.tensor_scalar(out=gsc[:], in0=gsc[:], scalar1=(1.0 - LAM_INIT1), scalar2=None, op0=ALU.mult), r=["gsc"], w=["gsc"])
        P.dve(lambda e: e.tensor_tensor(out=pr[:, 0, :], in0=lv[:, 0, :], in1=lv[:, 1, :], op=ALU.mult), r=["lv"], w=["pr"])
        P.dve(lambda e: e.tensor_tensor(out=pr[:, 1, :], in0=lv[:, 2, :], in1=lv[:, 3, :], op=ALU.mult), r=["lv"], w=["pr"])
        P.dve(lambda e: e.tensor_reduce(out=ls[:, 0:2], in_=pr[:], axis=mybir.AxisListType.X, op=ALU.add), r=["pr"], w=["ls"])
        P.act(lambda e: e.activation(out=ls[:, 0:2], in_=ls[:, 0:2], func=AF.Exp), r=["ls"], w=["ls"])
        P.dve(lambda e: e.tensor_tensor(out=ls[:, 2:3], in0=ls[:, 1:2], in1=ls[:, 0:1], op=ALU.subtract), r=["ls"], w=["ls"])
        P.dve(lambda e: e.tensor_scalar(out=ls[:, 3:4], in0=ls[:, 2:3], scalar1=-LAM_INIT1, scalar2=None, op0=ALU.add), r=["ls"], w=["ls"])
        P.pe(lambda e: e.matmul(LB[0][:, 0:1], lhsT=ones_f[0:1, 0:128], rhs=ls[0:1, 3:4], start=True, stop=True), r=["ones_f", "ls"], w=["LB0"])
        P.dve(lambda e: e.tensor_copy(out=nl[:], in_=LB[0][:, 0:1]), r=["LB0"], w=["nl"])

        gg = 0
        job = 0
        qj = 0
        gj = 0
        NSS = 3

        def run_units(units, qk, ex, pv):
            nonlocal gg
            n = len(units)
            LA = 2
            for i in range(min(LA, n)):
                qk(units[i], gg + i)
            for i in range(LA, n):
                qk(units[i], gg + i)
                ex(units[i - LA], gg + i - LA)
                pv(units[i - LA], gg + i - LA)
            for i in range(max(0, n - LA), n):
                ex(units[i], gg + i)
                pv(units[i], gg + i)
            gg += n

        for seg, qbase, kseg in ((SEG_S, 0, SEG_S), (SEG_PO, NBS, SEG_PF)):
            nkb = kseg["nb"]
            ko = kseg["o0"] * 128
            nch = seg["nb"] // 4
            static_sign = (seg["name"] == "S")
            P.pool(lambda e: e.memset(Kb[0][64:128, :], 0.0), w=["K0"])
            for kv in range(2):
                K_ = Kb[0]
                r0 = 4 * 128 + kv * 64
                for c0 in range(0, nkb * 128, 2048):
                    c1 = min(nkb * 128, c0 + 2048)
                    P.dma("sp", lambda e, c0=c0, c1=c1, r0=r0, K_=K_, ko=ko: e.dma_start(out=K_[0:64, c0:c1], in_=FT1[r0:r0 + 64, ko + c0:ko + c1]), w=["K0"])
                Vv = Vb[:, 0:nkb * 65].rearrange("p (b f) -> p b f", f=65)
                for b0 in range(0, nkb, 16):
                    b1 = min(nkb, b0 + 16)
                    P.dma("sp", lambda e, b0=b0, b1=b1, Vv=Vv, kv=kv, ko=ko: e.dma_start(
                        out=Vv[:, b0:b1, :], in_=VT1[ko + b0 * 128:ko + b1 * 128, kv * 65:(kv + 1) * 65].rearrange("(b p) f -> p b f", p=128)),
                        w=["V"])
                for h in range(4 * kv, 4 * kv + 4):
                    for ci in range(nch):
                        tok = (seg["o0"] + 4 * ci) * 128
                        qcol = (qbase + 4 * ci) * 128
                        Q_ = Qb[qj % 2]; Qk = "Q%d" % (qj % 2); qj += 1
                        G_ = Gb[gj % 2]; Gk = "G%d" % (gj % 2); gj += 1
                        rq = (h // 2) * 128 + (h % 2) * 64
                        rg = (5 + h // 2) * 128 + (h % 2) * 64
                        P.dma("pool", lambda e, Q_=Q_, rq=rq, tok=tok: e.dma_start(out=Q_[0:64, 0, :], in_=FT1[rq:rq + 64, tok:tok + 512]), w=[Qk])
                        P.dma("pool", lambda e, G_=G_, rg=rg, tok=tok: e.dma_start(out=G_[0:64, :], in_=FT1[rg:rg + 64, tok:tok + 512]), w=[Gk])
                        O_ = OTd[0]; Ok = "OTd0"
                        L_ = LB[0]; Lk = "LB0"

                        def qk(u, gg_, K_=K_, Q_=Q_, Qk=Qk):
                            S_ = SS[gg_ % NSS]
                            for j in range(2):
                                kb = 2 * u + j
                                P.pe(lambda e, S_=S_, j=j, kb=kb, Q_=Q_, K_=K_: e.matmul(S_[:, j, :], lhsT=K_[:, kb * 128:(kb + 1) * 128],
                                                                                    rhs=Q_[:, 0, :], start=True, stop=True),
                                     r=["K0", Qk], w=["SS%d" % (gg_ % NSS)])

                        def ex(u, gg_):
                            S_ = SS[gg_ % NSS]; p_ = PT[gg_ % 3]
                            P.act(lambda e, S_=S_, p_=p_: e.activation(out=p_[:], in_=S_[:], func=AF.Exp),
                                  r=["SS%d" % (gg_ % NSS)], w=["PT%d" % (gg_ % 3)])

                        def pv(u, gg_, O_=O_, Ok=Ok, Vv=Vv, nkb=nkb):
                            p_ = PT[gg_ % 3]
                            for j in range(2):
                                kb = 2 * u + j
                                P.pe(lambda e, p_=p_, j=j, kb=kb, O_=O_, Vv=Vv, nkb=nkb: e.matmul(O_[0:65, :], lhsT=Vv[:, kb, :], rhs=p_[:, j, :],
                                                                                             start=(kb == 0), stop=(kb == nkb - 1)),
                                     r=["V", "PT%d" % (gg_ % 3)], w=[Ok])
                        run_units(list(range(nkb // 2)), qk, ex, pv)
                        y_ = ybuf[job % 2]; yk = "ybuf%d" % (job % 2)
                        P.dve(lambda e, O_=O_: e.reciprocal(out=rinvz[64:65, :], in_=O_[64:65, :]), r=[Ok], w=["rinvz"])
                        P.dve(lambda e, O_=O_, G_=G_: e.tensor_tensor(out=on_[0:64, :], in0=O_[0:64, :], in1=G_[0:64, :], op=ALU.mult),
                              r=[Ok, Gk], w=["on_"])
                        P.pe(lambda e, L_=L_: e.matmul(L_[0:64, :], lhsT=ones_f[:, 0:64], rhs=rinvz[:, :], start=True, stop=True),
                             r=["ones_f", "rinvz"], w=[Lk])
                        P.dve(lambda e, L_=L_, y_=y_: e.tensor_tensor(out=y_[0:64, :], in0=on_[0:64, :], in1=L_[0:64, :], op=ALU.mult),
                              r=["on_", Lk], w=[yk])
                        P.dma("pool", lambda e, y_=y_, h=h, qcol=qcol: e.dma_start(out=YT1[h * 64:(h + 1) * 64, qcol:qcol + 512], in_=y_[0:64, :]), r=[yk])
                        job += 1
            for h in range(4):
                for m in range(2):
                    K_ = Kb[m]
                    r0 = (13 + h) * 128 + m * 64
                    for c0 in range(0, nkb * 128, 2048):
                        c1 = min(nkb * 128, c0 + 2048)
                        P.dma("sp", lambda e, K_=K_, c0=c0, c1=c1, r0=r0, ko=ko: e.dma_start(out=K_[0:64, c0:c1], in_=FT1[r0:r0 + 64, ko + c0:ko + c1]),
                              w=["K%d" % m])
                    P.dma("sp", lambda e, K_=K_, h=h, nkb=nkb, ko=ko: e.dma_start(out=K_[64:68, 0:nkb * 128], in_=ktab[h, :, ko:ko + nkb * 128]), w=["K%d" % m])
                Vv = Vb[:, 0:nkb * 128].rearrange("p (b f) -> p b f", f=128)
                for b0 in range(0, nkb, 16):
                    b1 = min(nkb, b0 + 16)
                    P.dma("sp", lambda e, b0=b0, b1=b1, Vv=Vv, h=h, ko=ko: e.dma_start(
                        out=Vv[:, b0:b1, :], in_=VT1[ko + b0 * 128:ko + b1 * 128, 130 + h * 128:130 + (h + 1) * 128].rearrange("(b p) f -> p b f", p=128)),
                        w=["V"])
                for ci in range(nch):
                    tok = (seg["o0"] + 4 * ci) * 128
                    qcol = (qbase + 4 * ci) * 128
                    G_ = Gb[gj % 2]; Gk = "G%d" % (gj % 2); gj += 1
                    rg = (17 + h) * 128
                    P.dma("pool", lambda e, G_=G_, rg=rg, tok=tok: e.dma_start(out=G_[:, :], in_=FT1[rg:rg + 128, tok:tok + 512]), w=[Gk])
                    if static_sign:
                        units = []
                        for g in range(nkb // 2):
                            kb0 = 2 * g
                            if kb0 + 1 < 4 * ci:
                                units.append(("far", 0, kb0))
                            elif kb0 > 4 * ci + 3:
                                units.append(("far", 1, kb0))
                            else:
                                units.append(("near", kb0))
                                units.append(("near", kb0 + 1))
                    else:
                        units = [("near", kb) for kb in range(nkb)]
                    for m in range(2):
                        K_ = Kb[m]; Kk = "K%d" % m
                        Q_ = Qb[qj % 2]; Qk = "Q%d" % (qj % 2); qj += 1
                        rq = (9 + h) * 128 + m * 64
                        for v in range(2):
                            P.dma("pool", lambda e, Q_=Q_, rq=rq, tok=tok, v=v: e.dma_start(out=Q_[0:64, v, :], in_=FT1[rq:rq + 64, tok:tok + 512]), w=[Qk])
                            P.dma("pool", lambda e, Q_=Q_, h=h, v=v, qcol=qcol: e.dma_start(out=Q_[64:68, v, :], in_=qtab[h, v, :, qcol:qcol + 512]), w=[Qk])
                        O_ = OTd[0]; Ok = "OTd0"
                        L_ = LB[0]; Lk = "LB0"

                        def qk(u, gg_, K_=K_, Kk=Kk, Q_=Q_, Qk=Qk):
                            S_ = SS[gg_ % NSS]
                            if u[0] == "far":
                                lst = [(j, u[1], u[2] + j) for j in range(2)]
                            else:
                                lst = [(v, v, u[1]) for v in range(2)]
                            for (slot, v, kb) in lst:
                                P.pe(lambda e, S_=S_, slot=slot, v=v, kb=kb, Q_=Q_, K_=K_: e.matmul(
                                    S_[:, slot, :], lhsT=K_[0:68, kb * 128:(kb + 1) * 128], rhs=Q_[0:68, v, :], start=True, stop=True),
                                    r=[Kk, Qk], w=["SS%d" % (gg_ % NSS)])

                        def ex(u, gg_):
                            S_ = SS[gg_ % NSS]; p_ = PT[gg_ % 3]; M_ = Mb[gg_ % 2]
                            if u[0] == "far":
                                P.act(lambda e, S_=S_, p_=p_: e.activation(out=p_[:], in_=S_[:], func=AF.Exp),
                                      r=["SS%d" % (gg_ % NSS)], w=["PT%d" % (gg_ % 3)])
                            else:
                                if gg_ % 2 == 0:
                                    P.act(lambda e, S_=S_, M_=M_: e.copy(out=M_[:], in_=S_[:, 0, :]), r=["SS%d" % (gg_ % NSS)], w=["Mb%d" % (gg_ % 2)])
                                else:
                                    P.dve(lambda e, S_=S_, M_=M_: e.tensor_copy(out=M_[:], in_=S_[:, 0, :]), r=["SS%d" % (gg_ % NSS)], w=["Mb%d" % (gg_ % 2)])
                                P.dve(lambda e, S_=S_, M_=M_: e.tensor_tensor(out=M_[:], in0=M_[:], in1=S_[:, 1, :], op=ALU.min),
                                      r=["SS%d" % (gg_ % NSS), "Mb%d" % (gg_ % 2)], w=["Mb%d" % (gg_ % 2)])
                                P.act(lambda e, M_=M_, p_=p_: e.activation(out=p_[:, 0, :], in_=M_[:], func=AF.Exp),
                                      r=["Mb%d" % (gg_ % 2)], w=["PT%d" % (gg_ % 3)])

                        def pv(u, gg_, O_=O_, L_=L_, Ok=Ok, Lk=Lk, Vv=Vv, nkb=nkb):
                            p_ = PT[gg_ % 3]
                            lst = [(j, u[2] + j) for j in range(2)] if u[0] == "far" else [(0, u[1])]
                            for (slot, kb) in lst:
                                P.pe(lambda e, p_=p_, kb=kb, slot=slot, O_=O_, Vv=Vv, nkb=nkb: e.matmul(
                                    O_[:, :], lhsT=Vv[:, kb, :], rhs=p_[:, slot, :], start=(kb == 0), stop=(kb == nkb - 1)),
                                    r=["V", "PT%d" % (gg_ % 3)], w=[Ok])
                            accop = P.dve if static_sign else P.pool
                            if u[0] == "far":
                                accop(lambda e, p_=p_: e.tensor_tensor(out=acc[:], in0=acc[:], in1=p_[:], op=ALU.add),
                                      r=["acc", "PT%d" % (gg_ % 3)], w=["acc"])
                            else:
                                accop(lambda e, p_=p_: e.tensor_tensor(out=acc[:, 0, :], in0=acc[:, 0, :], in1=p_[:, 0, :], op=ALU.add),
                                      r=["acc", "PT%d" % (gg_ % 3)], w=["acc"])
                        (P.dve if static_sign else P.pool)(lambda e: e.memset(acc[:], 0.0), w=["acc"])
                        run_units(units, qk, ex, pv)
                        for j_ in range(2):
                            P.pe(lambda e, L_=L_, j_=j_: e.matmul(L_[:, :], lhsT=ones_f[:, :], rhs=acc[:, j_, :], start=(j_ == 0), stop=(j_ == 1)),
                                 r=["ones_f", "acc"], w=[Lk])
                        P.dve(lambda e, L_=L_: e.reciprocal(out=rinv[:], in_=L_[:]), r=[Lk], w=["rinv"])
                        if m == 0:
                            P.dve(lambda e, O_=O_: e.tensor_tensor(out=o0[:], in0=O_[:], in1=rinv[:], op=ALU.mult), r=[Ok, "rinv"], w=["o0"])
                        else:
                            y_ = ybuf[job % 2]; yk = "ybuf%d" % (job % 2)
                            P.dve(lambda e, O_=O_: e.tensor_tensor(out=on_[:], in0=O_[:], in1=rinv[:], op=ALU.mult), r=[Ok, "rinv"], w=["on_"])
                            P.dve(lambda e: e.scalar_tensor_tensor(out=od[:], in0=on_[:], scalar=nl[:, 0:1], in1=o0[:], op0=ALU.mult, op1=ALU.add),
                                  r=["on_", "nl", "o0"], w=["od"])
                            P.dve(lambda e: e.tensor_tensor(out=sq[:], in0=od[:], in1=od[:], op=ALU.mult), r=["od"], w=["sq"])
                            P.pe(lambda e, L_=L_: e.matmul(L_[:, :], lhsT=ones_f[:, :], rhs=sq[:], start=True, stop=True), r=["ones_f", "sq"], w=[Lk])
                            P.dve(lambda e, L_=L_: e.tensor_scalar(out=sq[:], in0=L_[:], scalar1=1.0 / 128, scalar2=EPS, op0=ALU.mult, op1=ALU.add),
                                  r=[Lk], w=["sq"])
                            P.act(lambda e: e.activation(out=sq[:], in_=sq[:], func=AF.Sqrt), r=["sq"], w=["sq"])
                            P.dve(lambda e: e.reciprocal(out=sq[:], in_=sq[:]), r=["sq"], w=["sq"])
                            P.dve(lambda e: e.tensor_tensor(out=od[:], in0=od[:], in1=sq[:], op=ALU.mult), r=["od", "sq"], w=["od"])
                            P.dve(lambda e, y_=y_, G_=G_: e.scalar_tensor_tensor(out=y_[:], in0=od[:], scalar=gsc[:, 0:1], in1=G_[:], op0=ALU.mult, op1=ALU.mult),
                                  r=["od", "gsc", Gk], w=[yk])
                            P.dma("pool", lambda e, y_=y_, h=h, qcol=qcol: e.dma_start(
                                out=YT1[512 + h * 128:512 + (h + 1) * 128, qcol:qcol + 512], in_=y_[:]), r=[yk])
                        job += 1
        P.emit_block()

    yout = G["yout"]

    def qidx(seg, i):
        return (0 if seg["name"] == "S" else NBS) + i
    out_phase(1, YT1,
              lambda seg, i: X1[(seg["o0"] + i) * 128:(seg["o0"] + i + 1) * 128, :],
              lambda seg, i: yout[qidx(seg, i) * 128:(qidx(seg, i) + 1) * 128, :],
              [SEG_S, SEG_PO], G["w_out_cd"], lambda seg, i: qidx(seg, i) * 128)


def host_constants(cfg):
    c = {}
    c["eye"] = np.eye(128, dtype=np.float32)
    colmask, dc = nb_static()
    c["colmask"] = colmask
    k = np.arange(128)[:, None]
    q = np.arange(128)[None, :]
    sl8 = alibi_slopes(8)
    al = np.zeros((128, 3, 8, 128), np.float32)
    for r in range(3):
        dist = np.abs(128 * (r - 1) + k - q).astype(np.float32)
        for h in range(8):
            al[:, r, h, :] = np.where(dist <= 128, -sl8[h] * dist, NEG)
    c["alibia"] = al.astype(NPBF)
    sl4 = alibi_slopes(4)
    dg = np.zeros((128, 4, 128), np.float32)
    for h in range(4):
        dg[:, h, :] = -sl4[h] * np.abs(k - q)
    c["diagb"] = dg.astype(NPBF)
    return c


def seg_positions(cfg, core):
    nbs, nbpo, nbpf = cfg["NBS"], cfg["NBPO"], cfg["NBPF"]
    ps_ = np.arange(nbs * 128)
    ppo = core * nbpo * 128 + np.arange(nbpo * 128)
    ppf = np.arange(nbpf * 128)
    return ps_, ppo, ppf


def prepare_core(cfg, core, inp, consts):
    nbs, nbpo, nbpf = cfg["NBS"], cfg["NBPO"], cfg["NBPF"]
    SS, SP = nbs * 128, nbpf * 128
    pad = PAD * 128
    m = dict(consts)
    xs = inp["x_sample"][core]
    xp = inp["x_prompt"][0]
    z = np.zeros((pad, D), np.float32)
    xpp = np.concatenate([z, xp, z], axis=0)
    lo = core * nbpo * 128
    xin = np.concatenate([z, xs, z, xpp[lo:lo + nbpo * 128 + 2 * pad], xpp], axis=0)
    m["xin"] = np.ascontiguousarray(xin)
    vs = np.concatenate([np.zeros(pad), np.ones(SS), np.zeros(pad)])
    vpf = np.concatenate([np.zeros(pad), np.ones(SP), np.zeros(pad)])
    valid = np.concatenate([vs, vpf[lo:lo + nbpo * 128 + 2 * pad], vpf]).astype(np.float32)
    m["validin"] = np.ascontiguousarray(valid.reshape(-1, 128).T)
    pp = inp["p_prompt"][:, 0]
    m["pin"] = np.ascontiguousarray(np.concatenate(
        [inp["p_sample"][:, core], pp[:, lo:lo + nbpo * 128], pp], axis=1))
    m["gvec"] = np.ascontiguousarray(np.stack([inp["norm_pre"][0], inp["norm_post"][0], inp["norm_pre"][1], inp["norm_post"][1]]))
    m["w_in_ab"] = inp["w_in_ab"][0]; m["w_out_ab"] = inp["w_out_ab"][0]
    m["w_in_cd"] = inp["w_in_cd"][0]; m["w_out_cd"] = inp["w_out_cd"][0]
    m["w_ple"] = inp["w_ple"]; m["w_gate"] = inp["w_ple_gate"]
    m["a_sink"] = inp["a_sink"]
    rpb = inp["b_rpb"][0]
    kr = np.arange(2)[:, None, None, None]; kc = np.arange(64)[None, :, None, None]
    qr = np.arange(2)[None, None, :, None]; qc = np.arange(64)[None, None, None, :]
    dcx = np.broadcast_to(np.clip(kc - qc, -15, 15) + 15, (2, 64, 2, 64)).reshape(128, 128)
    g = np.zeros((128, 7, 8, 128), np.float32)
    for r in range(7):
        drx = np.broadcast_to(np.clip(2 * (r - 3) + kr - qr + 7, 0, 14), (2, 64, 2, 64)).reshape(128, 128)
        for h in range(8):
            g[:, r, h, :] = rpb[h][drx, dcx]
    m["rpbg"] = g
    rm = np.zeros((128, 2, 5, 7, 128), np.float32)
    rows_any = 64
    nblk_any = rows_any // 2
    reps = {0: 0, 1: 1, 2: nblk_any // 2, 3: nblk_any - 2, 4: nblk_any - 1}
    rows_p = nbpf * 2
    for cl in range(5):
        for r in range(7):
            n = reps[cl]
            rm[:, 0, cl, r, :] = nb_rowmask_tile(rows_any, n, n + r - 3)
    seg_po = cfg["segs"][1]
    done = set()
    for i in range(nbpo):
        cl = blk_class(seg_po, i)
        if cl in done:
            continue
        done.add(cl)
        n = core * nbpo + i
        for r in range(7):
            rm[:, 1, cl, r, :] = nb_rowmask_tile(rows_p, n, n + r - 3)
    for cl in range(5):
        if cl not in done:
            rm[:, 1, cl] = NEG
    m["rowmask"] = rm.astype(NPBF)
    m["qkgain"] = np.ascontiguousarray(np.concatenate([np.tile(inp["c_q_norm"][0], 8), np.tile(inp["c_k_norm"][0], 2)])[None, :])
    ps_, ppo, ppf = seg_positions(cfg, core)
    pos = np.concatenate([ps_, ppo, ppf])
    inv = (10000.0 ** (-2.0 * np.arange(16) / 32)).astype(np.float32)
    row = (pos // 64).astype(np.float32); col = (pos % 64).astype(np.float32)
    ang = np.concatenate([row[:, None] * inv, col[:, None] * inv], axis=-1).astype(np.float32)
    m["rope"] = np.ascontiguousarray(np.concatenate([np.cos(ang), np.sin(ang)], axis=-1).astype(np.float32))
    m["lamv"] = np.ascontiguousarray(np.stack([inp["d_lambda_q1"][0], inp["d_lambda_k1"][0], inp["d_lambda_q2"][0], inp["d_lambda_k2"][0]])[None])
    m["subln"] = np.ascontiguousarray(inp["d_subln"][0][:, None])
    sl4 = alibi_slopes(4)
    posq = np.concatenate([ps_, ppo]).astype(np.float32)
    qa_, qb_ = np.floor(posq / 128), np.mod(posq, 128)
    qt = np.zeros((4, 2, 4, posq.size), np.float32)
    kt = np.zeros((4, 4, pos.size), np.float32)
    ka_, kb_ = np.floor(pos / 128).astype(np.float32), np.mod(pos, 128).astype(np.float32)
    for h in range(4):
        s = sl4[h]
        qt[h, 0] = np.stack([-s * 128 * qa_, -s * qb_, np.ones_like(qa_), np.ones_like(qa_)])
        qt[h, 1] = np.stack([s * 128 * qa_, s * qb_, -np.ones_like(qa_), -np.ones_like(qa_)])
        kt[h] = np.stack([np.ones_like(ka_), np.ones_like(ka_), s * 128 * ka_, s * kb_])
    m["qtab"] = qt.astype(NPBF)
    m["ktab"] = kt.astype(NPBF)
    return m


_CACHE = {}


def kernel(**inputs):
    inp = {k: np.asarray(v) for k, v in inputs.items()}
    nbs = inp["x_sample"].shape[1] // 128
    nbpf = inp["x_prompt"].shape[1] // 128
    cfg = cfg_make(nbs, nbpf)
    key = (nbs, nbpf)
    if key not in _CACHE:
        _CACHE[key] = build(cfg)
    nc = _CACHE[key]
    consts = host_constants(cfg)
    maps = [prepare_core(cfg, c, inp, consts) for c in range(NCORE)]
    res = run_bass_kernel_spmd(nc, maps, core_ids=list(range(NCORE)))
    nbpo = cfg["NBPO"]
    ys = np.zeros((NCORE, nbs * 128, D), np.float32)
    yp = np.zeros((1, nbpf * 128, D), np.float32)
    for c in range(NCORE):
        y = np.asarray(res.results[c]["yout"])
        ys[c] = y[:nbs * 128]
        yp[0, c * nbpo * 128:(c + 1) * nbpo * 128] = y[nbs * 128:]
    return (yp, ys)
```
